# Optimizing a Trainium2 kernel written in Bass

```python
import math
import jax, jax.numpy as jnp
from jax import lax
import numpy as np

D_MODEL = 1024
BATCH = 1
SEQ = 16384
DEPTH = 1
DEC_BATCH = 32
DEC_SEQ = 2048
PAST_LEN = 128

HEAD_DIM = 64
N_ATTN_HEADS = 8
N_KV_HEADS = 2
GQA_GROUP = N_ATTN_HEADS // N_KV_HEADS
ATTN_WIDTH = N_ATTN_HEADS * HEAD_DIM
KV_WIDTH = N_KV_HEADS * HEAD_DIM
N_MLSTM_HEADS = 8
MLSTM_WIDTH = N_MLSTM_HEADS * HEAD_DIM
MIX_WIDTH = ATTN_WIDTH + MLSTM_WIDTH
WINDOW = 128
BLOCK = 128
N_BUCKETS = 32
MAX_DISTANCE = 128
CHUNK = 128
CONV_WIDTH = 5
D_FF = 2816
EPS = 1e-6
NEG = -1e30
SPLIT_SIZES = (ATTN_WIDTH, KV_WIDTH, KV_WIDTH,
               MLSTM_WIDTH, MLSTM_WIDTH, MLSTM_WIDTH,
               MLSTM_WIDTH, 4 * N_MLSTM_HEADS)
IN_WIDTH = sum(SPLIT_SIZES)

kernel_name = "hymba_swa_mlstm_macaron_encoder"


def rmsnorm(x, g):
    xf = x.astype(jnp.float32)
    y = xf * lax.rsqrt(jnp.mean(xf * xf, axis=-1, keepdims=True) + EPS)
    return (y * g.astype(jnp.float32)).astype(x.dtype)


def swiglu(x, w_gu, w_down):
    g, u = jnp.split(x @ w_gu, 2, axis=-1)
    return (jax.nn.silu(g) * u) @ w_down


def t5_bucket(rel):
    nb = N_BUCKETS // 2
    ret = (rel > 0).astype(np.int32) * nb
    n = np.abs(rel)
    max_exact = nb // 2
    large = max_exact + (np.log(np.maximum(n, 1) / max_exact)
                         / math.log(MAX_DISTANCE / max_exact) * (nb - max_exact)).astype(np.int32)
    large = np.minimum(large, nb - 1)
    return (ret + np.where(n < max_exact, n, large)).astype(np.int32)


def banded_attention(q, k, v, rel_table, sink):
    B, S = q.shape[:2]
    nb = S // BLOCK
    f32 = jnp.float32
    qb = q.astype(f32).reshape(B, nb, BLOCK, N_KV_HEADS, GQA_GROUP, HEAD_DIM)
    pad = ((0, 0), (BLOCK, BLOCK), (0, 0), (0, 0))
    kb = jnp.pad(k.astype(f32), pad).reshape(B, nb + 2, BLOCK, N_KV_HEADS, HEAD_DIM)
    vb = jnp.pad(v.astype(f32), pad).reshape(B, nb + 2, BLOCK, N_KV_HEADS, HEAD_DIM)
    kw = jnp.concatenate([kb[:, :-2], kb[:, 1:-1], kb[:, 2:]], axis=2)
    vw = jnp.concatenate([vb[:, :-2], vb[:, 1:-1], vb[:, 2:]], axis=2)
    s = jnp.einsum('bnqhgd,bnkhd->bnhgqk', qb, kw) * (HEAD_DIM ** -0.5)
    qi = np.arange(BLOCK)[:, None]
    kj = np.arange(3 * BLOCK)[None, :]
    rel = (kj - BLOCK) - qi
    key_pos = (np.arange(nb)[:, None, None] - 1) * BLOCK + kj[None]
    mask = (np.abs(rel) <= WINDOW)[None] & (key_pos >= 0) & (key_pos < S)
    bias = rel_table.astype(f32)[t5_bucket(rel)]
    bias = jnp.transpose(bias, (2, 0, 1)).reshape(N_KV_HEADS, GQA_GROUP, BLOCK, 3 * BLOCK)
    s = jnp.where(jnp.asarray(mask)[None, :, None, None], s + bias[None, None], NEG)
    sk = sink.astype(f32).reshape(1, 1, N_KV_HEADS, GQA_GROUP, 1, 1)
    m = jnp.maximum(s.max(axis=-1, keepdims=True), sk)
    p = jnp.exp(s - m)
    den = p.sum(axis=-1, keepdims=True) + jnp.exp(sk - m)
    o = jnp.einsum('bnhgqk,bnkhd->bnqhgd', p / den, vw)
    return o.reshape(B, S, ATTN_WIDTH)


def centred_dwconv(x, w):
    C = x.shape[-1]
    return lax.conv_general_dilated(x, w[:, None, :].astype(x.dtype), window_strides=(1,),
                                    padding=[(CONV_WIDTH // 2, CONV_WIDTH // 2)],
                                    dimension_numbers=('NWC', 'WIO', 'NWC'), feature_group_count=C)


def mlstm_chunkwise(q, k, v, i_pre, f_pre):
    B, S, H, Dh = q.shape
    nc = S // CHUNK
    q = q.reshape(B, nc, CHUNK, H, Dh)
    k = k.reshape(B, nc, CHUNK, H, Dh)
    v = v.reshape(B, nc, CHUNK, H, Dh)
    ig = i_pre.reshape(B, nc, CHUNK, H)
    a = jnp.cumsum(jax.nn.log_sigmoid(f_pre).reshape(B, nc, CHUNK, H), axis=2)
    A = a[:, :, -1]
    w_end = A[:, :, None] - a + ig
    m_loc = w_end.max(axis=2)
    e = jnp.exp(w_end - m_loc[:, :, None])
    C_loc = jnp.einsum('bnlh,bnlhk,bnlhv->bnhkv', e, k, v)
    n_loc = jnp.einsum('bnlh,bnlhk->bnhk', e, k)

    def step(carry, xs):
        C, n, m = carry
        A_c, m_l, C_l, n_l = xs
        m_new = jnp.maximum(A_c + m, m_l)
        s_old = jnp.exp(A_c + m - m_new)
        s_loc = jnp.exp(m_l - m_new)
        C_new = s_old[..., None, None] * C + s_loc[..., None, None] * C_l
        n_new = s_old[..., None] * n + s_loc[..., None] * n_l
        return (C_new, n_new, m_new), (C, n, m)

    init = (jnp.zeros((B, H, Dh, Dh), jnp.float32), jnp.zeros((B, H, Dh), jnp.float32),
            jnp.full((B, H), NEG, jnp.float32))
    xs = (jnp.moveaxis(A, 1, 0), jnp.moveaxis(m_loc, 1, 0), jnp.moveaxis(C_loc, 1, 0), jnp.moveaxis(n_loc, 1, 0))
    _, (C_prev, n_prev, m_prev) = lax.scan(step, init, xs)
    C_prev = jnp.moveaxis(C_prev, 0, 1)
    n_prev = jnp.moveaxis(n_prev, 0, 1)
    m_prev = jnp.moveaxis(m_prev, 0, 1)
    D = a[:, :, :, None, :] - a[:, :, None, :, :] + ig[:, :, None, :, :]
    causal = np.tril(np.ones((CHUNK, CHUNK), dtype=bool))
    D = jnp.where(jnp.asarray(causal)[None, None, :, :, None], D, NEG)
    m_inter = a + m_prev[:, :, None]
    m_t = jnp.maximum(m_inter, D.max(axis=3))
    qk = jnp.einsum('bnthd,bnshd->bntsh', q, k) * jnp.exp(D - m_t[:, :, :, None])
    inter = jnp.exp(m_inter - m_t)
    num = jnp.einsum('bntsh,bnshd->bnthd', qk, v) + inter[..., None] * jnp.einsum('bnthk,bnhkv->bnthv', q, C_prev)
    den = qk.sum(axis=3) + inter * jnp.einsum('bnthk,bnhk->bnth', q, n_prev)
    h = num / jnp.maximum(jnp.abs(den), jnp.exp(-m_t))[..., None]
    return h.reshape(B, S, H, Dh)


def hybrid_layer(x, g_ffn1, w_ffn1_gu, w_ffn1_down, g_mix, w_in, w_conv, b_gates, attn_sink,
                 g_mlstm_out, w_out, g_ffn2, w_ffn2_gu, w_ffn2_down, rel_table):
    B, S, _ = x.shape
    f32 = jnp.float32
    h = x + 0.5 * swiglu(rmsnorm(x, g_ffn1), w_ffn1_gu, w_ffn1_down)
    u = rmsnorm(h, g_mix)
    proj = u @ w_in
    q_a, k_a, v_a, q_m, k_m, v_m, o_m, gate_pre = jnp.split(proj, np.cumsum(SPLIT_SIZES)[:-1].tolist(), axis=-1)
    attn = banded_attention(q_a.reshape(B, S, N_ATTN_HEADS, HEAD_DIM),
                            k_a.reshape(B, S, N_KV_HEADS, HEAD_DIM),
                            v_a.reshape(B, S, N_KV_HEADS, HEAD_DIM), rel_table, attn_sink)
    qk_m = jax.nn.silu(centred_dwconv(jnp.concatenate([q_m, k_m], axis=-1), w_conv)).astype(f32)
    qm = qk_m[..., :MLSTM_WIDTH].reshape(B, S, N_MLSTM_HEADS, HEAD_DIM)
    km = qk_m[..., MLSTM_WIDTH:].reshape(B, S, N_MLSTM_HEADS, HEAD_DIM) * (HEAD_DIM ** -0.5)
    vm = v_m.astype(f32).reshape(B, S, N_MLSTM_HEADS, HEAD_DIM)
    gates = gate_pre.astype(f32).reshape(B, S, 4, N_MLSTM_HEADS) + b_gates.astype(f32)
    h_fwd = mlstm_chunkwise(qm, km, vm, gates[:, :, 0], gates[:, :, 1])
    flip = lambda t: jnp.flip(t, axis=1)
    h_bwd = flip(mlstm_chunkwise(flip(qm), flip(km), flip(vm), flip(gates[:, :, 2]), flip(gates[:, :, 3])))
    hm = jax.nn.sigmoid(o_m.astype(f32)).reshape(B, S, N_MLSTM_HEADS, HEAD_DIM) * (h_fwd + h_bwd)
    hm = hm * lax.rsqrt(jnp.mean(hm * hm, axis=-1, keepdims=True) + EPS)
    hm = hm * g_mlstm_out.astype(f32).reshape(N_MLSTM_HEADS, HEAD_DIM)
    mix = jnp.concatenate([attn, hm.reshape(B, S, MLSTM_WIDTH)], axis=-1).astype(x.dtype) @ w_out
    h = h + mix
    h = h + 0.5 * swiglu(rmsnorm(h, g_ffn2), w_ffn2_gu, w_ffn2_down)
    return h


def trunk(x, layer_params, rel_table, g_final):
    h = x
    for l in range(DEPTH):
        h = hybrid_layer(h, *[p[l] for p in layer_params], rel_table)
    return rmsnorm(h, g_final)


def setup_inputs(seed: int = 0) -> dict:
    key = jax.random.key(seed)
    ks = jax.random.split(key, 20)
    nrm = lambda k, shape, s: jax.random.normal(k, shape, jnp.float32) * s
    gain = lambda k, shape: 1.0 + nrm(k, shape, 0.02)
    forget_bias = jnp.linspace(3.0, 6.0, N_MLSTM_HEADS, dtype=jnp.float32)
    gb = nrm(ks[8], (DEPTH, 4, N_MLSTM_HEADS), 0.1)
    b_gates = gb + jnp.stack([jnp.zeros_like(forget_bias), forget_bias,
                              jnp.zeros_like(forget_bias), forget_bias])[None]
    return {
        "x_prompt": nrm(ks[0], (BATCH, SEQ, D_MODEL), 1.0),
        "x_sample": nrm(ks[1], (DEC_BATCH, DEC_SEQ, D_MODEL), 1.0),
        "g_ffn1": gain(ks[2], (DEPTH, D_MODEL)),
        "w_ffn1_gu": nrm(ks[3], (DEPTH, D_MODEL, 2 * D_FF), D_MODEL ** -0.5),
        "w_ffn1_down": nrm(ks[4], (DEPTH, D_FF, D_MODEL), D_FF ** -0.5),
        "g_mix": gain(ks[5], (DEPTH, D_MODEL)),
        "w_in": nrm(ks[6], (DEPTH, D_MODEL, IN_WIDTH), D_MODEL ** -0.5),
        "w_conv": nrm(ks[7], (DEPTH, CONV_WIDTH, 2 * MLSTM_WIDTH), CONV_WIDTH ** -0.5),
        "b_gates": b_gates,
        "attn_sink": nrm(ks[9], (DEPTH, N_ATTN_HEADS), 0.5),
        "g_mlstm_out": gain(ks[10], (DEPTH, MLSTM_WIDTH)),
        "w_out": nrm(ks[11], (DEPTH, MIX_WIDTH, D_MODEL), MIX_WIDTH ** -0.5),
        "g_ffn2": gain(ks[12], (DEPTH, D_MODEL)),
        "w_ffn2_gu": nrm(ks[13], (DEPTH, D_MODEL, 2 * D_FF), D_MODEL ** -0.5),
        "w_ffn2_down": nrm(ks[14], (DEPTH, D_FF, D_MODEL), D_FF ** -0.5),
        "rel_bias_table": nrm(ks[15], (N_BUCKETS, N_ATTN_HEADS), 0.2),
        "g_final": gain(ks[16], (D_MODEL,)),
    }


def reference(x_prompt, x_sample, g_ffn1, w_ffn1_gu, w_ffn1_down, g_mix, w_in, w_conv, b_gates,
              attn_sink, g_mlstm_out, w_out, g_ffn2, w_ffn2_gu, w_ffn2_down, rel_bias_table, g_final):
    layer_params = (g_ffn1, w_ffn1_gu, w_ffn1_down, g_mix, w_in, w_conv, b_gates, attn_sink,
                    g_mlstm_out, w_out, g_ffn2, w_ffn2_gu, w_ffn2_down)
    y_prompt = trunk(x_prompt, layer_params, rel_bias_table, g_final)
    y_sample = trunk(x_sample, layer_params, rel_bias_table, g_final)
    return (y_prompt, y_sample)
```

```python
import math
from contextlib import ExitStack

import numpy as np
import concourse.bass as bass
import concourse.mybir as mybir
from concourse.bass_utils import run_bass_kernel_spmd

F32 = mybir.dt.float32
BF16 = mybir.dt.bfloat16
ALU = mybir.AluOpType
AF = mybir.ActivationFunctionType
AX = mybir.AxisListType

HD = 64
NEG = -1e30
EPS = 1e-6
IN_W = 2848
N_CORES = 8


_PSUM_PREFIXES = ("pa", "pg", "pu", "pd", "ptp", "pf")


class _Res:
    __slots__ = ("w", "reads")

    def __init__(self):
        self.w = None
        self.reads = []


class _Eng:
    def __init__(self, name, eng, sem):
        self.name = name
        self.eng = eng
        self.sem = sem
        self.count = 0
        self.waited = {}


class FW:
    def __init__(self, nc, stack):
        self.nc = nc
        self.stack = stack
        self.res = {}
        self.engs = {}
        for name in ("pe", "act", "dve", "pool", "sp"):
            eng = {"pe": nc.tensor, "act": nc.scalar, "dve": nc.vector,
                   "pool": nc.gpsimd, "sp": nc.sync}[name]
            sem = stack.enter_context(nc.semaphore("prog_" + name))
            self.engs[name] = _Eng(name, eng, sem)
        self.dsems = {}
        self.dcount = {}
        self.n_wait = 0
        self.n_inst = 0
        self.stopped = False
        self.bg_keys = set()

    def _r(self, key):
        r = self.res.get(key)
        if r is None:
            r = self.res[key] = _Res()
        return r

    def _deps(self, reads, writes):
        deps = {}

        def add(tok):
            s, v = tok
            if deps.get(s, (None, 0))[1] < v:
                deps[s] = (s, v)

        for k in reads:
            r = self._r(k)
            if r.w is not None:
                add(r.w)
            if k.startswith(_PSUM_PREFIXES):
                for tok in r.reads:
                    add(tok)
        for k in writes:
            r = self._r(k)
            if r.w is not None:
                add(r.w)
            for tok in r.reads:
                add(tok)
        return deps

    def _emit_waits(self, E, deps, skip_self=False):
        for s, (sem, v) in deps.items():
            if skip_self and sem is E.sem:
                continue
            if E.waited.get(s, 0) < v:
                E.eng.wait_ge(sem, v)
                E.waited[s] = v
                self.n_wait += 1

    def _commit(self, tok, reads, writes):
        for k in reads:
            r = self._r(k)
            r.reads.append(tok)
            if len(r.reads) > 48:
                best = {}
                for (s, v) in r.reads:
                    if best.get(s, (None, 0))[1] < v:
                        best[s] = (s, v)
                r.reads = list(best.values())
        for k in writes:
            r = self._r(k)
            r.w = tok
            r.reads = []

    def op(self, ename, fn, reads=(), writes=()):
        if self.stopped:
            return None
        E = self.engs[ename]
        deps = self._deps(reads, writes)
        self._emit_waits(E, deps, skip_self=(ename == "pe"))
        ins = fn(E.eng)
        E.count += 1
        ins.then_inc(E.sem, 1)
        tok = (E.sem, E.count)
        self._commit(tok, reads, writes)
        self.n_inst += 1
        return tok

    def dma(self, qname, out, in_, reads=(), writes=(), sem_key=None, **kw):
        if self.stopped:
            return None
        E = self.engs[qname]
        deps = self._deps(reads, writes)
        self._emit_waits(E, deps)
        if sem_key is None:
            sem_key = writes[0] if writes else reads[0]
        s = self.dsems.get(sem_key)
        if s is None:
            s = self.stack.enter_context(self.nc.semaphore("d%d" % len(self.dsems)))
            self.dsems[sem_key] = s
            self.dcount[sem_key] = 0
        E.eng.dma_start(out=out, in_=in_, **kw).then_inc(s, 16)
        self.dcount[sem_key] += 16
        tok = (s, self.dcount[sem_key])
        self._commit(tok, reads, writes)
        return tok

    def custom(self, qname, fn, reads=(), writes=(), sem_key=None, inc=1):
        if self.stopped:
            return None
        E = self.engs[qname]
        deps = self._deps(reads, writes)
        self._emit_waits(E, deps)
        s = self.dsems.get(sem_key)
        if s is None:
            s = self.stack.enter_context(self.nc.semaphore("c%d" % len(self.dsems)))
            self.dsems[sem_key] = s
            self.dcount[sem_key] = 0
        fn(E.eng).then_inc(s, inc)
        self.dcount[sem_key] += inc
        tok = (s, self.dcount[sem_key])
        self._commit(tok, reads, writes)
        return tok

    def _all_tokens(self, include_bg=False):
        final = {}
        for E in self.engs.values():
            if E.count:
                final[E.sem] = (E.sem, E.count)
        for k, s in self.dsems.items():
            if self.dcount[k] and (include_bg or k not in self.bg_keys):
                final[s] = (s, self.dcount[k])
        return final

    def barrier(self):
        if self.stopped:
            return
        final = self._all_tokens()
        for E in self.engs.values():
            self._emit_waits(E, final)
        keep = {k: v for k, v in self.res.items() if k in self.bg_keys}
        self.res = keep

    def finish(self):
        self._emit_waits(self.engs["sp"], self._all_tokens(include_bg=True))


def _t5_bucket(rel):
    nb = 16
    ret = (rel > 0).astype(np.int32) * nb
    n = np.abs(rel)
    max_exact = nb // 2
    large = max_exact + (np.log(np.maximum(n, 1) / max_exact)
                         / math.log(128 / max_exact) * (nb - max_exact)).astype(np.int32)
    large = np.minimum(large, nb - 1)
    return (ret + np.where(n < max_exact, n, large)).astype(np.int32)


def _bucket_onehot():
    oh = np.zeros((33, 640), np.float32)
    for j in range(640):
        rel = j - 255
        if abs(rel) <= 128:
            oh[int(_t5_bucket(np.array(rel))), j] = 1.0
        else:
            oh[32, j] = 1.0
    return oh


def build_program(S=2048, NSAMP=4, DM=1024, DFF=2816, n_ranks=8, with_prompt=True, stop=None):
    KC = DM // 128
    FC = DFF // 128
    NT = S // 128
    GT = 4
    NTP = NT + 2
    nc = bass.Bass("TRN2", target_bir_lowering=False)

    def din(name, shape, dt=F32):
        return nc.dram_tensor(name, list(shape), dt, kind="ExternalInput").ap()

    def dscr(name, shape, dt=F32):
        return nc.dram_tensor(name, list(shape), dt, kind="Internal").ap()

    x_samp = din("x_samp", [NSAMP * S, DM])
    x_prm = din("x_prm", [NTP * 128, DM])
    x_pfull = din("x_pfull", [n_ranks * S + 256, DM])
    w1gu = din("w1gu", [DM, 2 * DFF]); w1d = din("w1d", [DFF, DM])
    w2gu = din("w2gu", [DM, 2 * DFF]); w2d = din("w2d", [DFF, DM])
    win = din("win", [DM, IN_W]); wout = din("wout", [1024, DM])
    gains = din("gains", [4, DM])
    wconv = din("wconv", [5, 1024])
    bgates = din("bgates", [1, 32])
    sink = din("sink", [1, 8])
    gml = din("gml", [1, 512])
    reltab = din("reltab", [32, 8])
    onehot = din("onehot", [33, 640])
    flags = din("flags", [1, 18])
    y_samp = nc.dram_tensor("y_samp", [NSAMP * S, DM], F32, kind="ExternalOutput").ap()
    y_prm = nc.dram_tensor("y_prm", [S, DM], F32, kind="ExternalOutput").ap()

    hbuf_s = dscr("hbuf_s", [NSAMP * S, DM]); hbuf_p = dscr("hbuf_p", [NTP * 128, DM])
    hbuf_pf = dscr("hbuf_pf", [n_ranks * S + 256, DM])
    mix_s = dscr("mix_s", [NSAMP * S, 1024], BF16); mix_p = dscr("mix_p", [S, 1024], BF16)
    w1gu_s = dscr("w1gu_s", [FC, 128, KC * 256], BF16); w2gu_s = dscr("w2gu_s", [FC, 128, KC * 256], BF16)
    w1d_s = dscr("w1d_s", [128, FC * DM], BF16); w2d_s = dscr("w2d_s", [128, FC * DM], BF16)
    win_s = dscr("win_s", [128, KC * IN_W], BF16); wout_s = dscr("wout_s", [128, 8 * DM], BF16)
    fd_s = dscr("fd_s", [8, 640])
    GW = 8 * 65 + 8
    g_src = dscr("g_src", [128, GW]); g_dst = dscr("g_dst", [n_ranks * 128, GW])

    with ExitStack() as top:
        fw = FW(nc, top)

        uid = [0]

        def chk(name):
            if stop == name:
                fw.stopped = True

        def sb(st, name, shape, dt=F32):
            uid[0] += 1
            return st.enter_context(nc.sbuf_tensor("%s_%d" % (name, uid[0]), list(shape), dt))

        def ps(st, name, shape, dt=F32):
            uid[0] += 1
            return st.enter_context(nc.psum_tensor("%s_%d" % (name, uid[0]), list(shape), dt))

        ident = sb(top, "ident", [128, 128], BF16)
        maskF = sb(top, "maskF", [128, 128], BF16)
        maskB = sb(top, "maskB", [128, 128], BF16)
        ones_b = sb(top, "ones_b", [128, 128], BF16)
        gT = sb(top, "gT", [128, 3, KC])
        gfin_b = sb(top, "gfin_b", [128, DM])
        wcv = sb(top, "wcv", [128, 8, 5])
        bg_b = sb(top, "bg_b", [128, 32])
        sink_b = sb(top, "sink_b", [128, 8])
        gml_b = sb(top, "gml_b", [128, 512])
        flg = sb(top, "flg", [128, 18])
        bias = sb(top, "bias", [128, 8, 384])
        finals_t = sb(top, "finals", [128, 4, 2, 65])
        fin_A_t = sb(top, "fin_A", [128, 4, 2])
        GWc = 8 * 65 + 8
        ist = sb(top, "ist", [128, 4, 2, 65])
        dec = sb(top, "dec", [128, 4, 2])

        def mk(fn, w):
            fw.op("pool", fn, writes=[w], reads=[])

        mk(lambda e: e.memset(ident[:], 1.0), "ident")
        fw.op("pool", lambda e: e.affine_select(out=ident[:], in_=ident[:], pattern=[[-1, 128]],
              compare_op=ALU.is_equal, fill=0.0, base=0, channel_multiplier=1), reads=["ident"], writes=["ident"])
        mk(lambda e: e.memset(maskF[:], 1.0), "maskF")
        fw.op("pool", lambda e: e.affine_select(out=maskF[:], in_=maskF[:], pattern=[[1, 128]],
              compare_op=ALU.is_ge, fill=0.0, base=0, channel_multiplier=-1), reads=["maskF"], writes=["maskF"])
        mk(lambda e: e.memset(maskB[:], 1.0), "maskB")
        fw.op("pool", lambda e: e.affine_select(out=maskB[:], in_=maskB[:], pattern=[[-1, 128]],
              compare_op=ALU.is_ge, fill=0.0, base=0, channel_multiplier=1), reads=["maskB"], writes=["maskB"])
        mk(lambda e: e.memset(ones_b[:], 1.0), "ones_b")

        fw.dma("sp", gT[:], gains[0:3, :].rearrange("g (kc p) -> p g kc", p=128), writes=["gT"],
               allow_slow_non_contiguous=True)
        fw.dma("sp", gfin_b[:], gains[3:4, :].partition_broadcast(128), writes=["gfin_b"])
        for tap in range(5):
            fw.dma("sp", wcv[:, :, tap:tap + 1], wconv[tap:tap + 1, :].rearrange("j (c p) -> p c j", p=128), writes=["wcv"],
                   sem_key="wcv", allow_slow_non_contiguous=True)
        fw.dma("sp", bg_b[:], bgates.partition_broadcast(128), writes=["bg_b"])
        fw.dma("sp", sink_b[:], sink.partition_broadcast(128), writes=["sink_b"])
        fw.dma("sp", gml_b[:], gml.partition_broadcast(128), writes=["gml_b"])
        fw.dma("sp", flg[:], flags.partition_broadcast(128), writes=["flg"])

        def conv_gu(src, dst, tag):
            v = src.rearrange("(kc p) (two f) -> p kc two f", p=128, two=2)
            for fc in range(FC):
                for two in range(2):
                    fw.dma("pool", dst[fc].rearrange("p (kc two j) -> p kc two j", kc=KC, two=2)[:, :, two, :],
                           v[:, :, two, fc * 128:(fc + 1) * 128], writes=[tag], sem_key=tag)

        fw.bg_keys.update(["win_s", "wout_s", "w2gu_s", "w2d_s"])
        conv_gu(w1gu, w1gu_s, "w1gu_s")
        def conv_rows(src, dst, nchunk, tag):
            sv = src.rearrange("(c p) d -> p c d", p=128)
            dv = dst.rearrange("p (c d) -> p c d", c=nchunk)
            for c in range(nchunk):
                fw.dma("pool", dv[:, c, :], sv[:, c, :], writes=[tag], sem_key=tag)

        conv_rows(w1d, w1d_s, FC, "w1d_s")
        win_v = win.rearrange("(kc p) c -> p kc c", p=128)
        wins_v = win_s.rearrange("p (kc c) -> p kc c", kc=KC)
        for kc in range(KC):
            for two in range(2):
                fw.dma("pool", wins_v[:, kc, 0:512].rearrange("p (c two j) -> p c two j", two=2, j=64)[:, :, two, :],
                       win_v[:, kc, two * 256:(two + 1) * 256].rearrange("p (c j) -> p c j", j=64),
                       writes=["win_s"], sem_key="win_s")
        for kc in range(KC):
            fw.dma("pool", wins_v[:, kc, 512:IN_W], win_v[:, kc, 512:IN_W], writes=["win_s"], sem_key="win_s")
        conv_rows(wout, wout_s, 8, "wout_s")
        conv_gu(w2gu, w2gu_s, "w2gu_s")
        conv_rows(w2d, w2d_s, FC, "w2d_s")

        if stop == "conv":
            fw.finish()
            return nc
        with ExitStack() as st:
            tab = sb(st, "tab", [33, 8]); tab_hi = sb(st, "tab_hi", [33, 8], BF16)
            tab_r = sb(st, "tab_r", [33, 8]); tab_lo = sb(st, "tab_lo", [33, 8], BF16)
            oh = sb(st, "oh", [33, 640]); oh_b = sb(st, "oh_b", [33, 640], BF16)
            fsb = sb(st, "fsb", [8, 640])
            pf = ps(st, "pf", [8, 1024])
            fw.op("dve", lambda e: e.memset(tab[:], NEG), writes=["tab"])
            fw.dma("sp", tab[0:32, :], reltab, reads=[], writes=["tab"])
            fw.dma("sp", oh[:], onehot, writes=["oh"])
            fw.op("dve", lambda e: e.tensor_copy(out=oh_b[:], in_=oh[:]), reads=["oh"], writes=["oh_b"])
            fw.op("dve", lambda e: e.tensor_copy(out=tab_hi[:], in_=tab[:]), reads=["tab"], writes=["tab_hi"])
            fw.op("dve", lambda e: e.tensor_tensor(out=tab_r[:], in0=tab[:], in1=tab_hi[:], op=ALU.subtract),
                  reads=["tab", "tab_hi"], writes=["tab_r"])
            fw.op("dve", lambda e: e.tensor_copy(out=tab_lo[:], in_=tab_r[:]), reads=["tab_r"], writes=["tab_lo"])

            def fmm(e):
                r = None
                for half in range(2):
                    sl = slice(half * 512, min(640, (half + 1) * 512))
                    e.matmul(pf[:, sl], lhsT=tab_hi[:], rhs=oh_b[:, sl], start=True, stop=False)
                    r = e.matmul(pf[:, sl], lhsT=tab_lo[:], rhs=oh_b[:, sl], start=False, stop=True)
                return r
            fw.op("pe", fmm, reads=["tab_hi", "tab_lo", "oh_b"], writes=["pf"])
            fw.op("dve", lambda e: e.tensor_copy(out=fsb[:], in_=pf[:, 0:640]), reads=["pf"], writes=["fsb"])
            fw.dma("sp", fd_s, fsb[:], reads=["fsb"], writes=["fd_s"])
            for q in range(128):
                src = bass.AP(fd_s.tensor, 127 - q, [[0, 1], [640, 8], [1, 384]])
                fw.dma("sp", bias[q:q + 1, :, :], src, reads=["fd_s"], writes=["bias"], sem_key="bias")
            fw.barrier()

        if stop == "bias":
            fw.finish()
            return nc

        def ffn_phase(st, tag):
            B = {}
            B["xn"] = sb(st, tag + "xn", [128, GT, DM], BF16)
            B["junk"] = sb(st, tag + "junk", [128, DM], BF16)
            B["ss"] = sb(st, tag + "ss", [128, GT])
            B["rstd"] = sb(st, tag + "rstd", [128, GT])
            B["xnT"] = sb(st, tag + "xnT", [128, KC, GT * 128], BF16)
            B["actT"] = sb(st, tag + "actT", [128, FC, GT * 128], BF16)
            B["sg"] = [sb(st, tag + "sg%d" % i, [128, GT * 128]) for i in range(2)]
            B["wgu"] = [sb(st, tag + "wgu%d" % i, [128, KC, 2, 128], BF16) for i in range(3)]
            B["wd"] = sb(st, tag + "wd", [128, FC, DM], BF16)
            B["ptp"] = ps(st, tag + "ptp", [128, 8, 128], BF16)
            B["pg"] = [ps(st, tag + "pg%d" % i, [128, 512]) for i in range(2)]
            B["pu"] = [ps(st, tag + "pu%d" % i, [128, 512]) for i in range(2)]
            B["pd"] = [ps(st, tag + "pd%d" % i, [128, 512]) for i in range(2)]
            B["cnt"] = 0
            return B

        def rms_stats(x_t, xkey, nt, B, D):
            for i in range(nt):
                fw.op("act", lambda e, i=i: e.activation(out=B["junk"][:, 0:D], in_=x_t[:, i, :], func=AF.Square,
                                                         accum_out=B["ss"][:, i:i + 1]),
                      reads=[xkey], writes=["junk", "ss"])
            fw.op("act", lambda e: e.activation(out=B["rstd"][:, 0:nt], in_=B["ss"][:, 0:nt], func=AF.Sqrt,
                                                scale=1.0 / D, bias=EPS), reads=["ss"], writes=["rstd"])
            fw.op("dve", lambda e: e.reciprocal(out=B["rstd"][:, 0:nt], in_=B["rstd"][:, 0:nt]),
                  reads=["rstd"], writes=["rstd"])

        def norm_T(x_t, xkey, nt, B, gidx, dstT, dkey, col0=0):
            rms_stats(x_t, xkey, nt, B, DM)
            for i in range(nt):
                fw.op("dve", lambda e, i=i: e.tensor_scalar(out=B["xn"][:, i, :], in0=x_t[:, i, :],
                                                             scalar1=B["rstd"][:, i:i + 1], scalar2=None, op0=ALU.mult),
                      reads=[xkey, "rstd"], writes=["xn"])

                def tr(e, i=i):
                    r = None
                    for kc in range(KC):
                        r = e.transpose(out=B["ptp"][:, kc, :], in_=B["xn"][:, i, kc * 128:(kc + 1) * 128],
                                        identity=ident[:])
                    return r
                fw.op("pe", tr, reads=["xn", "ident"], writes=["ptp"])
                fw.op("dve", lambda e, i=i: e.tensor_tensor(
                    out=dstT[:, :, col0 + i * 128: col0 + (i + 1) * 128], in0=B["ptp"][:, 0:KC, :],
                    in1=gT[:, gidx, :].unsqueeze(2).to_broadcast([128, KC, 128]), op=ALU.mult),
                    reads=["ptp", "gT"], writes=[dkey])

        def ffn_group(B, x_t, xkey, nt, gidx, wgu_scr, wgukey, wd_scr, wdkey, out_t, okey):
            N = nt * 128
            fw.dma("sp", B["wd"][:], wd_scr.rearrange("p (fc d) -> p fc d", fc=FC), reads=[wdkey], writes=["wd"])
            norm_T(x_t, xkey, nt, B, gidx, B["xnT"], "xnT")
            for fc in range(FC):
                c = B["cnt"]; B["cnt"] += 1
                wb = B["wgu"][c % 3]; wk = "wgu%d" % (c % 3)
                pg = B["pg"][c % 2]; pu = B["pu"][c % 2]; sg = B["sg"][c % 2]
                pgk, puk, sgk = "pg%d" % (c % 2), "pu%d" % (c % 2), "sg%d" % (c % 2)
                fw.dma("sp", wb[:], wgu_scr[fc].rearrange("p (kc two j) -> p kc two j", kc=KC, two=2),
                       reads=[wgukey], writes=[wk])

                def mm(e, wb=wb, pg=pg, pu=pu):
                    r = None
                    for two, pp in ((0, pg), (1, pu)):
                        for kc in range(KC):
                            r = e.matmul(pp[:, 0:N], lhsT=wb[:, kc, two, :], rhs=B["xnT"][:, kc, 0:N],
                                         start=(kc == 0), stop=(kc == KC - 1))
                    return r
                fw.op("pe", mm, reads=[wk, "xnT"], writes=[pgk, puk])
                fw.op("act", lambda e, pg=pg, sg=sg: e.activation(out=sg[:, 0:N], in_=pg[:, 0:N], func=AF.Silu),
                      reads=[pgk], writes=[sgk])
                fw.op("dve", lambda e, pu=pu, sg=sg, fc=fc: e.tensor_tensor(out=B["actT"][:, fc, 0:N], in0=sg[:, 0:N],
                                                                          in1=pu[:, 0:N], op=ALU.mult),
                      reads=[sgk, puk], writes=["actT"])
            for i in range(nt):
                for dh in range(DM // 512):
                    c = B["cnt"]; B["cnt"] += 1
                    pd = B["pd"][c % 2]; pdk = "pd%d" % (c % 2)

                    def mm2(e, i=i, dh=dh, pd=pd):
                        r = None
                        for fc in range(FC):
                            r = e.matmul(pd[:], lhsT=B["actT"][:, fc, i * 128:(i + 1) * 128],
                                         rhs=B["wd"][:, fc, dh * 512:(dh + 1) * 512],
                                         start=(fc == 0), stop=(fc == FC - 1))
                        return r
                    fw.op("pe", mm2, reads=["actT", "wd"], writes=[pdk])
                    fw.op("dve", lambda e, i=i, dh=dh, pd=pd: e.scalar_tensor_tensor(
                        out=out_t[:, i, dh * 512:(dh + 1) * 512], in0=pd[:], scalar=0.5,
                        in1=x_t[:, i, dh * 512:(dh + 1) * 512], op0=ALU.mult, op1=ALU.add),
                        reads=[pdk, xkey], writes=[okey])

        def phase1(x_src, h_dst, ntiles):
            with ExitStack() as st:
                B = ffn_phase(st, "p1")
                xt = [sb(st, "p1x%d" % i, [128, GT, DM]) for i in range(2)]
                ht = sb(st, "p1h", [128, GT, DM])
                g0 = 0; gi = 0
                while g0 < ntiles:
                    nt = min(GT, ntiles - g0)
                    x_t = xt[gi % 2]; xk = "x%d" % (gi % 2)
                    fw.dma("sp", x_t[:, 0:nt, :], x_src[g0 * 128:(g0 + nt) * 128, :].rearrange("(i p) d -> p i d", p=128),
                           writes=[xk])
                    ffn_group(B, x_t, xk, nt, 0, w1gu_s, "w1gu_s", w1d_s, "w1d_s", ht, "ht")
                    fw.dma("sp", h_dst[g0 * 128:(g0 + nt) * 128, :].rearrange("(i p) d -> p i d", p=128), ht[:, 0:nt, :],
                           reads=["ht"], writes=["hdst"])
                    g0 += nt; gi += 1
                fw.barrier()

        def phase3(h_src, mix_src, y_dst, ntiles):
            with ExitStack() as st:
                B = ffn_phase(st, "p3")
                hin = sb(st, "p3hin", [128, GT, DM])
                h2 = sb(st, "p3h2", [128, GT, DM])
                mt = sb(st, "p3mt", [128, GT, 1024], BF16)
                mT = sb(st, "p3mT", [128, 8, GT * 128], BF16)
                wo = sb(st, "p3wo", [128, 8, DM], BF16)
                fw.dma("sp", wo[:], wout_s.rearrange("p (cc d) -> p cc d", cc=8), reads=["wout_s"], writes=["wo"])
                g0 = 0
                while g0 < ntiles:
                    nt = min(GT, ntiles - g0)
                    rows = slice(g0 * 128, (g0 + nt) * 128)
                    fw.dma("sp", hin[:, 0:nt, :], h_src[rows, :].rearrange("(i p) d -> p i d", p=128), writes=["hin"])
                    fw.dma("sp", mt[:, 0:nt, :], mix_src[rows, :].rearrange("(i p) d -> p i d", p=128), writes=["mt"])
                    for i in range(nt):
                        def tr(e, i=i):
                            r = None
                            for cc in range(8):
                                r = e.transpose(out=B["ptp"][:, cc, :], in_=mt[:, i, cc * 128:(cc + 1) * 128], identity=ident[:])
                            return r
                        fw.op("pe", tr, reads=["mt", "ident"], writes=["ptp"])
                        fw.op("act", lambda e, i=i: e.copy(out=mT[:, :, i * 128:(i + 1) * 128], in_=B["ptp"][:, 0:8, :]),
                              reads=["ptp"], writes=["mT"])
                    for i in range(nt):
                        for dh in range(DM // 512):
                            c = B["cnt"]; B["cnt"] += 1
                            pd = B["pd"][c % 2]; pdk = "pd%d" % (c % 2)

                            def mm(e, i=i, dh=dh, pd=pd):
                                r = None
                                for cc in range(8):
                                    r = e.matmul(pd[:], lhsT=mT[:, cc, i * 128:(i + 1) * 128],
                                                 rhs=wo[:, cc, dh * 512:(dh + 1) * 512], start=(cc == 0), stop=(cc == 7))
                                return r
                            fw.op("pe", mm, reads=["mT", "wo"], writes=[pdk])
                            fw.op("dve", lambda e, i=i, dh=dh, pd=pd: e.tensor_tensor(
                                out=h2[:, i, dh * 512:(dh + 1) * 512], in0=pd[:], in1=hin[:, i, dh * 512:(dh + 1) * 512],
                                op=ALU.add), reads=[pdk, "hin"], writes=["h2"])
                    ffn_group(B, h2, "h2", nt, 2, w2gu_s, "w2gu_s", w2d_s, "w2d_s", h2, "h2")
                    rms_stats(h2, "h2", nt, B, DM)
                    for i in range(nt):
                        fw.op("dve", lambda e, i=i: e.scalar_tensor_tensor(
                            out=hin[:, i, :], in0=h2[:, i, :], scalar=B["rstd"][:, i:i + 1], in1=gfin_b[:],
                            op0=ALU.mult, op1=ALU.mult), reads=["h2", "rstd", "gfin_b"], writes=["hin"])
                    fw.dma("sp", y_dst[rows, :].rearrange("(i p) d -> p i d", p=128), hin[:, 0:nt, :],
                           reads=["hin"], writes=["ydst"])
                    g0 += nt
                fw.barrier()

        LN8 = math.log(0.125)
        VS = 80

        def phase2(h_src, mix_dst, prompt, mode, init_state=None):
            full = mode == "full"
            off = 1 if prompt else 0
            ntl = NT + 2 * off
            with ExitStack() as st:
                unT = sb(st, "unT", [128, KC, ntl * 128], BF16)
                gts = sb(st, "gts", [128, NT, 32])
                wsb = sb(st, "wsb", [128, KC, 768], BF16)
                ptp2 = ps(st, "p2ptp", [128, 8, 128], BF16)
                NB = {"ptp": ptp2}
                pA = [ps(st, "p2pa%d" % i, [128, 512]) for i in range(7)]
                cs = sb(st, "cs", [128, 2, NT, 8]); es = sb(st, "es", [128, 2, NT, 8])
                rt = sb(st, "rt", [128, 2, NT, 8]); eA = sb(st, "eA", [128, 2, NT, 8])
                Asum = sb(st, "Asum", [128, 2, 8])
                with ExitStack() as us:
                    NBu = {"xn": sb(us, "p2xn", [128, GT, DM], BF16), "junk": sb(us, "p2junk", [128, DM], BF16),
                           "ss": sb(us, "p2ss", [128, GT]), "rstd": sb(us, "p2rstd", [128, GT]), "ptp": ptp2}
                    hld = [sb(us, "p2h%d" % i, [128, GT, DM]) for i in range(2)]
                    g0 = 0; gi = 0
                    while g0 < ntl:
                        nt = min(GT, ntl - g0)
                        ht = hld[gi % 2]; hk = "hld%d" % (gi % 2)
                        fw.dma("sp", ht[:, 0:nt, :], h_src[g0 * 128:(g0 + nt) * 128, :].rearrange("(i p) d -> p i d", p=128),
                               writes=[hk])
                        norm_T(ht, hk, nt, NBu, 1, unT, "unT", col0=g0 * 128)
                        g0 += nt; gi += 1
                    fw.barrier()
                chk("p2a")
                winv = win_s.rearrange("p (kc c) -> p kc c", kc=KC)

                def load_w(c0, c1):
                    fw.dma("sp", wsb[:, :, 0:c1 - c0], winv[:, :, c0:c1], reads=["win_s"], writes=["wsb"])

                def proj_fm(col, ncol, tok0, ntok, dst_ps):
                    def f(e):
                        r = None
                        for kc in range(KC):
                            r = e.matmul(dst_ps[0:ncol, 0:ntok], lhsT=wsb[:, kc, col:col + ncol],
                                         rhs=unT[:, kc, tok0:tok0 + ntok], start=(kc == 0), stop=(kc == KC - 1))
                        return r
                    return f

                def proj_tm(col, ncol, tile, dst_ps):
                    def f(e):
                        r = None
                        for kc in range(KC):
                            r = e.matmul(dst_ps[:, 0:ncol], lhsT=unT[:, kc, tile * 128:(tile + 1) * 128],
                                         rhs=wsb[:, kc, col:col + ncol], start=(kc == 0), stop=(kc == KC - 1))
                        return r
                    return f

                with ExitStack() as gs:
                    load_w(2816, 2848)
                    for i in range(NT):
                        pp = pA[i % 2]; pk = "pa%d" % (i % 2)
                        fw.op("pe", proj_tm(0, 32, i + off, pp), reads=["wsb", "unT"], writes=[pk])
                        fw.op("dve", lambda e, i=i, pp=pp: e.tensor_tensor(out=gts[:, i, :], in0=pp[:, 0:32], in1=bg_b[:],
                                                                          op=ALU.add), reads=[pk, "bg_b"], writes=["gts"])
                    chk("p2b")
                    W = NT * 8
                    g4 = gts[:].rearrange("p n (a h) -> p n a h", a=4)
                    fx = sb(gs, "fx", [128, 2, NT, 8]); t1 = sb(gs, "t1", [128, 2, NT, 8]); t2 = sb(gs, "t2", [128, 2, NT, 8])
                    lf = sb(gs, "lf", [128, 2, NT, 8])
                    parts = [sb(gs, "lfp%d" % i, [128, 2, NT, 8], BF16) for i in range(3)]
                    for d in range(2):
                        fw.op("dve", lambda e, d=d: e.tensor_copy(out=fx[:, d, :, :], in_=g4[:, :, 1 + 2 * d, :]),
                              reads=["gts"], writes=["fx"])
                    fw.op("dve", lambda e: e.tensor_single_scalar(out=t2[:], in_=fx[:], scalar=0.0, op=ALU.min),
                          reads=["fx"], writes=["t2"])
                    fw.op("dve", lambda e: e.scalar_tensor_tensor(out=t1[:], in0=t2[:], scalar=2.0, in1=fx[:],
                                                                  op0=ALU.mult, op1=ALU.subtract),
                          reads=["fx", "t2"], writes=["t1"])
                    fw.op("act", lambda e: e.activation(out=t1[:], in_=t1[:], func=AF.Exp),
                          reads=["t1"], writes=["t1"])
                    fw.op("act", lambda e: e.activation(out=t1[:], in_=t1[:], func=AF.Ln, bias=1.0),
                          reads=["t1"], writes=["t1"])
                    fw.op("dve", lambda e: e.tensor_tensor(out=lf[:], in0=t2[:], in1=t1[:], op=ALU.subtract),
                          reads=["t1", "t2"], writes=["lf"])
                    fw.op("dve", lambda e: e.tensor_copy(out=parts[0][:], in_=lf[:]), reads=["lf"], writes=["lfp0"])
                    fw.op("dve", lambda e: e.tensor_tensor(out=t1[:], in0=lf[:], in1=parts[0][:], op=ALU.subtract),
                          reads=["lf", "lfp0"], writes=["t1"])
                    fw.op("dve", lambda e: e.tensor_copy(out=parts[1][:], in_=t1[:]), reads=["t1"], writes=["lfp1"])
                    fw.op("dve", lambda e: e.tensor_tensor(out=t2[:], in0=t1[:], in1=parts[1][:], op=ALU.subtract),
                          reads=["t1", "lfp1"], writes=["t2"])
                    fw.op("dve", lambda e: e.tensor_copy(out=parts[2][:], in_=t2[:]), reads=["t2"], writes=["lfp2"])
                    pcs = pA[2]

                    def cums(e):
                        r = None
                        for d in range(2):
                            tri = maskF if d == 0 else maskB
                            for (mat, o) in ((tri, d * W), (ones_b, 2 * W + d * W)):
                                for k in range(3):
                                    r = e.matmul(pcs[:, o:o + W], lhsT=mat[:],
                                                 rhs=parts[k][:, d, :, :].rearrange("p n h -> p (n h)"),
                                                 start=(k == 0), stop=(k == 2))
                        return r
                    chk("p2c0")
                    fw.op("pe", cums, reads=["lfp0", "lfp1", "lfp2", "maskF", "maskB", "ones_b"], writes=["pa2"])
                    chk("p2c1")
                    pcv = pcs[:, 0:4 * W].rearrange("p (q d n h) -> p q d n h", q=2, d=2, n=NT)
                    for d in range(2):
                        fw.op("dve", lambda e, d=d: e.tensor_tensor(out=t1[:, d, :, :], in0=g4[:, :, 2 * d, :],
                                                                     in1=pcv[:, 0, d, :, :], op=ALU.subtract),
                              reads=["gts", "pa2"], writes=["t1"])
                    fw.op("act", lambda e: e.activation(out=cs[:], in_=t1[:], func=AF.Exp, bias=LN8),
                          reads=["t1"], writes=["cs"])
                    fw.op("dve", lambda e: e.tensor_tensor(out=t2[:], in0=t1[:], in1=pcv[:, 1, :, :, :], op=ALU.add),
                          reads=["t1", "pa2"], writes=["t2"])
                    fw.op("act", lambda e: e.activation(out=es[:], in_=t2[:], func=AF.Exp, bias=LN8),
                          reads=["t2"], writes=["es"])
                    chk("p2c2")
                    fw.op("act", lambda e: e.activation(out=rt[:], in_=pcv[:, 0, :, :, :], func=AF.Exp),
                          reads=["pa2"], writes=["rt"])
                    fw.op("act", lambda e: e.activation(out=eA[:], in_=pcv[:, 1, :, :, :], func=AF.Exp),
                          reads=["pa2"], writes=["eA"])
                    chk("p2c3")
                    if not full:
                        fw.op("dve", lambda e: e.tensor_copy(out=t1[:], in_=pcv[:, 1, :, :, :]), reads=["pa2"], writes=["t1"])
                        fw.op("dve", lambda e: e.memset(lf[:], 0.0), reads=["lf"], writes=["lf"])
                        for c_ in range(NT - 2, -1, -1):
                            fw.op("dve", lambda e, c_=c_: e.tensor_tensor(out=lf[:, 0, c_, :], in0=lf[:, 0, c_ + 1, :],
                                                                         in1=t1[:, 0, c_ + 1, :], op=ALU.add),
                                  reads=["lf", "t1"], writes=["lf"])
                        for c_ in range(1, NT):
                            fw.op("dve", lambda e, c_=c_: e.tensor_tensor(out=lf[:, 1, c_, :], in0=lf[:, 1, c_ - 1, :],
                                                                         in1=t1[:, 1, c_ - 1, :], op=ALU.add),
                                  reads=["lf", "t1"], writes=["lf"])
                        fw.op("dve", lambda e: e.tensor_tensor(out=t2[:], in0=t2[:], in1=lf[:], op=ALU.add),
                              reads=["lf", "t2", "es"], writes=["t2"])
                        fw.op("act", lambda e: e.activation(out=es[:], in_=t2[:], func=AF.Exp, bias=LN8),
                              reads=["t2"], writes=["es"])
                        chk("p2c4")
                        fw.op("dve", lambda e: e.tensor_copy(out=Asum[:], in_=t1[:, :, 0, :]), reads=["t1"], writes=["Asum"])
                        for n_ in range(1, NT):
                            fw.op("dve", lambda e, n_=n_: e.tensor_tensor(out=Asum[:], in0=Asum[:], in1=t1[:, :, n_, :], op=ALU.add),
                                  reads=["t1", "Asum"], writes=["Asum"])
                    chk("p2c5")
                    fw.barrier()

                chk("p2c")
                if full:
                    with ExitStack() as at:
                        qaT = sb(at, "qaT", [128, 4, S], BF16)
                        kT = sb(at, "kT", [128, ntl * 128], BF16)
                        vtm = sb(at, "vtm", [128, ntl, 128], BF16)
                        ssb = sb(at, "ssb", [128, 4, 384]); pbf = sb(at, "pbf", [128, 4, 384], BF16)
                        pts = [sb(at, "pts%d" % i, [128, 3, 128], BF16) for i in range(2)]
                        mx = sb(at, "mx", [128, 4]); negm = sb(at, "negm", [128, 4]); rs = sb(at, "rs", [128, 4])
                        tmp4 = sb(at, "tmp4", [128, 4]); rinv = sb(at, "rinv", [128, 4])
                        mixa = [sb(at, "mixa%d" % i, [128, 512], BF16) for i in range(2)]
                        load_w(0, 768)
                        TB = 512
                        for c in range(4):
                            for t0 in range(0, S, TB):
                                n = min(TB, S - t0)
                                pp = pA[5 + (c + t0 // TB) % 2]; pk = "pa%d" % (5 + (c + t0 // TB) % 2)
                                fw.op("pe", proj_fm(c * 128, 128, off * 128 + t0, n, pp), reads=["wsb", "unT"], writes=[pk])
                                fw.op("act", lambda e, c=c, t0=t0, n=n, pp=pp: e.mul(
                                    out=qaT[:, c, t0:t0 + n], in_=pp[:, 0:n], mul=0.125),
                                    reads=[pk], writes=["qaT"])
                        for t0 in range(0, ntl * 128, TB):
                            n = min(TB, ntl * 128 - t0)
                            pp = pA[5 + (t0 // TB) % 2]; pk = "pa%d" % (5 + (t0 // TB) % 2)
                            fw.op("pe", proj_fm(512, 128, t0, n, pp), reads=["wsb", "unT"], writes=[pk])
                            fw.op("act", lambda e, t0=t0, n=n, pp=pp: e.copy(out=kT[:, t0:t0 + n], in_=pp[:, 0:n]),
                                  reads=[pk], writes=["kT"])
                        for i in range(ntl):
                            pp = pA[5 + i % 2]; pk = "pa%d" % (5 + i % 2)
                            fw.op("pe", proj_tm(640, 128, i, pp), reads=["wsb", "unT"], writes=[pk])
                            fw.op("act", lambda e, i=i, pp=pp: e.copy(out=vtm[:, i, :], in_=pp[:, 0:128]),
                                  reads=[pk], writes=["vtm"])
                        pO = pA[4]
                        for i in range(NT):
                            ti = i + off
                            lo = ti - 1 if ti - 1 >= 0 else ti
                            hi = ti + 1 if ti + 1 < ntl else ti
                            nk = hi - lo + 1
                            b0 = (lo - (ti - 1)) * 128
                            ma = mixa[i % 2]; mak = "mixa%d" % (i % 2)
                            for g in range(2):
                                def smm(e, g=g, i=i, lo=lo, nk=nk):
                                    r = None
                                    for c in range(4):
                                        r = e.matmul(pA[c][:, 0:nk * 128], lhsT=qaT[g * 64:(g + 1) * 64, c, i * 128:(i + 1) * 128],
                                                     rhs=kT[g * 64:(g + 1) * 64, lo * 128:(lo + nk) * 128], start=True, stop=True)
                                    return r
                                fw.op("pe", smm, reads=["qaT", "kT"], writes=["pa0", "pa1", "pa2", "pa3"])
                                for c in range(4):
                                    fw.op("dve", lambda e, c=c, g=g, nk=nk, b0=b0: e.tensor_tensor(
                                        out=ssb[:, c, 0:nk * 128], in0=pA[c][:, 0:nk * 128],
                                        in1=bias[:, g * 4 + c, b0:b0 + nk * 128], op=ALU.add),
                                        reads=["pa%d" % c, "bias"], writes=["ssb"])
                                if prompt and i == 0:
                                    fw.op("dve", lambda e: e.tensor_scalar(out=ssb[:, :, 0:128], in0=ssb[:, :, 0:128],
                                                                            scalar1=flg[:, 0:1], scalar2=None, op0=ALU.add),
                                          reads=["ssb", "flg"], writes=["ssb"])
                                if prompt and i == NT - 1:
                                    fw.op("dve", lambda e: e.tensor_scalar(out=ssb[:, :, 256:384], in0=ssb[:, :, 256:384],
                                                                            scalar1=flg[:, 1:2], scalar2=None, op0=ALU.add),
                                          reads=["ssb", "flg"], writes=["ssb"])
                                fw.op("dve", lambda e, nk=nk: e.tensor_reduce(out=mx[:], in_=ssb[:, :, 0:nk * 128], axis=AX.X,
                                                                              op=ALU.max), reads=["ssb"], writes=["mx"])
                                fw.op("dve", lambda e, g=g: e.tensor_tensor(out=mx[:], in0=mx[:], in1=sink_b[:, g * 4:(g + 1) * 4],
                                                                             op=ALU.max), reads=["mx", "sink_b"], writes=["mx"])
                                fw.op("dve", lambda e: e.tensor_scalar(out=negm[:], in0=mx[:], scalar1=-1.0, scalar2=None,
                                                                        op0=ALU.mult), reads=["mx"], writes=["negm"])
                                for c in range(4):
                                    fw.op("act", lambda e, c=c, nk=nk: e.activation(
                                        out=pbf[:, c, 0:nk * 128], in_=ssb[:, c, 0:nk * 128], func=AF.Exp,
                                        bias=negm[:, c:c + 1], accum_out=rs[:, c:c + 1]),
                                        reads=["ssb", "negm"], writes=["pbf", "rs"])
                                fw.op("dve", lambda e, g=g: e.tensor_tensor(out=tmp4[:], in0=sink_b[:, g * 4:(g + 1) * 4], in1=mx[:],
                                                                             op=ALU.subtract), reads=["mx", "sink_b"], writes=["tmp4"])
                                fw.op("act", lambda e: e.activation(out=tmp4[:], in_=tmp4[:], func=AF.Exp),
                                      reads=["tmp4"], writes=["tmp4"])
                                fw.op("dve", lambda e: e.tensor_tensor(out=tmp4[:], in0=tmp4[:], in1=rs[:], op=ALU.add),
                                      reads=["tmp4", "rs"], writes=["tmp4"])
                                fw.op("dve", lambda e: e.reciprocal(out=rinv[:], in_=tmp4[:]), reads=["tmp4"], writes=["rinv"])
                                for c in range(4):
                                    pt = pts[c % 2]; ptk = "pts%d" % (c % 2)

                                    def trp(e, c=c, nk=nk):
                                        r = None
                                        for kb in range(nk):
                                            r = e.transpose(out=NB["ptp"][:, kb, :], in_=pbf[:, c, kb * 128:(kb + 1) * 128],
                                                            identity=ident[:])
                                        return r
                                    fw.op("pe", trp, reads=["pbf", "ident"], writes=["ptp"])
                                    fw.op("act", lambda e, pt=pt, nk=nk: e.copy(out=pt[:, 0:nk, :], in_=NB["ptp"][:, 0:nk, :]),
                                          reads=["ptp"], writes=[ptk])

                                    def pv(e, c=c, nk=nk, lo=lo, g=g, pt=pt):
                                        r = None
                                        for kb in range(nk):
                                            r = e.matmul(pO[:, c * 64:(c + 1) * 64], lhsT=pt[:, kb, :],
                                                         rhs=vtm[:, lo + kb, g * 64:(g + 1) * 64],
                                                         start=(kb == 0), stop=(kb == nk - 1))
                                        return r
                                    fw.op("pe", pv, reads=[ptk, "vtm"], writes=["pa4"])
                                fw.op("dve", lambda e, g=g, ma=ma: e.tensor_tensor(
                                    out=ma[:, g * 256:(g + 1) * 256].rearrange("p (c d) -> p c d", c=4),
                                    in0=pO[:, 0:256].rearrange("p (c d) -> p c d", c=4),
                                    in1=rinv[:].unsqueeze(2).to_broadcast([128, 4, 64]), op=ALU.mult),
                                    reads=["pa4", "rinv"], writes=[mak])
                            fw.dma("sp", mix_dst[i * 128:(i + 1) * 128, 0:512], ma[:], reads=[mak], writes=["mixdst"])
                        fw.barrier()

                if full:
                    chk("s2a")
                finals, fin_A = finals_t, fin_A_t
                with ExitStack() as ml:
                    pre = [sb(ml, "pre%d" % i, [128, S + 4]) for i in range(2)]
                    acc = sb(ml, "acc", [128, S])
                    qkT = [sb(ml, "qkT%d" % i, [128, S], BF16) for i in range(2)]
                    ktok = sb(ml, "ktok", [128, NT, 128], BF16)
                    v1 = sb(ml, "v1", [128, NT, 2, VS], BF16)
                    v1s = [sb(ml, "v1s%d" % d, [128, NT, 2, VS], BF16) for d in range(2)]
                    v1e = [sb(ml, "v1e%d" % d, [128, NT, 2, VS], BF16) for d in range(2)]
                    og = sb(ml, "og", [128, NT, 128])
                    hsum = sb(ml, "hsum", [128, NT, 128])
                    stt = [sb(ml, "stt%d" % d, [128, 65]) for d in range(2)]
                    stb = [sb(ml, "stb%d" % d, [128, 2, VS], BF16) for d in range(2)]
                    qblk = sb(ml, "qblk", [128, NT, 2, 128], BF16)
                    PTs = [sb(ml, "PT%d" % d, [128, 2, 128], BF16) for d in range(2)]
                    dd = [sb(ml, "dd%d" % d, [128, 2]) for d in range(2)]
                    rr = [sb(ml, "rr%d" % d, [128, 2]) for d in range(2)]
                    msq = sb(ml, "msq", [128, NT, 2])
                    mixm = sb(ml, "mixm", [128, NT, 128], BF16)
                    fw.op("dve", lambda e: e.memset(v1[:], 1.0), writes=["v1"])
                    fw.op("dve", lambda e: e.memset(qblk[:], 0.0), writes=["qblk"])
                    for d in range(2):
                        fw.op("dve", lambda e, d=d: e.memset(stb[d][:], 0.0), writes=["stb%d" % d])
                    for j in range(4):
                        for bi, c0 in enumerate((768, 1280, 1792, 2304)):
                            fw.dma("sp", wsb[:, :, bi * 128:(bi + 1) * 128], winv[:, :, c0 + j * 128:c0 + (j + 1) * 128],
                                   reads=["win_s"], writes=["wsb"])
                        for qi in range(2):
                            if qi == 0 and not full:
                                continue
                            pr = pre[qi]; prk = "pre%d" % qi
                            if prompt:
                                for (t0, n, dcol) in ((off * 128 - 2, 2, 0), ((off + NT) * 128, 2, S + 2)):
                                    pp = pA[5]; pk = "pa5"
                                    fw.op("pe", proj_fm(qi * 128, 128, t0, n, pp), reads=["wsb", "unT"], writes=[pk])
                                    fw.op("act", lambda e, pr=pr, dcol=dcol, pp=pp: e.copy(out=pr[:, dcol:dcol + 2], in_=pp[:, 0:2]),
                                          reads=[pk], writes=[prk])
                            else:
                                fw.op("dve", lambda e, pr=pr: e.memset(pr[:, 0:2], 0.0), writes=[prk])
                                fw.op("dve", lambda e, pr=pr: e.memset(pr[:, S + 2:S + 4], 0.0), writes=[prk])
                            for t0 in range(0, S, 512):
                                n = min(512, S - t0)
                                pp = pA[5 + (t0 // 512) % 2]; pk = "pa%d" % (5 + (t0 // 512) % 2)
                                fw.op("pe", proj_fm(qi * 128, 128, off * 128 + t0, n, pp), reads=["wsb", "unT"], writes=[pk])
                                fw.op("act", lambda e, pr=pr, t0=t0, n=n, pp=pp: e.copy(out=pr[:, 2 + t0:2 + t0 + n], in_=pp[:, 0:n]),
                                      reads=[pk], writes=[prk])
                            ch = qi * 4 + j
                            fw.op("dve", lambda e, pr=pr, ch=ch: e.tensor_scalar(out=acc[:], in0=pr[:, 0:S], scalar1=wcv[:, ch, 0:1],
                                                                                  scalar2=None, op0=ALU.mult),
                                  reads=[prk, "wcv"], writes=["acc"])
                            for tap in range(1, 5):
                                fw.op("dve", lambda e, pr=pr, ch=ch, tap=tap: e.scalar_tensor_tensor(
                                    out=acc[:], in0=pr[:, tap:tap + S], scalar=wcv[:, ch, tap:tap + 1], in1=acc[:],
                                    op0=ALU.mult, op1=ALU.add), reads=[prk, "wcv", "acc"], writes=["acc"])
                            fw.op("act", lambda e, qi=qi: e.activation(out=qkT[qi][:], in_=acc[:], func=AF.Silu),
                                  reads=["acc"], writes=["qkT%d" % qi])
                            if qi == 0 and full:
                                for hh in range(2):
                                    hs = slice(hh * 64, (hh + 1) * 64)
                                    fw.op("act", lambda e, hh=hh, hs=hs: e.activation(
                                        out=qblk[hs, :, hh, :], in_=acc[hs, :].rearrange("p (n t) -> p n t", t=128), func=AF.Silu),
                                        reads=["acc"], writes=["qblk"])
                        chk("p2d")
                        for i in range(NT):
                            fw.op("pe", lambda e, i=i: e.transpose(out=NB["ptp"][:, i % 8, :], in_=qkT[1][:, i * 128:(i + 1) * 128],
                                                                    identity=ident[:]), reads=["qkT1", "ident"], writes=["ptp"])
                            fw.op("act", lambda e, i=i: e.copy(out=ktok[:, i, :], in_=NB["ptp"][:, i % 8, :]),
                                  reads=["ptp"], writes=["ktok"])
                        for i in range(NT):
                            pp = pA[5 + i % 2]; pk = "pa%d" % (5 + i % 2)
                            fw.op("pe", proj_tm(256, 256, i + off, pp), reads=["wsb", "unT"], writes=[pk])
                            fw.op("dve", lambda e, i=i, pp=pp: e.tensor_copy(
                                out=v1[:, i, :, 0:64], in_=pp[:, 0:128].rearrange("p (a d) -> p a d", a=2)),
                                reads=[pk], writes=["v1"])
                            if full:
                                fw.op("act", lambda e, i=i, pp=pp: e.activation(out=og[:, i, :], in_=pp[:, 128:256], func=AF.Sigmoid),
                                      reads=[pk], writes=["og"])
                        for d in range(2):
                            for (dst, scal, dk, sk) in ((v1s[d], cs, "v1s%d" % d, "cs"), (v1e[d], es, "v1e%d" % d, "es")):
                                if dst is v1s[d] and not full:
                                    continue
                                fw.op("dve", lambda e, dst=dst, scal=scal, d=d: e.tensor_tensor(
                                    out=dst[:], in0=v1[:],
                                    in1=scal[:, d, :, 2 * j:2 * j + 2].unsqueeze(3).to_broadcast([128, NT, 2, VS]),
                                    op=ALU.mult), reads=["v1", sk], writes=[dk])
                        chk("p2e")
                        if not full:
                            for d in range(2):
                                pC = pA[d]; pCk = "pa%d" % d

                                def accmm(e, d=d, pC=pC):
                                    r = None
                                    for c in range(NT):
                                        r = e.matmul(pC[:, 0:2 * VS], lhsT=ktok[:, c, :],
                                                     rhs=v1e[d][:, c, :, :].rearrange("p a c -> p (a c)"),
                                                     start=(c == 0), stop=(c == NT - 1))
                                    return r
                                fw.op("pe", accmm, reads=["ktok", "v1e%d" % d], writes=[pCk])
                                for hh in range(2):
                                    hs = slice(hh * 64, (hh + 1) * 64)
                                    fw.op("dve", lambda e, d=d, hh=hh, hs=hs, pC=pC: e.tensor_copy(
                                        out=finals[hs, j, d, :], in_=pC[hs, hh * VS:hh * VS + 65]),
                                        reads=[pCk], writes=["finals"])
                                    fw.op("dve", lambda e, d=d, hh=hh, hs=hs: e.tensor_copy(
                                        out=fin_A[hs, j, d:d + 1], in_=Asum[hs, d, 2 * j + hh:2 * j + hh + 1]),
                                        reads=["Asum"], writes=["fin_A"])
                            continue
                        for d in range(2):
                            if init_state is not None:
                                fw.op("dve", lambda e, d=d: e.tensor_copy(out=stt[d][:], in_=init_state[:, j, d, :]),
                                      reads=["init_state"], writes=["stt%d" % d])
                            else:
                                fw.op("dve", lambda e, d=d: e.memset(stt[d][:], 0.0), writes=["stt%d" % d])
                            for hh in range(2):
                                hs = slice(hh * 64, (hh + 1) * 64)
                                fw.op("act", lambda e, d=d, hh=hh, hs=hs: e.copy(out=stb[d][hs, hh, 0:65], in_=stt[d][hs, :]),
                                      reads=["stt%d" % d], writes=["stb%d" % d])
                        if full:
                            chk("m0")
                        for step in range(NT):
                            if full and step == 1:
                                chk("m1")
                            for d in range(2):
                                c = step if d == 0 else NT - 1 - step
                                pS, pN = pA[0 + d], pA[2 + d]
                                pSk, pNk = "pa%d" % d, "pa%d" % (2 + d)
                                pC = pA[4]; pCk = "pa4"
                                mk_ = maskF if d == 0 else maskB
                                if full:
                                    def smm(e, c=c, pS=pS):
                                        return e.matmul(pS[:, 0:256], lhsT=qkT[1][:, c * 128:(c + 1) * 128],
                                                        rhs=qblk[:, c, :, :].rearrange("p a t -> p (a t)"), start=True, stop=True)
                                    fw.op("pe", smm, reads=["qblk", "qkT1"], writes=[pSk])
                                    chk("q1")
                                    fw.op("dve", lambda e, d=d, pS=pS, mk_=mk_: e.tensor_tensor(
                                        out=PTs[d][:], in0=pS[:, 0:256].rearrange("p (a t) -> p a t", a=2),
                                        in1=mk_[:].unsqueeze(1).to_broadcast([128, 2, 128]), op=ALU.mult),
                                        reads=[pSk, "maskF", "maskB"], writes=["PT%d" % d])
                                    chk("q2")

                                    def nmm(e, c=c, d=d, pN=pN):
                                        e.matmul(pN[:, 0:2 * VS], lhsT=qkT[0][:, c * 128:(c + 1) * 128],
                                                 rhs=stb[d][:].rearrange("p a c -> p (a c)"), start=True, stop=False)
                                        r = None
                                        for hh in range(2):
                                            r = e.matmul(pN[:, hh * VS:(hh + 1) * VS], lhsT=PTs[d][:, hh, :], rhs=v1s[d][:, c, hh, :],
                                                         start=False, stop=(hh == 1))
                                        return r
                                    fw.op("pe", nmm, reads=["PT%d" % d, "v1s%d" % d, "qkT0", "stb%d" % d], writes=[pNk])
                                    chk("q3")
                                    pNv = pN[:, 0:2 * VS].rearrange("p (a c) -> p a c", a=2)
                                    rtv = rt[:, d, c, 2 * j:2 * j + 2]
                                    fw.op("dve", lambda e, d=d, pNv=pNv, rtv=rtv: e.tensor_tensor(
                                        out=dd[d][:].unsqueeze(2), in0=pNv[:, :, 64:65], in1=rtv.unsqueeze(2), op=ALU.mult),
                                        reads=[pNk, "rt"], writes=["dd%d" % d])
                                    fw.op("dve", lambda e, d=d: e.scalar_tensor_tensor(out=rr[d][:], in0=dd[d][:], scalar=-1.0,
                                                                                        in1=dd[d][:], op0=ALU.mult, op1=ALU.max),
                                          reads=["dd%d" % d], writes=["rr%d" % d])
                                    fw.op("dve", lambda e, d=d: e.tensor_scalar(out=rr[d][:], in0=rr[d][:], scalar1=1.0, scalar2=None,
                                                                                 op0=ALU.max),
                                          reads=["rr%d" % d], writes=["rr%d" % d])
                                    fw.op("dve", lambda e, d=d: e.reciprocal(out=rr[d][:], in_=rr[d][:]),
                                          reads=["rr%d" % d], writes=["rr%d" % d])
                                    fw.op("dve", lambda e, d=d, rtv=rtv: e.tensor_tensor(out=rr[d][:], in0=rtv, in1=rr[d][:],
                                                                                        op=ALU.mult),
                                          reads=["rr%d" % d, "rt"], writes=["rr%d" % d])
                                    chk("q4")
                                    step_f, step_b = c, NT - 1 - c
                                    first = (step_f < step_b) if d == 0 else (step_b < step_f)
                                    if step_f == step_b:
                                        first = (d == 0)
                                    for hh in range(2):
                                        if first:
                                            fw.op("dve", lambda e, d=d, c=c, hh=hh, pNv=pNv: e.tensor_scalar(
                                                out=hsum[:, c, hh * 64:(hh + 1) * 64], in0=pNv[:, hh, 0:64],
                                                scalar1=rr[d][:, hh:hh + 1], scalar2=None, op0=ALU.mult),
                                                reads=[pNk, "rr%d" % d], writes=["hsum"])
                                        else:
                                            fw.op("dve", lambda e, d=d, c=c, hh=hh, pNv=pNv: e.scalar_tensor_tensor(
                                                out=hsum[:, c, hh * 64:(hh + 1) * 64], in0=pNv[:, hh, 0:64],
                                                scalar=rr[d][:, hh:hh + 1], in1=hsum[:, c, hh * 64:(hh + 1) * 64],
                                                op0=ALU.mult, op1=ALU.add), reads=[pNk, "rr%d" % d, "hsum"], writes=["hsum"])
                                fw.op("pe", lambda e, c=c, d=d: e.matmul(
                                    pC[:, 0:2 * VS], lhsT=ktok[:, c, :], rhs=v1e[d][:, c, :, :].rearrange("p a c -> p (a c)"),
                                    start=True, stop=True), reads=["ktok", "v1e%d" % d], writes=[pCk])
                                for hh in range(2):
                                    hs = slice(hh * 64, (hh + 1) * 64)
                                    fw.op("dve", lambda e, d=d, c=c, hh=hh, hs=hs: e.scalar_tensor_tensor(
                                        out=stt[d][hs, :], in0=stt[d][hs, :], scalar=eA[hs, d, c, 2 * j + hh:2 * j + hh + 1],
                                        in1=pC[hs, hh * VS:hh * VS + 65], op0=ALU.mult, op1=ALU.add),
                                        reads=["stt%d" % d, "eA", pCk], writes=["stt%d" % d])
                                for hh in range(2):
                                    hs = slice(hh * 64, (hh + 1) * 64)
                                    fw.op("act", lambda e, d=d, hh=hh, hs=hs: e.copy(out=stb[d][hs, hh, 0:65], in_=stt[d][hs, :]),
                                          reads=["stt%d" % d], writes=["stb%d" % d])
                        if full:
                            chk("m2")
                            fw.op("dve", lambda e: e.tensor_tensor(out=hsum[:], in0=hsum[:], in1=og[:], op=ALU.mult),
                                  reads=["hsum", "og"], writes=["hsum"])
                            fw.op("dve", lambda e: e.tensor_tensor(out=og[:], in0=hsum[:], in1=hsum[:], op=ALU.mult),
                                  reads=["hsum", "og"], writes=["og"])
                            fw.op("dve", lambda e: e.tensor_reduce(out=msq[:], in_=og[:].rearrange("p n (a d) -> p n a d", a=2),
                                                                   axis=AX.X, op=ALU.add), reads=["og"], writes=["msq"])
                            fw.op("act", lambda e: e.activation(out=msq[:], in_=msq[:], func=AF.Sqrt, scale=1.0 / 64, bias=EPS),
                                  reads=["msq"], writes=["msq"])
                            fw.op("dve", lambda e: e.reciprocal(out=msq[:], in_=msq[:]), reads=["msq"], writes=["msq"])
                            fw.op("dve", lambda e: e.tensor_tensor(
                                out=hsum[:].rearrange("p n (a d) -> p n a d", a=2), in0=hsum[:].rearrange("p n (a d) -> p n a d", a=2),
                                in1=msq[:].unsqueeze(3).to_broadcast([128, NT, 2, 64]), op=ALU.mult),
                                reads=["hsum", "msq"], writes=["hsum"])
                            fw.op("dve", lambda e: e.tensor_tensor(
                                out=mixm[:], in0=hsum[:],
                                in1=gml_b[:, j * 128:(j + 1) * 128].unsqueeze(1).to_broadcast([128, NT, 128]), op=ALU.mult),
                                reads=["hsum", "gml_b"], writes=["mixm"])
                            chk("m3")
                            fw.dma("sp", mix_dst[:, 512 + j * 128:512 + (j + 1) * 128].rearrange("(n p) c -> p n c", p=128),
                                   mixm[:], reads=["mixm"], writes=["mixdst"])
                            chk("m4")
                        else:
                            for d in range(2):
                                fw.op("dve", lambda e, d=d: e.tensor_copy(out=finals[:, j, d, :], in_=stt[d][:]),
                                      reads=["stt%d" % d], writes=["finals"])
                                for hh in range(2):
                                    hs = slice(hh * 64, (hh + 1) * 64)
                                    fw.op("dve", lambda e, d=d, hh=hh, hs=hs: e.tensor_copy(
                                        out=fin_A[hs, j, d:d + 1], in_=Asum[hs, d, 2 * j + hh:2 * j + hh + 1]),
                                        reads=["Asum"], writes=["fin_A"])
                    fw.barrier()
                fw.barrier()
            return (finals, fin_A) if not full else None

        def main_schedule():
            if with_prompt:
                phase1(x_pfull, hbuf_pf, n_ranks * NT + 2)
                chk("p1")
                for r in range(n_ranks):
                    finals, fin_A = phase2(hbuf_pf[r * S:r * S + NTP * 128, :], None, True, "summary")
                    chk("p2s")
                    fw.dma("sp", g_dst[r * 128:(r + 1) * 128, 0:520], finals[:].rearrange("p a d c -> p (a d c)"),
                           reads=["finals"], writes=["g_dst"])
                    fw.dma("sp", g_dst[r * 128:(r + 1) * 128, 520:528], fin_A[:].rearrange("p a d -> p (a d)"),
                           reads=["fin_A"], writes=["g_dst"])
                    fw.barrier()
                phase1(x_prm, hbuf_p, NTP)
                chk("cc")
            for s in range(NSAMP):
                rows = slice(s * S, (s + 1) * S)
                phase1(x_samp[rows, :], hbuf_s[rows, :], NT)
                chk("s1")
                phase2(hbuf_s[rows, :], mix_s[rows, :], False, "full")
                chk("s2")
                phase3(hbuf_s[rows, :], mix_s[rows, :], y_samp[rows, :], NT)
                chk("s3")
            if with_prompt:
                cst = ExitStack()
                gat = sb(cst, "gat", [128, n_ranks, GWc])
                fw.dma("sp", gat[:], g_dst.rearrange("(r p) w -> p r w", p=128), reads=["g_dst"], writes=["gat"])
                fw.op("dve", lambda e: e.memset(ist[:], 0.0), writes=["ist"])
                for d in range(2):
                    order = range(n_ranks) if d == 0 else range(n_ranks - 1, -1, -1)
                    for r in order:
                        fcol = flg[:, 2 + 8 * d + r:3 + 8 * d + r]
                        Av = gat[:, r, 520:528].rearrange("p (a d) -> p a d", a=4)[:, :, d:d + 1]
                        Sv = gat[:, r, 0:520].rearrange("p (a d c) -> p a d c", a=4, d=2)[:, :, d, :]
                        fw.op("dve", lambda e, d=d, Av=Av, fcol=fcol: e.tensor_scalar(out=dec[:, :, d:d + 1], in0=Av, scalar1=fcol,
                                                                                      scalar2=None, op0=ALU.mult),
                              reads=["gat", "flg"], writes=["dec"])
                        fw.op("act", lambda e, d=d: e.activation(out=dec[:, :, d:d + 1], in_=dec[:, :, d:d + 1], func=AF.Exp),
                              reads=["dec"], writes=["dec"])
                        fw.op("dve", lambda e, d=d: e.tensor_tensor(out=ist[:, :, d, :], in0=ist[:, :, d, :],
                                                                     in1=dec[:, :, d:d + 1].to_broadcast([128, 4, 65]), op=ALU.mult),
                              reads=["ist", "dec"], writes=["ist"])
                        fw.op("dve", lambda e, d=d, Sv=Sv, fcol=fcol: e.scalar_tensor_tensor(
                            out=ist[:, :, d, :], in0=Sv, scalar=fcol, in1=ist[:, :, d, :], op0=ALU.mult, op1=ALU.add),
                            reads=["ist", "gat", "flg"], writes=["ist"])
                fw.barrier()
                cst.close()
                chk("comb")
                fw.res["init_state"] = _Res()
                phase2(hbuf_p, mix_p, True, "full", init_state=ist)
                phase3(hbuf_p[128:128 + S, :], mix_p, y_prm, NT)


        try:
            main_schedule()
        except _Stop:
            pass
        fw.finish()
    return nc


_CACHE = {}


def kernel(x_prompt, x_sample, g_ffn1, w_ffn1_gu, w_ffn1_down, g_mix, w_in, w_conv, b_gates,
           attn_sink, g_mlstm_out, w_out, g_ffn2, w_ffn2_gu, w_ffn2_down, rel_bias_table, g_final):
    f32 = np.float32
    S = 2048
    DM = 1024
    n = N_CORES
    x_prompt = np.asarray(x_prompt, f32)
    x_sample = np.asarray(x_sample, f32)
    if "nc" not in _CACHE:
        _CACHE["nc"] = build_program()
    nc = _CACHE["nc"]
    xp = x_prompt.reshape(-1, DM)
    xp_pad = np.zeros((xp.shape[0] + 256, DM), f32)
    xp_pad[128:128 + xp.shape[0]] = xp
    gains = np.stack([np.asarray(g_ffn1, f32)[0], np.asarray(g_mix, f32)[0], np.asarray(g_ffn2, f32)[0],
                      np.asarray(g_final, f32)])
    common = {
        "w1gu": np.ascontiguousarray(np.asarray(w_ffn1_gu, f32)[0]),
        "w1d": np.ascontiguousarray(np.asarray(w_ffn1_down, f32)[0]),
        "w2gu": np.ascontiguousarray(np.asarray(w_ffn2_gu, f32)[0]),
        "w2d": np.ascontiguousarray(np.asarray(w_ffn2_down, f32)[0]),
        "win": np.ascontiguousarray(np.asarray(w_in, f32)[0]),
        "wout": np.ascontiguousarray(np.asarray(w_out, f32)[0]),
        "gains": np.ascontiguousarray(gains),
        "wconv": np.ascontiguousarray(np.asarray(w_conv, f32)[0]),
        "bgates": np.ascontiguousarray(np.asarray(b_gates, f32)[0].reshape(1, 32)),
        "sink": np.ascontiguousarray(np.asarray(attn_sink, f32).reshape(1, 8)),
        "gml": np.ascontiguousarray(np.asarray(g_mlstm_out, f32).reshape(1, 512)),
        "reltab": np.ascontiguousarray(np.asarray(rel_bias_table, f32)),
        "onehot": _bucket_onehot(),
        "x_pfull": xp_pad,
    }
    in_maps = []
    for c in range(n):
        fl = np.zeros((1, 18), f32)
        fl[0, 0] = -30000.0 if c == 0 else 0.0
        fl[0, 1] = -30000.0 if c == n - 1 else 0.0
        for r in range(n):
            fl[0, 2 + r] = 1.0 if r < c else 0.0
            fl[0, 10 + r] = 1.0 if r > c else 0.0
        m = dict(common)
        m["x_samp"] = np.ascontiguousarray(x_sample[4 * c:4 * c + 4].reshape(4 * S, DM))
        m["x_prm"] = np.ascontiguousarray(xp_pad[c * S:c * S + S + 256])
        m["flags"] = fl
        in_maps.append(m)
    res = run_bass_kernel_spmd(nc, in_maps, core_ids=list(range(n)))
    y_p = np.concatenate([np.asarray(res.results[c]["y_prm"], f32) for c in range(n)], axis=0).reshape(x_prompt.shape)
    y_s = np.concatenate([np.asarray(res.results[c]["y_samp"], f32).reshape(4, S, DM) for c in range(n)], axis=0)
    return (y_p, y_s.reshape(x_sample.shape))
```

```python
import math
from contextlib import ExitStack

import numpy as np
import concourse.bass as bass
import concourse.mybir as mybir
from concourse.bass_utils import run_bass_kernel_spmd

F32 = mybir.dt.float32
BF16 = mybir.dt.bfloat16
ALU = mybir.AluOpType
AF = mybir.ActivationFunctionType
AX = mybir.AxisListType

HD = 64
NEG = -1e30
EPS = 1e-6
IN_W = 2848
N_CORES = 8


_PSUM_PREFIXES = ("pa", "pg", "pu", "pd", "ptp", "pf")


class _Res:
    __slots__ = ("w", "reads")

    def __init__(self):
        self.w = None
        self.reads = []


class _Eng:
    def __init__(self, name, eng, sem):
        self.name = name
        self.eng = eng
        self.sem = sem
        self.count = 0
        self.waited = {}


class FW:
    def __init__(self, nc, stack):
        self.nc = nc
        self.stack = stack
        self.res = {}
        self.engs = {}
        for name in ("pe", "act", "dve", "pool", "sp"):
            eng = {"pe": nc.tensor, "act": nc.scalar, "dve": nc.vector,
                   "pool": nc.gpsimd, "sp": nc.sync}[name]
            sem = stack.enter_context(nc.semaphore("prog_" + name))
            self.engs[name] = _Eng(name, eng, sem)
        self.dsems = {}
        self.dcount = {}
        self.n_wait = 0
        self.n_inst = 0
        self.stopped = False
        self.bg_keys = set()

    def _r(self, key):
        r = self.res.get(key)
        if r is None:
            r = self.res[key] = _Res()
        return r

    def _deps(self, reads, writes):
        deps = {}

        def add(tok):
            s, v = tok
            if deps.get(s, (None, 0))[1] < v:
                deps[s] = (s, v)

        for k in reads:
            r = self._r(k)
            if r.w is not None:
                add(r.w)
            if k.startswith(_PSUM_PREFIXES):
                for tok in r.reads:
                    add(tok)
        for k in writes:
            r = self._r(k)
            if r.w is not None:
                add(r.w)
            for tok in r.reads:
                add(tok)
        return deps

    def _emit_waits(self, E, deps, skip_self=False):
        for s, (sem, v) in deps.items():
            if skip_self and sem is E.sem:
                continue
            if E.waited.get(s, 0) < v:
                E.eng.wait_ge(sem, v)
                E.waited[s] = v
                self.n_wait += 1

    def _commit(self, tok, reads, writes):
        for k in reads:
            r = self._r(k)
            r.reads.append(tok)
            if len(r.reads) > 48:
                best = {}
                for (s, v) in r.reads:
                    if best.get(s, (None, 0))[1] < v:
                        best[s] = (s, v)
                r.reads = list(best.values())
        for k in writes:
            r = self._r(k)
            r.w = tok
            r.reads = []

    def op(self, ename, fn, reads=(), writes=()):
        if self.stopped:
            return None
        E = self.engs[ename]
        deps = self._deps(reads, writes)
        self._emit_waits(E, deps, skip_self=(ename == "pe"))
        ins = fn(E.eng)
        E.count += 1
        ins.then_inc(E.sem, 1)
        tok = (E.sem, E.count)
        self._commit(tok, reads, writes)
        self.n_inst += 1
        return tok

    def dma(self, qname, out, in_, reads=(), writes=(), sem_key=None, **kw):
        if self.stopped:
            return None
        E = self.engs[qname]
        deps = self._deps(reads, writes)
        self._emit_waits(E, deps)
        if sem_key is None:
            sem_key = writes[0] if writes else reads[0]
        s = self.dsems.get(sem_key)
        if s is None:
            s = self.stack.enter_context(self.nc.semaphore("d%d" % len(self.dsems)))
            self.dsems[sem_key] = s
            self.dcount[sem_key] = 0
        E.eng.dma_start(out=out, in_=in_, **kw).then_inc(s, 16)
        self.dcount[sem_key] += 16
        tok = (s, self.dcount[sem_key])
        self._commit(tok, reads, writes)
        return tok

    def custom(self, qname, fn, reads=(), writes=(), sem_key=None, inc=1):
        if self.stopped:
            return None
        E = self.engs[qname]
        deps = self._deps(reads, writes)
        self._emit_waits(E, deps)
        s = self.dsems.get(sem_key)
        if s is None:
            s = self.stack.enter_context(self.nc.semaphore("c%d" % len(self.dsems)))
            self.dsems[sem_key] = s
            self.dcount[sem_key] = 0
        fn(E.eng).then_inc(s, inc)
        self.dcount[sem_key] += inc
        tok = (s, self.dcount[sem_key])
        self._commit(tok, reads, writes)
        return tok

    def _all_tokens(self, include_bg=False):
        final = {}
        for E in self.engs.values():
            if E.count:
                final[E.sem] = (E.sem, E.count)
        for k, s in self.dsems.items():
            if self.dcount[k] and (include_bg or k not in self.bg_keys):
                final[s] = (s, self.dcount[k])
        return final

    def barrier(self):
        if self.stopped:
            return
        final = self._all_tokens()
        for E in self.engs.values():
            self._emit_waits(E, final)
        keep = {k: v for k, v in self.res.items() if k in self.bg_keys}
        self.res = keep

    def finish(self):
        self._emit_waits(self.engs["sp"], self._all_tokens(include_bg=True))


def _t5_bucket(rel):
    nb = 16
    ret = (rel > 0).astype(np.int32) * nb
    n = np.abs(rel)
    max_exact = nb // 2
    large = max_exact + (np.log(np.maximum(n, 1) / max_exact)
                         / math.log(128 / max_exact) * (nb - max_exact)).astype(np.int32)
    large = np.minimum(large, nb - 1)
    return (ret + np.where(n < max_exact, n, large)).astype(np.int32)


def _bucket_onehot():
    oh = np.zeros((33, 640), np.float32)
    for j in range(640):
        rel = j - 255
        if abs(rel) <= 128:
            oh[int(_t5_bucket(np.array(rel))), j] = 1.0
        else:
            oh[32, j] = 1.0
    return oh


def build_program(S=2048, NSAMP=4, DM=1024, DFF=2816, n_ranks=8, with_prompt=True, stop=None):
    KC = DM // 128
    FC = DFF // 128
    NT = S // 128
    GT = 4
    NTP = NT + 2
    nc = bass.Bass("TRN2", target_bir_lowering=False)

    def din(name, shape, dt=F32):
        return nc.dram_tensor(name, list(shape), dt, kind="ExternalInput").ap()

    def dscr(name, shape, dt=F32):
        return nc.dram_tensor(name, list(shape), dt, kind="Internal").ap()

    x_samp = din("x_samp", [NSAMP * S, DM])
    x_prm = din("x_prm", [NTP * 128, DM])
    x_pfull = din("x_pfull", [n_ranks * S + 256, DM])
    w1gu = din("w1gu", [DM, 2 * DFF]); w1d = din("w1d", [DFF, DM])
    w2gu = din("w2gu", [DM, 2 * DFF]); w2d = din("w2d", [DFF, DM])
    win = din("win", [DM, IN_W]); wout = din("wout", [1024, DM])
    gains = din("gains", [4, DM])
    wconv = din("wconv", [5, 1024])
    bgates = din("bgates", [1, 32])
    sink = din("sink", [1, 8])
    gml = din("gml", [1, 512])
    reltab = din("reltab", [32, 8])
    onehot = din("onehot", [33, 640])
    flags = din("flags", [1, 18])
    y_samp = nc.dram_tensor("y_samp", [NSAMP * S, DM], F32, kind="ExternalOutput").ap()
    y_prm = nc.dram_tensor("y_prm", [S, DM], F32, kind="ExternalOutput").ap()

    hbuf_s = dscr("hbuf_s", [NSAMP * S, DM]); hbuf_p = dscr("hbuf_p", [NTP * 128, DM])
    hbuf_pf = dscr("hbuf_pf", [n_ranks * S + 256, DM])
    mix_s = dscr("mix_s", [NSAMP * S, 1024], BF16); mix_p = dscr("mix_p", [S, 1024], BF16)
    w1gu_s = dscr("w1gu_s", [FC, 128, KC * 256], BF16); w2gu_s = dscr("w2gu_s", [FC, 128, KC * 256], BF16)
    w1d_s = dscr("w1d_s", [128, FC * DM], BF16); w2d_s = dscr("w2d_s", [128, FC * DM], BF16)
    win_s = dscr("win_s", [128, KC * IN_W], BF16); wout_s = dscr("wout_s", [128, 8 * DM], BF16)
    fd_s = dscr("fd_s", [8, 640])
    GW = 8 * 65 + 8
    g_src = dscr("g_src", [128, GW]); g_dst = dscr("g_dst", [n_ranks * 128, GW])

    with ExitStack() as top:
        fw = FW(nc, top)

        uid = [0]

        def chk(name):
            if stop == name:
                fw.stopped = True

        def sb(st, name, shape, dt=F32):
            uid[0] += 1
            return st.enter_context(nc.sbuf_tensor("%s_%d" % (name, uid[0]), list(shape), dt))

        def ps(st, name, shape, dt=F32):
            uid[0] += 1
            return st.enter_context(nc.psum_tensor("%s_%d" % (name, uid[0]), list(shape), dt))

        ident = sb(top, "ident", [128, 128], BF16)
        maskF = sb(top, "maskF", [128, 128], BF16)
        maskB = sb(top, "maskB", [128, 128], BF16)
        ones_b = sb(top, "ones_b", [128, 128], BF16)
        gT = sb(top, "gT", [128, 3, KC])
        gfin_b = sb(top, "gfin_b", [128, DM])
        wcv = sb(top, "wcv", [128, 8, 5])
        bg_b = sb(top, "bg_b", [128, 32])
        sink_b = sb(top, "sink_b", [128, 8])
        gml_b = sb(top, "gml_b", [128, 512])
        flg = sb(top, "flg", [128, 18])
        bias = sb(top, "bias", [128, 8, 384])
        finals_t = sb(top, "finals", [128, 4, 2, 65])
        fin_A_t = sb(top, "fin_A", [128, 4, 2])
        GWc = 8 * 65 + 8
        ist = sb(top, "ist", [128, 4, 2, 65])
        dec = sb(top, "dec", [128, 4, 2])

        def mk(fn, w):
            fw.op("pool", fn, writes=[w], reads=[])

        mk(lambda e: e.memset(ident[:], 1.0), "ident")
        fw.op("pool", lambda e: e.affine_select(out=ident[:], in_=ident[:], pattern=[[-1, 128]],
              compare_op=ALU.is_equal, fill=0.0, base=0, channel_multiplier=1), reads=["ident"], writes=["ident"])
        mk(lambda e: e.memset(maskF[:], 1.0), "maskF")
        fw.op("pool", lambda e: e.affine_select(out=maskF[:], in_=maskF[:], pattern=[[1, 128]],
              compare_op=ALU.is_ge, fill=0.0, base=0, channel_multiplier=-1), reads=["maskF"], writes=["maskF"])
        mk(lambda e: e.memset(maskB[:], 1.0), "maskB")
        fw.op("pool", lambda e: e.affine_select(out=maskB[:], in_=maskB[:], pattern=[[-1, 128]],
              compare_op=ALU.is_ge, fill=0.0, base=0, channel_multiplier=1), reads=["maskB"], writes=["maskB"])
        mk(lambda e: e.memset(ones_b[:], 1.0), "ones_b")

        fw.dma("sp", gT[:], gains[0:3, :].rearrange("g (kc p) -> p g kc", p=128), writes=["gT"],
               allow_slow_non_contiguous=True)
        fw.dma("sp", gfin_b[:], gains[3:4, :].partition_broadcast(128), writes=["gfin_b"])
        for tap in range(5):
            fw.dma("sp", wcv[:, :, tap:tap + 1], wconv[tap:tap + 1, :].rearrange("j (c p) -> p c j", p=128), writes=["wcv"],
                   sem_key="wcv", allow_slow_non_contiguous=True)
        fw.dma("sp", bg_b[:], bgates.partition_broadcast(128), writes=["bg_b"])
        fw.dma("sp", sink_b[:], sink.partition_broadcast(128), writes=["sink_b"])
        fw.dma("sp", gml_b[:], gml.partition_broadcast(128), writes=["gml_b"])
        fw.dma("sp", flg[:], flags.partition_broadcast(128), writes=["flg"])

        def conv_gu(src, dst, tag):
            v = src.rearrange("(kc p) (two f) -> p kc two f", p=128, two=2)
            for fc in range(FC):
                for two in range(2):
                    fw.dma("pool", dst[fc].rearrange("p (kc two j) -> p kc two j", kc=KC, two=2)[:, :, two, :],
                           v[:, :, two, fc * 128:(fc + 1) * 128], writes=[tag], sem_key=tag)

        fw.bg_keys.update(["win_s", "wout_s", "w2gu_s", "w2d_s"])
        conv_gu(w1gu, w1gu_s, "w1gu_s")
        def conv_rows(src, dst, nchunk, tag):
            sv = src.rearrange("(c p) d -> p c d", p=128)
            dv = dst.rearrange("p (c d) -> p c d", c=nchunk)
            for c in range(nchunk):
                fw.dma("pool", dv[:, c, :], sv[:, c, :], writes=[tag], sem_key=tag)

        conv_rows(w1d, w1d_s, FC, "w1d_s")
        win_v = win.rearrange("(kc p) c -> p kc c", p=128)
        wins_v = win_s.rearrange("p (kc c) -> p kc c", kc=KC)
        for kc in range(KC):
            for two in range(2):
                fw.dma("pool", wins_v[:, kc, 0:512].rearrange("p (c two j) -> p c two j", two=2, j=64)[:, :, two, :],
                       win_v[:, kc, two * 256:(two + 1) * 256].rearrange("p (c j) -> p c j", j=64),
                       writes=["win_s"], sem_key="win_s")
        for kc in range(KC):
            fw.dma("pool", wins_v[:, kc, 512:IN_W], win_v[:, kc, 512:IN_W], writes=["win_s"], sem_key="win_s")
        conv_rows(wout, wout_s, 8, "wout_s")
        conv_gu(w2gu, w2gu_s, "w2gu_s")
        conv_rows(w2d, w2d_s, FC, "w2d_s")

        if stop == "conv":
            fw.finish()
            return nc
        with ExitStack() as st:
            tab = sb(st, "tab", [33, 8]); tab_hi = sb(st, "tab_hi", [33, 8], BF16)
            tab_r = sb(st, "tab_r", [33, 8]); tab_lo = sb(st, "tab_lo", [33, 8], BF16)
            oh = sb(st, "oh", [33, 640]); oh_b = sb(st, "oh_b", [33, 640], BF16)
            fsb = sb(st, "fsb", [8, 640])
            pf = ps(st, "pf", [8, 1024])
            fw.op("dve", lambda e: e.memset(tab[:], NEG), writes=["tab"])
            fw.dma("sp", tab[0:32, :], reltab, reads=[], writes=["tab"])
            fw.dma("sp", oh[:], onehot, writes=["oh"])
            fw.op("dve", lambda e: e.tensor_copy(out=oh_b[:], in_=oh[:]), reads=["oh"], writes=["oh_b"])
            fw.op("dve", lambda e: e.tensor_copy(out=tab_hi[:], in_=tab[:]), reads=["tab"], writes=["tab_hi"])
            fw.op("dve", lambda e: e.tensor_tensor(out=tab_r[:], in0=tab[:], in1=tab_hi[:], op=ALU.subtract),
                  reads=["tab", "tab_hi"], writes=["tab_r"])
            fw.op("dve", lambda e: e.tensor_copy(out=tab_lo[:], in_=tab_r[:]), reads=["tab_r"], writes=["tab_lo"])

            def fmm(e):
                r = None
                for half in range(2):
                    sl = slice(half * 512, min(640, (half + 1) * 512))
                    e.matmul(pf[:, sl], lhsT=tab_hi[:], rhs=oh_b[:, sl], start=True, stop=False)
                    r = e.matmul(pf[:, sl], lhsT=tab_lo[:], rhs=oh_b[:, sl], start=False, stop=True)
                return r
            fw.op("pe", fmm, reads=["tab_hi", "tab_lo", "oh_b"], writes=["pf"])
            fw.op("dve", lambda e: e.tensor_copy(out=fsb[:], in_=pf[:, 0:640]), reads=["pf"], writes=["fsb"])
            fw.dma("sp", fd_s, fsb[:], reads=["fsb"], writes=["fd_s"])
            for q in range(128):
                src = bass.AP(fd_s.tensor, 127 - q, [[0, 1], [640, 8], [1, 384]])
                fw.dma("sp", bias[q:q + 1, :, :], src, reads=["fd_s"], writes=["bias"], sem_key="bias")
            fw.barrier()

        if stop == "bias":
            fw.finish()
            return nc

        def ffn_phase(st, tag):
            B = {}
            B["xn"] = sb(st, tag + "xn", [128, GT, DM], BF16)
            B["junk"] = sb(st, tag + "junk", [128, DM], BF16)
            B["ss"] = sb(st, tag + "ss", [128, GT])
            B["rstd"] = sb(st, tag + "rstd", [128, GT])
            B["xnT"] = sb(st, tag + "xnT", [128, KC, GT * 128], BF16)
            B["actT"] = sb(st, tag + "actT", [128, FC, GT * 128], BF16)
            B["sg"] = [sb(st, tag + "sg%d" % i, [128, GT * 128]) for i in range(2)]
            B["wgu"] = [sb(st, tag + "wgu%d" % i, [128, KC, 2, 128], BF16) for i in range(3)]
            B["wd"] = sb(st, tag + "wd", [128, FC, DM], BF16)
            B["ptp"] = ps(st, tag + "ptp", [128, 8, 128], BF16)
            B["pg"] = [ps(st, tag + "pg%d" % i, [128, 512]) for i in range(2)]
            B["pu"] = [ps(st, tag + "pu%d" % i, [128, 512]) for i in range(2)]
            B["pd"] = [ps(st, tag + "pd%d" % i, [128, 512]) for i in range(2)]
            B["cnt"] = 0
            return B

        def rms_stats(x_t, xkey, nt, B, D):
            for i in range(nt):
                fw.op("act", lambda e, i=i: e.activation(out=B["junk"][:, 0:D], in_=x_t[:, i, :], func=AF.Square,
                                                         accum_out=B["ss"][:, i:i + 1]),
                      reads=[xkey], writes=["junk", "ss"])
            fw.op("act", lambda e: e.activation(out=B["rstd"][:, 0:nt], in_=B["ss"][:, 0:nt], func=AF.Sqrt,
                                                scale=1.0 / D, bias=EPS), reads=["ss"], writes=["rstd"])
            fw.op("dve", lambda e: e.reciprocal(out=B["rstd"][:, 0:nt], in_=B["rstd"][:, 0:nt]),
                  reads=["rstd"], writes=["rstd"])

        def norm_T(x_t, xkey, nt, B, gidx, dstT, dkey, col0=0):
            rms_stats(x_t, xkey, nt, B, DM)
            for i in range(nt):
                fw.op("dve", lambda e, i=i: e.tensor_scalar(out=B["xn"][:, i, :], in0=x_t[:, i, :],
                                                             scalar1=B["rstd"][:, i:i + 1], scalar2=None, op0=ALU.mult),
                      reads=[xkey, "rstd"], writes=["xn"])

                def tr(e, i=i):
                    r = None
                    for kc in range(KC):
                        r = e.transpose(out=B["ptp"][:, kc, :], in_=B["xn"][:, i, kc * 128:(kc + 1) * 128],
                                        identity=ident[:])
                    return r
                fw.op("pe", tr, reads=["xn", "ident"], writes=["ptp"])
                fw.op("dve", lambda e, i=i: e.tensor_tensor(
                    out=dstT[:, :, col0 + i * 128: col0 + (i + 1) * 128], in0=B["ptp"][:, 0:KC, :],
                    in1=gT[:, gidx, :].unsqueeze(2).to_broadcast([128, KC, 128]), op=ALU.mult),
                    reads=["ptp", "gT"], writes=[dkey])

        def ffn_group(B, x_t, xkey, nt, gidx, wgu_scr, wgukey, wd_scr, wdkey, out_t, okey):
            N = nt * 128
            if not B.get("wd_loaded"):
                fw.dma("sp", B["wd"][:], wd_scr.rearrange("p (fc d) -> p fc d", fc=FC), reads=[wdkey], writes=["wd"])
                B["wd_loaded"] = True
            norm_T(x_t, xkey, nt, B, gidx, B["xnT"], "xnT")
            for fc in range(FC):
                c = B["cnt"]; B["cnt"] += 1
                wb = B["wgu"][c % 3]; wk = "wgu%d" % (c % 3)
                pg = B["pg"][c % 2]; pu = B["pu"][c % 2]; sg = B["sg"][c % 2]
                pgk, puk, sgk = "pg%d" % (c % 2), "pu%d" % (c % 2), "sg%d" % (c % 2)
                fw.dma("sp", wb[:], wgu_scr[fc].rearrange("p (kc two j) -> p kc two j", kc=KC, two=2),
                       reads=[wgukey], writes=[wk])

                def mm(e, wb=wb, pg=pg, pu=pu):
                    r = None
                    for two, pp in ((0, pg), (1, pu)):
                        for kc in range(KC):
                            r = e.matmul(pp[:, 0:N], lhsT=wb[:, kc, two, :], rhs=B["xnT"][:, kc, 0:N],
                                         start=(kc == 0), stop=(kc == KC - 1))
                    return r
                fw.op("pe", mm, reads=[wk, "xnT"], writes=[pgk, puk])
                fw.op("act", lambda e, pg=pg, sg=sg: e.activation(out=sg[:, 0:N], in_=pg[:, 0:N], func=AF.Silu),
                      reads=[pgk], writes=[sgk])
                fw.op("dve", lambda e, pu=pu, sg=sg, fc=fc: e.tensor_tensor(out=B["actT"][:, fc, 0:N], in0=sg[:, 0:N],
                                                                          in1=pu[:, 0:N], op=ALU.mult),
                      reads=[sgk, puk], writes=["actT"])
            for i in range(nt):
                for dh in range(DM // 512):
                    c = B["cnt"]; B["cnt"] += 1
                    pd = B["pd"][c % 2]; pdk = "pd%d" % (c % 2)

                    def mm2(e, i=i, dh=dh, pd=pd):
                        r = None
                        for fc in range(FC):
                            r = e.matmul(pd[:], lhsT=B["actT"][:, fc, i * 128:(i + 1) * 128],
                                         rhs=B["wd"][:, fc, dh * 512:(dh + 1) * 512],
                                         start=(fc == 0), stop=(fc == FC - 1))
                        return r
                    fw.op("pe", mm2, reads=["actT", "wd"], writes=[pdk])
                    fw.op("dve", lambda e, i=i, dh=dh, pd=pd: e.scalar_tensor_tensor(
                        out=out_t[:, i, dh * 512:(dh + 1) * 512], in0=pd[:], scalar=0.5,
                        in1=x_t[:, i, dh * 512:(dh + 1) * 512], op0=ALU.mult, op1=ALU.add),
                        reads=[pdk, xkey], writes=[okey])

        def phase1(x_src, h_dst, ntiles):
            with ExitStack() as st:
                B = ffn_phase(st, "p1")
                xt = [sb(st, "p1x%d" % i, [128, GT, DM]) for i in range(2)]
                ht = sb(st, "p1h", [128, GT, DM])
                g0 = 0; gi = 0
                while g0 < ntiles:
                    nt = min(GT, ntiles - g0)
                    x_t = xt[gi % 2]; xk = "x%d" % (gi % 2)
                    fw.dma("sp", x_t[:, 0:nt, :], x_src[g0 * 128:(g0 + nt) * 128, :].rearrange("(i p) d -> p i d", p=128),
                           writes=[xk])
                    ffn_group(B, x_t, xk, nt, 0, w1gu_s, "w1gu_s", w1d_s, "w1d_s", ht, "ht")
                    fw.dma("pool", h_dst[g0 * 128:(g0 + nt) * 128, :].rearrange("(i p) d -> p i d", p=128), ht[:, 0:nt, :],
                           reads=["ht"], writes=["hdst"])
                    g0 += nt; gi += 1
                fw.barrier()

        def phase3(h_src, mix_src, y_dst, ntiles):
            with ExitStack() as st:
                B = ffn_phase(st, "p3")
                hin = sb(st, "p3hin", [128, GT, DM])
                h2 = sb(st, "p3h2", [128, GT, DM])
                mt = sb(st, "p3mt", [128, GT, 1024], BF16)
                mT = sb(st, "p3mT", [128, 8, GT * 128], BF16)
                wo = sb(st, "p3wo", [128, 8, DM], BF16)
                fw.dma("sp", wo[:], wout_s.rearrange("p (cc d) -> p cc d", cc=8), reads=["wout_s"], writes=["wo"])
                g0 = 0
                while g0 < ntiles:
                    nt = min(GT, ntiles - g0)
                    rows = slice(g0 * 128, (g0 + nt) * 128)
                    fw.dma("sp", hin[:, 0:nt, :], h_src[rows, :].rearrange("(i p) d -> p i d", p=128), writes=["hin"])
                    fw.dma("sp", mt[:, 0:nt, :], mix_src[rows, :].rearrange("(i p) d -> p i d", p=128), writes=["mt"])
                    for i in range(nt):
                        def tr(e, i=i):
                            r = None
                            for cc in range(8):
                                r = e.transpose(out=B["ptp"][:, cc, :], in_=mt[:, i, cc * 128:(cc + 1) * 128], identity=ident[:])
                            return r
                        fw.op("pe", tr, reads=["mt", "ident"], writes=["ptp"])
                        fw.op("act", lambda e, i=i: e.copy(out=mT[:, :, i * 128:(i + 1) * 128], in_=B["ptp"][:, 0:8, :]),
                              reads=["ptp"], writes=["mT"])
                    for i in range(nt):
                        for dh in range(DM // 512):
                            c = B["cnt"]; B["cnt"] += 1
                            pd = B["pd"][c % 2]; pdk = "pd%d" % (c % 2)

                            def mm(e, i=i, dh=dh, pd=pd):
                                r = None
                                for cc in range(8):
                                    r = e.matmul(pd[:], lhsT=mT[:, cc, i * 128:(i + 1) * 128],
                                                 rhs=wo[:, cc, dh * 512:(dh + 1) * 512], start=(cc == 0), stop=(cc == 7))
                                return r
                            fw.op("pe", mm, reads=["mT", "wo"], writes=[pdk])
                            fw.op("dve", lambda e, i=i, dh=dh, pd=pd: e.tensor_tensor(
                                out=h2[:, i, dh * 512:(dh + 1) * 512], in0=pd[:], in1=hin[:, i, dh * 512:(dh + 1) * 512],
                                op=ALU.add), reads=[pdk, "hin"], writes=["h2"])
                    ffn_group(B, h2, "h2", nt, 2, w2gu_s, "w2gu_s", w2d_s, "w2d_s", h2, "h2")
                    rms_stats(h2, "h2", nt, B, DM)
                    for i in range(nt):
                        fw.op("dve", lambda e, i=i: e.scalar_tensor_tensor(
                            out=h2[:, i, :], in0=h2[:, i, :], scalar=B["rstd"][:, i:i + 1], in1=gfin_b[:],
                            op0=ALU.mult, op1=ALU.mult), reads=["h2", "rstd", "gfin_b"], writes=["h2"])
                    fw.dma("pool", y_dst[rows, :].rearrange("(i p) d -> p i d", p=128), h2[:, 0:nt, :],
                           reads=["h2"], writes=["ydst"])
                    g0 += nt
                fw.barrier()

        LN8 = math.log(0.125)
        VS = 80

        def phase2(h_src, mix_dst, prompt, mode, init_state=None):
            full = mode == "full"
            off = 1 if prompt else 0
            ntl = NT + 2 * off
            with ExitStack() as st:
                unT = sb(st, "unT", [128, KC, ntl * 128], BF16)
                gts = sb(st, "gts", [128, NT, 32])
                wsb = sb(st, "wsb", [128, KC, 768], BF16)
                ptp2 = ps(st, "p2ptp", [128, 8, 128], BF16)
                NB = {"ptp": ptp2}
                pA = [ps(st, "p2pa%d" % i, [128, 512]) for i in range(7)]
                cs = sb(st, "cs", [128, 2, NT, 8]); es = sb(st, "es", [128, 2, NT, 8])
                rt = sb(st, "rt", [128, 2, NT, 8]); eA = sb(st, "eA", [128, 2, NT, 8])
                Asum = sb(st, "Asum", [128, 2, 8])
                with ExitStack() as us:
                    NBu = {"xn": sb(us, "p2xn", [128, GT, DM], BF16), "junk": sb(us, "p2junk", [128, DM], BF16),
                           "ss": sb(us, "p2ss", [128, GT]), "rstd": sb(us, "p2rstd", [128, GT]), "ptp": ptp2}
                    hld = [sb(us, "p2h%d" % i, [128, GT, DM]) for i in range(2)]
                    g0 = 0; gi = 0
                    while g0 < ntl:
                        nt = min(GT, ntl - g0)
                        ht = hld[gi % 2]; hk = "hld%d" % (gi % 2)
                        fw.dma("sp", ht[:, 0:nt, :], h_src[g0 * 128:(g0 + nt) * 128, :].rearrange("(i p) d -> p i d", p=128),
                               writes=[hk])
                        norm_T(ht, hk, nt, NBu, 1, unT, "unT", col0=g0 * 128)
                        g0 += nt; gi += 1
                    fw.barrier()
                chk("p2a")
                winv = win_s.rearrange("p (kc c) -> p kc c", kc=KC)

                def load_w(c0, c1):
                    fw.dma("sp", wsb[:, :, 0:c1 - c0], winv[:, :, c0:c1], reads=["win_s"], writes=["wsb"])

                def proj_fm(col, ncol, tok0, ntok, dst_ps):
                    def f(e):
                        r = None
                        for kc in range(KC):
                            r = e.matmul(dst_ps[0:ncol, 0:ntok], lhsT=wsb[:, kc, col:col + ncol],
                                         rhs=unT[:, kc, tok0:tok0 + ntok], start=(kc == 0), stop=(kc == KC - 1))
                        return r
                    return f

                def proj_tm(col, ncol, tile, dst_ps):
                    def f(e):
                        r = None
                        for kc in range(KC):
                            r = e.matmul(dst_ps[:, 0:ncol], lhsT=unT[:, kc, tile * 128:(tile + 1) * 128],
                                         rhs=wsb[:, kc, col:col + ncol], start=(kc == 0), stop=(kc == KC - 1))
                        return r
                    return f

                with ExitStack() as gs:
                    load_w(2816, 2848)
                    for i in range(NT):
                        pp = pA[i % 2]; pk = "pa%d" % (i % 2)
                        fw.op("pe", proj_tm(0, 32, i + off, pp), reads=["wsb", "unT"], writes=[pk])
                        fw.op("dve", lambda e, i=i, pp=pp: e.tensor_tensor(out=gts[:, i, :], in0=pp[:, 0:32], in1=bg_b[:],
                                                                          op=ALU.add), reads=[pk, "bg_b"], writes=["gts"])
                    chk("p2b")
                    W = NT * 8
                    g4 = gts[:].rearrange("p n (a h) -> p n a h", a=4)
                    fx = sb(gs, "fx", [128, 2, NT, 8]); t1 = sb(gs, "t1", [128, 2, NT, 8]); t2 = sb(gs, "t2", [128, 2, NT, 8])
                    lf = sb(gs, "lf", [128, 2, NT, 8])
                    parts = [sb(gs, "lfp%d" % i, [128, 2, NT, 8], BF16) for i in range(3)]
                    for d in range(2):
                        fw.op("dve", lambda e, d=d: e.tensor_copy(out=fx[:, d, :, :], in_=g4[:, :, 1 + 2 * d, :]),
                              reads=["gts"], writes=["fx"])
                    fw.op("dve", lambda e: e.tensor_single_scalar(out=t2[:], in_=fx[:], scalar=0.0, op=ALU.min),
                          reads=["fx"], writes=["t2"])
                    fw.op("dve", lambda e: e.scalar_tensor_tensor(out=t1[:], in0=t2[:], scalar=2.0, in1=fx[:],
                                                                  op0=ALU.mult, op1=ALU.subtract),
                          reads=["fx", "t2"], writes=["t1"])
                    fw.op("act", lambda e: e.activation(out=t1[:], in_=t1[:], func=AF.Exp),
                          reads=["t1"], writes=["t1"])
                    fw.op("act", lambda e: e.activation(out=t1[:], in_=t1[:], func=AF.Ln, bias=1.0),
                          reads=["t1"], writes=["t1"])
                    fw.op("dve", lambda e: e.tensor_tensor(out=lf[:], in0=t2[:], in1=t1[:], op=ALU.subtract),
                          reads=["t1", "t2"], writes=["lf"])
                    fw.op("dve", lambda e: e.tensor_copy(out=parts[0][:], in_=lf[:]), reads=["lf"], writes=["lfp0"])
                    fw.op("dve", lambda e: e.tensor_tensor(out=t1[:], in0=lf[:], in1=parts[0][:], op=ALU.subtract),
                          reads=["lf", "lfp0"], writes=["t1"])
                    fw.op("dve", lambda e: e.tensor_copy(out=parts[1][:], in_=t1[:]), reads=["t1"], writes=["lfp1"])
                    fw.op("dve", lambda e: e.tensor_tensor(out=t2[:], in0=t1[:], in1=parts[1][:], op=ALU.subtract),
                          reads=["t1", "lfp1"], writes=["t2"])
                    fw.op("dve", lambda e: e.tensor_copy(out=parts[2][:], in_=t2[:]), reads=["t2"], writes=["lfp2"])
                    pcs = pA[2]

                    def cums(e):
                        r = None
                        for d in range(2):
                            tri = maskF if d == 0 else maskB
                            for (mat, o) in ((tri, d * W), (ones_b, 2 * W + d * W)):
                                for k in range(3):
                                    r = e.matmul(pcs[:, o:o + W], lhsT=mat[:],
                                                 rhs=parts[k][:, d, :, :].rearrange("p n h -> p (n h)"),
                                                 start=(k == 0), stop=(k == 2))
                        return r
                    chk("p2c0")
                    fw.op("pe", cums, reads=["lfp0", "lfp1", "lfp2", "maskF", "maskB", "ones_b"], writes=["pa2"])
                    chk("p2c1")
                    pcv = pcs[:, 0:4 * W].rearrange("p (q d n h) -> p q d n h", q=2, d=2, n=NT)
                    for d in range(2):
                        fw.op("dve", lambda e, d=d: e.tensor_tensor(out=t1[:, d, :, :], in0=g4[:, :, 2 * d, :],
                                                                     in1=pcv[:, 0, d, :, :], op=ALU.subtract),
                              reads=["gts", "pa2"], writes=["t1"])
                    fw.op("act", lambda e: e.activation(out=cs[:], in_=t1[:], func=AF.Exp, bias=LN8),
                          reads=["t1"], writes=["cs"])
                    fw.op("dve", lambda e: e.tensor_tensor(out=t2[:], in0=t1[:], in1=pcv[:, 1, :, :, :], op=ALU.add),
                          reads=["t1", "pa2"], writes=["t2"])
                    fw.op("act", lambda e: e.activation(out=es[:], in_=t2[:], func=AF.Exp, bias=LN8),
                          reads=["t2"], writes=["es"])
                    chk("p2c2")
                    fw.op("act", lambda e: e.activation(out=rt[:], in_=pcv[:, 0, :, :, :], func=AF.Exp),
                          reads=["pa2"], writes=["rt"])
                    fw.op("act", lambda e: e.activation(out=eA[:], in_=pcv[:, 1, :, :, :], func=AF.Exp),
                          reads=["pa2"], writes=["eA"])
                    chk("p2c3")
                    if not full:
                        fw.op("dve", lambda e: e.tensor_copy(out=t1[:], in_=pcv[:, 1, :, :, :]), reads=["pa2"], writes=["t1"])
                        fw.op("dve", lambda e: e.memset(lf[:], 0.0), reads=["lf"], writes=["lf"])
                        for c_ in range(NT - 2, -1, -1):
                            fw.op("dve", lambda e, c_=c_: e.tensor_tensor(out=lf[:, 0, c_, :], in0=lf[:, 0, c_ + 1, :],
                                                                         in1=t1[:, 0, c_ + 1, :], op=ALU.add),
                                  reads=["lf", "t1"], writes=["lf"])
                        for c_ in range(1, NT):
                            fw.op("dve", lambda e, c_=c_: e.tensor_tensor(out=lf[:, 1, c_, :], in0=lf[:, 1, c_ - 1, :],
                                                                         in1=t1[:, 1, c_ - 1, :], op=ALU.add),
                                  reads=["lf", "t1"], writes=["lf"])
                        fw.op("dve", lambda e: e.tensor_tensor(out=t2[:], in0=t2[:], in1=lf[:], op=ALU.add),
                              reads=["lf", "t2", "es"], writes=["t2"])
                        fw.op("act", lambda e: e.activation(out=es[:], in_=t2[:], func=AF.Exp, bias=LN8),
                              reads=["t2"], writes=["es"])
                        chk("p2c4")
                        fw.op("dve", lambda e: e.tensor_copy(out=Asum[:], in_=t1[:, :, 0, :]), reads=["t1"], writes=["Asum"])
                        for n_ in range(1, NT):
                            fw.op("dve", lambda e, n_=n_: e.tensor_tensor(out=Asum[:], in0=Asum[:], in1=t1[:, :, n_, :], op=ALU.add),
                                  reads=["t1", "Asum"], writes=["Asum"])
                    chk("p2c5")
                    fw.barrier()

                chk("p2c")
                if full:
                    with ExitStack() as at:
                        qaT = sb(at, "qaT", [128, 4, S], BF16)
                        kT = sb(at, "kT", [128, ntl * 128], BF16)
                        vtm = sb(at, "vtm", [128, ntl, 128], BF16)
                        ssb = sb(at, "ssb", [128, 4, 384]); pbf = sb(at, "pbf", [128, 4, 384], BF16)
                        pts = [sb(at, "pts%d" % i, [128, 3, 128], BF16) for i in range(2)]
                        mx = sb(at, "mx", [128, 4]); negm = sb(at, "negm", [128, 4]); rs = sb(at, "rs", [128, 4])
                        tmp4 = sb(at, "tmp4", [128, 4]); rinv = sb(at, "rinv", [128, 4])
                        mixa = [sb(at, "mixa%d" % i, [128, 512], BF16) for i in range(2)]
                        load_w(0, 768)
                        TB = 512
                        for c in range(4):
                            for t0 in range(0, S, TB):
                                n = min(TB, S - t0)
                                pp = pA[5 + (c + t0 // TB) % 2]; pk = "pa%d" % (5 + (c + t0 // TB) % 2)
                                fw.op("pe", proj_fm(c * 128, 128, off * 128 + t0, n, pp), reads=["wsb", "unT"], writes=[pk])
                                fw.op("act", lambda e, c=c, t0=t0, n=n, pp=pp: e.mul(
                                    out=qaT[:, c, t0:t0 + n], in_=pp[:, 0:n], mul=0.125),
                                    reads=[pk], writes=["qaT"])
                        for t0 in range(0, ntl * 128, TB):
                            n = min(TB, ntl * 128 - t0)
                            pp = pA[5 + (t0 // TB) % 2]; pk = "pa%d" % (5 + (t0 // TB) % 2)
                            fw.op("pe", proj_fm(512, 128, t0, n, pp), reads=["wsb", "unT"], writes=[pk])
                            fw.op("act", lambda e, t0=t0, n=n, pp=pp: e.copy(out=kT[:, t0:t0 + n], in_=pp[:, 0:n]),
                                  reads=[pk], writes=["kT"])
                        for i in range(ntl):
                            pp = pA[5 + i % 2]; pk = "pa%d" % (5 + i % 2)
                            fw.op("pe", proj_tm(640, 128, i, pp), reads=["wsb", "unT"], writes=[pk])
                            fw.op("act", lambda e, i=i, pp=pp: e.copy(out=vtm[:, i, :], in_=pp[:, 0:128]),
                                  reads=[pk], writes=["vtm"])
                        pO = pA[4]
                        for i in range(NT):
                            ti = i + off
                            lo = ti - 1 if ti - 1 >= 0 else ti
                            hi = ti + 1 if ti + 1 < ntl else ti
                            nk = hi - lo + 1
                            b0 = (lo - (ti - 1)) * 128
                            ma = mixa[i % 2]; mak = "mixa%d" % (i % 2)
                            for g in range(2):
                                def smm(e, g=g, i=i, lo=lo, nk=nk):
                                    r = None
                                    for c in range(4):
                                        r = e.matmul(pA[c][:, 0:nk * 128], lhsT=qaT[g * 64:(g + 1) * 64, c, i * 128:(i + 1) * 128],
                                                     rhs=kT[g * 64:(g + 1) * 64, lo * 128:(lo + nk) * 128], start=True, stop=True)
                                    return r
                                fw.op("pe", smm, reads=["qaT", "kT"], writes=["pa0", "pa1", "pa2", "pa3"])
                                for c in range(4):
                                    fw.op("dve", lambda e, c=c, g=g, nk=nk, b0=b0: e.tensor_tensor(
                                        out=ssb[:, c, 0:nk * 128], in0=pA[c][:, 0:nk * 128],
                                        in1=bias[:, g * 4 + c, b0:b0 + nk * 128], op=ALU.add),
                                        reads=["pa%d" % c, "bias"], writes=["ssb"])
                                if prompt and i == 0:
                                    fw.op("dve", lambda e: e.tensor_scalar(out=ssb[:, :, 0:128], in0=ssb[:, :, 0:128],
                                                                            scalar1=flg[:, 0:1], scalar2=None, op0=ALU.add),
                                          reads=["ssb", "flg"], writes=["ssb"])
                                if prompt and i == NT - 1:
                                    fw.op("dve", lambda e: e.tensor_scalar(out=ssb[:, :, 256:384], in0=ssb[:, :, 256:384],
                                                                            scalar1=flg[:, 1:2], scalar2=None, op0=ALU.add),
                                          reads=["ssb", "flg"], writes=["ssb"])
                                fw.op("dve", lambda e, nk=nk: e.tensor_reduce(out=mx[:], in_=ssb[:, :, 0:nk * 128], axis=AX.X,
                                                                              op=ALU.max), reads=["ssb"], writes=["mx"])
                                fw.op("dve", lambda e, g=g: e.tensor_tensor(out=mx[:], in0=mx[:], in1=sink_b[:, g * 4:(g + 1) * 4],
                                                                             op=ALU.max), reads=["mx", "sink_b"], writes=["mx"])
                                fw.op("dve", lambda e: e.tensor_scalar(out=negm[:], in0=mx[:], scalar1=-1.0, scalar2=None,
                                                                        op0=ALU.mult), reads=["mx"], writes=["negm"])
                                for c in range(4):
                                    fw.op("act", lambda e, c=c, nk=nk: e.activation(
                                        out=pbf[:, c, 0:nk * 128], in_=ssb[:, c, 0:nk * 128], func=AF.Exp,
                                        bias=negm[:, c:c + 1], accum_out=rs[:, c:c + 1]),
                                        reads=["ssb", "negm"], writes=["pbf", "rs"])
                                fw.op("dve", lambda e, g=g: e.tensor_tensor(out=tmp4[:], in0=sink_b[:, g * 4:(g + 1) * 4], in1=mx[:],
                                                                             op=ALU.subtract), reads=["mx", "sink_b"], writes=["tmp4"])
                                fw.op("act", lambda e: e.activation(out=tmp4[:], in_=tmp4[:], func=AF.Exp),
                                      reads=["tmp4"], writes=["tmp4"])
                                fw.op("dve", lambda e: e.tensor_tensor(out=tmp4[:], in0=tmp4[:], in1=rs[:], op=ALU.add),
                                      reads=["tmp4", "rs"], writes=["tmp4"])
                                fw.op("dve", lambda e: e.reciprocal(out=rinv[:], in_=tmp4[:]), reads=["tmp4"], writes=["rinv"])
                                for c in range(4):
                                    pt = pts[c % 2]; ptk = "pts%d" % (c % 2)

                                    def trp(e, c=c, nk=nk):
                                        r = None
                                        for kb in range(nk):
                                            r = e.transpose(out=NB["ptp"][:, kb, :], in_=pbf[:, c, kb * 128:(kb + 1) * 128],
                                                            identity=ident[:])
                                        return r
                                    fw.op("pe", trp, reads=["pbf", "ident"], writes=["ptp"])
                                    fw.op("act", lambda e, pt=pt, nk=nk: e.copy(out=pt[:, 0:nk, :], in_=NB["ptp"][:, 0:nk, :]),
                                          reads=["ptp"], writes=[ptk])

                                    def pv(e, c=c, nk=nk, lo=lo, g=g, pt=pt):
                                        r = None
                                        for kb in range(nk):
                                            r = e.matmul(pO[:, c * 64:(c + 1) * 64], lhsT=pt[:, kb, :],
                                                         rhs=vtm[:, lo + kb, g * 64:(g + 1) * 64],
                                                         start=(kb == 0), stop=(kb == nk - 1))
                                        return r
                                    fw.op("pe", pv, reads=[ptk, "vtm"], writes=["pa4"])
                                fw.op("dve", lambda e, g=g, ma=ma: e.tensor_tensor(
                                    out=ma[:, g * 256:(g + 1) * 256].rearrange("p (c d) -> p c d", c=4),
                                    in0=pO[:, 0:256].rearrange("p (c d) -> p c d", c=4),
                                    in1=rinv[:].unsqueeze(2).to_broadcast([128, 4, 64]), op=ALU.mult),
                                    reads=["pa4", "rinv"], writes=[mak])
                            fw.dma("sp", mix_dst[i * 128:(i + 1) * 128, 0:512], ma[:], reads=[mak], writes=["mixdst"])
                        fw.barrier()

                if full:
                    chk("s2a")
                finals, fin_A = finals_t, fin_A_t
                with ExitStack() as ml:
                    pre = [sb(ml, "pre%d" % i, [128, S + 4]) for i in range(2)]
                    acc = sb(ml, "acc", [128, S])
                    qkT = [sb(ml, "qkT%d" % i, [128, S], BF16) for i in range(2)]
                    ktok = sb(ml, "ktok", [128, NT, 128], BF16)
                    v1 = sb(ml, "v1", [128, NT, 2, VS], BF16)
                    v1s = [sb(ml, "v1s%d" % d, [128, NT, 2, VS], BF16) for d in range(2)]
                    v1e = [sb(ml, "v1e%d" % d, [128, NT, 2, VS], BF16) for d in range(2)]
                    og = sb(ml, "og", [128, NT, 128])
                    hsum = sb(ml, "hsum", [128, NT, 128])
                    stt = [sb(ml, "stt%d" % d, [128, 65]) for d in range(2)]
                    stb = [sb(ml, "stb%d" % d, [128, 2, VS], BF16) for d in range(2)]
                    qblk = sb(ml, "qblk", [128, NT, 2, 128], BF16)
                    PTs = [sb(ml, "PT%d" % d, [128, 2, 128], BF16) for d in range(2)]
                    dd = [sb(ml, "dd%d" % d, [128, 2]) for d in range(2)]
                    rr = [sb(ml, "rr%d" % d, [128, 2]) for d in range(2)]
                    msq = sb(ml, "msq", [128, NT, 2])
                    mixm = sb(ml, "mixm", [128, NT, 128], BF16)
                    fw.op("dve", lambda e: e.memset(v1[:], 1.0), writes=["v1"])
                    fw.op("dve", lambda e: e.memset(qblk[:], 0.0), writes=["qblk"])
                    for d in range(2):
                        fw.op("dve", lambda e, d=d: e.memset(stb[d][:], 0.0), writes=["stb%d" % d])
                    for j in range(4):
                        for bi, c0 in enumerate((768, 1280, 1792, 2304)):
                            fw.dma("sp", wsb[:, :, bi * 128:(bi + 1) * 128], winv[:, :, c0 + j * 128:c0 + (j + 1) * 128],
                                   reads=["win_s"], writes=["wsb"])
                        for qi in range(2):
                            if qi == 0 and not full:
                                continue
                            pr = pre[qi]; prk = "pre%d" % qi
                            if prompt:
                                for (t0, n, dcol) in ((off * 128 - 2, 2, 0), ((off + NT) * 128, 2, S + 2)):
                                    pp = pA[5]; pk = "pa5"
                                    fw.op("pe", proj_fm(qi * 128, 128, t0, n, pp), reads=["wsb", "unT"], writes=[pk])
                                    fw.op("act", lambda e, pr=pr, dcol=dcol, pp=pp: e.copy(out=pr[:, dcol:dcol + 2], in_=pp[:, 0:2]),
                                          reads=[pk], writes=[prk])
                            else:
                                fw.op("dve", lambda e, pr=pr: e.memset(pr[:, 0:2], 0.0), writes=[prk])
                                fw.op("dve", lambda e, pr=pr: e.memset(pr[:, S + 2:S + 4], 0.0), writes=[prk])
                            for t0 in range(0, S, 512):
                                n = min(512, S - t0)
                                pp = pA[5 + (t0 // 512) % 2]; pk = "pa%d" % (5 + (t0 // 512) % 2)
                                fw.op("pe", proj_fm(qi * 128, 128, off * 128 + t0, n, pp), reads=["wsb", "unT"], writes=[pk])
                                fw.op("act", lambda e, pr=pr, t0=t0, n=n, pp=pp: e.copy(out=pr[:, 2 + t0:2 + t0 + n], in_=pp[:, 0:n]),
                                      reads=[pk], writes=[prk])
                            ch = qi * 4 + j
                            fw.op("dve", lambda e, pr=pr, ch=ch: e.tensor_scalar(out=acc[:], in0=pr[:, 0:S], scalar1=wcv[:, ch, 0:1],
                                                                                  scalar2=None, op0=ALU.mult),
                                  reads=[prk, "wcv"], writes=["acc"])
                            for tap in range(1, 5):
                                fw.op("dve", lambda e, pr=pr, ch=ch, tap=tap: e.scalar_tensor_tensor(
                                    out=acc[:], in0=pr[:, tap:tap + S], scalar=wcv[:, ch, tap:tap + 1], in1=acc[:],
                                    op0=ALU.mult, op1=ALU.add), reads=[prk, "wcv", "acc"], writes=["acc"])
                            fw.op("act", lambda e, qi=qi: e.activation(out=qkT[qi][:], in_=acc[:], func=AF.Silu),
                                  reads=["acc"], writes=["qkT%d" % qi])
                            if qi == 0 and full:
                                for hh in range(2):
                                    hs = slice(hh * 64, (hh + 1) * 64)
                                    fw.op("act", lambda e, hh=hh, hs=hs: e.activation(
                                        out=qblk[hs, :, hh, :], in_=acc[hs, :].rearrange("p (n t) -> p n t", t=128), func=AF.Silu),
                                        reads=["acc"], writes=["qblk"])
                        chk("p2d")
                        for i in range(NT):
                            fw.op("pe", lambda e, i=i: e.transpose(out=NB["ptp"][:, i % 8, :], in_=qkT[1][:, i * 128:(i + 1) * 128],
                                                                    identity=ident[:]), reads=["qkT1", "ident"], writes=["ptp"])
                            fw.op("act", lambda e, i=i: e.copy(out=ktok[:, i, :], in_=NB["ptp"][:, i % 8, :]),
                                  reads=["ptp"], writes=["ktok"])
                        for i in range(NT):
                            pp = pA[5 + i % 2]; pk = "pa%d" % (5 + i % 2)
                            fw.op("pe", proj_tm(256, 256, i + off, pp), reads=["wsb", "unT"], writes=[pk])
                            fw.op("dve", lambda e, i=i, pp=pp: e.tensor_copy(
                                out=v1[:, i, :, 0:64], in_=pp[:, 0:128].rearrange("p (a d) -> p a d", a=2)),
                                reads=[pk], writes=["v1"])
                            if full:
                                fw.op("act", lambda e, i=i, pp=pp: e.activation(out=og[:, i, :], in_=pp[:, 128:256], func=AF.Sigmoid),
                                      reads=[pk], writes=["og"])
                        for d in range(2):
                            for (dst, scal, dk, sk) in ((v1s[d], cs, "v1s%d" % d, "cs"), (v1e[d], es, "v1e%d" % d, "es")):
                                if dst is v1s[d] and not full:
                                    continue
                                fw.op("dve", lambda e, dst=dst, scal=scal, d=d: e.tensor_tensor(
                                    out=dst[:], in0=v1[:],
                                    in1=scal[:, d, :, 2 * j:2 * j + 2].unsqueeze(3).to_broadcast([128, NT, 2, VS]),
                                    op=ALU.mult), reads=["v1", sk], writes=[dk])
                        chk("p2e")
                        if not full:
                            for d in range(2):
                                pC = pA[d]; pCk = "pa%d" % d

                                def accmm(e, d=d, pC=pC):
                                    r = None
                                    for c in range(NT):
                                        r = e.matmul(pC[:, 0:2 * VS], lhsT=ktok[:, c, :],
                                                     rhs=v1e[d][:, c, :, :].rearrange("p a c -> p (a c)"),
                                                     start=(c == 0), stop=(c == NT - 1))
                                    return r
                                fw.op("pe", accmm, reads=["ktok", "v1e%d" % d], writes=[pCk])
                                for hh in range(2):
                                    hs = slice(hh * 64, (hh + 1) * 64)
                                    fw.op("dve", lambda e, d=d, hh=hh, hs=hs, pC=pC: e.tensor_copy(
                                        out=finals[hs, j, d, :], in_=pC[hs, hh * VS:hh * VS + 65]),
                                        reads=[pCk], writes=["finals"])
                                    fw.op("dve", lambda e, d=d, hh=hh, hs=hs: e.tensor_copy(
                                        out=fin_A[hs, j, d:d + 1], in_=Asum[hs, d, 2 * j + hh:2 * j + hh + 1]),
                                        reads=["Asum"], writes=["fin_A"])
                            continue
                        for d in range(2):
                            if init_state is not None:
                                fw.op("dve", lambda e, d=d: e.tensor_copy(out=stt[d][:], in_=init_state[:, j, d, :]),
                                      reads=["init_state"], writes=["stt%d" % d])
                            else:
                                fw.op("dve", lambda e, d=d: e.memset(stt[d][:], 0.0), writes=["stt%d" % d])
                            for hh in range(2):
                                hs = slice(hh * 64, (hh + 1) * 64)
                                fw.op("act", lambda e, d=d, hh=hh, hs=hs: e.copy(out=stb[d][hs, hh, 0:65], in_=stt[d][hs, :]),
                                      reads=["stt%d" % d], writes=["stb%d" % d])
                        if full:
                            chk("m0")
                        for step in range(NT):
                            if full and step == 1:
                                chk("m1")
                            for d in range(2):
                                c = step if d == 0 else NT - 1 - step
                                pS, pN = pA[0 + d], pA[2 + d]
                                pSk, pNk = "pa%d" % d, "pa%d" % (2 + d)
                                pC = pA[4]; pCk = "pa4"
                                mk_ = maskF if d == 0 else maskB
                                if full:
                                    def smm(e, c=c, pS=pS):
                                        return e.matmul(pS[:, 0:256], lhsT=qkT[1][:, c * 128:(c + 1) * 128],
                                                        rhs=qblk[:, c, :, :].rearrange("p a t -> p (a t)"), start=True, stop=True)
                                    fw.op("pe", smm, reads=["qblk", "qkT1"], writes=[pSk])
                                    chk("q1")
                                    fw.op("dve", lambda e, d=d, pS=pS, mk_=mk_: e.tensor_tensor(
                                        out=PTs[d][:], in0=pS[:, 0:256].rearrange("p (a t) -> p a t", a=2),
                                        in1=mk_[:].unsqueeze(1).to_broadcast([128, 2, 128]), op=ALU.mult),
                                        reads=[pSk, "maskF", "maskB"], writes=["PT%d" % d])
                                    chk("q2")

                                    def nmm(e, c=c, d=d, pN=pN):
                                        e.matmul(pN[:, 0:2 * VS], lhsT=qkT[0][:, c * 128:(c + 1) * 128],
                                                 rhs=stb[d][:].rearrange("p a c -> p (a c)"), start=True, stop=False)
                                        r = None
                                        for hh in range(2):
                                            r = e.matmul(pN[:, hh * VS:(hh + 1) * VS], lhsT=PTs[d][:, hh, :], rhs=v1s[d][:, c, hh, :],
                                                         start=False, stop=(hh == 1))
                                        return r
                                    fw.op("pe", nmm, reads=["PT%d" % d, "v1s%d" % d, "qkT0", "stb%d" % d], writes=[pNk])
                                    chk("q3")
                                    pNv = pN[:, 0:2 * VS].rearrange("p (a c) -> p a c", a=2)
                                    rtv = rt[:, d, c, 2 * j:2 * j + 2]
                                    fw.op("dve", lambda e, d=d, pNv=pNv, rtv=rtv: e.tensor_tensor(
                                        out=dd[d][:].unsqueeze(2), in0=pNv[:, :, 64:65], in1=rtv.unsqueeze(2), op=ALU.mult),
                                        reads=[pNk, "rt"], writes=["dd%d" % d])
                                    fw.op("dve", lambda e, d=d: e.scalar_tensor_tensor(out=rr[d][:], in0=dd[d][:], scalar=-1.0,
                                                                                        in1=dd[d][:], op0=ALU.mult, op1=ALU.max),
                                          reads=["dd%d" % d], writes=["rr%d" % d])
                                    fw.op("dve", lambda e, d=d: e.tensor_scalar(out=rr[d][:], in0=rr[d][:], scalar1=1.0, scalar2=None,
                                                                                 op0=ALU.max),
                                          reads=["rr%d" % d], writes=["rr%d" % d])
                                    fw.op("dve", lambda e, d=d: e.reciprocal(out=rr[d][:], in_=rr[d][:]),
                                          reads=["rr%d" % d], writes=["rr%d" % d])
                                    fw.op("dve", lambda e, d=d, rtv=rtv: e.tensor_tensor(out=rr[d][:], in0=rtv, in1=rr[d][:],
                                                                                        op=ALU.mult),
                                          reads=["rr%d" % d, "rt"], writes=["rr%d" % d])
                                    chk("q4")
                                    step_f, step_b = c, NT - 1 - c
                                    first = (step_f < step_b) if d == 0 else (step_b < step_f)
                                    if step_f == step_b:
                                        first = (d == 0)
                                    for hh in range(2):
                                        if first:
                                            fw.op("dve", lambda e, d=d, c=c, hh=hh, pNv=pNv: e.tensor_scalar(
                                                out=hsum[:, c, hh * 64:(hh + 1) * 64], in0=pNv[:, hh, 0:64],
                                                scalar1=rr[d][:, hh:hh + 1], scalar2=None, op0=ALU.mult),
                                                reads=[pNk, "rr%d" % d], writes=["hsum"])
                                        else:
                                            fw.op("dve", lambda e, d=d, c=c, hh=hh, pNv=pNv: e.scalar_tensor_tensor(
                                                out=hsum[:, c, hh * 64:(hh + 1) * 64], in0=pNv[:, hh, 0:64],
                                                scalar=rr[d][:, hh:hh + 1], in1=hsum[:, c, hh * 64:(hh + 1) * 64],
                                                op0=ALU.mult, op1=ALU.add), reads=[pNk, "rr%d" % d, "hsum"], writes=["hsum"])
                                fw.op("pe", lambda e, c=c, d=d: e.matmul(
                                    pC[:, 0:2 * VS], lhsT=ktok[:, c, :], rhs=v1e[d][:, c, :, :].rearrange("p a c -> p (a c)"),
                                    start=True, stop=True), reads=["ktok", "v1e%d" % d], writes=[pCk])
                                for hh in range(2):
                                    hs = slice(hh * 64, (hh + 1) * 64)
                                    fw.op("dve", lambda e, d=d, c=c, hh=hh, hs=hs: e.scalar_tensor_tensor(
                                        out=stt[d][hs, :], in0=stt[d][hs, :], scalar=eA[hs, d, c, 2 * j + hh:2 * j + hh + 1],
                                        in1=pC[hs, hh * VS:hh * VS + 65], op0=ALU.mult, op1=ALU.add),
                                        reads=["stt%d" % d, "eA", pCk], writes=["stt%d" % d])
                                for hh in range(2):
                                    hs = slice(hh * 64, (hh + 1) * 64)
                                    fw.op("act", lambda e, d=d, hh=hh, hs=hs: e.copy(out=stb[d][hs, hh, 0:65], in_=stt[d][hs, :]),
                                          reads=["stt%d" % d], writes=["stb%d" % d])
                        if full:
                            chk("m2")
                            fw.op("dve", lambda e: e.tensor_tensor(out=hsum[:], in0=hsum[:], in1=og[:], op=ALU.mult),
                                  reads=["hsum", "og"], writes=["hsum"])
                            fw.op("dve", lambda e: e.tensor_tensor(out=og[:], in0=hsum[:], in1=hsum[:], op=ALU.mult),
                                  reads=["hsum", "og"], writes=["og"])
                            fw.op("dve", lambda e: e.tensor_reduce(out=msq[:], in_=og[:].rearrange("p n (a d) -> p n a d", a=2),
                                                                   axis=AX.X, op=ALU.add), reads=["og"], writes=["msq"])
                            fw.op("act", lambda e: e.activation(out=msq[:], in_=msq[:], func=AF.Sqrt, scale=1.0 / 64, bias=EPS),
                                  reads=["msq"], writes=["msq"])
                            fw.op("dve", lambda e: e.reciprocal(out=msq[:], in_=msq[:]), reads=["msq"], writes=["msq"])
                            fw.op("dve", lambda e: e.tensor_tensor(
                                out=hsum[:].rearrange("p n (a d) -> p n a d", a=2), in0=hsum[:].rearrange("p n (a d) -> p n a d", a=2),
                                in1=msq[:].unsqueeze(3).to_broadcast([128, NT, 2, 64]), op=ALU.mult),
                                reads=["hsum", "msq"], writes=["hsum"])
                            fw.op("dve", lambda e: e.tensor_tensor(
                                out=mixm[:], in0=hsum[:],
                                in1=gml_b[:, j * 128:(j + 1) * 128].unsqueeze(1).to_broadcast([128, NT, 128]), op=ALU.mult),
                                reads=["hsum", "gml_b"], writes=["mixm"])
                            chk("m3")
                            fw.dma("sp", mix_dst[:, 512 + j * 128:512 + (j + 1) * 128].rearrange("(n p) c -> p n c", p=128),
                                   mixm[:], reads=["mixm"], writes=["mixdst"])
                            chk("m4")
                        else:
                            for d in range(2):
                                fw.op("dve", lambda e, d=d: e.tensor_copy(out=finals[:, j, d, :], in_=stt[d][:]),
                                      reads=["stt%d" % d], writes=["finals"])
                                for hh in range(2):
                                    hs = slice(hh * 64, (hh + 1) * 64)
                                    fw.op("dve", lambda e, d=d, hh=hh, hs=hs: e.tensor_copy(
                                        out=fin_A[hs, j, d:d + 1], in_=Asum[hs, d, 2 * j + hh:2 * j + hh + 1]),
                                        reads=["Asum"], writes=["fin_A"])
                    fw.barrier()
                fw.barrier()
            return (finals, fin_A) if not full else None

        def main_schedule():
            if with_prompt:
                phase1(x_pfull, hbuf_pf, n_ranks * NT + 2)
                chk("p1")
                for r in range(n_ranks):
                    finals, fin_A = phase2(hbuf_pf[r * S:r * S + NTP * 128, :], None, True, "summary")
                    chk("p2s")
                    fw.dma("sp", g_dst[r * 128:(r + 1) * 128, 0:520], finals[:].rearrange("p a d c -> p (a d c)"),
                           reads=["finals"], writes=["g_dst"])
                    fw.dma("sp", g_dst[r * 128:(r + 1) * 128, 520:528], fin_A[:].rearrange("p a d -> p (a d)"),
                           reads=["fin_A"], writes=["g_dst"])
                    fw.barrier()
                phase1(x_prm, hbuf_p, NTP)
                chk("cc")
            for s in range(NSAMP):
                rows = slice(s * S, (s + 1) * S)
                phase1(x_samp[rows, :], hbuf_s[rows, :], NT)
                chk("s1")
                phase2(hbuf_s[rows, :], mix_s[rows, :], False, "full")
                chk("s2")
                phase3(hbuf_s[rows, :], mix_s[rows, :], y_samp[rows, :], NT)
                chk("s3")
            if with_prompt:
                cst = ExitStack()
                gat = sb(cst, "gat", [128, n_ranks, GWc])
                fw.dma("sp", gat[:], g_dst.rearrange("(r p) w -> p r w", p=128), reads=["g_dst"], writes=["gat"])
                fw.op("dve", lambda e: e.memset(ist[:], 0.0), writes=["ist"])
                for d in range(2):
                    order = range(n_ranks) if d == 0 else range(n_ranks - 1, -1, -1)
                    for r in order:
                        fcol = flg[:, 2 + 8 * d + r:3 + 8 * d + r]
                        Av = gat[:, r, 520:528].rearrange("p (a d) -> p a d", a=4)[:, :, d:d + 1]
                        Sv = gat[:, r, 0:520].rearrange("p (a d c) -> p a d c", a=4, d=2)[:, :, d, :]
                        fw.op("dve", lambda e, d=d, Av=Av, fcol=fcol: e.tensor_scalar(out=dec[:, :, d:d + 1], in0=Av, scalar1=fcol,
                                                                                      scalar2=None, op0=ALU.mult),
                              reads=["gat", "flg"], writes=["dec"])
                        fw.op("act", lambda e, d=d: e.activation(out=dec[:, :, d:d + 1], in_=dec[:, :, d:d + 1], func=AF.Exp),
                              reads=["dec"], writes=["dec"])
                        fw.op("dve", lambda e, d=d: e.tensor_tensor(out=ist[:, :, d, :], in0=ist[:, :, d, :],
                                                                     in1=dec[:, :, d:d + 1].to_broadcast([128, 4, 65]), op=ALU.mult),
                              reads=["ist", "dec"], writes=["ist"])
                        fw.op("dve", lambda e, d=d, Sv=Sv, fcol=fcol: e.scalar_tensor_tensor(
                            out=ist[:, :, d, :], in0=Sv, scalar=fcol, in1=ist[:, :, d, :], op0=ALU.mult, op1=ALU.add),
                            reads=["ist", "gat", "flg"], writes=["ist"])
                fw.barrier()
                cst.close()
                chk("comb")
                fw.res["init_state"] = _Res()
                phase2(hbuf_p, mix_p, True, "full", init_state=ist)
                phase3(hbuf_p[128:128 + S, :], mix_p, y_prm, NT)


        try:
            main_schedule()
        except _Stop:
            pass
        fw.finish()
    return nc


_CACHE = {}


def kernel(x_prompt, x_sample, g_ffn1, w_ffn1_gu, w_ffn1_down, g_mix, w_in, w_conv, b_gates,
           attn_sink, g_mlstm_out, w_out, g_ffn2, w_ffn2_gu, w_ffn2_down, rel_bias_table, g_final):
    f32 = np.float32
    S = 2048
    DM = 1024
    n = N_CORES
    x_prompt = np.asarray(x_prompt, f32)
    x_sample = np.asarray(x_sample, f32)
    if "nc" not in _CACHE:
        _CACHE["nc"] = build_program()
    nc = _CACHE["nc"]
    xp = x_prompt.reshape(-1, DM)
    xp_pad = np.zeros((xp.shape[0] + 256, DM), f32)
    xp_pad[128:128 + xp.shape[0]] = xp
    gains = np.stack([np.asarray(g_ffn1, f32)[0], np.asarray(g_mix, f32)[0], np.asarray(g_ffn2, f32)[0],
                      np.asarray(g_final, f32)])
    common = {
        "w1gu": np.ascontiguousarray(np.asarray(w_ffn1_gu, f32)[0]),
        "w1d": np.ascontiguousarray(np.asarray(w_ffn1_down, f32)[0]),
        "w2gu": np.ascontiguousarray(np.asarray(w_ffn2_gu, f32)[0]),
        "w2d": np.ascontiguousarray(np.asarray(w_ffn2_down, f32)[0]),
        "win": np.ascontiguousarray(np.asarray(w_in, f32)[0]),
        "wout": np.ascontiguousarray(np.asarray(w_out, f32)[0]),
        "gains": np.ascontiguousarray(gains),
        "wconv": np.ascontiguousarray(np.asarray(w_conv, f32)[0]),
        "bgates": np.ascontiguousarray(np.asarray(b_gates, f32)[0].reshape(1, 32)),
        "sink": np.ascontiguousarray(np.asarray(attn_sink, f32).reshape(1, 8)),
        "gml": np.ascontiguousarray(np.asarray(g_mlstm_out, f32).reshape(1, 512)),
        "reltab": np.ascontiguousarray(np.asarray(rel_bias_table, f32)),
        "onehot": _bucket_onehot(),
        "x_pfull": xp_pad,
    }
    in_maps = []
    for c in range(n):
        fl = np.zeros((1, 18), f32)
        fl[0, 0] = -30000.0 if c == 0 else 0.0
        fl[0, 1] = -30000.0 if c == n - 1 else 0.0
        for r in range(n):
            fl[0, 2 + r] = 1.0 if r < c else 0.0
            fl[0, 10 + r] = 1.0 if r > c else 0.0
        m = dict(common)
        m["x_samp"] = np.ascontiguousarray(x_sample[4 * c:4 * c + 4].reshape(4 * S, DM))
        m["x_prm"] = np.ascontiguousarray(xp_pad[c * S:c * S + S + 256])
        m["flags"] = fl
        in_maps.append(m)
    res = run_bass_kernel_spmd(nc, in_maps, core_ids=list(range(n)))
    y_p = np.concatenate([np.asarray(res.results[c]["y_prm"], f32) for c in range(n)], axis=0).reshape(x_prompt.shape)
    y_s = np.concatenate([np.asarray(res.results[c]["y_samp"], f32).reshape(4, S, DM) for c in range(n)], axis=0)
    return (y_p, y_s.reshape(x_sample.shape))
```

```python
import math
from contextlib import ExitStack

import numpy as np
import concourse.bass as bass
import concourse.mybir as mybir
from concourse.bass_utils import run_bass_kernel_spmd

F32 = mybir.dt.float32
BF16 = mybir.dt.bfloat16
ALU = mybir.AluOpType
AF = mybir.ActivationFunctionType
AX = mybir.AxisListType

HD = 64
NEG = -1e30
EPS = 1e-6
IN_W = 2848
N_CORES = 8


_PSUM_PREFIXES = ("pa", "pg", "pu", "pd", "ptp", "pf")


class _Res:
    __slots__ = ("w", "reads")

    def __init__(self):
        self.w = None
        self.reads = []


class _Eng:
    def __init__(self, name, eng, sem):
        self.name = name
        self.eng = eng
        self.sem = sem
        self.count = 0
        self.waited = {}


class FW:
    def __init__(self, nc, stack):
        self.nc = nc
        self.stack = stack
        self.res = {}
        self.engs = {}
        for name in ("pe", "act", "dve", "pool", "sp"):
            eng = {"pe": nc.tensor, "act": nc.scalar, "dve": nc.vector,
                   "pool": nc.gpsimd, "sp": nc.sync}[name]
            sem = stack.enter_context(nc.semaphore("prog_" + name))
            self.engs[name] = _Eng(name, eng, sem)
        self.dsems = {}
        self.dcount = {}
        self.n_wait = 0
        self.n_inst = 0
        self.stopped = False
        self.bg_keys = set()

    def _r(self, key):
        r = self.res.get(key)
        if r is None:
            r = self.res[key] = _Res()
        return r

    def _deps(self, reads, writes):
        deps = {}

        def add(tok):
            s, v = tok
            if deps.get(s, (None, 0))[1] < v:
                deps[s] = (s, v)

        for k in reads:
            r = self._r(k)
            if r.w is not None:
                add(r.w)
            if k.startswith(_PSUM_PREFIXES):
                for tok in r.reads:
                    add(tok)
        for k in writes:
            r = self._r(k)
            if r.w is not None:
                add(r.w)
            for tok in r.reads:
                add(tok)
        return deps

    def _emit_waits(self, E, deps, skip_self=False):
        for s, (sem, v) in deps.items():
            if skip_self and sem is E.sem:
                continue
            if E.waited.get(s, 0) < v:
                E.eng.wait_ge(sem, v)
                E.waited[s] = v
                self.n_wait += 1

    def _commit(self, tok, reads, writes):
        for k in reads:
            r = self._r(k)
            r.reads.append(tok)
            if len(r.reads) > 48:
                best = {}
                for (s, v) in r.reads:
                    if best.get(s, (None, 0))[1] < v:
                        best[s] = (s, v)
                r.reads = list(best.values())
        for k in writes:
            r = self._r(k)
            r.w = tok
            r.reads = []

    def op(self, ename, fn, reads=(), writes=()):
        if self.stopped:
            return None
        E = self.engs[ename]
        deps = self._deps(reads, writes)
        self._emit_waits(E, deps, skip_self=(ename == "pe"))
        ins = fn(E.eng)
        E.count += 1
        ins.then_inc(E.sem, 1)
        tok = (E.sem, E.count)
        self._commit(tok, reads, writes)
        self.n_inst += 1
        return tok

    def dma(self, qname, out, in_, reads=(), writes=(), sem_key=None, **kw):
        if self.stopped:
            return None
        E = self.engs[qname]
        deps = self._deps(reads, writes)
        self._emit_waits(E, deps)
        if sem_key is None:
            sem_key = writes[0] if writes else reads[0]
        s = self.dsems.get(sem_key)
        if s is None:
            s = self.stack.enter_context(self.nc.semaphore("d%d" % len(self.dsems)))
            self.dsems[sem_key] = s
            self.dcount[sem_key] = 0
        E.eng.dma_start(out=out, in_=in_, **kw).then_inc(s, 16)
        self.dcount[sem_key] += 16
        tok = (s, self.dcount[sem_key])
        self._commit(tok, reads, writes)
        return tok

    def custom(self, qname, fn, reads=(), writes=(), sem_key=None, inc=1):
        if self.stopped:
            return None
        E = self.engs[qname]
        deps = self._deps(reads, writes)
        self._emit_waits(E, deps)
        s = self.dsems.get(sem_key)
        if s is None:
            s = self.stack.enter_context(self.nc.semaphore("c%d" % len(self.dsems)))
            self.dsems[sem_key] = s
            self.dcount[sem_key] = 0
        fn(E.eng).then_inc(s, inc)
        self.dcount[sem_key] += inc
        tok = (s, self.dcount[sem_key])
        self._commit(tok, reads, writes)
        return tok

    def _all_tokens(self, include_bg=False):
        final = {}
        for E in self.engs.values():
            if E.count:
                final[E.sem] = (E.sem, E.count)
        for k, s in self.dsems.items():
            if self.dcount[k] and (include_bg or k not in self.bg_keys):
                final[s] = (s, self.dcount[k])
        return final

    def barrier(self):
        if self.stopped:
            return
        final = self._all_tokens()
        for E in self.engs.values():
            self._emit_waits(E, final)
        keep = {k: v for k, v in self.res.items() if k in self.bg_keys}
        self.res = keep

    def finish(self):
        self._emit_waits(self.engs["sp"], self._all_tokens(include_bg=True))


def _t5_bucket(rel):
    nb = 16
    ret = (rel > 0).astype(np.int32) * nb
    n = np.abs(rel)
    max_exact = nb // 2
    large = max_exact + (np.log(np.maximum(n, 1) / max_exact)
                         / math.log(128 / max_exact) * (nb - max_exact)).astype(np.int32)
    large = np.minimum(large, nb - 1)
    return (ret + np.where(n < max_exact, n, large)).astype(np.int32)


def _bucket_onehot():
    oh = np.zeros((33, 640), np.float32)
    for j in range(640):
        rel = j - 255
        if abs(rel) <= 128:
            oh[int(_t5_bucket(np.array(rel))), j] = 1.0
        else:
            oh[32, j] = 1.0
    return oh


def build_program(S=2048, NSAMP=4, DM=1024, DFF=2816, n_ranks=8, with_prompt=True, stop=None):
    KC = DM // 128
    FC = DFF // 128
    NT = S // 128
    GT = 4
    NTP = NT + 2
    nc = bass.Bass("TRN2", target_bir_lowering=False)

    def din(name, shape, dt=F32):
        return nc.dram_tensor(name, list(shape), dt, kind="ExternalInput").ap()

    def dscr(name, shape, dt=F32):
        return nc.dram_tensor(name, list(shape), dt, kind="Internal").ap()

    x_samp = din("x_samp", [NSAMP * S, DM])
    x_pfull = din("x_pfull", [n_ranks * S + 256, DM])
    w1gu = din("w1gu", [DM, 2 * DFF]); w1d = din("w1d", [DFF, DM])
    w2gu = din("w2gu", [DM, 2 * DFF]); w2d = din("w2d", [DFF, DM])
    win = din("win", [DM, IN_W]); wout = din("wout", [1024, DM])
    gains = din("gains", [4, DM])
    wconv = din("wconv", [5, 1024])
    bgates = din("bgates", [1, 32])
    sink = din("sink", [1, 8])
    gml = din("gml", [1, 512])
    reltab = din("reltab", [32, 8])
    onehot = din("onehot", [33, 640])
    flags = din("flags", [1, 34])
    y_samp = nc.dram_tensor("y_samp", [NSAMP * S, DM], F32, kind="ExternalOutput").ap()
    y_prm = nc.dram_tensor("y_prm", [S, DM], F32, kind="ExternalOutput").ap()

    hbuf_s = dscr("hbuf_s", [NSAMP * S, DM]); hbuf_p = dscr("hbuf_p", [NTP * 128, DM])
    hbuf_pf = dscr("hbuf_pf", [n_ranks * S + 256, DM])
    mix_s = dscr("mix_s", [NSAMP * S, 1024], BF16); mix_p = dscr("mix_p", [S, 1024], BF16)
    w1gu_s = dscr("w1gu_s", [FC, 128, KC * 256], BF16); w2gu_s = dscr("w2gu_s", [FC, 128, KC * 256], BF16)
    w1d_s = dscr("w1d_s", [128, FC * DM], BF16); w2d_s = dscr("w2d_s", [128, FC * DM], BF16)
    win_s = dscr("win_s", [128, KC * IN_W], BF16); wout_s = dscr("wout_s", [128, 8 * DM], BF16)
    fd_s = dscr("fd_s", [8, 640])
    GW = 8 * 65 + 8
    g_src = dscr("g_src", [128, GW]); g_dst = dscr("g_dst", [n_ranks * 128, GW])

    with ExitStack() as top:
        fw = FW(nc, top)

        uid = [0]

        def chk(name):
            if stop == name:
                fw.stopped = True

        def sb(st, name, shape, dt=F32):
            uid[0] += 1
            return st.enter_context(nc.sbuf_tensor("%s_%d" % (name, uid[0]), list(shape), dt))

        def ps(st, name, shape, dt=F32):
            uid[0] += 1
            return st.enter_context(nc.psum_tensor("%s_%d" % (name, uid[0]), list(shape), dt))

        ident = sb(top, "ident", [128, 128], BF16)
        maskF = sb(top, "maskF", [128, 128], BF16)
        maskB = sb(top, "maskB", [128, 128], BF16)
        ones_b = sb(top, "ones_b", [128, 128], BF16)
        gT = sb(top, "gT", [128, 3, KC])
        gfin_b = sb(top, "gfin_b", [128, DM])
        wcv = sb(top, "wcv", [128, 8, 5])
        bg_b = sb(top, "bg_b", [128, 32])
        sink_b = sb(top, "sink_b", [128, 8])
        gml_b = sb(top, "gml_b", [128, 512])
        flg = sb(top, "flg", [128, 34])
        bias = sb(top, "bias", [128, 8, 384])
        finals_t = sb(top, "finals", [128, 4, 2, 65])
        fin_A_t = sb(top, "fin_A", [128, 4, 2])
        GWc = 8 * 65 + 8
        ist = sb(top, "ist", [128, 4, 2, 65])
        dec = sb(top, "dec", [128, 4, 2])

        def mk(fn, w):
            fw.op("pool", fn, writes=[w], reads=[])

        mk(lambda e: e.memset(ident[:], 1.0), "ident")
        fw.op("pool", lambda e: e.affine_select(out=ident[:], in_=ident[:], pattern=[[-1, 128]],
              compare_op=ALU.is_equal, fill=0.0, base=0, channel_multiplier=1), reads=["ident"], writes=["ident"])
        mk(lambda e: e.memset(maskF[:], 1.0), "maskF")
        fw.op("pool", lambda e: e.affine_select(out=maskF[:], in_=maskF[:], pattern=[[1, 128]],
              compare_op=ALU.is_ge, fill=0.0, base=0, channel_multiplier=-1), reads=["maskF"], writes=["maskF"])
        mk(lambda e: e.memset(maskB[:], 1.0), "maskB")
        fw.op("pool", lambda e: e.affine_select(out=maskB[:], in_=maskB[:], pattern=[[-1, 128]],
              compare_op=ALU.is_ge, fill=0.0, base=0, channel_multiplier=1), reads=["maskB"], writes=["maskB"])
        mk(lambda e: e.memset(ones_b[:], 1.0), "ones_b")

        fw.dma("sp", gT[:], gains[0:3, :].rearrange("g (kc p) -> p g kc", p=128), writes=["gT"],
               allow_slow_non_contiguous=True)
        fw.dma("sp", gfin_b[:], gains[3:4, :].partition_broadcast(128), writes=["gfin_b"])
        for tap in range(5):
            fw.dma("sp", wcv[:, :, tap:tap + 1], wconv[tap:tap + 1, :].rearrange("j (c p) -> p c j", p=128), writes=["wcv"],
                   sem_key="wcv", allow_slow_non_contiguous=True)
        fw.dma("sp", bg_b[:], bgates.partition_broadcast(128), writes=["bg_b"])
        fw.dma("sp", sink_b[:], sink.partition_broadcast(128), writes=["sink_b"])
        fw.dma("sp", gml_b[:], gml.partition_broadcast(128), writes=["gml_b"])
        fw.dma("sp", flg[:], flags.partition_broadcast(128), writes=["flg"])

        def conv_gu(src, dst, tag):
            v = src.rearrange("(kc p) (two f) -> p kc two f", p=128, two=2)
            for fc in range(FC):
                for two in range(2):
                    fw.dma("pool", dst[fc].rearrange("p (kc two j) -> p kc two j", kc=KC, two=2)[:, :, two, :],
                           v[:, :, two, fc * 128:(fc + 1) * 128], writes=[tag], sem_key=tag)

        fw.bg_keys.update(["win_s", "wout_s", "w2gu_s", "w2d_s"])
        conv_gu(w1gu, w1gu_s, "w1gu_s")
        def conv_rows(src, dst, nchunk, tag):
            sv = src.rearrange("(c p) d -> p c d", p=128)
            dv = dst.rearrange("p (c d) -> p c d", c=nchunk)
            for c in range(nchunk):
                fw.dma("pool", dv[:, c, :], sv[:, c, :], writes=[tag], sem_key=tag)

        conv_rows(w1d, w1d_s, FC, "w1d_s")
        win_v = win.rearrange("(kc p) c -> p kc c", p=128)
        wins_v = win_s.rearrange("p (kc c) -> p kc c", kc=KC)
        for kc in range(KC):
            for two in range(2):
                fw.dma("pool", wins_v[:, kc, 0:512].rearrange("p (c two j) -> p c two j", two=2, j=64)[:, :, two, :],
                       win_v[:, kc, two * 256:(two + 1) * 256].rearrange("p (c j) -> p c j", j=64),
                       writes=["win_s"], sem_key="win_s")
        for kc in range(KC):
            fw.dma("pool", wins_v[:, kc, 512:IN_W], win_v[:, kc, 512:IN_W], writes=["win_s"], sem_key="win_s")
        conv_rows(wout, wout_s, 8, "wout_s")
        conv_gu(w2gu, w2gu_s, "w2gu_s")
        conv_rows(w2d, w2d_s, FC, "w2d_s")

        if stop == "conv":
            fw.finish()
            return nc
        with ExitStack() as st:
            tab = sb(st, "tab", [33, 8]); tab_hi = sb(st, "tab_hi", [33, 8], BF16)
            tab_r = sb(st, "tab_r", [33, 8]); tab_lo = sb(st, "tab_lo", [33, 8], BF16)
            oh = sb(st, "oh", [33, 640]); oh_b = sb(st, "oh_b", [33, 640], BF16)
            fsb = sb(st, "fsb", [8, 640])
            pf = ps(st, "pf", [8, 1024])
            fw.op("dve", lambda e: e.memset(tab[:], NEG), writes=["tab"])
            fw.dma("sp", tab[0:32, :], reltab, reads=[], writes=["tab"])
            fw.dma("sp", oh[:], onehot, writes=["oh"])
            fw.op("dve", lambda e: e.tensor_copy(out=oh_b[:], in_=oh[:]), reads=["oh"], writes=["oh_b"])
            fw.op("dve", lambda e: e.tensor_copy(out=tab_hi[:], in_=tab[:]), reads=["tab"], writes=["tab_hi"])
            fw.op("dve", lambda e: e.tensor_tensor(out=tab_r[:], in0=tab[:], in1=tab_hi[:], op=ALU.subtract),
                  reads=["tab", "tab_hi"], writes=["tab_r"])
            fw.op("dve", lambda e: e.tensor_copy(out=tab_lo[:], in_=tab_r[:]), reads=["tab_r"], writes=["tab_lo"])

            def fmm(e):
                r = None
                for half in range(2):
                    sl = slice(half * 512, min(640, (half + 1) * 512))
                    e.matmul(pf[:, sl], lhsT=tab_hi[:], rhs=oh_b[:, sl], start=True, stop=False)
                    r = e.matmul(pf[:, sl], lhsT=tab_lo[:], rhs=oh_b[:, sl], start=False, stop=True)
                return r
            fw.op("pe", fmm, reads=["tab_hi", "tab_lo", "oh_b"], writes=["pf"])
            fw.op("dve", lambda e: e.tensor_copy(out=fsb[:], in_=pf[:, 0:640]), reads=["pf"], writes=["fsb"])
            fw.dma("sp", fd_s, fsb[:], reads=["fsb"], writes=["fd_s"])
            for q in range(128):
                src = bass.AP(fd_s.tensor, 127 - q, [[0, 1], [640, 8], [1, 384]])
                fw.dma("sp", bias[q:q + 1, :, :], src, reads=["fd_s"], writes=["bias"], sem_key="bias")
            fw.barrier()

        if stop == "bias":
            fw.finish()
            return nc

        def ffn_phase(st, tag):
            B = {}
            B["xn"] = sb(st, tag + "xn", [128, GT, DM], BF16)
            B["junk"] = sb(st, tag + "junk", [128, DM], BF16)
            B["ss"] = sb(st, tag + "ss", [128, GT])
            B["rstd"] = sb(st, tag + "rstd", [128, GT])
            B["xnT"] = sb(st, tag + "xnT", [128, KC, GT * 128], BF16)
            B["actT"] = sb(st, tag + "actT", [128, FC, GT * 128], BF16)
            B["sg"] = [sb(st, tag + "sg%d" % i, [128, GT * 128]) for i in range(2)]
            B["wgu"] = [sb(st, tag + "wgu%d" % i, [128, KC, 2, 128], BF16) for i in range(3)]
            B["wd"] = sb(st, tag + "wd", [128, FC, DM], BF16)
            B["ptp"] = ps(st, tag + "ptp", [128, 8, 128], BF16)
            B["pg"] = [ps(st, tag + "pg%d" % i, [128, 512]) for i in range(2)]
            B["pu"] = [ps(st, tag + "pu%d" % i, [128, 512]) for i in range(2)]
            B["pd"] = [ps(st, tag + "pd%d" % i, [128, 512]) for i in range(2)]
            B["cnt"] = 0
            return B

        def rms_stats(x_t, xkey, nt, B, D):
            for i in range(nt):
                fw.op("act", lambda e, i=i: e.activation(out=B["junk"][:, 0:D], in_=x_t[:, i, :], func=AF.Square,
                                                         accum_out=B["ss"][:, i:i + 1]),
                      reads=[xkey], writes=["junk", "ss"])
            fw.op("act", lambda e: e.activation(out=B["rstd"][:, 0:nt], in_=B["ss"][:, 0:nt], func=AF.Sqrt,
                                                scale=1.0 / D, bias=EPS), reads=["ss"], writes=["rstd"])
            fw.op("dve", lambda e: e.reciprocal(out=B["rstd"][:, 0:nt], in_=B["rstd"][:, 0:nt]),
                  reads=["rstd"], writes=["rstd"])

        def norm_T(x_t, xkey, nt, B, gidx, dstT, dkey, col0=0):
            rms_stats(x_t, xkey, nt, B, DM)
            for i in range(nt):
                fw.op("dve", lambda e, i=i: e.tensor_scalar(out=B["xn"][:, i, :], in0=x_t[:, i, :],
                                                             scalar1=B["rstd"][:, i:i + 1], scalar2=None, op0=ALU.mult),
                      reads=[xkey, "rstd"], writes=["xn"])

                def tr(e, i=i):
                    r = None
                    for kc in range(KC):
                        r = e.transpose(out=B["ptp"][:, kc, :], in_=B["xn"][:, i, kc * 128:(kc + 1) * 128],
                                        identity=ident[:])
                    return r
                fw.op("pe", tr, reads=["xn", "ident"], writes=["ptp"])
                fw.op("dve", lambda e, i=i: e.tensor_tensor(
                    out=dstT[:, :, col0 + i * 128: col0 + (i + 1) * 128], in0=B["ptp"][:, 0:KC, :],
                    in1=gT[:, gidx, :].unsqueeze(2).to_broadcast([128, KC, 128]), op=ALU.mult),
                    reads=["ptp", "gT"], writes=[dkey])

        def ffn_group(B, x_t, xkey, nt, gidx, wgu_scr, wgukey, wd_scr, wdkey, out_t, okey):
            N = nt * 128
            if not B.get("wd_loaded"):
                fw.dma("sp", B["wd"][:], wd_scr.rearrange("p (fc d) -> p fc d", fc=FC), reads=[wdkey], writes=["wd"])
                B["wd_loaded"] = True
            norm_T(x_t, xkey, nt, B, gidx, B["xnT"], "xnT")
            for fc in range(FC):
                c = B["cnt"]; B["cnt"] += 1
                wb = B["wgu"][c % 3]; wk = "wgu%d" % (c % 3)
                pg = B["pg"][c % 2]; pu = B["pu"][c % 2]; sg = B["sg"][c % 2]
                pgk, puk, sgk = "pg%d" % (c % 2), "pu%d" % (c % 2), "sg%d" % (c % 2)
                fw.dma("sp", wb[:], wgu_scr[fc].rearrange("p (kc two j) -> p kc two j", kc=KC, two=2),
                       reads=[wgukey], writes=[wk])

                def mm(e, wb=wb, pg=pg, pu=pu):
                    r = None
                    for two, pp in ((0, pg), (1, pu)):
                        for kc in range(KC):
                            r = e.matmul(pp[:, 0:N], lhsT=wb[:, kc, two, :], rhs=B["xnT"][:, kc, 0:N],
                                         start=(kc == 0), stop=(kc == KC - 1))
                    return r
                fw.op("pe", mm, reads=[wk, "xnT"], writes=[pgk, puk])
                fw.op("act", lambda e, pg=pg, sg=sg: e.activation(out=sg[:, 0:N], in_=pg[:, 0:N], func=AF.Silu),
                      reads=[pgk], writes=[sgk])
                fw.op("dve", lambda e, pu=pu, sg=sg, fc=fc: e.tensor_tensor(out=B["actT"][:, fc, 0:N], in0=sg[:, 0:N],
                                                                          in1=pu[:, 0:N], op=ALU.mult),
                      reads=[sgk, puk], writes=["actT"])
            for i in range(nt):
                for dh in range(DM // 512):
                    c = B["cnt"]; B["cnt"] += 1
                    pd = B["pd"][c % 2]; pdk = "pd%d" % (c % 2)

                    def mm2(e, i=i, dh=dh, pd=pd):
                        r = None
                        for fc in range(FC):
                            r = e.matmul(pd[:], lhsT=B["actT"][:, fc, i * 128:(i + 1) * 128],
                                         rhs=B["wd"][:, fc, dh * 512:(dh + 1) * 512],
                                         start=(fc == 0), stop=(fc == FC - 1))
                        return r
                    fw.op("pe", mm2, reads=["actT", "wd"], writes=[pdk])
                    fw.op("dve", lambda e, i=i, dh=dh, pd=pd: e.scalar_tensor_tensor(
                        out=out_t[:, i, dh * 512:(dh + 1) * 512], in0=pd[:], scalar=0.5,
                        in1=x_t[:, i, dh * 512:(dh + 1) * 512], op0=ALU.mult, op1=ALU.add),
                        reads=[pdk, xkey], writes=[okey])

        def phase1(x_src, h_dst, ntiles):
            with ExitStack() as st:
                B = ffn_phase(st, "p1")
                xt = [sb(st, "p1x%d" % i, [128, GT, DM]) for i in range(2)]
                ht = sb(st, "p1h", [128, GT, DM])
                g0 = 0; gi = 0
                while g0 < ntiles:
                    nt = min(GT, ntiles - g0)
                    x_t = xt[gi % 2]; xk = "x%d" % (gi % 2)
                    fw.dma("sp", x_t[:, 0:nt, :], x_src[g0 * 128:(g0 + nt) * 128, :].rearrange("(i p) d -> p i d", p=128),
                           writes=[xk])
                    ffn_group(B, x_t, xk, nt, 0, w1gu_s, "w1gu_s", w1d_s, "w1d_s", ht, "ht")
                    fw.dma("pool", h_dst[g0 * 128:(g0 + nt) * 128, :].rearrange("(i p) d -> p i d", p=128), ht[:, 0:nt, :],
                           reads=["ht"], writes=["hdst"])
                    g0 += nt; gi += 1
                fw.barrier()

        def phase3(h_src, mix_src, y_dst, ntiles):
            with ExitStack() as st:
                B = ffn_phase(st, "p3")
                hin = sb(st, "p3hin", [128, GT, DM])
                h2 = sb(st, "p3h2", [128, GT, DM])
                mt = sb(st, "p3mt", [128, GT, 1024], BF16)
                mT = sb(st, "p3mT", [128, 8, GT * 128], BF16)
                wo = sb(st, "p3wo", [128, 8, DM], BF16)
                fw.dma("sp", wo[:], wout_s.rearrange("p (cc d) -> p cc d", cc=8), reads=["wout_s"], writes=["wo"])
                g0 = 0
                while g0 < ntiles:
                    nt = min(GT, ntiles - g0)
                    rows = slice(g0 * 128, (g0 + nt) * 128)
                    fw.dma("sp", hin[:, 0:nt, :], h_src[rows, :].rearrange("(i p) d -> p i d", p=128), writes=["hin"])
                    fw.dma("sp", mt[:, 0:nt, :], mix_src[rows, :].rearrange("(i p) d -> p i d", p=128), writes=["mt"])
                    for i in range(nt):
                        def tr(e, i=i):
                            r = None
                            for cc in range(8):
                                r = e.transpose(out=B["ptp"][:, cc, :], in_=mt[:, i, cc * 128:(cc + 1) * 128], identity=ident[:])
                            return r
                        fw.op("pe", tr, reads=["mt", "ident"], writes=["ptp"])
                        fw.op("act", lambda e, i=i: e.copy(out=mT[:, :, i * 128:(i + 1) * 128], in_=B["ptp"][:, 0:8, :]),
                              reads=["ptp"], writes=["mT"])
                    for i in range(nt):
                        for dh in range(DM // 512):
                            c = B["cnt"]; B["cnt"] += 1
                            pd = B["pd"][c % 2]; pdk = "pd%d" % (c % 2)

                            def mm(e, i=i, dh=dh, pd=pd):
                                r = None
                                for cc in range(8):
                                    r = e.matmul(pd[:], lhsT=mT[:, cc, i * 128:(i + 1) * 128],
                                                 rhs=wo[:, cc, dh * 512:(dh + 1) * 512], start=(cc == 0), stop=(cc == 7))
                                return r
                            fw.op("pe", mm, reads=["mT", "wo"], writes=[pdk])
                            fw.op("dve", lambda e, i=i, dh=dh, pd=pd: e.tensor_tensor(
                                out=h2[:, i, dh * 512:(dh + 1) * 512], in0=pd[:], in1=hin[:, i, dh * 512:(dh + 1) * 512],
                                op=ALU.add), reads=[pdk, "hin"], writes=["h2"])
                    ffn_group(B, h2, "h2", nt, 2, w2gu_s, "w2gu_s", w2d_s, "w2d_s", h2, "h2")
                    rms_stats(h2, "h2", nt, B, DM)
                    for i in range(nt):
                        fw.op("dve", lambda e, i=i: e.scalar_tensor_tensor(
                            out=h2[:, i, :], in0=h2[:, i, :], scalar=B["rstd"][:, i:i + 1], in1=gfin_b[:],
                            op0=ALU.mult, op1=ALU.mult), reads=["h2", "rstd", "gfin_b"], writes=["h2"])
                    fw.dma("pool", y_dst[rows, :].rearrange("(i p) d -> p i d", p=128), h2[:, 0:nt, :],
                           reads=["h2"], writes=["ydst"])
                    g0 += nt
                fw.barrier()

        LN8 = math.log(0.125)
        VS = 80

        def phase2(h_src, mix_dst, prompt, mode, init_state=None, pos=0):
            full = mode == "full"
            off = 1 if prompt else 0
            ntl = NT + 2 * off
            with ExitStack() as st:
                unT = sb(st, "unT", [128, KC, ntl * 128], BF16)
                gts = sb(st, "gts", [128, NT, 32])
                wsb = sb(st, "wsb", [128, KC, 768], BF16)
                ptp2 = ps(st, "p2ptp", [128, 8, 128], BF16)
                NB = {"ptp": ptp2}
                pA = [ps(st, "p2pa%d" % i, [128, 512]) for i in range(7)]
                cs = sb(st, "cs", [128, 2, NT, 8]); es = sb(st, "es", [128, 2, NT, 8])
                rt = sb(st, "rt", [128, 2, NT, 8]); eA = sb(st, "eA", [128, 2, NT, 8])
                Asum = sb(st, "Asum", [128, 2, 8])
                with ExitStack() as us:
                    NBu = {"xn": sb(us, "p2xn", [128, GT, DM], BF16), "junk": sb(us, "p2junk", [128, DM], BF16),
                           "ss": sb(us, "p2ss", [128, GT]), "rstd": sb(us, "p2rstd", [128, GT]), "ptp": ptp2}
                    hld = [sb(us, "p2h%d" % i, [128, GT, DM]) for i in range(2)]
                    g0 = 0; gi = 0
                    while g0 < ntl:
                        nt = min(GT, ntl - g0)
                        ht = hld[gi % 2]; hk = "hld%d" % (gi % 2)
                        fw.dma("sp", ht[:, 0:nt, :], h_src[g0 * 128:(g0 + nt) * 128, :].rearrange("(i p) d -> p i d", p=128),
                               writes=[hk])
                        norm_T(ht, hk, nt, NBu, 1, unT, "unT", col0=g0 * 128)
                        g0 += nt; gi += 1
                    fw.barrier()
                chk("p2a")
                winv = win_s.rearrange("p (kc c) -> p kc c", kc=KC)

                def load_w(c0, c1):
                    fw.dma("sp", wsb[:, :, 0:c1 - c0], winv[:, :, c0:c1], reads=["win_s"], writes=["wsb"])

                def proj_fm(col, ncol, tok0, ntok, dst_ps):
                    def f(e):
                        r = None
                        for kc in range(KC):
                            r = e.matmul(dst_ps[0:ncol, 0:ntok], lhsT=wsb[:, kc, col:col + ncol],
                                         rhs=unT[:, kc, tok0:tok0 + ntok], start=(kc == 0), stop=(kc == KC - 1))
                        return r
                    return f

                def proj_tm(col, ncol, tile, dst_ps):
                    def f(e):
                        r = None
                        for kc in range(KC):
                            r = e.matmul(dst_ps[:, 0:ncol], lhsT=unT[:, kc, tile * 128:(tile + 1) * 128],
                                         rhs=wsb[:, kc, col:col + ncol], start=(kc == 0), stop=(kc == KC - 1))
                        return r
                    return f

                with ExitStack() as gs:
                    load_w(2816, 2848)
                    for i in range(NT):
                        pp = pA[i % 2]; pk = "pa%d" % (i % 2)
                        fw.op("pe", proj_tm(0, 32, i + off, pp), reads=["wsb", "unT"], writes=[pk])
                        fw.op("dve", lambda e, i=i, pp=pp: e.tensor_tensor(out=gts[:, i, :], in0=pp[:, 0:32], in1=bg_b[:],
                                                                          op=ALU.add), reads=[pk, "bg_b"], writes=["gts"])
                    chk("p2b")
                    W = NT * 8
                    g4 = gts[:].rearrange("p n (a h) -> p n a h", a=4)
                    fx = sb(gs, "fx", [128, 2, NT, 8]); t1 = sb(gs, "t1", [128, 2, NT, 8]); t2 = sb(gs, "t2", [128, 2, NT, 8])
                    lf = sb(gs, "lf", [128, 2, NT, 8])
                    parts = [sb(gs, "lfp%d" % i, [128, 2, NT, 8], BF16) for i in range(3)]
                    for d in range(2):
                        fw.op("dve", lambda e, d=d: e.tensor_copy(out=fx[:, d, :, :], in_=g4[:, :, 1 + 2 * d, :]),
                              reads=["gts"], writes=["fx"])
                    fw.op("dve", lambda e: e.tensor_single_scalar(out=t2[:], in_=fx[:], scalar=0.0, op=ALU.min),
                          reads=["fx"], writes=["t2"])
                    fw.op("dve", lambda e: e.scalar_tensor_tensor(out=t1[:], in0=t2[:], scalar=2.0, in1=fx[:],
                                                                  op0=ALU.mult, op1=ALU.subtract),
                          reads=["fx", "t2"], writes=["t1"])
                    fw.op("act", lambda e: e.activation(out=t1[:], in_=t1[:], func=AF.Exp),
                          reads=["t1"], writes=["t1"])
                    fw.op("act", lambda e: e.activation(out=t1[:], in_=t1[:], func=AF.Ln, bias=1.0),
                          reads=["t1"], writes=["t1"])
                    fw.op("dve", lambda e: e.tensor_tensor(out=lf[:], in0=t2[:], in1=t1[:], op=ALU.subtract),
                          reads=["t1", "t2"], writes=["lf"])
                    fw.op("dve", lambda e: e.tensor_copy(out=parts[0][:], in_=lf[:]), reads=["lf"], writes=["lfp0"])
                    fw.op("dve", lambda e: e.tensor_tensor(out=t1[:], in0=lf[:], in1=parts[0][:], op=ALU.subtract),
                          reads=["lf", "lfp0"], writes=["t1"])
                    fw.op("dve", lambda e: e.tensor_copy(out=parts[1][:], in_=t1[:]), reads=["t1"], writes=["lfp1"])
                    fw.op("dve", lambda e: e.tensor_tensor(out=t2[:], in0=t1[:], in1=parts[1][:], op=ALU.subtract),
                          reads=["t1", "lfp1"], writes=["t2"])
                    fw.op("dve", lambda e: e.tensor_copy(out=parts[2][:], in_=t2[:]), reads=["t2"], writes=["lfp2"])
                    pcs = pA[2]

                    def cums(e):
                        r = None
                        for d in range(2):
                            tri = maskF if d == 0 else maskB
                            for (mat, o) in ((tri, d * W), (ones_b, 2 * W + d * W)):
                                for k in range(3):
                                    r = e.matmul(pcs[:, o:o + W], lhsT=mat[:],
                                                 rhs=parts[k][:, d, :, :].rearrange("p n h -> p (n h)"),
                                                 start=(k == 0), stop=(k == 2))
                        return r
                    chk("p2c0")
                    fw.op("pe", cums, reads=["lfp0", "lfp1", "lfp2", "maskF", "maskB", "ones_b"], writes=["pa2"])
                    chk("p2c1")
                    pcv = pcs[:, 0:4 * W].rearrange("p (q d n h) -> p q d n h", q=2, d=2, n=NT)
                    for d in range(2):
                        fw.op("dve", lambda e, d=d: e.tensor_tensor(out=t1[:, d, :, :], in0=g4[:, :, 2 * d, :],
                                                                     in1=pcv[:, 0, d, :, :], op=ALU.subtract),
                              reads=["gts", "pa2"], writes=["t1"])
                    fw.op("act", lambda e: e.activation(out=cs[:], in_=t1[:], func=AF.Exp, bias=LN8),
                          reads=["t1"], writes=["cs"])
                    fw.op("dve", lambda e: e.tensor_tensor(out=t2[:], in0=t1[:], in1=pcv[:, 1, :, :, :], op=ALU.add),
                          reads=["t1", "pa2"], writes=["t2"])
                    fw.op("act", lambda e: e.activation(out=es[:], in_=t2[:], func=AF.Exp, bias=LN8),
                          reads=["t2"], writes=["es"])
                    chk("p2c2")
                    fw.op("act", lambda e: e.activation(out=rt[:], in_=pcv[:, 0, :, :, :], func=AF.Exp),
                          reads=["pa2"], writes=["rt"])
                    fw.op("act", lambda e: e.activation(out=eA[:], in_=pcv[:, 1, :, :, :], func=AF.Exp),
                          reads=["pa2"], writes=["eA"])
                    chk("p2c3")
                    if not full:
                        fw.op("dve", lambda e: e.tensor_copy(out=t1[:], in_=pcv[:, 1, :, :, :]), reads=["pa2"], writes=["t1"])
                        fw.op("dve", lambda e: e.memset(lf[:], 0.0), reads=["lf"], writes=["lf"])
                        for c_ in range(NT - 2, -1, -1):
                            fw.op("dve", lambda e, c_=c_: e.tensor_tensor(out=lf[:, 0, c_, :], in0=lf[:, 0, c_ + 1, :],
                                                                         in1=t1[:, 0, c_ + 1, :], op=ALU.add),
                                  reads=["lf", "t1"], writes=["lf"])
                        for c_ in range(1, NT):
                            fw.op("dve", lambda e, c_=c_: e.tensor_tensor(out=lf[:, 1, c_, :], in0=lf[:, 1, c_ - 1, :],
                                                                         in1=t1[:, 1, c_ - 1, :], op=ALU.add),
                                  reads=["lf", "t1"], writes=["lf"])
                        fw.op("dve", lambda e: e.tensor_tensor(out=t2[:], in0=t2[:], in1=lf[:], op=ALU.add),
                              reads=["lf", "t2", "es"], writes=["t2"])
                        fw.op("act", lambda e: e.activation(out=es[:], in_=t2[:], func=AF.Exp, bias=LN8),
                              reads=["t2"], writes=["es"])
                        chk("p2c4")
                        fw.op("dve", lambda e: e.tensor_copy(out=Asum[:], in_=t1[:, :, 0, :]), reads=["t1"], writes=["Asum"])
                        for n_ in range(1, NT):
                            fw.op("dve", lambda e, n_=n_: e.tensor_tensor(out=Asum[:], in0=Asum[:], in1=t1[:, :, n_, :], op=ALU.add),
                                  reads=["t1", "Asum"], writes=["Asum"])
                    chk("p2c5")
                    fw.barrier()

                chk("p2c")
                if full:
                    with ExitStack() as at:
                        qaT = sb(at, "qaT", [128, 4, S], BF16)
                        kT = sb(at, "kT", [128, ntl * 128], BF16)
                        vtm = sb(at, "vtm", [128, ntl, 128], BF16)
                        ssb = sb(at, "ssb", [128, 4, 384]); pbf = sb(at, "pbf", [128, 4, 384], BF16)
                        pts = [sb(at, "pts%d" % i, [128, 3, 128], BF16) for i in range(2)]
                        mx = sb(at, "mx", [128, 4]); negm = sb(at, "negm", [128, 4]); rs = sb(at, "rs", [128, 4])
                        tmp4 = sb(at, "tmp4", [128, 4]); rinv = sb(at, "rinv", [128, 4])
                        mixa = [sb(at, "mixa%d" % i, [128, 512], BF16) for i in range(2)]
                        load_w(0, 768)
                        TB = 512
                        for c in range(4):
                            for t0 in range(0, S, TB):
                                n = min(TB, S - t0)
                                pp = pA[5 + (c + t0 // TB) % 2]; pk = "pa%d" % (5 + (c + t0 // TB) % 2)
                                fw.op("pe", proj_fm(c * 128, 128, off * 128 + t0, n, pp), reads=["wsb", "unT"], writes=[pk])
                                fw.op("act", lambda e, c=c, t0=t0, n=n, pp=pp: e.mul(
                                    out=qaT[:, c, t0:t0 + n], in_=pp[:, 0:n], mul=0.125),
                                    reads=[pk], writes=["qaT"])
                        for t0 in range(0, ntl * 128, TB):
                            n = min(TB, ntl * 128 - t0)
                            pp = pA[5 + (t0 // TB) % 2]; pk = "pa%d" % (5 + (t0 // TB) % 2)
                            fw.op("pe", proj_fm(512, 128, t0, n, pp), reads=["wsb", "unT"], writes=[pk])
                            fw.op("act", lambda e, t0=t0, n=n, pp=pp: e.copy(out=kT[:, t0:t0 + n], in_=pp[:, 0:n]),
                                  reads=[pk], writes=["kT"])
                        for i in range(ntl):
                            pp = pA[5 + i % 2]; pk = "pa%d" % (5 + i % 2)
                            fw.op("pe", proj_tm(640, 128, i, pp), reads=["wsb", "unT"], writes=[pk])
                            fw.op("act", lambda e, i=i, pp=pp: e.copy(out=vtm[:, i, :], in_=pp[:, 0:128]),
                                  reads=[pk], writes=["vtm"])
                        pO = pA[4]
                        for i in range(NT):
                            ti = i + off
                            lo = ti - 1 if ti - 1 >= 0 else ti
                            hi = ti + 1 if ti + 1 < ntl else ti
                            nk = hi - lo + 1
                            b0 = (lo - (ti - 1)) * 128
                            ma = mixa[i % 2]; mak = "mixa%d" % (i % 2)
                            for g in range(2):
                                def smm(e, g=g, i=i, lo=lo, nk=nk):
                                    r = None
                                    for c in range(4):
                                        r = e.matmul(pA[c][:, 0:nk * 128], lhsT=qaT[g * 64:(g + 1) * 64, c, i * 128:(i + 1) * 128],
                                                     rhs=kT[g * 64:(g + 1) * 64, lo * 128:(lo + nk) * 128], start=True, stop=True)
                                    return r
                                fw.op("pe", smm, reads=["qaT", "kT"], writes=["pa0", "pa1", "pa2", "pa3"])
                                for c in range(4):
                                    fw.op("dve", lambda e, c=c, g=g, nk=nk, b0=b0: e.tensor_tensor(
                                        out=ssb[:, c, 0:nk * 128], in0=pA[c][:, 0:nk * 128],
                                        in1=bias[:, g * 4 + c, b0:b0 + nk * 128], op=ALU.add),
                                        reads=["pa%d" % c, "bias"], writes=["ssb"])
                                if prompt and i == 0:
                                    fw.op("dve", lambda e: e.tensor_scalar(out=ssb[:, :, 0:128], in0=ssb[:, :, 0:128],
                                                                            scalar1=flg[:, 0:1], scalar2=None, op0=ALU.add),
                                          reads=["ssb", "flg"], writes=["ssb"])
                                if prompt and i == NT - 1:
                                    fw.op("dve", lambda e: e.tensor_scalar(out=ssb[:, :, 256:384], in0=ssb[:, :, 256:384],
                                                                            scalar1=flg[:, 1:2], scalar2=None, op0=ALU.add),
                                          reads=["ssb", "flg"], writes=["ssb"])
                                fw.op("dve", lambda e, nk=nk: e.tensor_reduce(out=mx[:], in_=ssb[:, :, 0:nk * 128], axis=AX.X,
                                                                              op=ALU.max), reads=["ssb"], writes=["mx"])
                                fw.op("dve", lambda e, g=g: e.tensor_tensor(out=mx[:], in0=mx[:], in1=sink_b[:, g * 4:(g + 1) * 4],
                                                                             op=ALU.max), reads=["mx", "sink_b"], writes=["mx"])
                                fw.op("dve", lambda e: e.tensor_scalar(out=negm[:], in0=mx[:], scalar1=-1.0, scalar2=None,
                                                                        op0=ALU.mult), reads=["mx"], writes=["negm"])
                                for c in range(4):
                                    fw.op("act", lambda e, c=c, nk=nk: e.activation(
                                        out=pbf[:, c, 0:nk * 128], in_=ssb[:, c, 0:nk * 128], func=AF.Exp,
                                        bias=negm[:, c:c + 1], accum_out=rs[:, c:c + 1]),
                                        reads=["ssb", "negm"], writes=["pbf", "rs"])
                                fw.op("dve", lambda e, g=g: e.tensor_tensor(out=tmp4[:], in0=sink_b[:, g * 4:(g + 1) * 4], in1=mx[:],
                                                                             op=ALU.subtract), reads=["mx", "sink_b"], writes=["tmp4"])
                                fw.op("act", lambda e: e.activation(out=tmp4[:], in_=tmp4[:], func=AF.Exp),
                                      reads=["tmp4"], writes=["tmp4"])
                                fw.op("dve", lambda e: e.tensor_tensor(out=tmp4[:], in0=tmp4[:], in1=rs[:], op=ALU.add),
                                      reads=["tmp4", "rs"], writes=["tmp4"])
                                fw.op("dve", lambda e: e.reciprocal(out=rinv[:], in_=tmp4[:]), reads=["tmp4"], writes=["rinv"])
                                for c in range(4):
                                    pt = pts[c % 2]; ptk = "pts%d" % (c % 2)

                                    def trp(e, c=c, nk=nk):
                                        r = None
                                        for kb in range(nk):
                                            r = e.transpose(out=NB["ptp"][:, kb, :], in_=pbf[:, c, kb * 128:(kb + 1) * 128],
                                                            identity=ident[:])
                                        return r
                                    fw.op("pe", trp, reads=["pbf", "ident"], writes=["ptp"])
                                    fw.op("act", lambda e, pt=pt, nk=nk: e.copy(out=pt[:, 0:nk, :], in_=NB["ptp"][:, 0:nk, :]),
                                          reads=["ptp"], writes=[ptk])

                                    def pv(e, c=c, nk=nk, lo=lo, g=g, pt=pt):
                                        r = None
                                        for kb in range(nk):
                                            r = e.matmul(pO[:, c * 64:(c + 1) * 64], lhsT=pt[:, kb, :],
                                                         rhs=vtm[:, lo + kb, g * 64:(g + 1) * 64],
                                                         start=(kb == 0), stop=(kb == nk - 1))
                                        return r
                                    fw.op("pe", pv, reads=[ptk, "vtm"], writes=["pa4"])
                                fw.op("dve", lambda e, g=g, ma=ma: e.tensor_tensor(
                                    out=ma[:, g * 256:(g + 1) * 256].rearrange("p (c d) -> p c d", c=4),
                                    in0=pO[:, 0:256].rearrange("p (c d) -> p c d", c=4),
                                    in1=rinv[:].unsqueeze(2).to_broadcast([128, 4, 64]), op=ALU.mult),
                                    reads=["pa4", "rinv"], writes=[mak])
                            fw.dma("sp", mix_dst[i * 128:(i + 1) * 128, 0:512], ma[:], reads=[mak], writes=["mixdst"])
                        fw.barrier()

                if full:
                    chk("s2a")
                finals, fin_A = finals_t, fin_A_t
                with ExitStack() as ml:
                    pre = [sb(ml, "pre%d" % i, [128, S + 4]) for i in range(2)]
                    acc = sb(ml, "acc", [128, S])
                    qkT = [sb(ml, "qkT%d" % i, [128, S], BF16) for i in range(2)]
                    ktok = sb(ml, "ktok", [128, NT, 128], BF16)
                    v1 = sb(ml, "v1", [128, NT, 2, VS], BF16)
                    v1s = [sb(ml, "v1s%d" % d, [128, NT, 2, VS], BF16) for d in range(2)]
                    v1e = [sb(ml, "v1e%d" % d, [128, NT, 2, VS], BF16) for d in range(2)]
                    og = sb(ml, "og", [128, NT, 128])
                    hsum = sb(ml, "hsum", [128, NT, 128])
                    stt = [sb(ml, "stt%d" % d, [128, 65]) for d in range(2)]
                    stb = [sb(ml, "stb%d" % d, [128, 2, VS], BF16) for d in range(2)]
                    qblk = sb(ml, "qblk", [128, NT, 2, 128], BF16)
                    PTs = [sb(ml, "PT%d" % d, [128, 2, 128], BF16) for d in range(2)]
                    dd = [sb(ml, "dd%d" % d, [128, 2]) for d in range(2)]
                    rr = [sb(ml, "rr%d" % d, [128, 2]) for d in range(2)]
                    msq = sb(ml, "msq", [128, NT, 2])
                    mixm = sb(ml, "mixm", [128, NT, 128], BF16)
                    fw.op("dve", lambda e: e.memset(v1[:], 1.0), writes=["v1"])
                    fw.op("dve", lambda e: e.memset(qblk[:], 0.0), writes=["qblk"])
                    for d in range(2):
                        fw.op("dve", lambda e, d=d: e.memset(stb[d][:], 0.0), writes=["stb%d" % d])
                    for j in range(4):
                        for bi, c0 in enumerate((768, 1280, 1792, 2304)):
                            fw.dma("sp", wsb[:, :, bi * 128:(bi + 1) * 128], winv[:, :, c0 + j * 128:c0 + (j + 1) * 128],
                                   reads=["win_s"], writes=["wsb"])
                        for qi in range(2):
                            if qi == 0 and not full:
                                continue
                            pr = pre[qi]; prk = "pre%d" % qi
                            if prompt:
                                for (t0, n, dcol) in ((off * 128 - 2, 2, 0), ((off + NT) * 128, 2, S + 2)):
                                    pp = pA[5]; pk = "pa5"
                                    fw.op("pe", proj_fm(qi * 128, 128, t0, n, pp), reads=["wsb", "unT"], writes=[pk])
                                    fw.op("act", lambda e, pr=pr, dcol=dcol, pp=pp: e.copy(out=pr[:, dcol:dcol + 2], in_=pp[:, 0:2]),
                                          reads=[pk], writes=[prk])
                                    fcol_ = (18 + pos) if dcol == 0 else (26 + pos)
                                    fw.op("dve", lambda e, pr=pr, dcol=dcol, fcol_=fcol_: e.tensor_scalar(
                                        out=pr[:, dcol:dcol + 2], in0=pr[:, dcol:dcol + 2], scalar1=flg[:, fcol_:fcol_ + 1],
                                        scalar2=None, op0=ALU.mult), reads=[prk, "flg"], writes=[prk])
                            else:
                                fw.op("dve", lambda e, pr=pr: e.memset(pr[:, 0:2], 0.0), writes=[prk])
                                fw.op("dve", lambda e, pr=pr: e.memset(pr[:, S + 2:S + 4], 0.0), writes=[prk])
                            for t0 in range(0, S, 512):
                                n = min(512, S - t0)
                                pp = pA[5 + (t0 // 512) % 2]; pk = "pa%d" % (5 + (t0 // 512) % 2)
                                fw.op("pe", proj_fm(qi * 128, 128, off * 128 + t0, n, pp), reads=["wsb", "unT"], writes=[pk])
                                fw.op("act", lambda e, pr=pr, t0=t0, n=n, pp=pp: e.copy(out=pr[:, 2 + t0:2 + t0 + n], in_=pp[:, 0:n]),
                                      reads=[pk], writes=[prk])
                            ch = qi * 4 + j
                            fw.op("dve", lambda e, pr=pr, ch=ch: e.tensor_scalar(out=acc[:], in0=pr[:, 0:S], scalar1=wcv[:, ch, 0:1],
                                                                                  scalar2=None, op0=ALU.mult),
                                  reads=[prk, "wcv"], writes=["acc"])
                            for tap in range(1, 5):
                                fw.op("dve", lambda e, pr=pr, ch=ch, tap=tap: e.scalar_tensor_tensor(
                                    out=acc[:], in0=pr[:, tap:tap + S], scalar=wcv[:, ch, tap:tap + 1], in1=acc[:],
                                    op0=ALU.mult, op1=ALU.add), reads=[prk, "wcv", "acc"], writes=["acc"])
                            fw.op("act", lambda e, qi=qi: e.activation(out=qkT[qi][:], in_=acc[:], func=AF.Silu),
                                  reads=["acc"], writes=["qkT%d" % qi])
                            if qi == 0 and full:
                                for hh in range(2):
                                    hs = slice(hh * 64, (hh + 1) * 64)
                                    fw.op("act", lambda e, hh=hh, hs=hs: e.activation(
                                        out=qblk[hs, :, hh, :], in_=acc[hs, :].rearrange("p (n t) -> p n t", t=128), func=AF.Silu),
                                        reads=["acc"], writes=["qblk"])
                        chk("p2d")
                        for i in range(NT):
                            fw.op("pe", lambda e, i=i: e.transpose(out=NB["ptp"][:, i % 8, :], in_=qkT[1][:, i * 128:(i + 1) * 128],
                                                                    identity=ident[:]), reads=["qkT1", "ident"], writes=["ptp"])
                            fw.op("act", lambda e, i=i: e.copy(out=ktok[:, i, :], in_=NB["ptp"][:, i % 8, :]),
                                  reads=["ptp"], writes=["ktok"])
                        for i in range(NT):
                            pp = pA[5 + i % 2]; pk = "pa%d" % (5 + i % 2)
                            fw.op("pe", proj_tm(256, 256, i + off, pp), reads=["wsb", "unT"], writes=[pk])
                            fw.op("dve", lambda e, i=i, pp=pp: e.tensor_copy(
                                out=v1[:, i, :, 0:64], in_=pp[:, 0:128].rearrange("p (a d) -> p a d", a=2)),
                                reads=[pk], writes=["v1"])
                            if full:
                                fw.op("act", lambda e, i=i, pp=pp: e.activation(out=og[:, i, :], in_=pp[:, 128:256], func=AF.Sigmoid),
                                      reads=[pk], writes=["og"])
                        for d in range(2):
                            for (dst, scal, dk, sk) in ((v1s[d], cs, "v1s%d" % d, "cs"), (v1e[d], es, "v1e%d" % d, "es")):
                                if dst is v1s[d] and not full:
                                    continue
                                fw.op("dve", lambda e, dst=dst, scal=scal, d=d: e.tensor_tensor(
                                    out=dst[:], in0=v1[:],
                                    in1=scal[:, d, :, 2 * j:2 * j + 2].unsqueeze(3).to_broadcast([128, NT, 2, VS]),
                                    op=ALU.mult), reads=["v1", sk], writes=[dk])
                        chk("p2e")
                        if not full:
                            for d in range(2):
                                pC = pA[d]; pCk = "pa%d" % d

                                def accmm(e, d=d, pC=pC):
                                    r = None
                                    for c in range(NT):
                                        r = e.matmul(pC[:, 0:2 * VS], lhsT=ktok[:, c, :],
                                                     rhs=v1e[d][:, c, :, :].rearrange("p a c -> p (a c)"),
                                                     start=(c == 0), stop=(c == NT - 1))
                                    return r
                                fw.op("pe", accmm, reads=["ktok", "v1e%d" % d], writes=[pCk])
                                for hh in range(2):
                                    hs = slice(hh * 64, (hh + 1) * 64)
                                    fw.op("dve", lambda e, d=d, hh=hh, hs=hs, pC=pC: e.tensor_copy(
                                        out=finals[hs, j, d, :], in_=pC[hs, hh * VS:hh * VS + 65]),
                                        reads=[pCk], writes=["finals"])
                                    fw.op("dve", lambda e, d=d, hh=hh, hs=hs: e.tensor_copy(
                                        out=fin_A[hs, j, d:d + 1], in_=Asum[hs, d, 2 * j + hh:2 * j + hh + 1]),
                                        reads=["Asum"], writes=["fin_A"])
                            continue
                        for d in range(2):
                            if init_state is not None:
                                fw.op("dve", lambda e, d=d: e.tensor_copy(out=stt[d][:], in_=init_state[:, j, d, :]),
                                      reads=["init_state"], writes=["stt%d" % d])
                            else:
                                fw.op("dve", lambda e, d=d: e.memset(stt[d][:], 0.0), writes=["stt%d" % d])
                            for hh in range(2):
                                hs = slice(hh * 64, (hh + 1) * 64)
                                fw.op("act", lambda e, d=d, hh=hh, hs=hs: e.copy(out=stb[d][hs, hh, 0:65], in_=stt[d][hs, :]),
                                      reads=["stt%d" % d], writes=["stb%d" % d])
                        if full:
                            chk("m0")
                        for step in range(NT):
                            if full and step == 1:
                                chk("m1")
                            for d in range(2):
                                c = step if d == 0 else NT - 1 - step
                                pS, pN = pA[0 + d], pA[2 + d]
                                pSk, pNk = "pa%d" % d, "pa%d" % (2 + d)
                                pC = pA[4]; pCk = "pa4"
                                mk_ = maskF if d == 0 else maskB
                                if full:
                                    def smm(e, c=c, pS=pS):
                                        return e.matmul(pS[:, 0:256], lhsT=qkT[1][:, c * 128:(c + 1) * 128],
                                                        rhs=qblk[:, c, :, :].rearrange("p a t -> p (a t)"), start=True, stop=True)
                                    fw.op("pe", smm, reads=["qblk", "qkT1"], writes=[pSk])
                                    chk("q1")
                                    fw.op("dve", lambda e, d=d, pS=pS, mk_=mk_: e.tensor_tensor(
                                        out=PTs[d][:], in0=pS[:, 0:256].rearrange("p (a t) -> p a t", a=2),
                                        in1=mk_[:].unsqueeze(1).to_broadcast([128, 2, 128]), op=ALU.mult),
                                        reads=[pSk, "maskF", "maskB"], writes=["PT%d" % d])
                                    chk("q2")

                                    def nmm(e, c=c, d=d, pN=pN):
                                        e.matmul(pN[:, 0:2 * VS], lhsT=qkT[0][:, c * 128:(c + 1) * 128],
                                                 rhs=stb[d][:].rearrange("p a c -> p (a c)"), start=True, stop=False)
                                        r = None
                                        for hh in range(2):
                                            r = e.matmul(pN[:, hh * VS:(hh + 1) * VS], lhsT=PTs[d][:, hh, :], rhs=v1s[d][:, c, hh, :],
                                                         start=False, stop=(hh == 1))
                                        return r
                                    fw.op("pe", nmm, reads=["PT%d" % d, "v1s%d" % d, "qkT0", "stb%d" % d], writes=[pNk])
                                    chk("q3")
                                    pNv = pN[:, 0:2 * VS].rearrange("p (a c) -> p a c", a=2)
                                    rtv = rt[:, d, c, 2 * j:2 * j + 2]
                                    fw.op("dve", lambda e, d=d, pNv=pNv, rtv=rtv: e.tensor_tensor(
                                        out=dd[d][:].unsqueeze(2), in0=pNv[:, :, 64:65], in1=rtv.unsqueeze(2), op=ALU.mult),
                                        reads=[pNk, "rt"], writes=["dd%d" % d])
                                    fw.op("dve", lambda e, d=d: e.scalar_tensor_tensor(out=rr[d][:], in0=dd[d][:], scalar=-1.0,
                                                                                        in1=dd[d][:], op0=ALU.mult, op1=ALU.max),
                                          reads=["dd%d" % d], writes=["rr%d" % d])
                                    fw.op("dve", lambda e, d=d: e.tensor_scalar(out=rr[d][:], in0=rr[d][:], scalar1=1.0, scalar2=None,
                                                                                 op0=ALU.max),
                                          reads=["rr%d" % d], writes=["rr%d" % d])
                                    fw.op("dve", lambda e, d=d: e.reciprocal(out=rr[d][:], in_=rr[d][:]),
                                          reads=["rr%d" % d], writes=["rr%d" % d])
                                    fw.op("dve", lambda e, d=d, rtv=rtv: e.tensor_tensor(out=rr[d][:], in0=rtv, in1=rr[d][:],
                                                                                        op=ALU.mult),
                                          reads=["rr%d" % d, "rt"], writes=["rr%d" % d])
                                    chk("q4")
                                    step_f, step_b = c, NT - 1 - c
                                    first = (step_f < step_b) if d == 0 else (step_b < step_f)
                                    if step_f == step_b:
                                        first = (d == 0)
                                    for hh in range(2):
                                        if first:
                                            fw.op("dve", lambda e, d=d, c=c, hh=hh, pNv=pNv: e.tensor_scalar(
                                                out=hsum[:, c, hh * 64:(hh + 1) * 64], in0=pNv[:, hh, 0:64],
                                                scalar1=rr[d][:, hh:hh + 1], scalar2=None, op0=ALU.mult),
                                                reads=[pNk, "rr%d" % d], writes=["hsum"])
                                        else:
                                            fw.op("dve", lambda e, d=d, c=c, hh=hh, pNv=pNv: e.scalar_tensor_tensor(
                                                out=hsum[:, c, hh * 64:(hh + 1) * 64], in0=pNv[:, hh, 0:64],
                                                scalar=rr[d][:, hh:hh + 1], in1=hsum[:, c, hh * 64:(hh + 1) * 64],
                                                op0=ALU.mult, op1=ALU.add), reads=[pNk, "rr%d" % d, "hsum"], writes=["hsum"])
                                fw.op("pe", lambda e, c=c, d=d: e.matmul(
                                    pC[:, 0:2 * VS], lhsT=ktok[:, c, :], rhs=v1e[d][:, c, :, :].rearrange("p a c -> p (a c)"),
                                    start=True, stop=True), reads=["ktok", "v1e%d" % d], writes=[pCk])
                                for hh in range(2):
                                    hs = slice(hh * 64, (hh + 1) * 64)
                                    fw.op("dve", lambda e, d=d, c=c, hh=hh, hs=hs: e.scalar_tensor_tensor(
                                        out=stt[d][hs, :], in0=stt[d][hs, :], scalar=eA[hs, d, c, 2 * j + hh:2 * j + hh + 1],
                                        in1=pC[hs, hh * VS:hh * VS + 65], op0=ALU.mult, op1=ALU.add),
                                        reads=["stt%d" % d, "eA", pCk], writes=["stt%d" % d])
                                for hh in range(2):
                                    hs = slice(hh * 64, (hh + 1) * 64)
                                    fw.op("act", lambda e, d=d, hh=hh, hs=hs: e.copy(out=stb[d][hs, hh, 0:65], in_=stt[d][hs, :]),
                                          reads=["stt%d" % d], writes=["stb%d" % d])
                        if full:
                            chk("m2")
                            fw.op("dve", lambda e: e.tensor_tensor(out=hsum[:], in0=hsum[:], in1=og[:], op=ALU.mult),
                                  reads=["hsum", "og"], writes=["hsum"])
                            fw.op("dve", lambda e: e.tensor_tensor(out=og[:], in0=hsum[:], in1=hsum[:], op=ALU.mult),
                                  reads=["hsum", "og"], writes=["og"])
                            fw.op("dve", lambda e: e.tensor_reduce(out=msq[:], in_=og[:].rearrange("p n (a d) -> p n a d", a=2),
                                                                   axis=AX.X, op=ALU.add), reads=["og"], writes=["msq"])
                            fw.op("act", lambda e: e.activation(out=msq[:], in_=msq[:], func=AF.Sqrt, scale=1.0 / 64, bias=EPS),
                                  reads=["msq"], writes=["msq"])
                            fw.op("dve", lambda e: e.reciprocal(out=msq[:], in_=msq[:]), reads=["msq"], writes=["msq"])
                            fw.op("dve", lambda e: e.tensor_tensor(
                                out=hsum[:].rearrange("p n (a d) -> p n a d", a=2), in0=hsum[:].rearrange("p n (a d) -> p n a d", a=2),
                                in1=msq[:].unsqueeze(3).to_broadcast([128, NT, 2, 64]), op=ALU.mult),
                                reads=["hsum", "msq"], writes=["hsum"])
                            fw.op("dve", lambda e: e.tensor_tensor(
                                out=mixm[:], in0=hsum[:],
                                in1=gml_b[:, j * 128:(j + 1) * 128].unsqueeze(1).to_broadcast([128, NT, 128]), op=ALU.mult),
                                reads=["hsum", "gml_b"], writes=["mixm"])
                            chk("m3")
                            fw.dma("sp", mix_dst[:, 512 + j * 128:512 + (j + 1) * 128].rearrange("(n p) c -> p n c", p=128),
                                   mixm[:], reads=["mixm"], writes=["mixdst"])
                            chk("m4")
                        else:
                            for d in range(2):
                                fw.op("dve", lambda e, d=d: e.tensor_copy(out=finals[:, j, d, :], in_=stt[d][:]),
                                      reads=["stt%d" % d], writes=["finals"])
                                for hh in range(2):
                                    hs = slice(hh * 64, (hh + 1) * 64)
                                    fw.op("dve", lambda e, d=d, hh=hh, hs=hs: e.tensor_copy(
                                        out=fin_A[hs, j, d:d + 1], in_=Asum[hs, d, 2 * j + hh:2 * j + hh + 1]),
                                        reads=["Asum"], writes=["fin_A"])
                    fw.barrier()
                fw.barrier()
            return (finals, fin_A) if not full else None

        def main_schedule():
            if with_prompt:
                phase1(x_pfull, hbuf_pf, n_ranks * NT + 2)
                chk("p1")
                for r in range(1, n_ranks):
                    finals, fin_A = phase2(hbuf_pf[r * S:r * S + NTP * 128, :], None, True, "summary", pos=r)
                    chk("p2s")
                    fw.dma("sp", g_dst[r * 128:(r + 1) * 128, 0:520], finals[:].rearrange("p a d c -> p (a d c)"),
                           reads=["finals"], writes=["g_dst"])
                    fw.dma("sp", g_dst[r * 128:(r + 1) * 128, 520:528], fin_A[:].rearrange("p a d -> p (a d)"),
                           reads=["fin_A"], writes=["g_dst"])
                    fw.barrier()
                chk("cc")
            for s in range(NSAMP):
                rows = slice(s * S, (s + 1) * S)
                phase1(x_samp[rows, :], hbuf_s[rows, :], NT)
                chk("s1")
                phase2(hbuf_s[rows, :], mix_s[rows, :], False, "full")
                chk("s2")
                phase3(hbuf_s[rows, :], mix_s[rows, :], y_samp[rows, :], NT)
                chk("s3")
            if with_prompt:
                cst = ExitStack()
                gat = sb(cst, "gat", [128, n_ranks, GWc])
                fw.dma("sp", gat[:, 1:n_ranks, :], g_dst[128:n_ranks * 128, :].rearrange("(r p) w -> p r w", p=128),
                       reads=["g_dst"], writes=["gat"])
                fw.op("dve", lambda e: e.memset(ist[:], 0.0), writes=["ist"])
                for d in range(2):
                    order = range(1, n_ranks) if d == 0 else range(n_ranks - 1, 0, -1)
                    for r in order:
                        fcol = flg[:, 2 + 8 * d + r:3 + 8 * d + r]
                        Av = gat[:, r, 520:528].rearrange("p (a d) -> p a d", a=4)[:, :, d:d + 1]
                        Sv = gat[:, r, 0:520].rearrange("p (a d c) -> p a d c", a=4, d=2)[:, :, d, :]
                        fw.op("dve", lambda e, d=d, Av=Av, fcol=fcol: e.tensor_scalar(out=dec[:, :, d:d + 1], in0=Av, scalar1=fcol,
                                                                                      scalar2=None, op0=ALU.mult),
                              reads=["gat", "flg"], writes=["dec"])
                        fw.op("act", lambda e, d=d: e.activation(out=dec[:, :, d:d + 1], in_=dec[:, :, d:d + 1], func=AF.Exp),
                              reads=["dec"], writes=["dec"])
                        fw.op("dve", lambda e, d=d: e.tensor_tensor(out=ist[:, :, d, :], in0=ist[:, :, d, :],
                                                                     in1=dec[:, :, d:d + 1].to_broadcast([128, 4, 65]), op=ALU.mult),
                              reads=["ist", "dec"], writes=["ist"])
                        fw.op("dve", lambda e, d=d, Sv=Sv, fcol=fcol: e.scalar_tensor_tensor(
                            out=ist[:, :, d, :], in0=Sv, scalar=fcol, in1=ist[:, :, d, :], op0=ALU.mult, op1=ALU.add),
                            reads=["ist", "gat", "flg"], writes=["ist"])
                fw.barrier()
                cst.close()
                chk("comb")
                fw.res["init_state"] = _Res()
                phase2(hbuf_pf[0:NTP * 128, :], mix_p, True, "full", init_state=ist, pos=0)
                phase3(hbuf_pf[128:128 + S, :], mix_p, y_prm, NT)


        try:
            main_schedule()
        except _Stop:
            pass
        fw.finish()
    return nc


_CACHE = {}


def kernel(x_prompt, x_sample, g_ffn1, w_ffn1_gu, w_ffn1_down, g_mix, w_in, w_conv, b_gates,
           attn_sink, g_mlstm_out, w_out, g_ffn2, w_ffn2_gu, w_ffn2_down, rel_bias_table, g_final):
    f32 = np.float32
    S = 2048
    DM = 1024
    n = N_CORES
    x_prompt = np.asarray(x_prompt, f32)
    x_sample = np.asarray(x_sample, f32)
    if "nc" not in _CACHE:
        _CACHE["nc"] = build_program()
    nc = _CACHE["nc"]
    xp = x_prompt.reshape(-1, DM)
    gains = np.stack([np.asarray(g_ffn1, f32)[0], np.asarray(g_mix, f32)[0], np.asarray(g_ffn2, f32)[0],
                      np.asarray(g_final, f32)])
    common = {
        "w1gu": np.ascontiguousarray(np.asarray(w_ffn1_gu, f32)[0]),
        "w1d": np.ascontiguousarray(np.asarray(w_ffn1_down, f32)[0]),
        "w2gu": np.ascontiguousarray(np.asarray(w_ffn2_gu, f32)[0]),
        "w2d": np.ascontiguousarray(np.asarray(w_ffn2_down, f32)[0]),
        "win": np.ascontiguousarray(np.asarray(w_in, f32)[0]),
        "wout": np.ascontiguousarray(np.asarray(w_out, f32)[0]),
        "gains": np.ascontiguousarray(gains),
        "wconv": np.ascontiguousarray(np.asarray(w_conv, f32)[0]),
        "bgates": np.ascontiguousarray(np.asarray(b_gates, f32)[0].reshape(1, 32)),
        "sink": np.ascontiguousarray(np.asarray(attn_sink, f32).reshape(1, 8)),
        "gml": np.ascontiguousarray(np.asarray(g_mlstm_out, f32).reshape(1, 512)),
        "reltab": np.ascontiguousarray(np.asarray(rel_bias_table, f32)),
        "onehot": _bucket_onehot(),
    }
    in_maps = []
    for c in range(n):
        fl = np.zeros((1, 34), f32)
        fl[0, 0] = -30000.0 if c == 0 else 0.0
        fl[0, 1] = -30000.0 if c == n - 1 else 0.0
        xr = np.empty((n * S + 256, DM), f32)
        xr[128:128 + n * S] = np.roll(xp, -c * S, axis=0)
        xr[0:128] = xr[n * S:n * S + 128]
        xr[128 + n * S:] = xr[128:256]
        if c == 0:
            xr[0:128] = 0.0
            xr[128 + n * S:] = 0.0
        for k in range(n):
            r = (c + k) % n
            fl[0, 2 + k] = 1.0 if r < c else 0.0
            fl[0, 10 + k] = 1.0 if r > c else 0.0
            fl[0, 18 + k] = 0.0 if r == 0 else 1.0
            fl[0, 26 + k] = 0.0 if r == n - 1 else 1.0
        m = dict(common)
        m["x_samp"] = np.ascontiguousarray(x_sample[4 * c:4 * c + 4].reshape(4 * S, DM))
        m["x_pfull"] = xr
        m["flags"] = fl
        in_maps.append(m)
    res = run_bass_kernel_spmd(nc, in_maps, core_ids=list(range(n)))
    y_p = np.concatenate([np.asarray(res.results[c]["y_prm"], f32) for c in range(n)], axis=0).reshape(x_prompt.shape)
    y_s = np.concatenate([np.asarray(res.results[c]["y_samp"], f32).reshape(4, S, DM) for c in range(n)], axis=0)
    return (y_p, y_s.reshape(x_sample.shape))
```

```python
import math
from contextlib import ExitStack

import numpy as np
import concourse.bass as bass
import concourse.mybir as mybir
from concourse.bass_utils import run_bass_kernel_spmd

F32 = mybir.dt.float32
BF16 = mybir.dt.bfloat16
ALU = mybir.AluOpType
AF = mybir.ActivationFunctionType
AX = mybir.AxisListType

HD = 64
NEG = -1e30
EPS = 1e-6
IN_W = 2848
N_CORES = 8


_PSUM_PREFIXES = ("pa", "pg", "pu", "pd", "ptp", "pf")


class _Res:
    __slots__ = ("w", "reads")

    def __init__(self):
        self.w = None
        self.reads = []


class _Eng:
    def __init__(self, name, eng, sem):
        self.name = name
        self.eng = eng
        self.sem = sem
        self.count = 0
        self.waited = {}


class FW:
    def __init__(self, nc, stack):
        self.nc = nc
        self.stack = stack
        self.res = {}
        self.engs = {}
        for name in ("pe", "act", "dve", "pool", "sp"):
            eng = {"pe": nc.tensor, "act": nc.scalar, "dve": nc.vector,
                   "pool": nc.gpsimd, "sp": nc.sync}[name]
            sem = stack.enter_context(nc.semaphore("prog_" + name))
            self.engs[name] = _Eng(name, eng, sem)
        self.dsems = {}
        self.dcount = {}
        self.n_wait = 0
        self.n_inst = 0
        self.stopped = False
        self.bg_keys = set()

    def _r(self, key):
        r = self.res.get(key)
        if r is None:
            r = self.res[key] = _Res()
        return r

    def _deps(self, reads, writes):
        deps = {}

        def add(tok):
            s, v = tok
            if deps.get(s, (None, 0))[1] < v:
                deps[s] = (s, v)

        for k in reads:
            r = self._r(k)
            if r.w is not None:
                add(r.w)
            if k.startswith(_PSUM_PREFIXES):
                for tok in r.reads:
                    add(tok)
        for k in writes:
            r = self._r(k)
            if r.w is not None:
                add(r.w)
            for tok in r.reads:
                add(tok)
        return deps

    def _emit_waits(self, E, deps, skip_self=False):
        for s, (sem, v) in deps.items():
            if skip_self and sem is E.sem:
                continue
            if E.waited.get(s, 0) < v:
                E.eng.wait_ge(sem, v)
                E.waited[s] = v
                self.n_wait += 1

    def _commit(self, tok, reads, writes):
        for k in reads:
            r = self._r(k)
            r.reads.append(tok)
            if len(r.reads) > 48:
                best = {}
                for (s, v) in r.reads:
                    if best.get(s, (None, 0))[1] < v:
                        best[s] = (s, v)
                r.reads = list(best.values())
        for k in writes:
            r = self._r(k)
            r.w = tok
            r.reads = []

    def op(self, ename, fn, reads=(), writes=()):
        if self.stopped:
            return None
        E = self.engs[ename]
        deps = self._deps(reads, writes)
        self._emit_waits(E, deps, skip_self=(ename == "pe"))
        ins = fn(E.eng)
        E.count += 1
        ins.then_inc(E.sem, 1)
        tok = (E.sem, E.count)
        self._commit(tok, reads, writes)
        self.n_inst += 1
        return tok

    def dma(self, qname, out, in_, reads=(), writes=(), sem_key=None, **kw):
        if self.stopped:
            return None
        E = self.engs[qname]
        deps = self._deps(reads, writes)
        self._emit_waits(E, deps)
        if sem_key is None:
            sem_key = writes[0] if writes else reads[0]
        s = self.dsems.get(sem_key)
        if s is None:
            s = self.stack.enter_context(self.nc.semaphore("d%d" % len(self.dsems)))
            self.dsems[sem_key] = s
            self.dcount[sem_key] = 0
        E.eng.dma_start(out=out, in_=in_, **kw).then_inc(s, 16)
        self.dcount[sem_key] += 16
        tok = (s, self.dcount[sem_key])
        self._commit(tok, reads, writes)
        return tok

    def custom(self, qname, fn, reads=(), writes=(), sem_key=None, inc=1):
        if self.stopped:
            return None
        E = self.engs[qname]
        deps = self._deps(reads, writes)
        self._emit_waits(E, deps)
        s = self.dsems.get(sem_key)
        if s is None:
            s = self.stack.enter_context(self.nc.semaphore("c%d" % len(self.dsems)))
            self.dsems[sem_key] = s
            self.dcount[sem_key] = 0
        fn(E.eng).then_inc(s, inc)
        self.dcount[sem_key] += inc
        tok = (s, self.dcount[sem_key])
        self._commit(tok, reads, writes)
        return tok

    def _all_tokens(self, include_bg=False):
        final = {}
        for E in self.engs.values():
            if E.count:
                final[E.sem] = (E.sem, E.count)
        for k, s in self.dsems.items():
            if self.dcount[k] and (include_bg or k not in self.bg_keys):
                final[s] = (s, self.dcount[k])
        return final

    def barrier(self):
        if self.stopped:
            return
        final = self._all_tokens()
        for E in self.engs.values():
            self._emit_waits(E, final)
        keep = {k: v for k, v in self.res.items() if k in self.bg_keys}
        self.res = keep

    def finish(self):
        self._emit_waits(self.engs["sp"], self._all_tokens(include_bg=True))


def _t5_bucket(rel):
    nb = 16
    ret = (rel > 0).astype(np.int32) * nb
    n = np.abs(rel)
    max_exact = nb // 2
    large = max_exact + (np.log(np.maximum(n, 1) / max_exact)
                         / math.log(128 / max_exact) * (nb - max_exact)).astype(np.int32)
    large = np.minimum(large, nb - 1)
    return (ret + np.where(n < max_exact, n, large)).astype(np.int32)


def _bucket_onehot():
    oh = np.zeros((33, 640), np.float32)
    for j in range(640):
        rel = j - 255
        if abs(rel) <= 128:
            oh[int(_t5_bucket(np.array(rel))), j] = 1.0
        else:
            oh[32, j] = 1.0
    return oh


def build_program(S=2048, NSAMP=4, DM=1024, DFF=2816, n_ranks=8, with_prompt=True, stop=None):
    KC = DM // 128
    FC = DFF // 128
    NT = S // 128
    GT = 4
    NTP = NT + 2
    nc = bass.Bass("TRN2", target_bir_lowering=False)

    def din(name, shape, dt=F32):
        return nc.dram_tensor(name, list(shape), dt, kind="ExternalInput").ap()

    def dscr(name, shape, dt=F32):
        return nc.dram_tensor(name, list(shape), dt, kind="Internal").ap()

    x_samp = din("x_samp", [NSAMP * S, DM])
    x_pfull = din("x_pfull", [n_ranks * S + 256, DM])
    w1gu = din("w1gu", [DM, 2 * DFF]); w1d = din("w1d", [DFF, DM])
    w2gu = din("w2gu", [DM, 2 * DFF]); w2d = din("w2d", [DFF, DM])
    win = din("win", [DM, IN_W]); wout = din("wout", [1024, DM])
    gains = din("gains", [4, DM])
    wconv = din("wconv", [5, 1024])
    bgates = din("bgates", [1, 32])
    sink = din("sink", [1, 8])
    gml = din("gml", [1, 512])
    reltab = din("reltab", [32, 8])
    onehot = din("onehot", [33, 640])
    flags = din("flags", [1, 34])
    y_samp = nc.dram_tensor("y_samp", [NSAMP * S, DM], F32, kind="ExternalOutput").ap()
    y_prm = nc.dram_tensor("y_prm", [S, DM], F32, kind="ExternalOutput").ap()

    hbuf_s = dscr("hbuf_s", [NSAMP * S, DM]); hbuf_p = dscr("hbuf_p", [NTP * 128, DM])
    hbuf_pf = dscr("hbuf_pf", [n_ranks * S + 256, DM])
    mix_s = dscr("mix_s", [NSAMP * S, 1024], BF16); mix_p = dscr("mix_p", [S, 1024], BF16)
    w1gu_s = dscr("w1gu_s", [FC, 128, KC * 256], BF16); w2gu_s = dscr("w2gu_s", [FC, 128, KC * 256], BF16)
    w1d_s = dscr("w1d_s", [128, FC * DM], BF16); w2d_s = dscr("w2d_s", [128, FC * DM], BF16)
    win_s = dscr("win_s", [128, KC * IN_W], BF16); wout_s = dscr("wout_s", [128, 8 * DM], BF16)
    fd_s = dscr("fd_s", [8, 640])
    GW = 8 * 65 + 8
    g_src = dscr("g_src", [128, GW]); g_dst = dscr("g_dst", [n_ranks * 128, GW])

    with ExitStack() as top:
        fw = FW(nc, top)

        uid = [0]

        def chk(name):
            if stop == name:
                fw.stopped = True

        def sb(st, name, shape, dt=F32):
            uid[0] += 1
            return st.enter_context(nc.sbuf_tensor("%s_%d" % (name, uid[0]), list(shape), dt))

        def ps(st, name, shape, dt=F32):
            uid[0] += 1
            return st.enter_context(nc.psum_tensor("%s_%d" % (name, uid[0]), list(shape), dt))

        ident = sb(top, "ident", [128, 128], BF16)
        maskF = sb(top, "maskF", [128, 128], BF16)
        maskB = sb(top, "maskB", [128, 128], BF16)
        ones_b = sb(top, "ones_b", [128, 128], BF16)
        gT = sb(top, "gT", [128, 3, KC])
        gfin_b = sb(top, "gfin_b", [128, DM])
        wcv = sb(top, "wcv", [128, 8, 5])
        bg_b = sb(top, "bg_b", [128, 32])
        sink_b = sb(top, "sink_b", [128, 8])
        gml_b = sb(top, "gml_b", [128, 512])
        flg = sb(top, "flg", [128, 34])
        bias = sb(top, "bias", [128, 8, 384])
        finals_t = sb(top, "finals", [128, 4, 2, 65])
        fin_A_t = sb(top, "fin_A", [128, 4, 2])
        GWc = 8 * 65 + 8
        ist = sb(top, "ist", [128, 4, 2, 65])
        dec = sb(top, "dec", [128, 4, 2])

        def mk(fn, w):
            fw.op("pool", fn, writes=[w], reads=[])

        mk(lambda e: e.memset(ident[:], 1.0), "ident")
        fw.op("pool", lambda e: e.affine_select(out=ident[:], in_=ident[:], pattern=[[-1, 128]],
              compare_op=ALU.is_equal, fill=0.0, base=0, channel_multiplier=1), reads=["ident"], writes=["ident"])
        mk(lambda e: e.memset(maskF[:], 1.0), "maskF")
        fw.op("pool", lambda e: e.affine_select(out=maskF[:], in_=maskF[:], pattern=[[1, 128]],
              compare_op=ALU.is_ge, fill=0.0, base=0, channel_multiplier=-1), reads=["maskF"], writes=["maskF"])
        mk(lambda e: e.memset(maskB[:], 1.0), "maskB")
        fw.op("pool", lambda e: e.affine_select(out=maskB[:], in_=maskB[:], pattern=[[-1, 128]],
              compare_op=ALU.is_ge, fill=0.0, base=0, channel_multiplier=1), reads=["maskB"], writes=["maskB"])
        mk(lambda e: e.memset(ones_b[:], 1.0), "ones_b")

        fw.dma("sp", gT[:], gains[0:3, :].rearrange("g (kc p) -> p g kc", p=128), writes=["gT"],
               allow_slow_non_contiguous=True)
        fw.dma("sp", gfin_b[:], gains[3:4, :].partition_broadcast(128), writes=["gfin_b"])
        for tap in range(5):
            fw.dma("sp", wcv[:, :, tap:tap + 1], wconv[tap:tap + 1, :].rearrange("j (c p) -> p c j", p=128), writes=["wcv"],
                   sem_key="wcv", allow_slow_non_contiguous=True)
        fw.dma("sp", bg_b[:], bgates.partition_broadcast(128), writes=["bg_b"])
        fw.dma("sp", sink_b[:], sink.partition_broadcast(128), writes=["sink_b"])
        fw.dma("sp", gml_b[:], gml.partition_broadcast(128), writes=["gml_b"])
        fw.dma("sp", flg[:], flags.partition_broadcast(128), writes=["flg"])

        def conv_gu(src, dst, tag):
            v = src.rearrange("(kc p) (two f) -> p kc two f", p=128, two=2)
            for fc in range(FC):
                for two in range(2):
                    fw.dma("pool", dst[fc].rearrange("p (kc two j) -> p kc two j", kc=KC, two=2)[:, :, two, :],
                           v[:, :, two, fc * 128:(fc + 1) * 128], writes=[tag], sem_key=tag)

        fw.bg_keys.update(["win_s", "wout_s", "w2gu_s", "w2d_s"])
        conv_gu(w1gu, w1gu_s, "w1gu_s")
        def conv_rows(src, dst, nchunk, tag):
            sv = src.rearrange("(c p) d -> p c d", p=128)
            dv = dst.rearrange("p (c d) -> p c d", c=nchunk)
            for c in range(nchunk):
                fw.dma("pool", dv[:, c, :], sv[:, c, :], writes=[tag], sem_key=tag)

        conv_rows(w1d, w1d_s, FC, "w1d_s")
        win_v = win.rearrange("(kc p) c -> p kc c", p=128)
        wins_v = win_s.rearrange("p (kc c) -> p kc c", kc=KC)
        for kc in range(KC):
            for two in range(2):
                fw.dma("pool", wins_v[:, kc, 0:512].rearrange("p (c two j) -> p c two j", two=2, j=64)[:, :, two, :],
                       win_v[:, kc, two * 256:(two + 1) * 256].rearrange("p (c j) -> p c j", j=64),
                       writes=["win_s"], sem_key="win_s")
        for kc in range(KC):
            fw.dma("pool", wins_v[:, kc, 512:IN_W], win_v[:, kc, 512:IN_W], writes=["win_s"], sem_key="win_s")
        conv_rows(wout, wout_s, 8, "wout_s")
        conv_gu(w2gu, w2gu_s, "w2gu_s")
        conv_rows(w2d, w2d_s, FC, "w2d_s")

        if stop == "conv":
            fw.finish()
            return nc
        with ExitStack() as st:
            tab = sb(st, "tab", [33, 8]); tab_hi = sb(st, "tab_hi", [33, 8], BF16)
            tab_r = sb(st, "tab_r", [33, 8]); tab_lo = sb(st, "tab_lo", [33, 8], BF16)
            oh = sb(st, "oh", [33, 640]); oh_b = sb(st, "oh_b", [33, 640], BF16)
            fsb = sb(st, "fsb", [8, 640])
            pf = ps(st, "pf", [8, 1024])
            fw.op("dve", lambda e: e.memset(tab[:], NEG), writes=["tab"])
            fw.dma("sp", tab[0:32, :], reltab, reads=[], writes=["tab"])
            fw.dma("sp", oh[:], onehot, writes=["oh"])
            fw.op("dve", lambda e: e.tensor_copy(out=oh_b[:], in_=oh[:]), reads=["oh"], writes=["oh_b"])
            fw.op("dve", lambda e: e.tensor_copy(out=tab_hi[:], in_=tab[:]), reads=["tab"], writes=["tab_hi"])
            fw.op("dve", lambda e: e.tensor_tensor(out=tab_r[:], in0=tab[:], in1=tab_hi[:], op=ALU.subtract),
                  reads=["tab", "tab_hi"], writes=["tab_r"])
            fw.op("dve", lambda e: e.tensor_copy(out=tab_lo[:], in_=tab_r[:]), reads=["tab_r"], writes=["tab_lo"])

            def fmm(e):
                r = None
                for half in range(2):
                    sl = slice(half * 512, min(640, (half + 1) * 512))
                    e.matmul(pf[:, sl], lhsT=tab_hi[:], rhs=oh_b[:, sl], start=True, stop=False)
                    r = e.matmul(pf[:, sl], lhsT=tab_lo[:], rhs=oh_b[:, sl], start=False, stop=True)
                return r
            fw.op("pe", fmm, reads=["tab_hi", "tab_lo", "oh_b"], writes=["pf"])
            fw.op("dve", lambda e: e.tensor_copy(out=fsb[:], in_=pf[:, 0:640]), reads=["pf"], writes=["fsb"])
            fw.dma("sp", fd_s, fsb[:], reads=["fsb"], writes=["fd_s"])
            for q in range(128):
                src = bass.AP(fd_s.tensor, 127 - q, [[0, 1], [640, 8], [1, 384]])
                fw.dma("sp", bias[q:q + 1, :, :], src, reads=["fd_s"], writes=["bias"], sem_key="bias")
            fw.barrier()

        if stop == "bias":
            fw.finish()
            return nc

        def ffn_phase(st, tag):
            B = {}
            B["xn"] = sb(st, tag + "xn", [128, GT, DM], BF16)
            B["junk"] = sb(st, tag + "junk", [128, DM], BF16)
            B["ss"] = sb(st, tag + "ss", [128, GT])
            B["rstd"] = sb(st, tag + "rstd", [128, GT])
            B["xnT"] = sb(st, tag + "xnT", [128, KC, GT * 128], BF16)
            B["actT"] = sb(st, tag + "actT", [128, FC, GT * 128], BF16)
            B["sg"] = [sb(st, tag + "sg%d" % i, [128, GT * 128]) for i in range(2)]
            B["wgu"] = [sb(st, tag + "wgu%d" % i, [128, KC, 2, 128], BF16) for i in range(3)]
            B["wd"] = sb(st, tag + "wd", [128, FC, DM], BF16)
            B["ptp"] = ps(st, tag + "ptp", [128, 8, 128], BF16)
            B["pg"] = [ps(st, tag + "pg%d" % i, [128, 512]) for i in range(2)]
            B["pu"] = [ps(st, tag + "pu%d" % i, [128, 512]) for i in range(2)]
            B["pd"] = [ps(st, tag + "pd%d" % i, [128, 512]) for i in range(2)]
            B["cnt"] = 0
            return B

        def rms_stats(x_t, xkey, nt, B, D):
            for i in range(nt):
                fw.op("act", lambda e, i=i: e.activation(out=B["junk"][:, 0:D], in_=x_t[:, i, :], func=AF.Square,
                                                         accum_out=B["ss"][:, i:i + 1]),
                      reads=[xkey], writes=["junk", "ss"])
            fw.op("act", lambda e: e.activation(out=B["rstd"][:, 0:nt], in_=B["ss"][:, 0:nt], func=AF.Sqrt,
                                                scale=1.0 / D, bias=EPS), reads=["ss"], writes=["rstd"])
            fw.op("dve", lambda e: e.reciprocal(out=B["rstd"][:, 0:nt], in_=B["rstd"][:, 0:nt]),
                  reads=["rstd"], writes=["rstd"])

        def norm_T(x_t, xkey, nt, B, gidx, dstT, dkey, col0=0):
            rms_stats(x_t, xkey, nt, B, DM)
            for i in range(nt):
                fw.op("dve", lambda e, i=i: e.tensor_scalar(out=B["xn"][:, i, :], in0=x_t[:, i, :],
                                                             scalar1=B["rstd"][:, i:i + 1], scalar2=None, op0=ALU.mult),
                      reads=[xkey, "rstd"], writes=["xn"])

                def tr(e, i=i):
                    r = None
                    for kc in range(KC):
                        r = e.transpose(out=B["ptp"][:, kc, :], in_=B["xn"][:, i, kc * 128:(kc + 1) * 128],
                                        identity=ident[:])
                    return r
                fw.op("pe", tr, reads=["xn", "ident"], writes=["ptp"])
                fw.op("dve", lambda e, i=i: e.tensor_tensor(
                    out=dstT[:, :, col0 + i * 128: col0 + (i + 1) * 128], in0=B["ptp"][:, 0:KC, :],
                    in1=gT[:, gidx, :].unsqueeze(2).to_broadcast([128, KC, 128]), op=ALU.mult),
                    reads=["ptp", "gT"], writes=[dkey])

        def ffn_group(B, x_t, xkey, nt, gidx, wgu_scr, wgukey, wd_scr, wdkey, out_t, okey):
            N = nt * 128
            if not B.get("wd_loaded"):
                fw.dma("sp", B["wd"][:], wd_scr.rearrange("p (fc d) -> p fc d", fc=FC), reads=[wdkey], writes=["wd"])
                B["wd_loaded"] = True
            norm_T(x_t, xkey, nt, B, gidx, B["xnT"], "xnT")
            for fc in range(FC):
                c = B["cnt"]; B["cnt"] += 1
                wb = B["wgu"][c % 3]; wk = "wgu%d" % (c % 3)
                pg = B["pg"][c % 2]; pu = B["pu"][c % 2]; sg = B["sg"][c % 2]
                pgk, puk, sgk = "pg%d" % (c % 2), "pu%d" % (c % 2), "sg%d" % (c % 2)
                fw.dma("sp", wb[:], wgu_scr[fc].rearrange("p (kc two j) -> p kc two j", kc=KC, two=2),
                       reads=[wgukey], writes=[wk])

                def mm(e, wb=wb, pg=pg, pu=pu):
                    r = None
                    for two, pp in ((0, pg), (1, pu)):
                        for kc in range(KC):
                            r = e.matmul(pp[:, 0:N], lhsT=wb[:, kc, two, :], rhs=B["xnT"][:, kc, 0:N],
                                         start=(kc == 0), stop=(kc == KC - 1))
                    return r
                fw.op("pe", mm, reads=[wk, "xnT"], writes=[pgk, puk])
                fw.op("act", lambda e, pg=pg, sg=sg: e.activation(out=sg[:, 0:N], in_=pg[:, 0:N], func=AF.Silu),
                      reads=[pgk], writes=[sgk])
                fw.op("dve", lambda e, pu=pu, sg=sg, fc=fc: e.tensor_tensor(out=B["actT"][:, fc, 0:N], in0=sg[:, 0:N],
                                                                          in1=pu[:, 0:N], op=ALU.mult),
                      reads=[sgk, puk], writes=["actT"])
            for i in range(nt):
                for dh in range(DM // 512):
                    c = B["cnt"]; B["cnt"] += 1
                    pd = B["pd"][c % 2]; pdk = "pd%d" % (c % 2)

                    def mm2(e, i=i, dh=dh, pd=pd):
                        r = None
                        for fc in range(FC):
                            r = e.matmul(pd[:], lhsT=B["actT"][:, fc, i * 128:(i + 1) * 128],
                                         rhs=B["wd"][:, fc, dh * 512:(dh + 1) * 512],
                                         start=(fc == 0), stop=(fc == FC - 1))
                        return r
                    fw.op("pe", mm2, reads=["actT", "wd"], writes=[pdk])
                    fw.op("dve", lambda e, i=i, dh=dh, pd=pd: e.scalar_tensor_tensor(
                        out=out_t[:, i, dh * 512:(dh + 1) * 512], in0=pd[:], scalar=0.5,
                        in1=x_t[:, i, dh * 512:(dh + 1) * 512], op0=ALU.mult, op1=ALU.add),
                        reads=[pdk, xkey], writes=[okey])

        def phase1(x_src, h_dst, ntiles):
            with ExitStack() as st:
                B = ffn_phase(st, "p1")
                xt = [sb(st, "p1x%d" % i, [128, GT, DM]) for i in range(2)]
                ht = sb(st, "p1h", [128, GT, DM])
                g0 = 0; gi = 0
                while g0 < ntiles:
                    nt = min(GT, ntiles - g0)
                    x_t = xt[gi % 2]; xk = "x%d" % (gi % 2)
                    fw.dma("sp", x_t[:, 0:nt, :], x_src[g0 * 128:(g0 + nt) * 128, :].rearrange("(i p) d -> p i d", p=128),
                           writes=[xk])
                    ffn_group(B, x_t, xk, nt, 0, w1gu_s, "w1gu_s", w1d_s, "w1d_s", ht, "ht")
                    fw.dma("pool", h_dst[g0 * 128:(g0 + nt) * 128, :].rearrange("(i p) d -> p i d", p=128), ht[:, 0:nt, :],
                           reads=["ht"], writes=["hdst"])
                    g0 += nt; gi += 1
                fw.barrier()

        def phase3(h_src, mix_src, y_dst, ntiles):
            with ExitStack() as st:
                B = ffn_phase(st, "p3")
                hin = sb(st, "p3hin", [128, GT, DM])
                h2 = sb(st, "p3h2", [128, GT, DM])
                mt = sb(st, "p3mt", [128, GT, 1024], BF16)
                mT = sb(st, "p3mT", [128, 8, GT * 128], BF16)
                wo = sb(st, "p3wo", [128, 8, DM], BF16)
                fw.dma("sp", wo[:], wout_s.rearrange("p (cc d) -> p cc d", cc=8), reads=["wout_s"], writes=["wo"])
                g0 = 0
                while g0 < ntiles:
                    nt = min(GT, ntiles - g0)
                    rows = slice(g0 * 128, (g0 + nt) * 128)
                    fw.dma("sp", hin[:, 0:nt, :], h_src[rows, :].rearrange("(i p) d -> p i d", p=128), writes=["hin"])
                    fw.dma("sp", mt[:, 0:nt, :], mix_src[rows, :].rearrange("(i p) d -> p i d", p=128), writes=["mt"])
                    for i in range(nt):
                        def tr(e, i=i):
                            r = None
                            for cc in range(8):
                                r = e.transpose(out=B["ptp"][:, cc, :], in_=mt[:, i, cc * 128:(cc + 1) * 128], identity=ident[:])
                            return r
                        fw.op("pe", tr, reads=["mt", "ident"], writes=["ptp"])
                        fw.op("act", lambda e, i=i: e.copy(out=mT[:, :, i * 128:(i + 1) * 128], in_=B["ptp"][:, 0:8, :]),
                              reads=["ptp"], writes=["mT"])
                    for i in range(nt):
                        for dh in range(DM // 512):
                            c = B["cnt"]; B["cnt"] += 1
                            pd = B["pd"][c % 2]; pdk = "pd%d" % (c % 2)

                            def mm(e, i=i, dh=dh, pd=pd):
                                r = None
                                for cc in range(8):
                                    r = e.matmul(pd[:], lhsT=mT[:, cc, i * 128:(i + 1) * 128],
                                                 rhs=wo[:, cc, dh * 512:(dh + 1) * 512], start=(cc == 0), stop=(cc == 7))
                                return r
                            fw.op("pe", mm, reads=["mT", "wo"], writes=[pdk])
                            fw.op("dve", lambda e, i=i, dh=dh, pd=pd: e.tensor_tensor(
                                out=h2[:, i, dh * 512:(dh + 1) * 512], in0=pd[:], in1=hin[:, i, dh * 512:(dh + 1) * 512],
                                op=ALU.add), reads=[pdk, "hin"], writes=["h2"])
                    ffn_group(B, h2, "h2", nt, 2, w2gu_s, "w2gu_s", w2d_s, "w2d_s", h2, "h2")
                    rms_stats(h2, "h2", nt, B, DM)
                    for i in range(nt):
                        fw.op("dve", lambda e, i=i: e.scalar_tensor_tensor(
                            out=h2[:, i, :], in0=h2[:, i, :], scalar=B["rstd"][:, i:i + 1], in1=gfin_b[:],
                            op0=ALU.mult, op1=ALU.mult), reads=["h2", "rstd", "gfin_b"], writes=["h2"])
                    fw.dma("pool", y_dst[rows, :].rearrange("(i p) d -> p i d", p=128), h2[:, 0:nt, :],
                           reads=["h2"], writes=["ydst"])
                    g0 += nt
                fw.barrier()

        LN8 = math.log(0.125)
        VS = 80

        def phase2(h_src, mix_dst, prompt, mode, init_state=None, pos=0):
            full = mode == "full"
            off = 1 if prompt else 0
            ntl = NT + 2 * off
            with ExitStack() as st:
                unT = sb(st, "unT", [128, KC, ntl * 128], BF16)
                gts = sb(st, "gts", [128, NT, 32])
                wsb = sb(st, "wsb", [128, KC, 768], BF16)
                ptp2 = ps(st, "p2ptp", [128, 8, 128], BF16)
                NB = {"ptp": ptp2}
                pA = [ps(st, "p2pa%d" % i, [128, 512]) for i in range(7)]
                cs = sb(st, "cs", [128, 2, NT, 8]); es = sb(st, "es", [128, 2, NT, 8])
                rt = sb(st, "rt", [128, 2, NT, 8]); eA = sb(st, "eA", [128, 2, NT, 8])
                Asum = sb(st, "Asum", [128, 2, 8])
                with ExitStack() as us:
                    NBu = {"xn": sb(us, "p2xn", [128, GT, DM], BF16), "junk": sb(us, "p2junk", [128, DM], BF16),
                           "ss": sb(us, "p2ss", [128, GT]), "rstd": sb(us, "p2rstd", [128, GT]), "ptp": ptp2}
                    hld = [sb(us, "p2h%d" % i, [128, GT, DM]) for i in range(2)]
                    g0 = 0; gi = 0
                    while g0 < ntl:
                        nt = min(GT, ntl - g0)
                        ht = hld[gi % 2]; hk = "hld%d" % (gi % 2)
                        fw.dma("sp", ht[:, 0:nt, :], h_src[g0 * 128:(g0 + nt) * 128, :].rearrange("(i p) d -> p i d", p=128),
                               writes=[hk])
                        norm_T(ht, hk, nt, NBu, 1, unT, "unT", col0=g0 * 128)
                        g0 += nt; gi += 1
                    fw.barrier()
                chk("p2a")
                winv = win_s.rearrange("p (kc c) -> p kc c", kc=KC)

                def load_w(c0, c1):
                    fw.dma("sp", wsb[:, :, 0:c1 - c0], winv[:, :, c0:c1], reads=["win_s"], writes=["wsb"])

                def proj_fm(col, ncol, tok0, ntok, dst_ps):
                    def f(e):
                        r = None
                        for kc in range(KC):
                            r = e.matmul(dst_ps[0:ncol, 0:ntok], lhsT=wsb[:, kc, col:col + ncol],
                                         rhs=unT[:, kc, tok0:tok0 + ntok], start=(kc == 0), stop=(kc == KC - 1))
                        return r
                    return f

                def proj_tm(col, ncol, tile, dst_ps):
                    def f(e):
                        r = None
                        for kc in range(KC):
                            r = e.matmul(dst_ps[:, 0:ncol], lhsT=unT[:, kc, tile * 128:(tile + 1) * 128],
                                         rhs=wsb[:, kc, col:col + ncol], start=(kc == 0), stop=(kc == KC - 1))
                        return r
                    return f

                with ExitStack() as gs:
                    load_w(2816, 2848)
                    for i in range(NT):
                        pp = pA[i % 2]; pk = "pa%d" % (i % 2)
                        fw.op("pe", proj_tm(0, 32, i + off, pp), reads=["wsb", "unT"], writes=[pk])
                        fw.op("dve", lambda e, i=i, pp=pp: e.tensor_tensor(out=gts[:, i, :], in0=pp[:, 0:32], in1=bg_b[:],
                                                                          op=ALU.add), reads=[pk, "bg_b"], writes=["gts"])
                    chk("p2b")
                    W = NT * 8
                    g4 = gts[:].rearrange("p n (a h) -> p n a h", a=4)
                    fx = sb(gs, "fx", [128, 2, NT, 8]); t1 = sb(gs, "t1", [128, 2, NT, 8]); t2 = sb(gs, "t2", [128, 2, NT, 8])
                    lf = sb(gs, "lf", [128, 2, NT, 8])
                    parts = [sb(gs, "lfp%d" % i, [128, 2, NT, 8], BF16) for i in range(3)]
                    for d in range(2):
                        fw.op("dve", lambda e, d=d: e.tensor_copy(out=fx[:, d, :, :], in_=g4[:, :, 1 + 2 * d, :]),
                              reads=["gts"], writes=["fx"])
                    fw.op("dve", lambda e: e.tensor_single_scalar(out=t2[:], in_=fx[:], scalar=0.0, op=ALU.min),
                          reads=["fx"], writes=["t2"])
                    fw.op("dve", lambda e: e.scalar_tensor_tensor(out=t1[:], in0=t2[:], scalar=2.0, in1=fx[:],
                                                                  op0=ALU.mult, op1=ALU.subtract),
                          reads=["fx", "t2"], writes=["t1"])
                    fw.op("act", lambda e: e.activation(out=t1[:], in_=t1[:], func=AF.Exp),
                          reads=["t1"], writes=["t1"])
                    fw.op("act", lambda e: e.activation(out=t1[:], in_=t1[:], func=AF.Ln, bias=1.0),
                          reads=["t1"], writes=["t1"])
                    fw.op("dve", lambda e: e.tensor_tensor(out=lf[:], in0=t2[:], in1=t1[:], op=ALU.subtract),
                          reads=["t1", "t2"], writes=["lf"])
                    fw.op("dve", lambda e: e.tensor_copy(out=parts[0][:], in_=lf[:]), reads=["lf"], writes=["lfp0"])
                    fw.op("dve", lambda e: e.tensor_tensor(out=t1[:], in0=lf[:], in1=parts[0][:], op=ALU.subtract),
                          reads=["lf", "lfp0"], writes=["t1"])
                    fw.op("dve", lambda e: e.tensor_copy(out=parts[1][:], in_=t1[:]), reads=["t1"], writes=["lfp1"])
                    fw.op("dve", lambda e: e.tensor_tensor(out=t2[:], in0=t1[:], in1=parts[1][:], op=ALU.subtract),
                          reads=["t1", "lfp1"], writes=["t2"])
                    fw.op("dve", lambda e: e.tensor_copy(out=parts[2][:], in_=t2[:]), reads=["t2"], writes=["lfp2"])
                    pcs = pA[2]

                    def cums(e):
                        r = None
                        for d in range(2):
                            tri = maskF if d == 0 else maskB
                            for (mat, o) in ((tri, d * W), (ones_b, 2 * W + d * W)):
                                for k in range(3):
                                    r = e.matmul(pcs[:, o:o + W], lhsT=mat[:],
                                                 rhs=parts[k][:, d, :, :].rearrange("p n h -> p (n h)"),
                                                 start=(k == 0), stop=(k == 2))
                        return r
                    chk("p2c0")
                    fw.op("pe", cums, reads=["lfp0", "lfp1", "lfp2", "maskF", "maskB", "ones_b"], writes=["pa2"])
                    chk("p2c1")
                    pcv = pcs[:, 0:4 * W].rearrange("p (q d n h) -> p q d n h", q=2, d=2, n=NT)
                    for d in range(2):
                        fw.op("dve", lambda e, d=d: e.tensor_tensor(out=t1[:, d, :, :], in0=g4[:, :, 2 * d, :],
                                                                     in1=pcv[:, 0, d, :, :], op=ALU.subtract),
                              reads=["gts", "pa2"], writes=["t1"])
                    fw.op("act", lambda e: e.activation(out=cs[:], in_=t1[:], func=AF.Exp, bias=LN8),
                          reads=["t1"], writes=["cs"])
                    fw.op("dve", lambda e: e.tensor_tensor(out=t2[:], in0=t1[:], in1=pcv[:, 1, :, :, :], op=ALU.add),
                          reads=["t1", "pa2"], writes=["t2"])
                    fw.op("act", lambda e: e.activation(out=es[:], in_=t2[:], func=AF.Exp, bias=LN8),
                          reads=["t2"], writes=["es"])
                    chk("p2c2")
                    fw.op("act", lambda e: e.activation(out=rt[:], in_=pcv[:, 0, :, :, :], func=AF.Exp),
                          reads=["pa2"], writes=["rt"])
                    fw.op("act", lambda e: e.activation(out=eA[:], in_=pcv[:, 1, :, :, :], func=AF.Exp),
                          reads=["pa2"], writes=["eA"])
                    chk("p2c3")
                    if not full:
                        fw.op("dve", lambda e: e.tensor_copy(out=t1[:], in_=pcv[:, 1, :, :, :]), reads=["pa2"], writes=["t1"])
                        fw.op("dve", lambda e: e.memset(lf[:], 0.0), reads=["lf"], writes=["lf"])
                        for c_ in range(NT - 2, -1, -1):
                            fw.op("dve", lambda e, c_=c_: e.tensor_tensor(out=lf[:, 0, c_, :], in0=lf[:, 0, c_ + 1, :],
                                                                         in1=t1[:, 0, c_ + 1, :], op=ALU.add),
                                  reads=["lf", "t1"], writes=["lf"])
                        for c_ in range(1, NT):
                            fw.op("dve", lambda e, c_=c_: e.tensor_tensor(out=lf[:, 1, c_, :], in0=lf[:, 1, c_ - 1, :],
                                                                         in1=t1[:, 1, c_ - 1, :], op=ALU.add),
                                  reads=["lf", "t1"], writes=["lf"])
                        fw.op("dve", lambda e: e.tensor_tensor(out=t2[:], in0=t2[:], in1=lf[:], op=ALU.add),
                              reads=["lf", "t2", "es"], writes=["t2"])
                        fw.op("act", lambda e: e.activation(out=es[:], in_=t2[:], func=AF.Exp, bias=LN8),
                              reads=["t2"], writes=["es"])
                        chk("p2c4")
                        fw.op("dve", lambda e: e.tensor_copy(out=Asum[:], in_=t1[:, :, 0, :]), reads=["t1"], writes=["Asum"])
                        for n_ in range(1, NT):
                            fw.op("dve", lambda e, n_=n_: e.tensor_tensor(out=Asum[:], in0=Asum[:], in1=t1[:, :, n_, :], op=ALU.add),
                                  reads=["t1", "Asum"], writes=["Asum"])
                    chk("p2c5")
                    fw.barrier()

                chk("p2c")
                if full:
                    with ExitStack() as at:
                        qaT = sb(at, "qaT", [128, 4, S], BF16)
                        kT = sb(at, "kT", [128, ntl * 128], BF16)
                        vtm = sb(at, "vtm", [128, ntl, 128], BF16)
                        ssb = sb(at, "ssb", [128, 4, 384]); pbf = sb(at, "pbf", [128, 4, 384], BF16)
                        pts = [sb(at, "pts%d" % i, [128, 3, 128], BF16) for i in range(2)]
                        mx = sb(at, "mx", [128, 4]); negm = sb(at, "negm", [128, 4]); rs = sb(at, "rs", [128, 4])
                        tmp4 = sb(at, "tmp4", [128, 4]); rinv = sb(at, "rinv", [128, 4])
                        mixa = [sb(at, "mixa%d" % i, [128, 512], BF16) for i in range(2)]
                        load_w(0, 768)
                        TB = 512
                        for c in range(4):
                            for t0 in range(0, S, TB):
                                n = min(TB, S - t0)
                                pp = pA[5 + (c + t0 // TB) % 2]; pk = "pa%d" % (5 + (c + t0 // TB) % 2)
                                fw.op("pe", proj_fm(c * 128, 128, off * 128 + t0, n, pp), reads=["wsb", "unT"], writes=[pk])
                                fw.op("act", lambda e, c=c, t0=t0, n=n, pp=pp: e.mul(
                                    out=qaT[:, c, t0:t0 + n], in_=pp[:, 0:n], mul=0.125),
                                    reads=[pk], writes=["qaT"])
                        for t0 in range(0, ntl * 128, TB):
                            n = min(TB, ntl * 128 - t0)
                            pp = pA[5 + (t0 // TB) % 2]; pk = "pa%d" % (5 + (t0 // TB) % 2)
                            fw.op("pe", proj_fm(512, 128, t0, n, pp), reads=["wsb", "unT"], writes=[pk])
                            fw.op("act", lambda e, t0=t0, n=n, pp=pp: e.copy(out=kT[:, t0:t0 + n], in_=pp[:, 0:n]),
                                  reads=[pk], writes=["kT"])
                        for i in range(ntl):
                            pp = pA[5 + i % 2]; pk = "pa%d" % (5 + i % 2)
                            fw.op("pe", proj_tm(640, 128, i, pp), reads=["wsb", "unT"], writes=[pk])
                            fw.op("act", lambda e, i=i, pp=pp: e.copy(out=vtm[:, i, :], in_=pp[:, 0:128]),
                                  reads=[pk], writes=["vtm"])
                        pO = pA[4]
                        for i in range(NT):
                            ti = i + off
                            lo = ti - 1 if ti - 1 >= 0 else ti
                            hi = ti + 1 if ti + 1 < ntl else ti
                            nk = hi - lo + 1
                            b0 = (lo - (ti - 1)) * 128
                            ma = mixa[i % 2]; mak = "mixa%d" % (i % 2)
                            for g in range(2):
                                def smm(e, g=g, i=i, lo=lo, nk=nk):
                                    r = None
                                    for c in range(4):
                                        r = e.matmul(pA[c][:, 0:nk * 128], lhsT=qaT[g * 64:(g + 1) * 64, c, i * 128:(i + 1) * 128],
                                                     rhs=kT[g * 64:(g + 1) * 64, lo * 128:(lo + nk) * 128], start=True, stop=True)
                                    return r
                                fw.op("pe", smm, reads=["qaT", "kT"], writes=["pa0", "pa1", "pa2", "pa3"])
                                for c in range(4):
                                    fw.op("dve", lambda e, c=c, g=g, nk=nk, b0=b0: e.tensor_tensor(
                                        out=ssb[:, c, 0:nk * 128], in0=pA[c][:, 0:nk * 128],
                                        in1=bias[:, g * 4 + c, b0:b0 + nk * 128], op=ALU.add),
                                        reads=["pa%d" % c, "bias"], writes=["ssb"])
                                if prompt and i == 0:
                                    fw.op("dve", lambda e: e.tensor_scalar(out=ssb[:, :, 0:128], in0=ssb[:, :, 0:128],
                                                                            scalar1=flg[:, 0:1], scalar2=None, op0=ALU.add),
                                          reads=["ssb", "flg"], writes=["ssb"])
                                if prompt and i == NT - 1:
                                    fw.op("dve", lambda e: e.tensor_scalar(out=ssb[:, :, 256:384], in0=ssb[:, :, 256:384],
                                                                            scalar1=flg[:, 1:2], scalar2=None, op0=ALU.add),
                                          reads=["ssb", "flg"], writes=["ssb"])
                                fw.op("dve", lambda e, nk=nk: e.tensor_reduce(out=mx[:], in_=ssb[:, :, 0:nk * 128], axis=AX.X,
                                                                              op=ALU.max), reads=["ssb"], writes=["mx"])
                                fw.op("dve", lambda e, g=g: e.tensor_tensor(out=mx[:], in0=mx[:], in1=sink_b[:, g * 4:(g + 1) * 4],
                                                                             op=ALU.max), reads=["mx", "sink_b"], writes=["mx"])
                                fw.op("dve", lambda e: e.tensor_scalar(out=negm[:], in0=mx[:], scalar1=-1.0, scalar2=None,
                                                                        op0=ALU.mult), reads=["mx"], writes=["negm"])
                                for c in range(4):
                                    fw.op("act", lambda e, c=c, nk=nk: e.activation(
                                        out=pbf[:, c, 0:nk * 128], in_=ssb[:, c, 0:nk * 128], func=AF.Exp,
                                        bias=negm[:, c:c + 1], accum_out=rs[:, c:c + 1]),
                                        reads=["ssb", "negm"], writes=["pbf", "rs"])
                                fw.op("dve", lambda e, g=g: e.tensor_tensor(out=tmp4[:], in0=sink_b[:, g * 4:(g + 1) * 4], in1=mx[:],
                                                                             op=ALU.subtract), reads=["mx", "sink_b"], writes=["tmp4"])
                                fw.op("act", lambda e: e.activation(out=tmp4[:], in_=tmp4[:], func=AF.Exp),
                                      reads=["tmp4"], writes=["tmp4"])
                                fw.op("dve", lambda e: e.tensor_tensor(out=tmp4[:], in0=tmp4[:], in1=rs[:], op=ALU.add),
                                      reads=["tmp4", "rs"], writes=["tmp4"])
                                fw.op("dve", lambda e: e.reciprocal(out=rinv[:], in_=tmp4[:]), reads=["tmp4"], writes=["rinv"])
                                for c in range(4):
                                    pt = pts[c % 2]; ptk = "pts%d" % (c % 2)

                                    def trp(e, c=c, nk=nk):
                                        r = None
                                        for kb in range(nk):
                                            r = e.transpose(out=NB["ptp"][:, kb, :], in_=pbf[:, c, kb * 128:(kb + 1) * 128],
                                                            identity=ident[:])
                                        return r
                                    fw.op("pe", trp, reads=["pbf", "ident"], writes=["ptp"])
                                    fw.op("act", lambda e, pt=pt, nk=nk: e.copy(out=pt[:, 0:nk, :], in_=NB["ptp"][:, 0:nk, :]),
                                          reads=["ptp"], writes=[ptk])

                                    def pv(e, c=c, nk=nk, lo=lo, g=g, pt=pt):
                                        r = None
                                        for kb in range(nk):
                                            r = e.matmul(pO[:, c * 64:(c + 1) * 64], lhsT=pt[:, kb, :],
                                                         rhs=vtm[:, lo + kb, g * 64:(g + 1) * 64],
                                                         start=(kb == 0), stop=(kb == nk - 1))
                                        return r
                                    fw.op("pe", pv, reads=[ptk, "vtm"], writes=["pa4"])
                                fw.op("dve", lambda e, g=g, ma=ma: e.tensor_tensor(
                                    out=ma[:, g * 256:(g + 1) * 256].rearrange("p (c d) -> p c d", c=4),
                                    in0=pO[:, 0:256].rearrange("p (c d) -> p c d", c=4),
                                    in1=rinv[:].unsqueeze(2).to_broadcast([128, 4, 64]), op=ALU.mult),
                                    reads=["pa4", "rinv"], writes=[mak])
                            fw.dma("sp", mix_dst[i * 128:(i + 1) * 128, 0:512], ma[:], reads=[mak], writes=["mixdst"])
                        fw.barrier()

                if full:
                    chk("s2a")
                finals, fin_A = finals_t, fin_A_t
                with ExitStack() as ml:
                    pre = [sb(ml, "pre%d" % i, [128, S + 4]) for i in range(2)]
                    acc = sb(ml, "acc", [128, S])
                    qkT = [sb(ml, "qkT%d" % i, [128, S], BF16) for i in range(2)]
                    ktok = sb(ml, "ktok", [128, NT, 128], BF16)
                    v1 = sb(ml, "v1", [128, NT, 2, VS], BF16)
                    v1s = [sb(ml, "v1s%d" % d, [128, NT, 2, VS], BF16) for d in range(2)]
                    v1e = [sb(ml, "v1e%d" % d, [128, NT, 2, VS], BF16) for d in range(2)]
                    og = sb(ml, "og", [128, NT, 128])
                    hsum = sb(ml, "hsum", [128, NT, 128])
                    stt = [sb(ml, "stt%d" % d, [128, 65]) for d in range(2)]
                    stb = [sb(ml, "stb%d" % d, [128, 2, VS], BF16) for d in range(2)]
                    qblk = sb(ml, "qblk", [128, NT, 2, 128], BF16)
                    PTs = [sb(ml, "PT%d" % d, [128, 2, 128], BF16) for d in range(2)]
                    dd = [sb(ml, "dd%d" % d, [128, 2]) for d in range(2)]
                    rr = [sb(ml, "rr%d" % d, [128, 2]) for d in range(2)]
                    msq = sb(ml, "msq", [128, NT, 2])
                    mixm = sb(ml, "mixm", [128, NT, 128], BF16)
                    fw.op("dve", lambda e: e.memset(v1[:], 1.0), writes=["v1"])
                    fw.op("dve", lambda e: e.memset(qblk[:], 0.0), writes=["qblk"])
                    for d in range(2):
                        fw.op("dve", lambda e, d=d: e.memset(stb[d][:], 0.0), writes=["stb%d" % d])
                    for j in range(4):
                        for bi, c0 in enumerate((768, 1280, 1792, 2304)):
                            fw.dma("sp", wsb[:, :, bi * 128:(bi + 1) * 128], winv[:, :, c0 + j * 128:c0 + (j + 1) * 128],
                                   reads=["win_s"], writes=["wsb"])
                        for qi in range(2):
                            if qi == 0 and not full:
                                continue
                            pr = pre[qi]; prk = "pre%d" % qi
                            if prompt:
                                for (t0, n, dcol) in ((off * 128 - 2, 2, 0), ((off + NT) * 128, 2, S + 2)):
                                    pp = pA[5]; pk = "pa5"
                                    fw.op("pe", proj_fm(qi * 128, 128, t0, n, pp), reads=["wsb", "unT"], writes=[pk])
                                    fw.op("act", lambda e, pr=pr, dcol=dcol, pp=pp: e.copy(out=pr[:, dcol:dcol + 2], in_=pp[:, 0:2]),
                                          reads=[pk], writes=[prk])
                                    fcol_ = (18 + pos) if dcol == 0 else (26 + pos)
                                    fw.op("dve", lambda e, pr=pr, dcol=dcol, fcol_=fcol_: e.tensor_scalar(
                                        out=pr[:, dcol:dcol + 2], in0=pr[:, dcol:dcol + 2], scalar1=flg[:, fcol_:fcol_ + 1],
                                        scalar2=None, op0=ALU.mult), reads=[prk, "flg"], writes=[prk])
                            else:
                                fw.op("dve", lambda e, pr=pr: e.memset(pr[:, 0:2], 0.0), writes=[prk])
                                fw.op("dve", lambda e, pr=pr: e.memset(pr[:, S + 2:S + 4], 0.0), writes=[prk])
                            for t0 in range(0, S, 512):
                                n = min(512, S - t0)
                                pp = pA[5 + (t0 // 512) % 2]; pk = "pa%d" % (5 + (t0 // 512) % 2)
                                fw.op("pe", proj_fm(qi * 128, 128, off * 128 + t0, n, pp), reads=["wsb", "unT"], writes=[pk])
                                fw.op("act", lambda e, pr=pr, t0=t0, n=n, pp=pp: e.copy(out=pr[:, 2 + t0:2 + t0 + n], in_=pp[:, 0:n]),
                                      reads=[pk], writes=[prk])
                            ch = qi * 4 + j
                            fw.op("dve", lambda e, pr=pr, ch=ch: e.tensor_scalar(out=acc[:], in0=pr[:, 0:S], scalar1=wcv[:, ch, 0:1],
                                                                                  scalar2=None, op0=ALU.mult),
                                  reads=[prk, "wcv"], writes=["acc"])
                            for tap in range(1, 5):
                                fw.op("dve", lambda e, pr=pr, ch=ch, tap=tap: e.scalar_tensor_tensor(
                                    out=acc[:], in0=pr[:, tap:tap + S], scalar=wcv[:, ch, tap:tap + 1], in1=acc[:],
                                    op0=ALU.mult, op1=ALU.add), reads=[prk, "wcv", "acc"], writes=["acc"])
                            fw.op("act", lambda e, qi=qi: e.activation(out=qkT[qi][:], in_=acc[:], func=AF.Silu),
                                  reads=["acc"], writes=["qkT%d" % qi])
                            if qi == 0 and full:
                                for hh in range(2):
                                    hs = slice(hh * 64, (hh + 1) * 64)
                                    fw.op("act", lambda e, hh=hh, hs=hs: e.activation(
                                        out=qblk[hs, :, hh, :], in_=acc[hs, :].rearrange("p (n t) -> p n t", t=128), func=AF.Silu),
                                        reads=["acc"], writes=["qblk"])
                        chk("p2d")
                        for i in range(NT):
                            fw.op("pe", lambda e, i=i: e.transpose(out=NB["ptp"][:, i % 8, :], in_=qkT[1][:, i * 128:(i + 1) * 128],
                                                                    identity=ident[:]), reads=["qkT1", "ident"], writes=["ptp"])
                            fw.op("act", lambda e, i=i: e.copy(out=ktok[:, i, :], in_=NB["ptp"][:, i % 8, :]),
                                  reads=["ptp"], writes=["ktok"])
                        for i in range(NT):
                            pp = pA[5 + i % 2]; pk = "pa%d" % (5 + i % 2)
                            fw.op("pe", proj_tm(256, 256, i + off, pp), reads=["wsb", "unT"], writes=[pk])
                            fw.op("dve", lambda e, i=i, pp=pp: e.tensor_copy(
                                out=v1[:, i, :, 0:64], in_=pp[:, 0:128].rearrange("p (a d) -> p a d", a=2)),
                                reads=[pk], writes=["v1"])
                            if full:
                                fw.op("act", lambda e, i=i, pp=pp: e.activation(out=og[:, i, :], in_=pp[:, 128:256], func=AF.Sigmoid),
                                      reads=[pk], writes=["og"])
                        for d in range(2):
                            for (dst, scal, dk, sk) in ((v1s[d], cs, "v1s%d" % d, "cs"), (v1e[d], es, "v1e%d" % d, "es")):
                                if dst is v1s[d] and not full:
                                    continue
                                fw.op("dve", lambda e, dst=dst, scal=scal, d=d: e.tensor_tensor(
                                    out=dst[:], in0=v1[:],
                                    in1=scal[:, d, :, 2 * j:2 * j + 2].unsqueeze(3).to_broadcast([128, NT, 2, VS]),
                                    op=ALU.mult), reads=["v1", sk], writes=[dk])
                        chk("p2e")
                        if not full:
                            for d in range(2):
                                pC = pA[d]; pCk = "pa%d" % d

                                def accmm(e, d=d, pC=pC):
                                    r = None
                                    for c in range(NT):
                                        r = e.matmul(pC[:, 0:2 * VS], lhsT=ktok[:, c, :],
                                                     rhs=v1e[d][:, c, :, :].rearrange("p a c -> p (a c)"),
                                                     start=(c == 0), stop=(c == NT - 1))
                                    return r
                                fw.op("pe", accmm, reads=["ktok", "v1e%d" % d], writes=[pCk])
                                for hh in range(2):
                                    hs = slice(hh * 64, (hh + 1) * 64)
                                    fw.op("dve", lambda e, d=d, hh=hh, hs=hs, pC=pC: e.tensor_copy(
                                        out=finals[hs, j, d, :], in_=pC[hs, hh * VS:hh * VS + 65]),
                                        reads=[pCk], writes=["finals"])
                                    fw.op("dve", lambda e, d=d, hh=hh, hs=hs: e.tensor_copy(
                                        out=fin_A[hs, j, d:d + 1], in_=Asum[hs, d, 2 * j + hh:2 * j + hh + 1]),
                                        reads=["Asum"], writes=["fin_A"])
                            continue
                        for d in range(2):
                            if init_state is not None:
                                fw.op("dve", lambda e, d=d: e.tensor_copy(out=stt[d][:], in_=init_state[:, j, d, :]),
                                      reads=["init_state"], writes=["stt%d" % d])
                            else:
                                fw.op("dve", lambda e, d=d: e.memset(stt[d][:], 0.0), writes=["stt%d" % d])
                            for hh in range(2):
                                hs = slice(hh * 64, (hh + 1) * 64)
                                fw.op("act", lambda e, d=d, hh=hh, hs=hs: e.copy(out=stb[d][hs, hh, 0:65], in_=stt[d][hs, :]),
                                      reads=["stt%d" % d], writes=["stb%d" % d])
                        if full:
                            chk("m0")
                        for step in range(NT):
                            if full and step == 1:
                                chk("m1")
                            for d in range(2):
                                c = step if d == 0 else NT - 1 - step
                                pS, pN = pA[0 + d], pA[2 + d]
                                pSk, pNk = "pa%d" % d, "pa%d" % (2 + d)
                                pC = pA[4]; pCk = "pa4"
                                mk_ = maskF if d == 0 else maskB
                                if full:
                                    def smm(e, c=c, pS=pS):
                                        return e.matmul(pS[:, 0:256], lhsT=qkT[1][:, c * 128:(c + 1) * 128],
                                                        rhs=qblk[:, c, :, :].rearrange("p a t -> p (a t)"), start=True, stop=True)
                                    fw.op("pe", smm, reads=["qblk", "qkT1"], writes=[pSk])
                                    chk("q1")
                                    fw.op("dve", lambda e, d=d, pS=pS, mk_=mk_: e.tensor_tensor(
                                        out=PTs[d][:], in0=pS[:, 0:256].rearrange("p (a t) -> p a t", a=2),
                                        in1=mk_[:].unsqueeze(1).to_broadcast([128, 2, 128]), op=ALU.mult),
                                        reads=[pSk, "maskF", "maskB"], writes=["PT%d" % d])
                                    chk("q2")

                                    def nmm(e, c=c, d=d, pN=pN):
                                        e.matmul(pN[:, 0:2 * VS], lhsT=qkT[0][:, c * 128:(c + 1) * 128],
                                                 rhs=stb[d][:].rearrange("p a c -> p (a c)"), start=True, stop=False)
                                        r = None
                                        for hh in range(2):
                                            r = e.matmul(pN[:, hh * VS:(hh + 1) * VS], lhsT=PTs[d][:, hh, :], rhs=v1s[d][:, c, hh, :],
                                                         start=False, stop=(hh == 1))
                                        return r
                                    fw.op("pe", nmm, reads=["PT%d" % d, "v1s%d" % d, "qkT0", "stb%d" % d], writes=[pNk])
                                    chk("q3")
                                    pNv = pN[:, 0:2 * VS].rearrange("p (a c) -> p a c", a=2)
                                    rtv = rt[:, d, c, 2 * j:2 * j + 2]
                                    fw.op("dve", lambda e, d=d, pNv=pNv, rtv=rtv: e.tensor_tensor(
                                        out=dd[d][:].unsqueeze(2), in0=pNv[:, :, 64:65], in1=rtv.unsqueeze(2), op=ALU.mult),
                                        reads=[pNk, "rt"], writes=["dd%d" % d])
                                    fw.op("dve", lambda e, d=d: e.scalar_tensor_tensor(out=rr[d][:], in0=dd[d][:], scalar=-1.0,
                                                                                        in1=dd[d][:], op0=ALU.mult, op1=ALU.max),
                                          reads=["dd%d" % d], writes=["rr%d" % d])
                                    fw.op("dve", lambda e, d=d: e.tensor_scalar(out=rr[d][:], in0=rr[d][:], scalar1=1.0, scalar2=None,
                                                                                 op0=ALU.max),
                                          reads=["rr%d" % d], writes=["rr%d" % d])
                                    fw.op("dve", lambda e, d=d: e.reciprocal(out=rr[d][:], in_=rr[d][:]),
                                          reads=["rr%d" % d], writes=["rr%d" % d])
                                    fw.op("dve", lambda e, d=d, rtv=rtv: e.tensor_tensor(out=rr[d][:], in0=rtv, in1=rr[d][:],
                                                                                        op=ALU.mult),
                                          reads=["rr%d" % d, "rt"], writes=["rr%d" % d])
                                    chk("q4")
                                    step_f, step_b = c, NT - 1 - c
                                    first = (step_f < step_b) if d == 0 else (step_b < step_f)
                                    if step_f == step_b:
                                        first = (d == 0)
                                    for hh in range(2):
                                        if first:
                                            fw.op("dve", lambda e, d=d, c=c, hh=hh, pNv=pNv: e.tensor_scalar(
                                                out=hsum[:, c, hh * 64:(hh + 1) * 64], in0=pNv[:, hh, 0:64],
                                                scalar1=rr[d][:, hh:hh + 1], scalar2=None, op0=ALU.mult),
                                                reads=[pNk, "rr%d" % d], writes=["hsum"])
                                        else:
                                            fw.op("dve", lambda e, d=d, c=c, hh=hh, pNv=pNv: e.scalar_tensor_tensor(
                                                out=hsum[:, c, hh * 64:(hh + 1) * 64], in0=pNv[:, hh, 0:64],
                                                scalar=rr[d][:, hh:hh + 1], in1=hsum[:, c, hh * 64:(hh + 1) * 64],
                                                op0=ALU.mult, op1=ALU.add), reads=[pNk, "rr%d" % d, "hsum"], writes=["hsum"])
                                fw.op("pe", lambda e, c=c, d=d: e.matmul(
                                    pC[:, 0:2 * VS], lhsT=ktok[:, c, :], rhs=v1e[d][:, c, :, :].rearrange("p a c -> p (a c)"),
                                    start=True, stop=True), reads=["ktok", "v1e%d" % d], writes=[pCk])
                                for hh in range(2):
                                    hs = slice(hh * 64, (hh + 1) * 64)
                                    fw.op("dve", lambda e, d=d, c=c, hh=hh, hs=hs: e.scalar_tensor_tensor(
                                        out=stt[d][hs, :], in0=stt[d][hs, :], scalar=eA[hs, d, c, 2 * j + hh:2 * j + hh + 1],
                                        in1=pC[hs, hh * VS:hh * VS + 65], op0=ALU.mult, op1=ALU.add),
                                        reads=["stt%d" % d, "eA", pCk], writes=["stt%d" % d])
                                for hh in range(2):
                                    hs = slice(hh * 64, (hh + 1) * 64)
                                    fw.op("act", lambda e, d=d, hh=hh, hs=hs: e.copy(out=stb[d][hs, hh, 0:65], in_=stt[d][hs, :]),
                                          reads=["stt%d" % d], writes=["stb%d" % d])
                        if full:
                            chk("m2")
                            fw.op("dve", lambda e: e.tensor_tensor(out=hsum[:], in0=hsum[:], in1=og[:], op=ALU.mult),
                                  reads=["hsum", "og"], writes=["hsum"])
                            fw.op("dve", lambda e: e.tensor_tensor(out=og[:], in0=hsum[:], in1=hsum[:], op=ALU.mult),
                                  reads=["hsum", "og"], writes=["og"])
                            fw.op("dve", lambda e: e.tensor_reduce(out=msq[:], in_=og[:].rearrange("p n (a d) -> p n a d", a=2),
                                                                   axis=AX.X, op=ALU.add), reads=["og"], writes=["msq"])
                            fw.op("act", lambda e: e.activation(out=msq[:], in_=msq[:], func=AF.Sqrt, scale=1.0 / 64, bias=EPS),
                                  reads=["msq"], writes=["msq"])
                            fw.op("dve", lambda e: e.reciprocal(out=msq[:], in_=msq[:]), reads=["msq"], writes=["msq"])
                            fw.op("dve", lambda e: e.tensor_tensor(
                                out=hsum[:].rearrange("p n (a d) -> p n a d", a=2), in0=hsum[:].rearrange("p n (a d) -> p n a d", a=2),
                                in1=msq[:].unsqueeze(3).to_broadcast([128, NT, 2, 64]), op=ALU.mult),
                                reads=["hsum", "msq"], writes=["hsum"])
                            fw.op("dve", lambda e: e.tensor_tensor(
                                out=mixm[:], in0=hsum[:],
                                in1=gml_b[:, j * 128:(j + 1) * 128].unsqueeze(1).to_broadcast([128, NT, 128]), op=ALU.mult),
                                reads=["hsum", "gml_b"], writes=["mixm"])
                            chk("m3")
                            fw.dma("sp", mix_dst[:, 512 + j * 128:512 + (j + 1) * 128].rearrange("(n p) c -> p n c", p=128),
                                   mixm[:], reads=["mixm"], writes=["mixdst"])
                            chk("m4")
                        else:
                            for d in range(2):
                                fw.op("dve", lambda e, d=d: e.tensor_copy(out=finals[:, j, d, :], in_=stt[d][:]),
                                      reads=["stt%d" % d], writes=["finals"])
                                for hh in range(2):
                                    hs = slice(hh * 64, (hh + 1) * 64)
                                    fw.op("dve", lambda e, d=d, hh=hh, hs=hs: e.tensor_copy(
                                        out=fin_A[hs, j, d:d + 1], in_=Asum[hs, d, 2 * j + hh:2 * j + hh + 1]),
                                        reads=["Asum"], writes=["fin_A"])
                    fw.barrier()
                fw.barrier()
            return (finals, fin_A) if not full else None

        def main_schedule():
            if with_prompt:
                phase1(x_pfull, hbuf_pf, n_ranks * NT + 2)
                chk("p1")
                for r in range(1, n_ranks):
                    finals, fin_A = phase2(hbuf_pf[r * S:r * S + NTP * 128, :], None, True, "summary", pos=r)
                    chk("p2s")
                    fw.dma("sp", g_dst[r * 128:(r + 1) * 128, 0:520], finals[:].rearrange("p a d c -> p (a d c)"),
                           reads=["finals"], writes=["g_dst"])
                    fw.dma("sp", g_dst[r * 128:(r + 1) * 128, 520:528], fin_A[:].rearrange("p a d -> p (a d)"),
                           reads=["fin_A"], writes=["g_dst"])
                    fw.barrier()
                chk("cc")
            phase1(x_samp, hbuf_s, NSAMP * NT)
            chk("s1")
            for s in range(NSAMP):
                rows = slice(s * S, (s + 1) * S)
                phase2(hbuf_s[rows, :], mix_s[rows, :], False, "full")
                chk("s2")
            phase3(hbuf_s, mix_s, y_samp, NSAMP * NT)
            chk("s3")
            if with_prompt:
                cst = ExitStack()
                gat = sb(cst, "gat", [128, n_ranks, GWc])
                fw.dma("sp", gat[:, 1:n_ranks, :], g_dst[128:n_ranks * 128, :].rearrange("(r p) w -> p r w", p=128),
                       reads=["g_dst"], writes=["gat"])
                fw.op("dve", lambda e: e.memset(ist[:], 0.0), writes=["ist"])
                for d in range(2):
                    order = range(1, n_ranks) if d == 0 else range(n_ranks - 1, 0, -1)
                    for r in order:
                        fcol = flg[:, 2 + 8 * d + r:3 + 8 * d + r]
                        Av = gat[:, r, 520:528].rearrange("p (a d) -> p a d", a=4)[:, :, d:d + 1]
                        Sv = gat[:, r, 0:520].rearrange("p (a d c) -> p a d c", a=4, d=2)[:, :, d, :]
                        fw.op("dve", lambda e, d=d, Av=Av, fcol=fcol: e.tensor_scalar(out=dec[:, :, d:d + 1], in0=Av, scalar1=fcol,
                                                                                      scalar2=None, op0=ALU.mult),
                              reads=["gat", "flg"], writes=["dec"])
                        fw.op("act", lambda e, d=d: e.activation(out=dec[:, :, d:d + 1], in_=dec[:, :, d:d + 1], func=AF.Exp),
                              reads=["dec"], writes=["dec"])
                        fw.op("dve", lambda e, d=d: e.tensor_tensor(out=ist[:, :, d, :], in0=ist[:, :, d, :],
                                                                     in1=dec[:, :, d:d + 1].to_broadcast([128, 4, 65]), op=ALU.mult),
                              reads=["ist", "dec"], writes=["ist"])
                        fw.op("dve", lambda e, d=d, Sv=Sv, fcol=fcol: e.scalar_tensor_tensor(
                            out=ist[:, :, d, :], in0=Sv, scalar=fcol, in1=ist[:, :, d, :], op0=ALU.mult, op1=ALU.add),
                            reads=["ist", "gat", "flg"], writes=["ist"])
                fw.barrier()
                cst.close()
                chk("comb")
                fw.res["init_state"] = _Res()
                phase2(hbuf_pf[0:NTP * 128, :], mix_p, True, "full", init_state=ist, pos=0)
                phase3(hbuf_pf[128:128 + S, :], mix_p, y_prm, NT)


        try:
            main_schedule()
        except _Stop:
            pass
        fw.finish()
    return nc


_CACHE = {}


def kernel(x_prompt, x_sample, g_ffn1, w_ffn1_gu, w_ffn1_down, g_mix, w_in, w_conv, b_gates,
           attn_sink, g_mlstm_out, w_out, g_ffn2, w_ffn2_gu, w_ffn2_down, rel_bias_table, g_final):
    f32 = np.float32
    S = 2048
    DM = 1024
    n = N_CORES
    x_prompt = np.asarray(x_prompt, f32)
    x_sample = np.asarray(x_sample, f32)
    if "nc" not in _CACHE:
        _CACHE["nc"] = build_program()
    nc = _CACHE["nc"]
    xp = x_prompt.reshape(-1, DM)
    gains = np.stack([np.asarray(g_ffn1, f32)[0], np.asarray(g_mix, f32)[0], np.asarray(g_ffn2, f32)[0],
                      np.asarray(g_final, f32)])
    common = {
        "w1gu": np.ascontiguousarray(np.asarray(w_ffn1_gu, f32)[0]),
        "w1d": np.ascontiguousarray(np.asarray(w_ffn1_down, f32)[0]),
        "w2gu": np.ascontiguousarray(np.asarray(w_ffn2_gu, f32)[0]),
        "w2d": np.ascontiguousarray(np.asarray(w_ffn2_down, f32)[0]),
        "win": np.ascontiguousarray(np.asarray(w_in, f32)[0]),
        "wout": np.ascontiguousarray(np.asarray(w_out, f32)[0]),
        "gains": np.ascontiguousarray(gains),
        "wconv": np.ascontiguousarray(np.asarray(w_conv, f32)[0]),
        "bgates": np.ascontiguousarray(np.asarray(b_gates, f32)[0].reshape(1, 32)),
        "sink": np.ascontiguousarray(np.asarray(attn_sink, f32).reshape(1, 8)),
        "gml": np.ascontiguousarray(np.asarray(g_mlstm_out, f32).reshape(1, 512)),
        "reltab": np.ascontiguousarray(np.asarray(rel_bias_table, f32)),
        "onehot": _bucket_onehot(),
    }
    in_maps = []
    for c in range(n):
        fl = np.zeros((1, 34), f32)
        fl[0, 0] = -30000.0 if c == 0 else 0.0
        fl[0, 1] = -30000.0 if c == n - 1 else 0.0
        xr = np.empty((n * S + 256, DM), f32)
        xr[128:128 + n * S] = np.roll(xp, -c * S, axis=0)
        xr[0:128] = xr[n * S:n * S + 128]
        xr[128 + n * S:] = xr[128:256]
        if c == 0:
            xr[0:128] = 0.0
            xr[128 + n * S:] = 0.0
        for k in range(n):
            r = (c + k) % n
            fl[0, 2 + k] = 1.0 if r < c else 0.0
            fl[0, 10 + k] = 1.0 if r > c else 0.0
            fl[0, 18 + k] = 0.0 if r == 0 else 1.0
            fl[0, 26 + k] = 0.0 if r == n - 1 else 1.0
        m = dict(common)
        m["x_samp"] = np.ascontiguousarray(x_sample[4 * c:4 * c + 4].reshape(4 * S, DM))
        m["x_pfull"] = xr
        m["flags"] = fl
        in_maps.append(m)
    res = run_bass_kernel_spmd(nc, in_maps, core_ids=list(range(n)))
    y_p = np.concatenate([np.asarray(res.results[c]["y_prm"], f32) for c in range(n)], axis=0).reshape(x_prompt.shape)
    y_s = np.concatenate([np.asarray(res.results[c]["y_samp"], f32).reshape(4, S, DM) for c in range(n)], axis=0)
    return (y_p, y_s.reshape(x_sample.shape))
```

```python
import math
from contextlib import ExitStack

import numpy as np
import concourse.bass as bass
import concourse.mybir as mybir
from concourse.bass_utils import run_bass_kernel_spmd

F32 = mybir.dt.float32
BF16 = mybir.dt.bfloat16
ALU = mybir.AluOpType
AF = mybir.ActivationFunctionType
AX = mybir.AxisListType

HD = 64
NEG = -1e30
EPS = 1e-6
IN_W = 2848
N_CORES = 8


_PSUM_PREFIXES = ("pa", "pg", "pu", "pd", "ptp", "pf")


class _Stop(Exception):
    pass


class _Res:
    __slots__ = ("w", "reads")

    def __init__(self):
        self.w = None
        self.reads = []


class _Eng:
    def __init__(self, name, eng, sem):
        self.name = name
        self.eng = eng
        self.sem = sem
        self.count = 0
        self.waited = {}


class FW:
    def __init__(self, nc, stack):
        self.nc = nc
        self.stack = stack
        self.res = {}
        self.engs = {}
        for name in ("pe", "act", "dve", "pool", "sp"):
            eng = {"pe": nc.tensor, "act": nc.scalar, "dve": nc.vector,
                   "pool": nc.gpsimd, "sp": nc.sync}[name]
            sem = stack.enter_context(nc.semaphore("prog_" + name))
            self.engs[name] = _Eng(name, eng, sem)
        self.dsems = {}
        self.dcount = {}
        self.n_wait = 0
        self.n_inst = 0
        self.stopped = False
        self.bg_keys = set()

    def _r(self, key):
        r = self.res.get(key)
        if r is None:
            r = self.res[key] = _Res()
        return r

    def _deps(self, reads, writes):
        deps = {}

        def add(tok):
            s, v = tok
            if deps.get(s, (None, 0))[1] < v:
                deps[s] = (s, v)

        for k in reads:
            r = self._r(k)
            if r.w is not None:
                add(r.w)
            if k.startswith(_PSUM_PREFIXES):
                for tok in r.reads:
                    add(tok)
        for k in writes:
            r = self._r(k)
            if r.w is not None:
                add(r.w)
            for tok in r.reads:
                add(tok)
        return deps

    def _emit_waits(self, E, deps, skip_self=False):
        for s, (sem, v) in deps.items():
            if skip_self and sem is E.sem:
                continue
            if E.waited.get(s, 0) < v:
                E.eng.wait_ge(sem, v)
                E.waited[s] = v
                self.n_wait += 1

    def _commit(self, tok, reads, writes):
        for k in reads:
            r = self._r(k)
            r.reads.append(tok)
            if len(r.reads) > 48:
                best = {}
                for (s, v) in r.reads:
                    if best.get(s, (None, 0))[1] < v:
                        best[s] = (s, v)
                r.reads = list(best.values())
        for k in writes:
            r = self._r(k)
            r.w = tok
            r.reads = []

    def op(self, ename, fn, reads=(), writes=()):
        if self.stopped:
            return None
        E = self.engs[ename]
        deps = self._deps(reads, writes)
        self._emit_waits(E, deps, skip_self=(ename == "pe"))
        ins = fn(E.eng)
        E.count += 1
        ins.then_inc(E.sem, 1)
        tok = (E.sem, E.count)
        self._commit(tok, reads, writes)
        self.n_inst += 1
        return tok

    def dma(self, qname, out, in_, reads=(), writes=(), sem_key=None, **kw):
        if self.stopped:
            return None
        E = self.engs[qname]
        deps = self._deps(reads, writes)
        self._emit_waits(E, deps)
        if sem_key is None:
            sem_key = writes[0] if writes else reads[0]
        s = self.dsems.get(sem_key)
        if s is None:
            s = self.stack.enter_context(self.nc.semaphore("d%d" % len(self.dsems)))
            self.dsems[sem_key] = s
            self.dcount[sem_key] = 0
        E.eng.dma_start(out=out, in_=in_, **kw).then_inc(s, 16)
        self.dcount[sem_key] += 16
        tok = (s, self.dcount[sem_key])
        self._commit(tok, reads, writes)
        return tok

    def custom(self, qname, fn, reads=(), writes=(), sem_key=None, inc=1):
        if self.stopped:
            return None
        E = self.engs[qname]
        deps = self._deps(reads, writes)
        self._emit_waits(E, deps)
        s = self.dsems.get(sem_key)
        if s is None:
            s = self.stack.enter_context(self.nc.semaphore("c%d" % len(self.dsems)))
            self.dsems[sem_key] = s
            self.dcount[sem_key] = 0
        fn(E.eng).then_inc(s, inc)
        self.dcount[sem_key] += inc
        tok = (s, self.dcount[sem_key])
        self._commit(tok, reads, writes)
        return tok

    def _all_tokens(self, include_bg=False):
        final = {}
        for E in self.engs.values():
            if E.count:
                final[E.sem] = (E.sem, E.count)
        for k, s in self.dsems.items():
            if self.dcount[k] and (include_bg or k not in self.bg_keys):
                final[s] = (s, self.dcount[k])
        return final

    def barrier(self):
        if self.stopped:
            return
        final = self._all_tokens()
        for E in self.engs.values():
            self._emit_waits(E, final)
        keep = {k: v for k, v in self.res.items() if k in self.bg_keys}
        self.res = keep

    def finish(self):
        self._emit_waits(self.engs["sp"], self._all_tokens(include_bg=True))


def _t5_bucket(rel):
    nb = 16
    ret = (rel > 0).astype(np.int32) * nb
    n = np.abs(rel)
    max_exact = nb // 2
    large = max_exact + (np.log(np.maximum(n, 1) / max_exact)
                         / math.log(128 / max_exact) * (nb - max_exact)).astype(np.int32)
    large = np.minimum(large, nb - 1)
    return (ret + np.where(n < max_exact, n, large)).astype(np.int32)


def _bucket_onehot():
    oh = np.zeros((33, 640), np.float32)
    for j in range(640):
        rel = j - 255
        if abs(rel) <= 128:
            oh[int(_t5_bucket(np.array(rel))), j] = 1.0
        else:
            oh[32, j] = 1.0
    return oh


def build_program(S=2048, NSAMP=4, DM=1024, DFF=2816, n_ranks=8, with_prompt=True, stop=None):
    KC = DM // 128
    FC = DFF // 128
    NT = S // 128
    GT = 4
    NTP = NT + 2
    nc = bass.Bass("TRN2", target_bir_lowering=False)

    def din(name, shape, dt=F32):
        return nc.dram_tensor(name, list(shape), dt, kind="ExternalInput").ap()

    def dscr(name, shape, dt=F32):
        return nc.dram_tensor(name, list(shape), dt, kind="Internal").ap()

    x_samp = din("x_samp", [NSAMP * S, DM])
    x_pfull = din("x_pfull", [n_ranks * S + 256, DM])
    w1gu = din("w1gu", [DM, 2 * DFF]); w1d = din("w1d", [DFF, DM])
    w2gu = din("w2gu", [DM, 2 * DFF]); w2d = din("w2d", [DFF, DM])
    win = din("win", [DM, IN_W]); wout = din("wout", [1024, DM])
    gains = din("gains", [4, DM])
    wconv = din("wconv", [5, 1024])
    bgates = din("bgates", [1, 32])
    sink = din("sink", [1, 8])
    gml = din("gml", [1, 512])
    reltab = din("reltab", [32, 8])
    onehot = din("onehot", [33, 640])
    flags = din("flags", [1, 34])
    y_samp = nc.dram_tensor("y_samp", [NSAMP * S, DM], F32, kind="ExternalOutput").ap()
    y_prm = nc.dram_tensor("y_prm", [S, DM], F32, kind="ExternalOutput").ap()

    hbuf_s = dscr("hbuf_s", [NSAMP * S, DM]); hbuf_p = dscr("hbuf_p", [NTP * 128, DM])
    hbuf_pf = dscr("hbuf_pf", [n_ranks * S + 256, DM])
    mix_s = dscr("mix_s", [NSAMP * S, 1024], BF16); mix_p = dscr("mix_p", [S, 1024], BF16)
    w1gu_s = dscr("w1gu_s", [FC, 128, KC * 256], BF16); w2gu_s = dscr("w2gu_s", [FC, 128, KC * 256], BF16)
    w1d_s = dscr("w1d_s", [128, FC * DM], BF16); w2d_s = dscr("w2d_s", [128, FC * DM], BF16)
    win_s = dscr("win_s", [128, KC * IN_W], BF16); wout_s = dscr("wout_s", [128, 8 * DM], BF16)
    fd_s = dscr("fd_s", [8, 640])
    GW = 8 * 65 + 8
    g_src = dscr("g_src", [128, GW]); g_dst = dscr("g_dst", [n_ranks * 128, GW])

    with ExitStack() as top:
        fw = FW(nc, top)

        uid = [0]

        def chk(name):
            if stop == name:
                fw.stopped = True

        def sb(st, name, shape, dt=F32):
            uid[0] += 1
            return st.enter_context(nc.sbuf_tensor("%s_%d" % (name, uid[0]), list(shape), dt))

        def ps(st, name, shape, dt=F32):
            uid[0] += 1
            return st.enter_context(nc.psum_tensor("%s_%d" % (name, uid[0]), list(shape), dt))

        ident = sb(top, "ident", [128, 128], BF16)
        maskF = sb(top, "maskF", [128, 128], BF16)
        maskB = sb(top, "maskB", [128, 128], BF16)
        ones_b = sb(top, "ones_b", [128, 128], BF16)
        gT = sb(top, "gT", [128, 3, KC])
        gfin_b = sb(top, "gfin_b", [128, DM])
        wcv = sb(top, "wcv", [128, 8, 5])
        bg_b = sb(top, "bg_b", [128, 32])
        sink_b = sb(top, "sink_b", [128, 8])
        gml_b = sb(top, "gml_b", [128, 512])
        flg = sb(top, "flg", [128, 34])
        bias = sb(top, "bias", [128, 8, 384])
        finals_t = sb(top, "finals", [128, 4, 2, 65])
        fin_A_t = sb(top, "fin_A", [128, 4, 2])
        GWc = 8 * 65 + 8
        ist = sb(top, "ist", [128, 4, 2, 65])
        dec = sb(top, "dec", [128, 4, 2])

        def mk(fn, w):
            fw.op("pool", fn, writes=[w], reads=[])

        mk(lambda e: e.memset(ident[:], 1.0), "ident")
        fw.op("pool", lambda e: e.affine_select(out=ident[:], in_=ident[:], pattern=[[-1, 128]],
              compare_op=ALU.is_equal, fill=0.0, base=0, channel_multiplier=1), reads=["ident"], writes=["ident"])
        mk(lambda e: e.memset(maskF[:], 1.0), "maskF")
        fw.op("pool", lambda e: e.affine_select(out=maskF[:], in_=maskF[:], pattern=[[1, 128]],
              compare_op=ALU.is_ge, fill=0.0, base=0, channel_multiplier=-1), reads=["maskF"], writes=["maskF"])
        mk(lambda e: e.memset(maskB[:], 1.0), "maskB")
        fw.op("pool", lambda e: e.affine_select(out=maskB[:], in_=maskB[:], pattern=[[-1, 128]],
              compare_op=ALU.is_ge, fill=0.0, base=0, channel_multiplier=1), reads=["maskB"], writes=["maskB"])
        mk(lambda e: e.memset(ones_b[:], 1.0), "ones_b")

        fw.dma("sp", gT[:], gains[0:3, :].rearrange("g (kc p) -> p g kc", p=128), writes=["gT"],
               allow_slow_non_contiguous=True)
        fw.dma("sp", gfin_b[:], gains[3:4, :].partition_broadcast(128), writes=["gfin_b"])
        for tap in range(5):
            fw.dma("sp", wcv[:, :, tap:tap + 1], wconv[tap:tap + 1, :].rearrange("j (c p) -> p c j", p=128), writes=["wcv"],
                   sem_key="wcv", allow_slow_non_contiguous=True)
        fw.dma("sp", bg_b[:], bgates.partition_broadcast(128), writes=["bg_b"])
        fw.dma("sp", sink_b[:], sink.partition_broadcast(128), writes=["sink_b"])
        fw.dma("sp", gml_b[:], gml.partition_broadcast(128), writes=["gml_b"])
        fw.dma("sp", flg[:], flags.partition_broadcast(128), writes=["flg"])

        def conv_gu(src, dst, tag):
            v = src.rearrange("(kc p) (two f) -> p kc two f", p=128, two=2)
            for fc in range(FC):
                for two in range(2):
                    fw.dma("pool", dst[fc].rearrange("p (kc two j) -> p kc two j", kc=KC, two=2)[:, :, two, :],
                           v[:, :, two, fc * 128:(fc + 1) * 128], writes=[tag], sem_key=tag)

        fw.bg_keys.update(["win_s", "wout_s", "w2gu_s", "w2d_s"])
        conv_gu(w1gu, w1gu_s, "w1gu_s")
        def conv_rows(src, dst, nchunk, tag):
            sv = src.rearrange("(c p) d -> p c d", p=128)
            dv = dst.rearrange("p (c d) -> p c d", c=nchunk)
            for c in range(nchunk):
                fw.dma("pool", dv[:, c, :], sv[:, c, :], writes=[tag], sem_key=tag)

        conv_rows(w1d, w1d_s, FC, "w1d_s")
        win_v = win.rearrange("(kc p) c -> p kc c", p=128)
        wins_v = win_s.rearrange("p (kc c) -> p kc c", kc=KC)
        for kc in range(KC):
            for two in range(2):
                fw.dma("pool", wins_v[:, kc, 0:512].rearrange("p (c two j) -> p c two j", two=2, j=64)[:, :, two, :],
                       win_v[:, kc, two * 256:(two + 1) * 256].rearrange("p (c j) -> p c j", j=64),
                       writes=["win_s"], sem_key="win_s")
        for kc in range(KC):
            fw.dma("pool", wins_v[:, kc, 512:IN_W], win_v[:, kc, 512:IN_W], writes=["win_s"], sem_key="win_s")
        conv_rows(wout, wout_s, 8, "wout_s")
        conv_gu(w2gu, w2gu_s, "w2gu_s")
        conv_rows(w2d, w2d_s, FC, "w2d_s")

        if stop == "conv":
            fw.finish()
            return nc
        with ExitStack() as st:
            tab = sb(st, "tab", [33, 8]); tab_hi = sb(st, "tab_hi", [33, 8], BF16)
            tab_r = sb(st, "tab_r", [33, 8]); tab_lo = sb(st, "tab_lo", [33, 8], BF16)
            oh = sb(st, "oh", [33, 640]); oh_b = sb(st, "oh_b", [33, 640], BF16)
            fsb = sb(st, "fsb", [8, 640])
            pf = ps(st, "pf", [8, 1024])
            fw.op("dve", lambda e: e.memset(tab[:], NEG), writes=["tab"])
            fw.dma("sp", tab[0:32, :], reltab, reads=[], writes=["tab"])
            fw.dma("sp", oh[:], onehot, writes=["oh"])
            fw.op("dve", lambda e: e.tensor_copy(out=oh_b[:], in_=oh[:]), reads=["oh"], writes=["oh_b"])
            fw.op("dve", lambda e: e.tensor_copy(out=tab_hi[:], in_=tab[:]), reads=["tab"], writes=["tab_hi"])
            fw.op("dve", lambda e: e.tensor_tensor(out=tab_r[:], in0=tab[:], in1=tab_hi[:], op=ALU.subtract),
                  reads=["tab", "tab_hi"], writes=["tab_r"])
            fw.op("dve", lambda e: e.tensor_copy(out=tab_lo[:], in_=tab_r[:]), reads=["tab_r"], writes=["tab_lo"])

            def fmm(e):
                r = None
                for half in range(2):
                    sl = slice(half * 512, min(640, (half + 1) * 512))
                    e.matmul(pf[:, sl], lhsT=tab_hi[:], rhs=oh_b[:, sl], start=True, stop=False)
                    r = e.matmul(pf[:, sl], lhsT=tab_lo[:], rhs=oh_b[:, sl], start=False, stop=True)
                return r
            fw.op("pe", fmm, reads=["tab_hi", "tab_lo", "oh_b"], writes=["pf"])
            fw.op("dve", lambda e: e.tensor_copy(out=fsb[:], in_=pf[:, 0:640]), reads=["pf"], writes=["fsb"])
            fw.dma("sp", fd_s, fsb[:], reads=["fsb"], writes=["fd_s"])
            for q in range(128):
                src = bass.AP(fd_s.tensor, 127 - q, [[0, 1], [640, 8], [1, 384]])
                fw.dma("sp", bias[q:q + 1, :, :], src, reads=["fd_s"], writes=["bias"], sem_key="bias")
            fw.barrier()

        if stop == "bias":
            fw.finish()
            return nc

        def ffn_phase(st, tag):
            B = {}
            B["xn"] = sb(st, tag + "xn", [128, GT, DM], BF16)
            B["junk"] = sb(st, tag + "junk", [128, DM], BF16)
            B["ss"] = sb(st, tag + "ss", [128, GT])
            B["rstd"] = sb(st, tag + "rstd", [128, GT])
            B["xnT"] = sb(st, tag + "xnT", [128, KC, GT * 128], BF16)
            B["actT"] = sb(st, tag + "actT", [128, FC, GT * 128], BF16)
            B["sg"] = [sb(st, tag + "sg%d" % i, [128, GT * 128]) for i in range(2)]
            B["wgu"] = [sb(st, tag + "wgu%d" % i, [128, KC, 2, 128], BF16) for i in range(3)]
            B["wd"] = sb(st, tag + "wd", [128, FC, DM], BF16)
            B["ptp"] = ps(st, tag + "ptp", [128, 8, 128], BF16)
            B["pg"] = [ps(st, tag + "pg%d" % i, [128, 512]) for i in range(2)]
            B["pu"] = [ps(st, tag + "pu%d" % i, [128, 512]) for i in range(2)]
            B["pd"] = [ps(st, tag + "pd%d" % i, [128, 512]) for i in range(2)]
            B["cnt"] = 0
            return B

        def rms_stats(x_t, xkey, nt, B, D):
            for i in range(nt):
                fw.op("act", lambda e, i=i: e.activation(out=B["junk"][:, 0:D], in_=x_t[:, i, :], func=AF.Square,
                                                         accum_out=B["ss"][:, i:i + 1]),
                      reads=[xkey], writes=["junk", "ss"])
            fw.op("act", lambda e: e.activation(out=B["rstd"][:, 0:nt], in_=B["ss"][:, 0:nt], func=AF.Sqrt,
                                                scale=1.0 / D, bias=EPS), reads=["ss"], writes=["rstd"])
            fw.op("dve", lambda e: e.reciprocal(out=B["rstd"][:, 0:nt], in_=B["rstd"][:, 0:nt]),
                  reads=["rstd"], writes=["rstd"])

        def norm_T(x_t, xkey, nt, B, gidx, dstT, dkey, col0=0):
            rms_stats(x_t, xkey, nt, B, DM)
            for i in range(nt):
                fw.op("dve", lambda e, i=i: e.tensor_scalar(out=B["xn"][:, i, :], in0=x_t[:, i, :],
                                                             scalar1=B["rstd"][:, i:i + 1], scalar2=None, op0=ALU.mult),
                      reads=[xkey, "rstd"], writes=["xn"])

                def tr(e, i=i):
                    r = None
                    for kc in range(KC):
                        r = e.transpose(out=B["ptp"][:, kc, :], in_=B["xn"][:, i, kc * 128:(kc + 1) * 128],
                                        identity=ident[:])
                    return r
                fw.op("pe", tr, reads=["xn", "ident"], writes=["ptp"])
                fw.op("dve", lambda e, i=i: e.tensor_tensor(
                    out=dstT[:, :, col0 + i * 128: col0 + (i + 1) * 128], in0=B["ptp"][:, 0:KC, :],
                    in1=gT[:, gidx, :].unsqueeze(2).to_broadcast([128, KC, 128]), op=ALU.mult),
                    reads=["ptp", "gT"], writes=[dkey])

        def ffn_group(B, x_t, xkey, nt, gidx, wgu_scr, wgukey, wd_scr, wdkey, out_t, okey):
            N = nt * 128
            if not B.get("wd_loaded"):
                fw.dma("sp", B["wd"][:], wd_scr.rearrange("p (fc d) -> p fc d", fc=FC), reads=[wdkey], writes=["wd"])
                B["wd_loaded"] = True
            norm_T(x_t, xkey, nt, B, gidx, B["xnT"], "xnT")
            for fc in range(FC):
                c = B["cnt"]; B["cnt"] += 1
                wb = B["wgu"][c % 3]; wk = "wgu%d" % (c % 3)
                pg = B["pg"][c % 2]; pu = B["pu"][c % 2]; sg = B["sg"][c % 2]
                pgk, puk, sgk = "pg%d" % (c % 2), "pu%d" % (c % 2), "sg%d" % (c % 2)
                fw.dma("sp", wb[:], wgu_scr[fc].rearrange("p (kc two j) -> p kc two j", kc=KC, two=2),
                       reads=[wgukey], writes=[wk])

                def mm(e, wb=wb, pg=pg, pu=pu):
                    r = None
                    for two, pp in ((0, pg), (1, pu)):
                        for kc in range(KC):
                            r = e.matmul(pp[:, 0:N], lhsT=wb[:, kc, two, :], rhs=B["xnT"][:, kc, 0:N],
                                         start=(kc == 0), stop=(kc == KC - 1))
                    return r
                fw.op("pe", mm, reads=[wk, "xnT"], writes=[pgk, puk])
                fw.op("act", lambda e, pg=pg, sg=sg: e.activation(out=sg[:, 0:N], in_=pg[:, 0:N], func=AF.Silu),
                      reads=[pgk], writes=[sgk])
                fw.op("dve", lambda e, pu=pu, sg=sg, fc=fc: e.tensor_tensor(out=B["actT"][:, fc, 0:N], in0=sg[:, 0:N],
                                                                          in1=pu[:, 0:N], op=ALU.mult),
                      reads=[sgk, puk], writes=["actT"])
            for i in range(nt):
                for dh in range(DM // 512):
                    c = B["cnt"]; B["cnt"] += 1
                    pd = B["pd"][c % 2]; pdk = "pd%d" % (c % 2)

                    def mm2(e, i=i, dh=dh, pd=pd):
                        r = None
                        for fc in range(FC):
                            r = e.matmul(pd[:], lhsT=B["actT"][:, fc, i * 128:(i + 1) * 128],
                                         rhs=B["wd"][:, fc, dh * 512:(dh + 1) * 512],
                                         start=(fc == 0), stop=(fc == FC - 1))
                        return r
                    fw.op("pe", mm2, reads=["actT", "wd"], writes=[pdk])
                    fw.op("dve", lambda e, i=i, dh=dh, pd=pd: e.scalar_tensor_tensor(
                        out=out_t[:, i, dh * 512:(dh + 1) * 512], in0=pd[:], scalar=0.5,
                        in1=x_t[:, i, dh * 512:(dh + 1) * 512], op0=ALU.mult, op1=ALU.add),
                        reads=[pdk, xkey], writes=[okey])

        def phase1(x_src, h_dst, ntiles):
            with ExitStack() as st:
                B = ffn_phase(st, "p1")
                xt = [sb(st, "p1x%d" % i, [128, GT, DM]) for i in range(2)]
                ht = sb(st, "p1h", [128, GT, DM])
                g0 = 0; gi = 0
                while g0 < ntiles:
                    nt = min(GT, ntiles - g0)
                    x_t = xt[gi % 2]; xk = "x%d" % (gi % 2)
                    fw.dma("sp", x_t[:, 0:nt, :], x_src[g0 * 128:(g0 + nt) * 128, :].rearrange("(i p) d -> p i d", p=128),
                           writes=[xk])
                    ffn_group(B, x_t, xk, nt, 0, w1gu_s, "w1gu_s", w1d_s, "w1d_s", ht, "ht")
                    fw.dma("pool", h_dst[g0 * 128:(g0 + nt) * 128, :].rearrange("(i p) d -> p i d", p=128), ht[:, 0:nt, :],
                           reads=["ht"], writes=["hdst"])
                    g0 += nt; gi += 1
                fw.barrier()

        def phase3(h_src, mix_src, y_dst, ntiles):
            with ExitStack() as st:
                B = ffn_phase(st, "p3")
                hin = sb(st, "p3hin", [128, GT, DM])
                h2 = sb(st, "p3h2", [128, GT, DM])
                mt = sb(st, "p3mt", [128, GT, 1024], BF16)
                mT = sb(st, "p3mT", [128, 8, GT * 128], BF16)
                wo = sb(st, "p3wo", [128, 8, DM], BF16)
                fw.dma("sp", wo[:], wout_s.rearrange("p (cc d) -> p cc d", cc=8), reads=["wout_s"], writes=["wo"])
                g0 = 0
                while g0 < ntiles:
                    nt = min(GT, ntiles - g0)
                    rows = slice(g0 * 128, (g0 + nt) * 128)
                    fw.dma("sp", hin[:, 0:nt, :], h_src[rows, :].rearrange("(i p) d -> p i d", p=128), writes=["hin"])
                    fw.dma("sp", mt[:, 0:nt, :], mix_src[rows, :].rearrange("(i p) d -> p i d", p=128), writes=["mt"])
                    for i in range(nt):
                        def tr(e, i=i):
                            r = None
                            for cc in range(8):
                                r = e.transpose(out=B["ptp"][:, cc, :], in_=mt[:, i, cc * 128:(cc + 1) * 128], identity=ident[:])
                            return r
                        fw.op("pe", tr, reads=["mt", "ident"], writes=["ptp"])
                        fw.op("act", lambda e, i=i: e.copy(out=mT[:, :, i * 128:(i + 1) * 128], in_=B["ptp"][:, 0:8, :]),
                              reads=["ptp"], writes=["mT"])
                    for i in range(nt):
                        for dh in range(DM // 512):
                            c = B["cnt"]; B["cnt"] += 1
                            pd = B["pd"][c % 2]; pdk = "pd%d" % (c % 2)

                            def mm(e, i=i, dh=dh, pd=pd):
                                r = None
                                for cc in range(8):
                                    r = e.matmul(pd[:], lhsT=mT[:, cc, i * 128:(i + 1) * 128],
                                                 rhs=wo[:, cc, dh * 512:(dh + 1) * 512], start=(cc == 0), stop=(cc == 7))
                                return r
                            fw.op("pe", mm, reads=["mT", "wo"], writes=[pdk])
                            fw.op("dve", lambda e, i=i, dh=dh, pd=pd: e.tensor_tensor(
                                out=h2[:, i, dh * 512:(dh + 1) * 512], in0=pd[:], in1=hin[:, i, dh * 512:(dh + 1) * 512],
                                op=ALU.add), reads=[pdk, "hin"], writes=["h2"])
                    ffn_group(B, h2, "h2", nt, 2, w2gu_s, "w2gu_s", w2d_s, "w2d_s", h2, "h2")
                    rms_stats(h2, "h2", nt, B, DM)
                    for i in range(nt):
                        fw.op("dve", lambda e, i=i: e.scalar_tensor_tensor(
                            out=h2[:, i, :], in0=h2[:, i, :], scalar=B["rstd"][:, i:i + 1], in1=gfin_b[:],
                            op0=ALU.mult, op1=ALU.mult), reads=["h2", "rstd", "gfin_b"], writes=["h2"])
                    fw.dma("pool", y_dst[rows, :].rearrange("(i p) d -> p i d", p=128), h2[:, 0:nt, :],
                           reads=["h2"], writes=["ydst"])
                    g0 += nt
                fw.barrier()

        LN8 = math.log(0.125)
        VS = 80

        def phase2(h_src, mix_dst, prompt, mode, init_state=None, pos=0):
            full = mode == "full"
            off = 1 if prompt else 0
            ntl = NT + 2 * off
            with ExitStack() as st:
                unT = sb(st, "unT", [128, KC, ntl * 128], BF16)
                gts = sb(st, "gts", [128, NT, 32])
                wsb = sb(st, "wsb", [128, KC, 768], BF16)
                wsb2 = sb(st, "wsb2", [128, KC, 512], BF16)
                wcur = {"t": wsb, "k": "wsb"}
                ptp2 = ps(st, "p2ptp", [128, 8, 128], BF16)
                NB = {"ptp": ptp2}
                pA = [ps(st, "p2pa%d" % i, [128, 512]) for i in range(7)]
                cs = sb(st, "cs", [128, 2, NT, 8]); es = sb(st, "es", [128, 2, NT, 8])
                rt = sb(st, "rt", [128, 2, NT, 8]); eA = sb(st, "eA", [128, 2, NT, 8])
                Asum = sb(st, "Asum", [128, 2, 8])
                with ExitStack() as us:
                    NBu = {"xn": sb(us, "p2xn", [128, GT, DM], BF16), "junk": sb(us, "p2junk", [128, DM], BF16),
                           "ss": sb(us, "p2ss", [128, GT]), "rstd": sb(us, "p2rstd", [128, GT]), "ptp": ptp2}
                    hld = [sb(us, "p2h%d" % i, [128, GT, DM]) for i in range(2)]
                    g0 = 0; gi = 0
                    while g0 < ntl:
                        nt = min(GT, ntl - g0)
                        ht = hld[gi % 2]; hk = "hld%d" % (gi % 2)
                        fw.dma("sp", ht[:, 0:nt, :], h_src[g0 * 128:(g0 + nt) * 128, :].rearrange("(i p) d -> p i d", p=128),
                               writes=[hk])
                        norm_T(ht, hk, nt, NBu, 1, unT, "unT", col0=g0 * 128)
                        g0 += nt; gi += 1
                    fw.barrier()
                chk("p2a")
                winv = win_s.rearrange("p (kc c) -> p kc c", kc=KC)

                def load_w(c0, c1):
                    fw.dma("sp", wsb[:, :, 0:c1 - c0], winv[:, :, c0:c1], reads=["win_s"], writes=["wsb"])

                def proj_fm(col, ncol, tok0, ntok, dst_ps):
                    wt = wcur["t"]

                    def f(e):
                        r = None
                        for kc in range(KC):
                            r = e.matmul(dst_ps[0:ncol, 0:ntok], lhsT=wt[:, kc, col:col + ncol],
                                         rhs=unT[:, kc, tok0:tok0 + ntok], start=(kc == 0), stop=(kc == KC - 1))
                        return r
                    return f

                def proj_tm(col, ncol, tile, dst_ps):
                    wt = wcur["t"]

                    def f(e):
                        r = None
                        for kc in range(KC):
                            r = e.matmul(dst_ps[:, 0:ncol], lhsT=unT[:, kc, tile * 128:(tile + 1) * 128],
                                         rhs=wt[:, kc, col:col + ncol], start=(kc == 0), stop=(kc == KC - 1))
                        return r
                    return f

                with ExitStack() as gs:
                    load_w(2816, 2848)
                    for i in range(NT):
                        pp = pA[i % 2]; pk = "pa%d" % (i % 2)
                        fw.op("pe", proj_tm(0, 32, i + off, pp), reads=["wsb", "unT"], writes=[pk])
                        fw.op("dve", lambda e, i=i, pp=pp: e.tensor_tensor(out=gts[:, i, :], in0=pp[:, 0:32], in1=bg_b[:],
                                                                          op=ALU.add), reads=[pk, "bg_b"], writes=["gts"])
                    chk("p2b")
                    W = NT * 8
                    g4 = gts[:].rearrange("p n (a h) -> p n a h", a=4)
                    fx = sb(gs, "fx", [128, 2, NT, 8]); t1 = sb(gs, "t1", [128, 2, NT, 8]); t2 = sb(gs, "t2", [128, 2, NT, 8])
                    lf = sb(gs, "lf", [128, 2, NT, 8])
                    parts = [sb(gs, "lfp%d" % i, [128, 2, NT, 8], BF16) for i in range(3)]
                    for d in range(2):
                        fw.op("dve", lambda e, d=d: e.tensor_copy(out=fx[:, d, :, :], in_=g4[:, :, 1 + 2 * d, :]),
                              reads=["gts"], writes=["fx"])
                    fw.op("dve", lambda e: e.tensor_single_scalar(out=t2[:], in_=fx[:], scalar=0.0, op=ALU.min),
                          reads=["fx"], writes=["t2"])
                    fw.op("dve", lambda e: e.scalar_tensor_tensor(out=t1[:], in0=t2[:], scalar=2.0, in1=fx[:],
                                                                  op0=ALU.mult, op1=ALU.subtract),
                          reads=["fx", "t2"], writes=["t1"])
                    fw.op("act", lambda e: e.activation(out=t1[:], in_=t1[:], func=AF.Exp),
                          reads=["t1"], writes=["t1"])
                    fw.op("act", lambda e: e.activation(out=t1[:], in_=t1[:], func=AF.Ln, bias=1.0),
                          reads=["t1"], writes=["t1"])
                    fw.op("dve", lambda e: e.tensor_tensor(out=lf[:], in0=t2[:], in1=t1[:], op=ALU.subtract),
                          reads=["t1", "t2"], writes=["lf"])
                    fw.op("dve", lambda e: e.tensor_copy(out=parts[0][:], in_=lf[:]), reads=["lf"], writes=["lfp0"])
                    fw.op("dve", lambda e: e.tensor_tensor(out=t1[:], in0=lf[:], in1=parts[0][:], op=ALU.subtract),
                          reads=["lf", "lfp0"], writes=["t1"])
                    fw.op("dve", lambda e: e.tensor_copy(out=parts[1][:], in_=t1[:]), reads=["t1"], writes=["lfp1"])
                    fw.op("dve", lambda e: e.tensor_tensor(out=t2[:], in0=t1[:], in1=parts[1][:], op=ALU.subtract),
                          reads=["t1", "lfp1"], writes=["t2"])
                    fw.op("dve", lambda e: e.tensor_copy(out=parts[2][:], in_=t2[:]), reads=["t2"], writes=["lfp2"])
                    pcs = pA[2]

                    def cums(e):
                        r = None
                        for d in range(2):
                            tri = maskF if d == 0 else maskB
                            for (mat, o) in ((tri, d * W), (ones_b, 2 * W + d * W)):
                                for k in range(3):
                                    r = e.matmul(pcs[:, o:o + W], lhsT=mat[:],
                                                 rhs=parts[k][:, d, :, :].rearrange("p n h -> p (n h)"),
                                                 start=(k == 0), stop=(k == 2))
                        return r
                    chk("p2c0")
                    fw.op("pe", cums, reads=["lfp0", "lfp1", "lfp2", "maskF", "maskB", "ones_b"], writes=["pa2"])
                    chk("p2c1")
                    pcv = pcs[:, 0:4 * W].rearrange("p (q d n h) -> p q d n h", q=2, d=2, n=NT)
                    for d in range(2):
                        fw.op("dve", lambda e, d=d: e.tensor_tensor(out=t1[:, d, :, :], in0=g4[:, :, 2 * d, :],
                                                                     in1=pcv[:, 0, d, :, :], op=ALU.subtract),
                              reads=["gts", "pa2"], writes=["t1"])
                    fw.op("act", lambda e: e.activation(out=cs[:], in_=t1[:], func=AF.Exp, bias=LN8),
                          reads=["t1"], writes=["cs"])
                    fw.op("dve", lambda e: e.tensor_tensor(out=t2[:], in0=t1[:], in1=pcv[:, 1, :, :, :], op=ALU.add),
                          reads=["t1", "pa2"], writes=["t2"])
                    fw.op("act", lambda e: e.activation(out=es[:], in_=t2[:], func=AF.Exp, bias=LN8),
                          reads=["t2"], writes=["es"])
                    chk("p2c2")
                    fw.op("act", lambda e: e.activation(out=rt[:], in_=pcv[:, 0, :, :, :], func=AF.Exp),
                          reads=["pa2"], writes=["rt"])
                    fw.op("act", lambda e: e.activation(out=eA[:], in_=pcv[:, 1, :, :, :], func=AF.Exp),
                          reads=["pa2"], writes=["eA"])
                    chk("p2c3")
                    if not full:
                        fw.op("dve", lambda e: e.tensor_copy(out=t1[:], in_=pcv[:, 1, :, :, :]), reads=["pa2"], writes=["t1"])
                        fw.op("dve", lambda e: e.memset(lf[:], 0.0), reads=["lf"], writes=["lf"])
                        for c_ in range(NT - 2, -1, -1):
                            fw.op("dve", lambda e, c_=c_: e.tensor_tensor(out=lf[:, 0, c_, :], in0=lf[:, 0, c_ + 1, :],
                                                                         in1=t1[:, 0, c_ + 1, :], op=ALU.add),
                                  reads=["lf", "t1"], writes=["lf"])
                        for c_ in range(1, NT):
                            fw.op("dve", lambda e, c_=c_: e.tensor_tensor(out=lf[:, 1, c_, :], in0=lf[:, 1, c_ - 1, :],
                                                                         in1=t1[:, 1, c_ - 1, :], op=ALU.add),
                                  reads=["lf", "t1"], writes=["lf"])
                        fw.op("dve", lambda e: e.tensor_tensor(out=t2[:], in0=t2[:], in1=lf[:], op=ALU.add),
                              reads=["lf", "t2", "es"], writes=["t2"])
                        fw.op("act", lambda e: e.activation(out=es[:], in_=t2[:], func=AF.Exp, bias=LN8),
                              reads=["t2"], writes=["es"])
                        chk("p2c4")
                        fw.op("dve", lambda e: e.tensor_copy(out=Asum[:], in_=t1[:, :, 0, :]), reads=["t1"], writes=["Asum"])
                        for n_ in range(1, NT):
                            fw.op("dve", lambda e, n_=n_: e.tensor_tensor(out=Asum[:], in0=Asum[:], in1=t1[:, :, n_, :], op=ALU.add),
                                  reads=["t1", "Asum"], writes=["Asum"])
                    chk("p2c5")
                    fw.barrier()

                chk("p2c")
                if full:
                    with ExitStack() as at:
                        qaT = sb(at, "qaT", [128, 4, S], BF16)
                        kT = sb(at, "kT", [128, ntl * 128], BF16)
                        vtm = sb(at, "vtm", [128, ntl, 128], BF16)
                        ssb = sb(at, "ssb", [128, 4, 384]); pbf = sb(at, "pbf", [128, 4, 384], BF16)
                        pts = [sb(at, "pts%d" % i, [128, 3, 128], BF16) for i in range(2)]
                        mx = sb(at, "mx", [128, 4]); negm = sb(at, "negm", [128, 4]); rs = sb(at, "rs", [128, 4])
                        tmp4 = sb(at, "tmp4", [128, 4]); rinv = sb(at, "rinv", [128, 4])
                        mixa = [sb(at, "mixa%d" % i, [128, 512], BF16) for i in range(2)]
                        load_w(0, 768)
                        TB = 512
                        for c in range(4):
                            for t0 in range(0, S, TB):
                                n = min(TB, S - t0)
                                pp = pA[5 + (c + t0 // TB) % 2]; pk = "pa%d" % (5 + (c + t0 // TB) % 2)
                                fw.op("pe", proj_fm(c * 128, 128, off * 128 + t0, n, pp), reads=["wsb", "unT"], writes=[pk])
                                fw.op("act", lambda e, c=c, t0=t0, n=n, pp=pp: e.mul(
                                    out=qaT[:, c, t0:t0 + n], in_=pp[:, 0:n], mul=0.125),
                                    reads=[pk], writes=["qaT"])
                        for t0 in range(0, ntl * 128, TB):
                            n = min(TB, ntl * 128 - t0)
                            pp = pA[5 + (t0 // TB) % 2]; pk = "pa%d" % (5 + (t0 // TB) % 2)
                            fw.op("pe", proj_fm(512, 128, t0, n, pp), reads=["wsb", "unT"], writes=[pk])
                            fw.op("act", lambda e, t0=t0, n=n, pp=pp: e.copy(out=kT[:, t0:t0 + n], in_=pp[:, 0:n]),
                                  reads=[pk], writes=["kT"])
                        for i in range(ntl):
                            pp = pA[5 + i % 2]; pk = "pa%d" % (5 + i % 2)
                            fw.op("pe", proj_tm(640, 128, i, pp), reads=["wsb", "unT"], writes=[pk])
                            fw.op("act", lambda e, i=i, pp=pp: e.copy(out=vtm[:, i, :], in_=pp[:, 0:128]),
                                  reads=[pk], writes=["vtm"])
                        pO = pA[4]
                        for i in range(NT):
                            ti = i + off
                            lo = ti - 1 if ti - 1 >= 0 else ti
                            hi = ti + 1 if ti + 1 < ntl else ti
                            nk = hi - lo + 1
                            b0 = (lo - (ti - 1)) * 128
                            ma = mixa[i % 2]; mak = "mixa%d" % (i % 2)
                            for g in range(2):
                                def smm(e, g=g, i=i, lo=lo, nk=nk):
                                    r = None
                                    for c in range(4):
                                        r = e.matmul(pA[c][:, 0:nk * 128], lhsT=qaT[g * 64:(g + 1) * 64, c, i * 128:(i + 1) * 128],
                                                     rhs=kT[g * 64:(g + 1) * 64, lo * 128:(lo + nk) * 128], start=True, stop=True)
                                    return r
                                fw.op("pe", smm, reads=["qaT", "kT"], writes=["pa0", "pa1", "pa2", "pa3"])
                                for c in range(4):
                                    fw.op("dve", lambda e, c=c, g=g, nk=nk, b0=b0: e.tensor_tensor(
                                        out=ssb[:, c, 0:nk * 128], in0=pA[c][:, 0:nk * 128],
                                        in1=bias[:, g * 4 + c, b0:b0 + nk * 128], op=ALU.add),
                                        reads=["pa%d" % c, "bias"], writes=["ssb"])
                                if prompt and i == 0:
                                    fw.op("dve", lambda e: e.tensor_scalar(out=ssb[:, :, 0:128], in0=ssb[:, :, 0:128],
                                                                            scalar1=flg[:, 0:1], scalar2=None, op0=ALU.add),
                                          reads=["ssb", "flg"], writes=["ssb"])
                                if prompt and i == NT - 1:
                                    fw.op("dve", lambda e: e.tensor_scalar(out=ssb[:, :, 256:384], in0=ssb[:, :, 256:384],
                                                                            scalar1=flg[:, 1:2], scalar2=None, op0=ALU.add),
                                          reads=["ssb", "flg"], writes=["ssb"])
                                fw.op("dve", lambda e, nk=nk: e.tensor_reduce(out=mx[:], in_=ssb[:, :, 0:nk * 128], axis=AX.X,
                                                                              op=ALU.max), reads=["ssb"], writes=["mx"])
                                fw.op("dve", lambda e, g=g: e.tensor_tensor(out=mx[:], in0=mx[:], in1=sink_b[:, g * 4:(g + 1) * 4],
                                                                             op=ALU.max), reads=["mx", "sink_b"], writes=["mx"])
                                fw.op("dve", lambda e: e.tensor_scalar(out=negm[:], in0=mx[:], scalar1=-1.0, scalar2=None,
                                                                        op0=ALU.mult), reads=["mx"], writes=["negm"])
                                for c in range(4):
                                    fw.op("act", lambda e, c=c, nk=nk: e.activation(
                                        out=pbf[:, c, 0:nk * 128], in_=ssb[:, c, 0:nk * 128], func=AF.Exp,
                                        bias=negm[:, c:c + 1], accum_out=rs[:, c:c + 1]),
                                        reads=["ssb", "negm"], writes=["pbf", "rs"])
                                fw.op("dve", lambda e, g=g: e.tensor_tensor(out=tmp4[:], in0=sink_b[:, g * 4:(g + 1) * 4], in1=mx[:],
                                                                             op=ALU.subtract), reads=["mx", "sink_b"], writes=["tmp4"])
                                fw.op("act", lambda e: e.activation(out=tmp4[:], in_=tmp4[:], func=AF.Exp),
                                      reads=["tmp4"], writes=["tmp4"])
                                fw.op("dve", lambda e: e.tensor_tensor(out=tmp4[:], in0=tmp4[:], in1=rs[:], op=ALU.add),
                                      reads=["tmp4", "rs"], writes=["tmp4"])
                                fw.op("dve", lambda e: e.reciprocal(out=rinv[:], in_=tmp4[:]), reads=["tmp4"], writes=["rinv"])
                                for c in range(4):
                                    pt = pts[c % 2]; ptk = "pts%d" % (c % 2)

                                    def trp(e, c=c, nk=nk):
                                        r = None
                                        for kb in range(nk):
                                            r = e.transpose(out=NB["ptp"][:, kb, :], in_=pbf[:, c, kb * 128:(kb + 1) * 128],
                                                            identity=ident[:])
                                        return r
                                    fw.op("pe", trp, reads=["pbf", "ident"], writes=["ptp"])
                                    fw.op("act", lambda e, pt=pt, nk=nk: e.copy(out=pt[:, 0:nk, :], in_=NB["ptp"][:, 0:nk, :]),
                                          reads=["ptp"], writes=[ptk])

                                    def pv(e, c=c, nk=nk, lo=lo, g=g, pt=pt):
                                        r = None
                                        for kb in range(nk):
                                            r = e.matmul(pO[:, c * 64:(c + 1) * 64], lhsT=pt[:, kb, :],
                                                         rhs=vtm[:, lo + kb, g * 64:(g + 1) * 64],
                                                         start=(kb == 0), stop=(kb == nk - 1))
                                        return r
                                    fw.op("pe", pv, reads=[ptk, "vtm"], writes=["pa4"])
                                fw.op("dve", lambda e, g=g, ma=ma: e.tensor_tensor(
                                    out=ma[:, g * 256:(g + 1) * 256].rearrange("p (c d) -> p c d", c=4),
                                    in0=pO[:, 0:256].rearrange("p (c d) -> p c d", c=4),
                                    in1=rinv[:].unsqueeze(2).to_broadcast([128, 4, 64]), op=ALU.mult),
                                    reads=["pa4", "rinv"], writes=[mak])
                            fw.dma("sp", mix_dst[i * 128:(i + 1) * 128, 0:512], ma[:], reads=[mak], writes=["mixdst"])
                        fw.barrier()

                if full:
                    chk("s2a")
                finals, fin_A = finals_t, fin_A_t
                with ExitStack() as ml:
                    pre = [sb(ml, "pre%d" % i, [128, S + 4]) for i in range(2)]
                    acc = sb(ml, "acc", [128, S])
                    qkT = [sb(ml, "qkT%d" % i, [128, S], BF16) for i in range(2)]
                    ktok = sb(ml, "ktok", [128, NT, 128], BF16)
                    v1 = sb(ml, "v1", [128, NT, 2, VS], BF16)
                    v1s = [sb(ml, "v1s%d" % d, [128, NT, 2, VS], BF16) for d in range(2)]
                    v1e = [sb(ml, "v1e%d" % d, [128, NT, 2, VS], BF16) for d in range(2)]
                    og = sb(ml, "og", [128, NT, 128])
                    hsum = sb(ml, "hsum", [128, NT, 128])
                    stt = [sb(ml, "stt%d" % d, [128, 65]) for d in range(2)]
                    stb = [sb(ml, "stb%d" % d, [128, 2, VS], BF16) for d in range(2)]
                    qblk = sb(ml, "qblk", [128, NT, 2, 128], BF16)
                    PTs = [sb(ml, "PT%d" % d, [128, 2, 128], BF16) for d in range(2)]
                    dd = [sb(ml, "dd%d" % d, [128, 2]) for d in range(2)]
                    rr = [sb(ml, "rr%d" % d, [128, 2]) for d in range(2)]
                    msq = sb(ml, "msq", [128, NT, 2])
                    mixm = sb(ml, "mixm", [128, NT, 128], BF16)
                    fw.op("dve", lambda e: e.memset(v1[:], 1.0), writes=["v1"])
                    fw.op("dve", lambda e: e.memset(qblk[:], 0.0), writes=["qblk"])
                    for d in range(2):
                        fw.op("dve", lambda e, d=d: e.memset(stb[d][:], 0.0), writes=["stb%d" % d])
                    for j in range(4):
                        wcur["t"], wcur["k"] = (wsb, "wsb") if j % 2 == 0 else (wsb2, "wsb2")
                        for bi, c0 in enumerate((768, 1280, 1792, 2304)):
                            fw.dma("sp", wcur["t"][:, :, bi * 128:(bi + 1) * 128], winv[:, :, c0 + j * 128:c0 + (j + 1) * 128],
                                   reads=["win_s"], writes=[wcur["k"]])
                        for qi in range(2):
                            if qi == 0 and not full:
                                continue
                            pr = pre[qi]; prk = "pre%d" % qi
                            if prompt:
                                for (t0, n, dcol) in ((off * 128 - 2, 2, 0), ((off + NT) * 128, 2, S + 2)):
                                    pp = pA[5]; pk = "pa5"
                                    fw.op("pe", proj_fm(qi * 128, 128, t0, n, pp), reads=[wcur["k"], "unT"], writes=[pk])
                                    fw.op("act", lambda e, pr=pr, dcol=dcol, pp=pp: e.copy(out=pr[:, dcol:dcol + 2], in_=pp[:, 0:2]),
                                          reads=[pk], writes=[prk])
                                    fcol_ = (18 + pos) if dcol == 0 else (26 + pos)
                                    fw.op("dve", lambda e, pr=pr, dcol=dcol, fcol_=fcol_: e.tensor_scalar(
                                        out=pr[:, dcol:dcol + 2], in0=pr[:, dcol:dcol + 2], scalar1=flg[:, fcol_:fcol_ + 1],
                                        scalar2=None, op0=ALU.mult), reads=[prk, "flg"], writes=[prk])
                            else:
                                fw.op("dve", lambda e, pr=pr: e.memset(pr[:, 0:2], 0.0), writes=[prk])
                                fw.op("dve", lambda e, pr=pr: e.memset(pr[:, S + 2:S + 4], 0.0), writes=[prk])
                            for t0 in range(0, S, 512):
                                n = min(512, S - t0)
                                pp = pA[5 + (t0 // 512) % 2]; pk = "pa%d" % (5 + (t0 // 512) % 2)
                                fw.op("pe", proj_fm(qi * 128, 128, off * 128 + t0, n, pp), reads=[wcur["k"], "unT"], writes=[pk])
                                fw.op("act", lambda e, pr=pr, t0=t0, n=n, pp=pp: e.copy(out=pr[:, 2 + t0:2 + t0 + n], in_=pp[:, 0:n]),
                                      reads=[pk], writes=[prk])
                            ch = qi * 4 + j
                            fw.op("dve", lambda e, pr=pr, ch=ch: e.tensor_scalar(out=acc[:], in0=pr[:, 0:S], scalar1=wcv[:, ch, 0:1],
                                                                                  scalar2=None, op0=ALU.mult),
                                  reads=[prk, "wcv"], writes=["acc"])
                            for tap in range(1, 5):
                                fw.op("dve", lambda e, pr=pr, ch=ch, tap=tap: e.scalar_tensor_tensor(
                                    out=acc[:], in0=pr[:, tap:tap + S], scalar=wcv[:, ch, tap:tap + 1], in1=acc[:],
                                    op0=ALU.mult, op1=ALU.add), reads=[prk, "wcv", "acc"], writes=["acc"])
                            fw.op("act", lambda e, qi=qi: e.activation(out=qkT[qi][:], in_=acc[:], func=AF.Silu),
                                  reads=["acc"], writes=["qkT%d" % qi])
                            if qi == 0 and full:
                                for hh in range(2):
                                    hs = slice(hh * 64, (hh + 1) * 64)
                                    fw.op("act", lambda e, hh=hh, hs=hs: e.activation(
                                        out=qblk[hs, :, hh, :], in_=acc[hs, :].rearrange("p (n t) -> p n t", t=128), func=AF.Silu),
                                        reads=["acc"], writes=["qblk"])
                        chk("p2d")
                        for i in range(NT):
                            fw.op("pe", lambda e, i=i: e.transpose(out=NB["ptp"][:, i % 8, :], in_=qkT[1][:, i * 128:(i + 1) * 128],
                                                                    identity=ident[:]), reads=["qkT1", "ident"], writes=["ptp"])
                            fw.op("act", lambda e, i=i: e.copy(out=ktok[:, i, :], in_=NB["ptp"][:, i % 8, :]),
                                  reads=["ptp"], writes=["ktok"])
                        for i in range(NT):
                            pp = pA[5 + i % 2]; pk = "pa%d" % (5 + i % 2)
                            fw.op("pe", proj_tm(256, 256, i + off, pp), reads=[wcur["k"], "unT"], writes=[pk])
                            fw.op("dve", lambda e, i=i, pp=pp: e.tensor_copy(
                                out=v1[:, i, :, 0:64], in_=pp[:, 0:128].rearrange("p (a d) -> p a d", a=2)),
                                reads=[pk], writes=["v1"])
                            if full:
                                fw.op("act", lambda e, i=i, pp=pp: e.activation(out=og[:, i, :], in_=pp[:, 128:256], func=AF.Sigmoid),
                                      reads=[pk], writes=["og"])
                        for d in range(2):
                            for (dst, scal, dk, sk) in ((v1s[d], cs, "v1s%d" % d, "cs"), (v1e[d], es, "v1e%d" % d, "es")):
                                if dst is v1s[d] and not full:
                                    continue
                                fw.op("dve", lambda e, dst=dst, scal=scal, d=d: e.tensor_tensor(
                                    out=dst[:], in0=v1[:],
                                    in1=scal[:, d, :, 2 * j:2 * j + 2].unsqueeze(3).to_broadcast([128, NT, 2, VS]),
                                    op=ALU.mult), reads=["v1", sk], writes=[dk])
                        chk("p2e")
                        if not full:
                            for d in range(2):
                                pC = pA[d]; pCk = "pa%d" % d

                                def accmm(e, d=d, pC=pC):
                                    r = None
                                    for c in range(NT):
                                        r = e.matmul(pC[:, 0:2 * VS], lhsT=ktok[:, c, :],
                                                     rhs=v1e[d][:, c, :, :].rearrange("p a c -> p (a c)"),
                                                     start=(c == 0), stop=(c == NT - 1))
                                    return r
                                fw.op("pe", accmm, reads=["ktok", "v1e%d" % d], writes=[pCk])
                                for hh in range(2):
                                    hs = slice(hh * 64, (hh + 1) * 64)
                                    fw.op("dve", lambda e, d=d, hh=hh, hs=hs, pC=pC: e.tensor_copy(
                                        out=finals[hs, j, d, :], in_=pC[hs, hh * VS:hh * VS + 65]),
                                        reads=[pCk], writes=["finals"])
                                    fw.op("dve", lambda e, d=d, hh=hh, hs=hs: e.tensor_copy(
                                        out=fin_A[hs, j, d:d + 1], in_=Asum[hs, d, 2 * j + hh:2 * j + hh + 1]),
                                        reads=["Asum"], writes=["fin_A"])
                            continue
                        for d in range(2):
                            if init_state is not None:
                                fw.op("dve", lambda e, d=d: e.tensor_copy(out=stt[d][:], in_=init_state[:, j, d, :]),
                                      reads=["init_state"], writes=["stt%d" % d])
                            else:
                                fw.op("dve", lambda e, d=d: e.memset(stt[d][:], 0.0), writes=["stt%d" % d])
                            for hh in range(2):
                                hs = slice(hh * 64, (hh + 1) * 64)
                                fw.op("act", lambda e, d=d, hh=hh, hs=hs: e.copy(out=stb[d][hs, hh, 0:65], in_=stt[d][hs, :]),
                                      reads=["stt%d" % d], writes=["stb%d" % d])
                        if full:
                            chk("m0")
                        for step in range(NT):
                            if full and step == 1:
                                chk("m1")
                            for d in range(2):
                                c = step if d == 0 else NT - 1 - step
                                pS, pN = pA[0 + d], pA[2 + d]
                                pSk, pNk = "pa%d" % d, "pa%d" % (2 + d)
                                pC = pA[4]; pCk = "pa4"
                                mk_ = maskF if d == 0 else maskB
                                if full:
                                    def smm(e, c=c, pS=pS):
                                        return e.matmul(pS[:, 0:256], lhsT=qkT[1][:, c * 128:(c + 1) * 128],
                                                        rhs=qblk[:, c, :, :].rearrange("p a t -> p (a t)"), start=True, stop=True)
                                    fw.op("pe", smm, reads=["qblk", "qkT1"], writes=[pSk])
                                    chk("q1")
                                    fw.op("dve", lambda e, d=d, pS=pS, mk_=mk_: e.tensor_tensor(
                                        out=PTs[d][:], in0=pS[:, 0:256].rearrange("p (a t) -> p a t", a=2),
                                        in1=mk_[:].unsqueeze(1).to_broadcast([128, 2, 128]), op=ALU.mult),
                                        reads=[pSk, "maskF", "maskB"], writes=["PT%d" % d])
                                    chk("q2")

                                    def nmm(e, c=c, d=d, pN=pN):
                                        e.matmul(pN[:, 0:2 * VS], lhsT=qkT[0][:, c * 128:(c + 1) * 128],
                                                 rhs=stb[d][:].rearrange("p a c -> p (a c)"), start=True, stop=False)
                                        r = None
                                        for hh in range(2):
                                            r = e.matmul(pN[:, hh * VS:(hh + 1) * VS], lhsT=PTs[d][:, hh, :], rhs=v1s[d][:, c, hh, :],
                                                         start=False, stop=(hh == 1))
                                        return r
                                    fw.op("pe", nmm, reads=["PT%d" % d, "v1s%d" % d, "qkT0", "stb%d" % d], writes=[pNk])
                                    chk("q3")
                                    pNv = pN[:, 0:2 * VS].rearrange("p (a c) -> p a c", a=2)
                                    rtv = rt[:, d, c, 2 * j:2 * j + 2]
                                    fw.op("dve", lambda e, d=d, pNv=pNv, rtv=rtv: e.tensor_tensor(
                                        out=dd[d][:].unsqueeze(2), in0=pNv[:, :, 64:65], in1=rtv.unsqueeze(2), op=ALU.mult),
                                        reads=[pNk, "rt"], writes=["dd%d" % d])
                                    fw.op("dve", lambda e, d=d: e.scalar_tensor_tensor(out=rr[d][:], in0=dd[d][:], scalar=-1.0,
                                                                                        in1=dd[d][:], op0=ALU.mult, op1=ALU.max),
                                          reads=["dd%d" % d], writes=["rr%d" % d])
                                    fw.op("dve", lambda e, d=d: e.tensor_scalar(out=rr[d][:], in0=rr[d][:], scalar1=1.0, scalar2=None,
                                                                                 op0=ALU.max),
                                          reads=["rr%d" % d], writes=["rr%d" % d])
                                    fw.op("dve", lambda e, d=d: e.reciprocal(out=rr[d][:], in_=rr[d][:]),
                                          reads=["rr%d" % d], writes=["rr%d" % d])
                                    fw.op("dve", lambda e, d=d, rtv=rtv: e.tensor_tensor(out=rr[d][:], in0=rtv, in1=rr[d][:],
                                                                                        op=ALU.mult),
                                          reads=["rr%d" % d, "rt"], writes=["rr%d" % d])
                                    chk("q4")
                                    step_f, step_b = c, NT - 1 - c
                                    first = (step_f < step_b) if d == 0 else (step_b < step_f)
                                    if step_f == step_b:
                                        first = (d == 0)
                                    for hh in range(2):
                                        if first:
                                            fw.op("dve", lambda e, d=d, c=c, hh=hh, pNv=pNv: e.tensor_scalar(
                                                out=hsum[:, c, hh * 64:(hh + 1) * 64], in0=pNv[:, hh, 0:64],
                                                scalar1=rr[d][:, hh:hh + 1], scalar2=None, op0=ALU.mult),
                                                reads=[pNk, "rr%d" % d], writes=["hsum"])
                                        else:
                                            fw.op("dve", lambda e, d=d, c=c, hh=hh, pNv=pNv: e.scalar_tensor_tensor(
                                                out=hsum[:, c, hh * 64:(hh + 1) * 64], in0=pNv[:, hh, 0:64],
                                                scalar=rr[d][:, hh:hh + 1], in1=hsum[:, c, hh * 64:(hh + 1) * 64],
                                                op0=ALU.mult, op1=ALU.add), reads=[pNk, "rr%d" % d, "hsum"], writes=["hsum"])
                                fw.op("pe", lambda e, c=c, d=d: e.matmul(
                                    pC[:, 0:2 * VS], lhsT=ktok[:, c, :], rhs=v1e[d][:, c, :, :].rearrange("p a c -> p (a c)"),
                                    start=True, stop=True), reads=["ktok", "v1e%d" % d], writes=[pCk])
                                for hh in range(2):
                                    hs = slice(hh * 64, (hh + 1) * 64)
                                    fw.op("dve", lambda e, d=d, c=c, hh=hh, hs=hs: e.scalar_tensor_tensor(
                                        out=stt[d][hs, :], in0=stt[d][hs, :], scalar=eA[hs, d, c, 2 * j + hh:2 * j + hh + 1],
                                        in1=pC[hs, hh * VS:hh * VS + 65], op0=ALU.mult, op1=ALU.add),
                                        reads=["stt%d" % d, "eA", pCk], writes=["stt%d" % d])
                                for hh in range(2):
                                    hs = slice(hh * 64, (hh + 1) * 64)
                                    fw.op("act", lambda e, d=d, hh=hh, hs=hs: e.copy(out=stb[d][hs, hh, 0:65], in_=stt[d][hs, :]),
                                          reads=["stt%d" % d], writes=["stb%d" % d])
                        if full:
                            chk("m2")
                            fw.op("dve", lambda e: e.tensor_tensor(out=hsum[:], in0=hsum[:], in1=og[:], op=ALU.mult),
                                  reads=["hsum", "og"], writes=["hsum"])
                            fw.op("dve", lambda e: e.tensor_tensor(out=og[:], in0=hsum[:], in1=hsum[:], op=ALU.mult),
                                  reads=["hsum", "og"], writes=["og"])
                            fw.op("dve", lambda e: e.tensor_reduce(out=msq[:], in_=og[:].rearrange("p n (a d) -> p n a d", a=2),
                                                                   axis=AX.X, op=ALU.add), reads=["og"], writes=["msq"])
                            fw.op("act", lambda e: e.activation(out=msq[:], in_=msq[:], func=AF.Sqrt, scale=1.0 / 64, bias=EPS),
                                  reads=["msq"], writes=["msq"])
                            fw.op("dve", lambda e: e.reciprocal(out=msq[:], in_=msq[:]), reads=["msq"], writes=["msq"])
                            fw.op("dve", lambda e: e.tensor_tensor(
                                out=hsum[:].rearrange("p n (a d) -> p n a d", a=2), in0=hsum[:].rearrange("p n (a d) -> p n a d", a=2),
                                in1=msq[:].unsqueeze(3).to_broadcast([128, NT, 2, 64]), op=ALU.mult),
                                reads=["hsum", "msq"], writes=["hsum"])
                            fw.op("dve", lambda e: e.tensor_tensor(
                                out=mixm[:], in0=hsum[:],
                                in1=gml_b[:, j * 128:(j + 1) * 128].unsqueeze(1).to_broadcast([128, NT, 128]), op=ALU.mult),
                                reads=["hsum", "gml_b"], writes=["mixm"])
                            chk("m3")
                            fw.dma("pool", mix_dst[:, 512 + j * 128:512 + (j + 1) * 128].rearrange("(n p) c -> p n c", p=128),
                                   mixm[:], reads=["mixm"], writes=["mixdst"])
                            chk("m4")
                        else:
                            for d in range(2):
                                fw.op("dve", lambda e, d=d: e.tensor_copy(out=finals[:, j, d, :], in_=stt[d][:]),
                                      reads=["stt%d" % d], writes=["finals"])
                                for hh in range(2):
                                    hs = slice(hh * 64, (hh + 1) * 64)
                                    fw.op("dve", lambda e, d=d, hh=hh, hs=hs: e.tensor_copy(
                                        out=fin_A[hs, j, d:d + 1], in_=Asum[hs, d, 2 * j + hh:2 * j + hh + 1]),
                                        reads=["Asum"], writes=["fin_A"])
                    fw.barrier()
                fw.barrier()
            return (finals, fin_A) if not full else None

        def main_schedule():
            if with_prompt:
                phase1(x_pfull, hbuf_pf, n_ranks * NT + 2)
                chk("p1")
                for r in range(1, n_ranks):
                    finals, fin_A = phase2(hbuf_pf[r * S:r * S + NTP * 128, :], None, True, "summary", pos=r)
                    chk("p2s")
                    fw.dma("sp", g_dst[r * 128:(r + 1) * 128, 0:520], finals[:].rearrange("p a d c -> p (a d c)"),
                           reads=["finals"], writes=["g_dst"])
                    fw.dma("sp", g_dst[r * 128:(r + 1) * 128, 520:528], fin_A[:].rearrange("p a d -> p (a d)"),
                           reads=["fin_A"], writes=["g_dst"])
                    fw.barrier()
                chk("cc")
            for s in range(NSAMP):
                rows = slice(s * S, (s + 1) * S)
                phase1(x_samp[rows, :], hbuf_s[rows, :], NT)
                chk("s1")
                phase2(hbuf_s[rows, :], mix_s[rows, :], False, "full")
                chk("s2")
                phase3(hbuf_s[rows, :], mix_s[rows, :], y_samp[rows, :], NT)
                chk("s3")
            if with_prompt:
                cst = ExitStack()
                gat = sb(cst, "gat", [128, n_ranks, GWc])
                fw.dma("sp", gat[:, 1:n_ranks, :], g_dst[128:n_ranks * 128, :].rearrange("(r p) w -> p r w", p=128),
                       reads=["g_dst"], writes=["gat"])
                fw.op("dve", lambda e: e.memset(ist[:], 0.0), writes=["ist"])
                for d in range(2):
                    order = range(1, n_ranks) if d == 0 else range(n_ranks - 1, 0, -1)
                    for r in order:
                        fcol = flg[:, 2 + 8 * d + r:3 + 8 * d + r]
                        Av = gat[:, r, 520:528].rearrange("p (a d) -> p a d", a=4)[:, :, d:d + 1]
                        Sv = gat[:, r, 0:520].rearrange("p (a d c) -> p a d c", a=4, d=2)[:, :, d, :]
                        fw.op("dve", lambda e, d=d, Av=Av, fcol=fcol: e.tensor_scalar(out=dec[:, :, d:d + 1], in0=Av, scalar1=fcol,
                                                                                      scalar2=None, op0=ALU.mult),
                              reads=["gat", "flg"], writes=["dec"])
                        fw.op("act", lambda e, d=d: e.activation(out=dec[:, :, d:d + 1], in_=dec[:, :, d:d + 1], func=AF.Exp),
                              reads=["dec"], writes=["dec"])
                        fw.op("dve", lambda e, d=d: e.tensor_tensor(out=ist[:, :, d, :], in0=ist[:, :, d, :],
                                                                     in1=dec[:, :, d:d + 1].to_broadcast([128, 4, 65]), op=ALU.mult),
                              reads=["ist", "dec"], writes=["ist"])
                        fw.op("dve", lambda e, d=d, Sv=Sv, fcol=fcol: e.scalar_tensor_tensor(
                            out=ist[:, :, d, :], in0=Sv, scalar=fcol, in1=ist[:, :, d, :], op0=ALU.mult, op1=ALU.add),
                            reads=["ist", "gat", "flg"], writes=["ist"])
                fw.barrier()
                cst.close()
                chk("comb")
                fw.res["init_state"] = _Res()
                phase2(hbuf_pf[0:NTP * 128, :], mix_p, True, "full", init_state=ist, pos=0)
                phase3(hbuf_pf[128:128 + S, :], mix_p, y_prm, NT)


        try:
            main_schedule()
        except _Stop:
            pass
        fw.finish()
    return nc


_CACHE = {}


def kernel(x_prompt, x_sample, g_ffn1, w_ffn1_gu, w_ffn1_down, g_mix, w_in, w_conv, b_gates,
           attn_sink, g_mlstm_out, w_out, g_ffn2, w_ffn2_gu, w_ffn2_down, rel_bias_table, g_final):
    f32 = np.float32
    S = 2048
    DM = 1024
    n = N_CORES
    x_prompt = np.asarray(x_prompt, f32)
    x_sample = np.asarray(x_sample, f32)
    if "nc" not in _CACHE:
        _CACHE["nc"] = build_program()
    nc = _CACHE["nc"]
    xp = x_prompt.reshape(-1, DM)
    gains = np.stack([np.asarray(g_ffn1, f32)[0], np.asarray(g_mix, f32)[0], np.asarray(g_ffn2, f32)[0],
                      np.asarray(g_final, f32)])
    common = {
        "w1gu": np.ascontiguousarray(np.asarray(w_ffn1_gu, f32)[0]),
        "w1d": np.ascontiguousarray(np.asarray(w_ffn1_down, f32)[0]),
        "w2gu": np.ascontiguousarray(np.asarray(w_ffn2_gu, f32)[0]),
        "w2d": np.ascontiguousarray(np.asarray(w_ffn2_down, f32)[0]),
        "win": np.ascontiguousarray(np.asarray(w_in, f32)[0]),
        "wout": np.ascontiguousarray(np.asarray(w_out, f32)[0]),
        "gains": np.ascontiguousarray(gains),
        "wconv": np.ascontiguousarray(np.asarray(w_conv, f32)[0]),
        "bgates": np.ascontiguousarray(np.asarray(b_gates, f32)[0].reshape(1, 32)),
        "sink": np.ascontiguousarray(np.asarray(attn_sink, f32).reshape(1, 8)),
        "gml": np.ascontiguousarray(np.asarray(g_mlstm_out, f32).reshape(1, 512)),
        "reltab": np.ascontiguousarray(np.asarray(rel_bias_table, f32)),
        "onehot": _bucket_onehot(),
    }
    in_maps = []
    for c in range(n):
        fl = np.zeros((1, 34), f32)
        fl[0, 0] = -30000.0 if c == 0 else 0.0
        fl[0, 1] = -30000.0 if c == n - 1 else 0.0
        xr = np.empty((n * S + 256, DM), f32)
        xr[128:128 + n * S] = np.roll(xp, -c * S, axis=0)
        xr[0:128] = xr[n * S:n * S + 128]
        xr[128 + n * S:] = xr[128:256]
        if c == 0:
            xr[0:128] = 0.0
            xr[128 + n * S:] = 0.0
        for k in range(n):
            r = (c + k) % n
            fl[0, 2 + k] = 1.0 if r < c else 0.0
            fl[0, 10 + k] = 1.0 if r > c else 0.0
            fl[0, 18 + k] = 0.0 if r == 0 else 1.0
            fl[0, 26 + k] = 0.0 if r == n - 1 else 1.0
        m = dict(common)
        m["x_samp"] = np.ascontiguousarray(x_sample[4 * c:4 * c + 4].reshape(4 * S, DM))
        m["x_pfull"] = xr
        m["flags"] = fl
        in_maps.append(m)
    res = run_bass_kernel_spmd(nc, in_maps, core_ids=list(range(n)))
    y_p = np.concatenate([np.asarray(res.results[c]["y_prm"], f32) for c in range(n)], axis=0).reshape(x_prompt.shape)
    y_s = np.concatenate([np.asarray(res.results[c]["y_samp"], f32).reshape(4, S, DM) for c in range(n)], axis=0)
    return (y_p, y_s.reshape(x_sample.shape))
```

```python
import math
from contextlib import ExitStack

import numpy as np
import concourse.bass as bass
import concourse.mybir as mybir
from concourse.bass_utils import run_bass_kernel_spmd

F32 = mybir.dt.float32
BF16 = mybir.dt.bfloat16
ALU = mybir.AluOpType
AF = mybir.ActivationFunctionType
AX = mybir.AxisListType

HD = 64
NEG = -1e30
EPS = 1e-6
IN_W = 2848
N_CORES = 8


_PSUM_PREFIXES = ("pa", "pg", "pu", "pd", "ptp", "pf")


class _Res:
    __slots__ = ("w", "reads")

    def __init__(self):
        self.w = None
        self.reads = []


class _Eng:
    def __init__(self, name, eng, sem):
        self.name = name
        self.eng = eng
        self.sem = sem
        self.count = 0
        self.waited = {}


class FW:
    def __init__(self, nc, stack):
        self.nc = nc
        self.stack = stack
        self.res = {}
        self.engs = {}
        for name in ("pe", "act", "dve", "pool", "sp"):
            eng = {"pe": nc.tensor, "act": nc.scalar, "dve": nc.vector,
                   "pool": nc.gpsimd, "sp": nc.sync}[name]
            sem = stack.enter_context(nc.semaphore("prog_" + name))
            self.engs[name] = _Eng(name, eng, sem)
        self.dsems = {}
        self.dcount = {}
        self.n_wait = 0
        self.n_inst = 0
        self.stopped = False
        self.bg_keys = set()

    def _r(self, key):
        r = self.res.get(key)
        if r is None:
            r = self.res[key] = _Res()
        return r

    def _deps(self, reads, writes):
        deps = {}

        def add(tok):
            s, v = tok
            if deps.get(s, (None, 0))[1] < v:
                deps[s] = (s, v)

        for k in reads:
            r = self._r(k)
            if r.w is not None:
                add(r.w)
            if k.startswith(_PSUM_PREFIXES):
                for tok in r.reads:
                    add(tok)
        for k in writes:
            r = self._r(k)
            if r.w is not None:
                add(r.w)
            for tok in r.reads:
                add(tok)
        return deps

    def _emit_waits(self, E, deps, skip_self=False):
        for s, (sem, v) in deps.items():
            if skip_self and sem is E.sem:
                continue
            if E.waited.get(s, 0) < v:
                E.eng.wait_ge(sem, v)
                E.waited[s] = v
                self.n_wait += 1

    def _commit(self, tok, reads, writes):
        for k in reads:
            r = self._r(k)
            r.reads.append(tok)
            if len(r.reads) > 48:
                best = {}
                for (s, v) in r.reads:
                    if best.get(s, (None, 0))[1] < v:
                        best[s] = (s, v)
                r.reads = list(best.values())
        for k in writes:
            r = self._r(k)
            r.w = tok
            r.reads = []

    def op(self, ename, fn, reads=(), writes=()):
        if self.stopped:
            return None
        E = self.engs[ename]
        deps = self._deps(reads, writes)
        self._emit_waits(E, deps, skip_self=(ename == "pe"))
        ins = fn(E.eng)
        E.count += 1
        ins.then_inc(E.sem, 1)
        tok = (E.sem, E.count)
        self._commit(tok, reads, writes)
        self.n_inst += 1
        return tok

    def dma(self, qname, out, in_, reads=(), writes=(), sem_key=None, **kw):
        if self.stopped:
            return None
        E = self.engs[qname]
        deps = self._deps(reads, writes)
        self._emit_waits(E, deps)
        if sem_key is None:
            sem_key = writes[0] if writes else reads[0]
        s = self.dsems.get(sem_key)
        if s is None:
            s = self.stack.enter_context(self.nc.semaphore("d%d" % len(self.dsems)))
            self.dsems[sem_key] = s
            self.dcount[sem_key] = 0
        E.eng.dma_start(out=out, in_=in_, **kw).then_inc(s, 16)
        self.dcount[sem_key] += 16
        tok = (s, self.dcount[sem_key])
        self._commit(tok, reads, writes)
        return tok

    def custom(self, qname, fn, reads=(), writes=(), sem_key=None, inc=1):
        if self.stopped:
            return None
        E = self.engs[qname]
        deps = self._deps(reads, writes)
        self._emit_waits(E, deps)
        s = self.dsems.get(sem_key)
        if s is None:
            s = self.stack.enter_context(self.nc.semaphore("c%d" % len(self.dsems)))
            self.dsems[sem_key] = s
            self.dcount[sem_key] = 0
        fn(E.eng).then_inc(s, inc)
        self.dcount[sem_key] += inc
        tok = (s, self.dcount[sem_key])
        self._commit(tok, reads, writes)
        return tok

    def _all_tokens(self, include_bg=False):
        final = {}
        for E in self.engs.values():
            if E.count:
                final[E.sem] = (E.sem, E.count)
        for k, s in self.dsems.items():
            if self.dcount[k] and (include_bg or k not in self.bg_keys):
                final[s] = (s, self.dcount[k])
        return final

    def barrier(self):
        if self.stopped:
            return
        final = self._all_tokens()
        for E in self.engs.values():
            self._emit_waits(E, final)
        keep = {k: v for k, v in self.res.items() if k in self.bg_keys}
        self.res = keep

    def finish(self):
        self._emit_waits(self.engs["sp"], self._all_tokens(include_bg=True))


def _t5_bucket(rel):
    nb = 16
    ret = (rel > 0).astype(np.int32) * nb
    n = np.abs(rel)
    max_exact = nb // 2
    large = max_exact + (np.log(np.maximum(n, 1) / max_exact)
                         / math.log(128 / max_exact) * (nb - max_exact)).astype(np.int32)
    large = np.minimum(large, nb - 1)
    return (ret + np.where(n < max_exact, n, large)).astype(np.int32)


def _bucket_onehot():
    oh = np.zeros((33, 640), np.float32)
    for j in range(640):
        rel = j - 255
        if abs(rel) <= 128:
            oh[int(_t5_bucket(np.array(rel))), j] = 1.0
        else:
            oh[32, j] = 1.0
    return oh


def build_program(S=2048, NSAMP=4, DM=1024, DFF=2816, n_ranks=8, with_prompt=True, stop=None):
    KC = DM // 128
    FC = DFF // 128
    NT = S // 128
    GT = 4
    NTP = NT + 2
    nc = bass.Bass("TRN2", target_bir_lowering=False)

    def din(name, shape, dt=F32):
        return nc.dram_tensor(name, list(shape), dt, kind="ExternalInput").ap()

    def dscr(name, shape, dt=F32):
        return nc.dram_tensor(name, list(shape), dt, kind="Internal").ap()

    x_samp = din("x_samp", [NSAMP * S, DM])
    x_pfull = din("x_pfull", [n_ranks * S + 256, DM])
    w1gu = din("w1gu", [DM, 2 * DFF]); w1d = din("w1d", [DFF, DM])
    w2gu = din("w2gu", [DM, 2 * DFF]); w2d = din("w2d", [DFF, DM])
    win = din("win", [DM, IN_W]); wout = din("wout", [1024, DM])
    gains = din("gains", [4, DM])
    wconv = din("wconv", [5, 1024])
    bgates = din("bgates", [1, 32])
    sink = din("sink", [1, 8])
    gml = din("gml", [1, 512])
    reltab = din("reltab", [32, 8])
    onehot = din("onehot", [33, 640])
    flags = din("flags", [1, 34])
    y_samp = nc.dram_tensor("y_samp", [NSAMP * S, DM], F32, kind="ExternalOutput").ap()
    y_prm = nc.dram_tensor("y_prm", [S, DM], F32, kind="ExternalOutput").ap()

    hbuf_s = dscr("hbuf_s", [NSAMP * S, DM]); hbuf_p = dscr("hbuf_p", [NTP * 128, DM])
    hbuf_pf = dscr("hbuf_pf", [n_ranks * S + 256, DM])
    mix_s = dscr("mix_s", [NSAMP * S, 1024], BF16); mix_p = dscr("mix_p", [S, 1024], BF16)
    w1gu_s = dscr("w1gu_s", [FC, 128, KC * 256], BF16); w2gu_s = dscr("w2gu_s", [FC, 128, KC * 256], BF16)
    w1d_s = dscr("w1d_s", [128, FC * DM], BF16); w2d_s = dscr("w2d_s", [128, FC * DM], BF16)
    win_s = dscr("win_s", [128, KC * IN_W], BF16); wout_s = dscr("wout_s", [128, 8 * DM], BF16)
    fd_s = dscr("fd_s", [8, 640])
    GW = 8 * 65 + 8
    g_src = dscr("g_src", [128, GW]); g_dst = dscr("g_dst", [n_ranks * 128, GW])

    with ExitStack() as top:
        fw = FW(nc, top)

        uid = [0]

        def chk(name):
            if stop == name:
                fw.stopped = True

        def sb(st, name, shape, dt=F32):
            uid[0] += 1
            return st.enter_context(nc.sbuf_tensor("%s_%d" % (name, uid[0]), list(shape), dt))

        def ps(st, name, shape, dt=F32):
            uid[0] += 1
            return st.enter_context(nc.psum_tensor("%s_%d" % (name, uid[0]), list(shape), dt))

        ident = sb(top, "ident", [128, 128], BF16)
        maskF = sb(top, "maskF", [128, 128], BF16)
        maskB = sb(top, "maskB", [128, 128], BF16)
        ones_b = sb(top, "ones_b", [128, 128], BF16)
        gT = sb(top, "gT", [128, 3, KC])
        gfin_b = sb(top, "gfin_b", [128, DM])
        wcv = sb(top, "wcv", [128, 8, 5])
        bg_b = sb(top, "bg_b", [128, 32])
        sink_b = sb(top, "sink_b", [128, 8])
        gml_b = sb(top, "gml_b", [128, 512])
        flg = sb(top, "flg", [128, 34])
        bias = sb(top, "bias", [128, 8, 384])
        finals_t = sb(top, "finals", [128, 4, 2, 65])
        fin_A_t = sb(top, "fin_A", [128, 4, 2])
        GWc = 8 * 65 + 8
        ist = sb(top, "ist", [128, 4, 2, 65])
        dec = sb(top, "dec", [128, 4, 2])

        def mk(fn, w):
            fw.op("pool", fn, writes=[w], reads=[])

        mk(lambda e: e.memset(ident[:], 1.0), "ident")
        fw.op("pool", lambda e: e.affine_select(out=ident[:], in_=ident[:], pattern=[[-1, 128]],
              compare_op=ALU.is_equal, fill=0.0, base=0, channel_multiplier=1), reads=["ident"], writes=["ident"])
        mk(lambda e: e.memset(maskF[:], 1.0), "maskF")
        fw.op("pool", lambda e: e.affine_select(out=maskF[:], in_=maskF[:], pattern=[[1, 128]],
              compare_op=ALU.is_ge, fill=0.0, base=0, channel_multiplier=-1), reads=["maskF"], writes=["maskF"])
        mk(lambda e: e.memset(maskB[:], 1.0), "maskB")
        fw.op("pool", lambda e: e.affine_select(out=maskB[:], in_=maskB[:], pattern=[[-1, 128]],
              compare_op=ALU.is_ge, fill=0.0, base=0, channel_multiplier=1), reads=["maskB"], writes=["maskB"])
        mk(lambda e: e.memset(ones_b[:], 1.0), "ones_b")

        fw.dma("sp", gT[:], gains[0:3, :].rearrange("g (kc p) -> p g kc", p=128), writes=["gT"],
               allow_slow_non_contiguous=True)
        fw.dma("sp", gfin_b[:], gains[3:4, :].partition_broadcast(128), writes=["gfin_b"])
        for tap in range(5):
            fw.dma("sp", wcv[:, :, tap:tap + 1], wconv[tap:tap + 1, :].rearrange("j (c p) -> p c j", p=128), writes=["wcv"],
                   sem_key="wcv", allow_slow_non_contiguous=True)
        fw.dma("sp", bg_b[:], bgates.partition_broadcast(128), writes=["bg_b"])
        fw.dma("sp", sink_b[:], sink.partition_broadcast(128), writes=["sink_b"])
        fw.dma("sp", gml_b[:], gml.partition_broadcast(128), writes=["gml_b"])
        fw.dma("sp", flg[:], flags.partition_broadcast(128), writes=["flg"])

        def conv_gu(src, dst, tag):
            v = src.rearrange("(kc p) (two f) -> p kc two f", p=128, two=2)
            for fc in range(FC):
                for two in range(2):
                    fw.dma("pool", dst[fc].rearrange("p (kc two j) -> p kc two j", kc=KC, two=2)[:, :, two, :],
                           v[:, :, two, fc * 128:(fc + 1) * 128], writes=[tag], sem_key=tag)

        fw.bg_keys.update(["win_s", "wout_s", "w2gu_s", "w2d_s"])
        conv_gu(w1gu, w1gu_s, "w1gu_s")
        def conv_rows(src, dst, nchunk, tag):
            sv = src.rearrange("(c p) d -> p c d", p=128)
            dv = dst.rearrange("p (c d) -> p c d", c=nchunk)
            for c in range(nchunk):
                fw.dma("pool", dv[:, c, :], sv[:, c, :], writes=[tag], sem_key=tag)

        conv_rows(w1d, w1d_s, FC, "w1d_s")
        win_v = win.rearrange("(kc p) c -> p kc c", p=128)
        wins_v = win_s.rearrange("p (kc c) -> p kc c", kc=KC)
        for kc in range(KC):
            for two in range(2):
                fw.dma("pool", wins_v[:, kc, 0:512].rearrange("p (c two j) -> p c two j", two=2, j=64)[:, :, two, :],
                       win_v[:, kc, two * 256:(two + 1) * 256].rearrange("p (c j) -> p c j", j=64),
                       writes=["win_s"], sem_key="win_s")
        for kc in range(KC):
            fw.dma("pool", wins_v[:, kc, 512:IN_W], win_v[:, kc, 512:IN_W], writes=["win_s"], sem_key="win_s")
        conv_rows(wout, wout_s, 8, "wout_s")
        conv_gu(w2gu, w2gu_s, "w2gu_s")
        conv_rows(w2d, w2d_s, FC, "w2d_s")

        if stop == "conv":
            fw.finish()
            return nc
        with ExitStack() as st:
            tab = sb(st, "tab", [33, 8]); tab_hi = sb(st, "tab_hi", [33, 8], BF16)
            tab_r = sb(st, "tab_r", [33, 8]); tab_lo = sb(st, "tab_lo", [33, 8], BF16)
            oh = sb(st, "oh", [33, 640]); oh_b = sb(st, "oh_b", [33, 640], BF16)
            fsb = sb(st, "fsb", [8, 640])
            pf = ps(st, "pf", [8, 1024])
            fw.op("dve", lambda e: e.memset(tab[:], NEG), writes=["tab"])
            fw.dma("sp", tab[0:32, :], reltab, reads=[], writes=["tab"])
            fw.dma("sp", oh[:], onehot, writes=["oh"])
            fw.op("dve", lambda e: e.tensor_copy(out=oh_b[:], in_=oh[:]), reads=["oh"], writes=["oh_b"])
            fw.op("dve", lambda e: e.tensor_copy(out=tab_hi[:], in_=tab[:]), reads=["tab"], writes=["tab_hi"])
            fw.op("dve", lambda e: e.tensor_tensor(out=tab_r[:], in0=tab[:], in1=tab_hi[:], op=ALU.subtract),
                  reads=["tab", "tab_hi"], writes=["tab_r"])
            fw.op("dve", lambda e: e.tensor_copy(out=tab_lo[:], in_=tab_r[:]), reads=["tab_r"], writes=["tab_lo"])

            def fmm(e):
                r = None
                for half in range(2):
                    sl = slice(half * 512, min(640, (half + 1) * 512))
                    e.matmul(pf[:, sl], lhsT=tab_hi[:], rhs=oh_b[:, sl], start=True, stop=False)
                    r = e.matmul(pf[:, sl], lhsT=tab_lo[:], rhs=oh_b[:, sl], start=False, stop=True)
                return r
            fw.op("pe", fmm, reads=["tab_hi", "tab_lo", "oh_b"], writes=["pf"])
            fw.op("dve", lambda e: e.tensor_copy(out=fsb[:], in_=pf[:, 0:640]), reads=["pf"], writes=["fsb"])
            fw.dma("sp", fd_s, fsb[:], reads=["fsb"], writes=["fd_s"])
            for q in range(128):
                src = bass.AP(fd_s.tensor, 127 - q, [[0, 1], [640, 8], [1, 384]])
                fw.dma("sp", bias[q:q + 1, :, :], src, reads=["fd_s"], writes=["bias"], sem_key="bias")
            fw.barrier()

        if stop == "bias":
            fw.finish()
            return nc

        def ffn_phase(st, tag):
            B = {}
            B["xn"] = sb(st, tag + "xn", [128, GT, DM], BF16)
            B["junk"] = sb(st, tag + "junk", [128, DM], BF16)
            B["ss"] = sb(st, tag + "ss", [128, GT])
            B["rstd"] = sb(st, tag + "rstd", [128, GT])
            B["xnT"] = sb(st, tag + "xnT", [128, KC, GT * 128], BF16)
            B["actT"] = sb(st, tag + "actT", [128, FC, GT * 128], BF16)
            B["sg"] = [sb(st, tag + "sg%d" % i, [128, GT * 128]) for i in range(2)]
            B["wgu"] = [sb(st, tag + "wgu%d" % i, [128, KC, 2, 128], BF16) for i in range(3)]
            B["wd"] = sb(st, tag + "wd", [128, FC, DM], BF16)
            B["ptp"] = [ps(st, tag + "ptp%d" % i, [128, 8, 128], BF16) for i in range(2)]
            B["pg"] = [ps(st, tag + "pg%d" % i, [128, 512]) for i in range(2)]
            B["pu"] = [ps(st, tag + "pu%d" % i, [128, 512]) for i in range(2)]
            B["pd"] = [ps(st, tag + "pd%d" % i, [128, 512]) for i in range(2)]
            B["cnt"] = 0
            return B

        def rms_stats(x_t, xkey, nt, B, D):
            for i in range(nt):
                fw.op("act", lambda e, i=i: e.activation(out=B["junk"][:, 0:D], in_=x_t[:, i, :], func=AF.Square,
                                                         accum_out=B["ss"][:, i:i + 1]),
                      reads=[xkey], writes=["junk", "ss"])
            fw.op("act", lambda e: e.activation(out=B["rstd"][:, 0:nt], in_=B["ss"][:, 0:nt], func=AF.Sqrt,
                                                scale=1.0 / D, bias=EPS), reads=["ss"], writes=["rstd"])
            fw.op("dve", lambda e: e.reciprocal(out=B["rstd"][:, 0:nt], in_=B["rstd"][:, 0:nt]),
                  reads=["rstd"], writes=["rstd"])

        def norm_T(x_t, xkey, nt, B, gidx, dstT, dkey, col0=0):
            rms_stats(x_t, xkey, nt, B, DM)
            for i in range(nt):
                fw.op("dve", lambda e, i=i: e.tensor_scalar(out=B["xn"][:, i, :], in0=x_t[:, i, :],
                                                             scalar1=B["rstd"][:, i:i + 1], scalar2=None, op0=ALU.mult),
                      reads=[xkey, "rstd"], writes=["xn"])

                if isinstance(B["ptp"], list):
                    ptb, ptk = B["ptp"][i % 2], "ptp%d" % (i % 2)
                else:
                    ptb, ptk = B["ptp"], "ptp"

                def tr(e, i=i, ptb=ptb):
                    r = None
                    for kc in range(KC):
                        r = e.transpose(out=ptb[:, kc, :], in_=B["xn"][:, i, kc * 128:(kc + 1) * 128],
                                        identity=ident[:])
                    return r
                fw.op("pe", tr, reads=["xn", "ident"], writes=[ptk])
                fw.op("dve", lambda e, i=i, ptb=ptb: e.tensor_tensor(
                    out=dstT[:, :, col0 + i * 128: col0 + (i + 1) * 128], in0=ptb[:, 0:KC, :],
                    in1=gT[:, gidx, :].unsqueeze(2).to_broadcast([128, KC, 128]), op=ALU.mult),
                    reads=[ptk, "gT"], writes=[dkey])

        def ffn_group(B, x_t, xkey, nt, gidx, wgu_scr, wgukey, wd_scr, wdkey, out_t, okey):
            N = nt * 128
            if not B.get("wd_loaded"):
                fw.dma("sp", B["wd"][:], wd_scr.rearrange("p (fc d) -> p fc d", fc=FC), reads=[wdkey], writes=["wd"])
                B["wd_loaded"] = True
            norm_T(x_t, xkey, nt, B, gidx, B["xnT"], "xnT")
            for fc in range(FC):
                c = B["cnt"]; B["cnt"] += 1
                wb = B["wgu"][c % 3]; wk = "wgu%d" % (c % 3)
                pg = B["pg"][c % 2]; pu = B["pu"][c % 2]; sg = B["sg"][c % 2]
                pgk, puk, sgk = "pg%d" % (c % 2), "pu%d" % (c % 2), "sg%d" % (c % 2)
                fw.dma("sp", wb[:], wgu_scr[fc].rearrange("p (kc two j) -> p kc two j", kc=KC, two=2),
                       reads=[wgukey], writes=[wk])

                def mm(e, wb=wb, pg=pg, pu=pu):
                    r = None
                    for two, pp in ((0, pg), (1, pu)):
                        for kc in range(KC):
                            r = e.matmul(pp[:, 0:N], lhsT=wb[:, kc, two, :], rhs=B["xnT"][:, kc, 0:N],
                                         start=(kc == 0), stop=(kc == KC - 1))
                    return r
                fw.op("pe", mm, reads=[wk, "xnT"], writes=[pgk, puk])
                fw.op("act", lambda e, pg=pg, sg=sg: e.activation(out=sg[:, 0:N], in_=pg[:, 0:N], func=AF.Silu),
                      reads=[pgk], writes=[sgk])
                fw.op("dve", lambda e, pu=pu, sg=sg, fc=fc: e.tensor_tensor(out=B["actT"][:, fc, 0:N], in0=sg[:, 0:N],
                                                                          in1=pu[:, 0:N], op=ALU.mult),
                      reads=[sgk, puk], writes=["actT"])
            for i in range(nt):
                for dh in range(DM // 512):
                    c = B["cnt"]; B["cnt"] += 1
                    pd = B["pd"][c % 2]; pdk = "pd%d" % (c % 2)

                    def mm2(e, i=i, dh=dh, pd=pd):
                        r = None
                        for fc in range(FC):
                            r = e.matmul(pd[:], lhsT=B["actT"][:, fc, i * 128:(i + 1) * 128],
                                         rhs=B["wd"][:, fc, dh * 512:(dh + 1) * 512],
                                         start=(fc == 0), stop=(fc == FC - 1))
                        return r
                    fw.op("pe", mm2, reads=["actT", "wd"], writes=[pdk])
                    fw.op("dve", lambda e, i=i, dh=dh, pd=pd: e.scalar_tensor_tensor(
                        out=out_t[:, i, dh * 512:(dh + 1) * 512], in0=pd[:], scalar=0.5,
                        in1=x_t[:, i, dh * 512:(dh + 1) * 512], op0=ALU.mult, op1=ALU.add),
                        reads=[pdk, xkey], writes=[okey])

        def phase1(x_src, h_dst, ntiles):
            with ExitStack() as st:
                B = ffn_phase(st, "p1")
                xt = [sb(st, "p1x%d" % i, [128, GT, DM]) for i in range(2)]
                ht = sb(st, "p1h", [128, GT, DM])
                g0 = 0; gi = 0
                while g0 < ntiles:
                    nt = min(GT, ntiles - g0)
                    x_t = xt[gi % 2]; xk = "x%d" % (gi % 2)
                    fw.dma("sp", x_t[:, 0:nt, :], x_src[g0 * 128:(g0 + nt) * 128, :].rearrange("(i p) d -> p i d", p=128),
                           writes=[xk])
                    ffn_group(B, x_t, xk, nt, 0, w1gu_s, "w1gu_s", w1d_s, "w1d_s", ht, "ht")
                    fw.dma("pool", h_dst[g0 * 128:(g0 + nt) * 128, :].rearrange("(i p) d -> p i d", p=128), ht[:, 0:nt, :],
                           reads=["ht"], writes=["hdst"])
                    g0 += nt; gi += 1
                fw.barrier()

        def phase3(h_src, mix_src, y_dst, ntiles):
            with ExitStack() as st:
                B = ffn_phase(st, "p3")
                hin = sb(st, "p3hin", [128, GT, DM])
                h2 = sb(st, "p3h2", [128, GT, DM])
                mt = sb(st, "p3mt", [128, GT, 1024], BF16)
                mT = sb(st, "p3mT", [128, 8, GT * 128], BF16)
                wo = sb(st, "p3wo", [128, 8, DM], BF16)
                fw.dma("sp", wo[:], wout_s.rearrange("p (cc d) -> p cc d", cc=8), reads=["wout_s"], writes=["wo"])
                g0 = 0
                while g0 < ntiles:
                    nt = min(GT, ntiles - g0)
                    rows = slice(g0 * 128, (g0 + nt) * 128)
                    fw.dma("sp", hin[:, 0:nt, :], h_src[rows, :].rearrange("(i p) d -> p i d", p=128), writes=["hin"])
                    fw.dma("sp", mt[:, 0:nt, :], mix_src[rows, :].rearrange("(i p) d -> p i d", p=128), writes=["mt"])
                    for i in range(nt):
                        ptb, ptk = B["ptp"][i % 2], "ptp%d" % (i % 2)

                        def tr(e, i=i, ptb=ptb):
                            r = None
                            for cc in range(8):
                                r = e.transpose(out=ptb[:, cc, :], in_=mt[:, i, cc * 128:(cc + 1) * 128], identity=ident[:])
                            return r
                        fw.op("pe", tr, reads=["mt", "ident"], writes=[ptk])
                        fw.op("act", lambda e, i=i, ptb=ptb: e.copy(out=mT[:, :, i * 128:(i + 1) * 128], in_=ptb[:, 0:8, :]),
                              reads=[ptk], writes=["mT"])
                    for i in range(nt):
                        for dh in range(DM // 512):
                            c = B["cnt"]; B["cnt"] += 1
                            pd = B["pd"][c % 2]; pdk = "pd%d" % (c % 2)

                            def mm(e, i=i, dh=dh, pd=pd):
                                r = None
                                for cc in range(8):
                                    r = e.matmul(pd[:], lhsT=mT[:, cc, i * 128:(i + 1) * 128],
                                                 rhs=wo[:, cc, dh * 512:(dh + 1) * 512], start=(cc == 0), stop=(cc == 7))
                                return r
                            fw.op("pe", mm, reads=["mT", "wo"], writes=[pdk])
                            fw.op("dve", lambda e, i=i, dh=dh, pd=pd: e.tensor_tensor(
                                out=h2[:, i, dh * 512:(dh + 1) * 512], in0=pd[:], in1=hin[:, i, dh * 512:(dh + 1) * 512],
                                op=ALU.add), reads=[pdk, "hin"], writes=["h2"])
                    ffn_group(B, h2, "h2", nt, 2, w2gu_s, "w2gu_s", w2d_s, "w2d_s", h2, "h2")
                    rms_stats(h2, "h2", nt, B, DM)
                    for i in range(nt):
                        fw.op("dve", lambda e, i=i: e.scalar_tensor_tensor(
                            out=h2[:, i, :], in0=h2[:, i, :], scalar=B["rstd"][:, i:i + 1], in1=gfin_b[:],
                            op0=ALU.mult, op1=ALU.mult), reads=["h2", "rstd", "gfin_b"], writes=["h2"])
                    fw.dma("pool", y_dst[rows, :].rearrange("(i p) d -> p i d", p=128), h2[:, 0:nt, :],
                           reads=["h2"], writes=["ydst"])
                    g0 += nt
                fw.barrier()

        LN8 = math.log(0.125)
        VS = 80

        def phase2(h_src, mix_dst, prompt, mode, init_state=None, pos=0):
            full = mode == "full"
            off = 1 if prompt else 0
            ntl = NT + 2 * off
            with ExitStack() as st:
                unT = sb(st, "unT", [128, KC, ntl * 128], BF16)
                gts = sb(st, "gts", [128, NT, 32])
                wsb = sb(st, "wsb", [128, KC, 768], BF16)
                ptp2 = ps(st, "p2ptp", [128, 8, 128], BF16)
                NB = {"ptp": ptp2}
                pA = [ps(st, "p2pa%d" % i, [128, 512]) for i in range(7)]
                cs = sb(st, "cs", [128, 2, NT, 8]); es = sb(st, "es", [128, 2, NT, 8])
                rt = sb(st, "rt", [128, 2, NT, 8]); eA = sb(st, "eA", [128, 2, NT, 8])
                Asum = sb(st, "Asum", [128, 2, 8])
                with ExitStack() as us:
                    NBu = {"xn": sb(us, "p2xn", [128, GT, DM], BF16), "junk": sb(us, "p2junk", [128, DM], BF16),
                           "ss": sb(us, "p2ss", [128, GT]), "rstd": sb(us, "p2rstd", [128, GT]), "ptp": ptp2}
                    hld = [sb(us, "p2h%d" % i, [128, GT, DM]) for i in range(2)]
                    g0 = 0; gi = 0
                    while g0 < ntl:
                        nt = min(GT, ntl - g0)
                        ht = hld[gi % 2]; hk = "hld%d" % (gi % 2)
                        fw.dma("sp", ht[:, 0:nt, :], h_src[g0 * 128:(g0 + nt) * 128, :].rearrange("(i p) d -> p i d", p=128),
                               writes=[hk])
                        norm_T(ht, hk, nt, NBu, 1, unT, "unT", col0=g0 * 128)
                        g0 += nt; gi += 1
                    fw.barrier()
                chk("p2a")
                winv = win_s.rearrange("p (kc c) -> p kc c", kc=KC)

                def load_w(c0, c1):
                    fw.dma("sp", wsb[:, :, 0:c1 - c0], winv[:, :, c0:c1], reads=["win_s"], writes=["wsb"])

                def proj_fm(col, ncol, tok0, ntok, dst_ps):
                    def f(e):
                        r = None
                        for kc in range(KC):
                            r = e.matmul(dst_ps[0:ncol, 0:ntok], lhsT=wsb[:, kc, col:col + ncol],
                                         rhs=unT[:, kc, tok0:tok0 + ntok], start=(kc == 0), stop=(kc == KC - 1))
                        return r
                    return f

                def proj_tm(col, ncol, tile, dst_ps):
                    def f(e):
                        r = None
                        for kc in range(KC):
                            r = e.matmul(dst_ps[:, 0:ncol], lhsT=unT[:, kc, tile * 128:(tile + 1) * 128],
                                         rhs=wsb[:, kc, col:col + ncol], start=(kc == 0), stop=(kc == KC - 1))
                        return r
                    return f

                with ExitStack() as gs:
                    load_w(2816, 2848)
                    for i in range(NT):
                        pp = pA[i % 2]; pk = "pa%d" % (i % 2)
                        fw.op("pe", proj_tm(0, 32, i + off, pp), reads=["wsb", "unT"], writes=[pk])
                        fw.op("dve", lambda e, i=i, pp=pp: e.tensor_tensor(out=gts[:, i, :], in0=pp[:, 0:32], in1=bg_b[:],
                                                                          op=ALU.add), reads=[pk, "bg_b"], writes=["gts"])
                    chk("p2b")
                    W = NT * 8
                    g4 = gts[:].rearrange("p n (a h) -> p n a h", a=4)
                    fx = sb(gs, "fx", [128, 2, NT, 8]); t1 = sb(gs, "t1", [128, 2, NT, 8]); t2 = sb(gs, "t2", [128, 2, NT, 8])
                    lf = sb(gs, "lf", [128, 2, NT, 8])
                    parts = [sb(gs, "lfp%d" % i, [128, 2, NT, 8], BF16) for i in range(3)]
                    for d in range(2):
                        fw.op("dve", lambda e, d=d: e.tensor_copy(out=fx[:, d, :, :], in_=g4[:, :, 1 + 2 * d, :]),
                              reads=["gts"], writes=["fx"])
                    fw.op("dve", lambda e: e.tensor_single_scalar(out=t2[:], in_=fx[:], scalar=0.0, op=ALU.min),
                          reads=["fx"], writes=["t2"])
                    fw.op("dve", lambda e: e.scalar_tensor_tensor(out=t1[:], in0=t2[:], scalar=2.0, in1=fx[:],
                                                                  op0=ALU.mult, op1=ALU.subtract),
                          reads=["fx", "t2"], writes=["t1"])
                    fw.op("act", lambda e: e.activation(out=t1[:], in_=t1[:], func=AF.Exp),
                          reads=["t1"], writes=["t1"])
                    fw.op("act", lambda e: e.activation(out=t1[:], in_=t1[:], func=AF.Ln, bias=1.0),
                          reads=["t1"], writes=["t1"])
                    fw.op("dve", lambda e: e.tensor_tensor(out=lf[:], in0=t2[:], in1=t1[:], op=ALU.subtract),
                          reads=["t1", "t2"], writes=["lf"])
                    fw.op("dve", lambda e: e.tensor_copy(out=parts[0][:], in_=lf[:]), reads=["lf"], writes=["lfp0"])
                    fw.op("dve", lambda e: e.tensor_tensor(out=t1[:], in0=lf[:], in1=parts[0][:], op=ALU.subtract),
                          reads=["lf", "lfp0"], writes=["t1"])
                    fw.op("dve", lambda e: e.tensor_copy(out=parts[1][:], in_=t1[:]), reads=["t1"], writes=["lfp1"])
                    fw.op("dve", lambda e: e.tensor_tensor(out=t2[:], in0=t1[:], in1=parts[1][:], op=ALU.subtract),
                          reads=["t1", "lfp1"], writes=["t2"])
                    fw.op("dve", lambda e: e.tensor_copy(out=parts[2][:], in_=t2[:]), reads=["t2"], writes=["lfp2"])
                    pcs = pA[2]

                    def cums(e):
                        r = None
                        for d in range(2):
                            tri = maskF if d == 0 else maskB
                            for (mat, o) in ((tri, d * W), (ones_b, 2 * W + d * W)):
                                for k in range(3):
                                    r = e.matmul(pcs[:, o:o + W], lhsT=mat[:],
                                                 rhs=parts[k][:, d, :, :].rearrange("p n h -> p (n h)"),
                                                 start=(k == 0), stop=(k == 2))
                        return r
                    chk("p2c0")
                    fw.op("pe", cums, reads=["lfp0", "lfp1", "lfp2", "maskF", "maskB", "ones_b"], writes=["pa2"])
                    chk("p2c1")
                    pcv = pcs[:, 0:4 * W].rearrange("p (q d n h) -> p q d n h", q=2, d=2, n=NT)
                    for d in range(2):
                        fw.op("dve", lambda e, d=d: e.tensor_tensor(out=t1[:, d, :, :], in0=g4[:, :, 2 * d, :],
                                                                     in1=pcv[:, 0, d, :, :], op=ALU.subtract),
                              reads=["gts", "pa2"], writes=["t1"])
                    fw.op("act", lambda e: e.activation(out=cs[:], in_=t1[:], func=AF.Exp, bias=LN8),
                          reads=["t1"], writes=["cs"])
                    fw.op("dve", lambda e: e.tensor_tensor(out=t2[:], in0=t1[:], in1=pcv[:, 1, :, :, :], op=ALU.add),
                          reads=["t1", "pa2"], writes=["t2"])
                    fw.op("act", lambda e: e.activation(out=es[:], in_=t2[:], func=AF.Exp, bias=LN8),
                          reads=["t2"], writes=["es"])
                    chk("p2c2")
                    fw.op("act", lambda e: e.activation(out=rt[:], in_=pcv[:, 0, :, :, :], func=AF.Exp),
                          reads=["pa2"], writes=["rt"])
                    fw.op("act", lambda e: e.activation(out=eA[:], in_=pcv[:, 1, :, :, :], func=AF.Exp),
                          reads=["pa2"], writes=["eA"])
                    chk("p2c3")
                    if not full:
                        fw.op("dve", lambda e: e.tensor_copy(out=t1[:], in_=pcv[:, 1, :, :, :]), reads=["pa2"], writes=["t1"])
                        fw.op("dve", lambda e: e.memset(lf[:], 0.0), reads=["lf"], writes=["lf"])
                        for c_ in range(NT - 2, -1, -1):
                            fw.op("dve", lambda e, c_=c_: e.tensor_tensor(out=lf[:, 0, c_, :], in0=lf[:, 0, c_ + 1, :],
                                                                         in1=t1[:, 0, c_ + 1, :], op=ALU.add),
                                  reads=["lf", "t1"], writes=["lf"])
                        for c_ in range(1, NT):
                            fw.op("dve", lambda e, c_=c_: e.tensor_tensor(out=lf[:, 1, c_, :], in0=lf[:, 1, c_ - 1, :],
                                                                         in1=t1[:, 1, c_ - 1, :], op=ALU.add),
                                  reads=["lf", "t1"], writes=["lf"])
                        fw.op("dve", lambda e: e.tensor_tensor(out=t2[:], in0=t2[:], in1=lf[:], op=ALU.add),
                              reads=["lf", "t2", "es"], writes=["t2"])
                        fw.op("act", lambda e: e.activation(out=es[:], in_=t2[:], func=AF.Exp, bias=LN8),
                              reads=["t2"], writes=["es"])
                        chk("p2c4")
                        fw.op("dve", lambda e: e.tensor_copy(out=Asum[:], in_=t1[:, :, 0, :]), reads=["t1"], writes=["Asum"])
                        for n_ in range(1, NT):
                            fw.op("dve", lambda e, n_=n_: e.tensor_tensor(out=Asum[:], in0=Asum[:], in1=t1[:, :, n_, :], op=ALU.add),
                                  reads=["t1", "Asum"], writes=["Asum"])
                    chk("p2c5")
                    fw.barrier()

                chk("p2c")
                if full:
                    with ExitStack() as at:
                        qaT = sb(at, "qaT", [128, 4, S], BF16)
                        kT = sb(at, "kT", [128, ntl * 128], BF16)
                        vtm = sb(at, "vtm", [128, ntl, 128], BF16)
                        ssb = sb(at, "ssb", [128, 4, 384]); pbf = sb(at, "pbf", [128, 4, 384], BF16)
                        pts = [sb(at, "pts%d" % i, [128, 3, 128], BF16) for i in range(2)]
                        mx = sb(at, "mx", [128, 4]); negm = sb(at, "negm", [128, 4]); rs = sb(at, "rs", [128, 4])
                        tmp4 = sb(at, "tmp4", [128, 4]); rinv = sb(at, "rinv", [128, 4])
                        mixa = [sb(at, "mixa%d" % i, [128, 512], BF16) for i in range(2)]
                        load_w(0, 768)
                        TB = 512
                        for c in range(4):
                            for t0 in range(0, S, TB):
                                n = min(TB, S - t0)
                                pp = pA[5 + (c + t0 // TB) % 2]; pk = "pa%d" % (5 + (c + t0 // TB) % 2)
                                fw.op("pe", proj_fm(c * 128, 128, off * 128 + t0, n, pp), reads=["wsb", "unT"], writes=[pk])
                                fw.op("act", lambda e, c=c, t0=t0, n=n, pp=pp: e.mul(
                                    out=qaT[:, c, t0:t0 + n], in_=pp[:, 0:n], mul=0.125),
                                    reads=[pk], writes=["qaT"])
                        for t0 in range(0, ntl * 128, TB):
                            n = min(TB, ntl * 128 - t0)
                            pp = pA[5 + (t0 // TB) % 2]; pk = "pa%d" % (5 + (t0 // TB) % 2)
                            fw.op("pe", proj_fm(512, 128, t0, n, pp), reads=["wsb", "unT"], writes=[pk])
                            fw.op("act", lambda e, t0=t0, n=n, pp=pp: e.copy(out=kT[:, t0:t0 + n], in_=pp[:, 0:n]),
                                  reads=[pk], writes=["kT"])
                        for i in range(ntl):
                            pp = pA[5 + i % 2]; pk = "pa%d" % (5 + i % 2)
                            fw.op("pe", proj_tm(640, 128, i, pp), reads=["wsb", "unT"], writes=[pk])
                            fw.op("act", lambda e, i=i, pp=pp: e.copy(out=vtm[:, i, :], in_=pp[:, 0:128]),
                                  reads=[pk], writes=["vtm"])
                        pO = pA[4]
                        for i in range(NT):
                            ti = i + off
                            lo = ti - 1 if ti - 1 >= 0 else ti
                            hi = ti + 1 if ti + 1 < ntl else ti
                            nk = hi - lo + 1
                            b0 = (lo - (ti - 1)) * 128
                            ma = mixa[i % 2]; mak = "mixa%d" % (i % 2)
                            for g in range(2):
                                def smm(e, g=g, i=i, lo=lo, nk=nk):
                                    r = None
                                    for c in range(4):
                                        r = e.matmul(pA[c][:, 0:nk * 128], lhsT=qaT[g * 64:(g + 1) * 64, c, i * 128:(i + 1) * 128],
                                                     rhs=kT[g * 64:(g + 1) * 64, lo * 128:(lo + nk) * 128], start=True, stop=True)
                                    return r
                                fw.op("pe", smm, reads=["qaT", "kT"], writes=["pa0", "pa1", "pa2", "pa3"])
                                for c in range(4):
                                    fw.op("dve", lambda e, c=c, g=g, nk=nk, b0=b0: e.tensor_tensor(
                                        out=ssb[:, c, 0:nk * 128], in0=pA[c][:, 0:nk * 128],
                                        in1=bias[:, g * 4 + c, b0:b0 + nk * 128], op=ALU.add),
                                        reads=["pa%d" % c, "bias"], writes=["ssb"])
                                if prompt and i == 0:
                                    fw.op("dve", lambda e: e.tensor_scalar(out=ssb[:, :, 0:128], in0=ssb[:, :, 0:128],
                                                                            scalar1=flg[:, 0:1], scalar2=None, op0=ALU.add),
                                          reads=["ssb", "flg"], writes=["ssb"])
                                if prompt and i == NT - 1:
                                    fw.op("dve", lambda e: e.tensor_scalar(out=ssb[:, :, 256:384], in0=ssb[:, :, 256:384],
                                                                            scalar1=flg[:, 1:2], scalar2=None, op0=ALU.add),
                                          reads=["ssb", "flg"], writes=["ssb"])
                                fw.op("dve", lambda e, nk=nk: e.tensor_reduce(out=mx[:], in_=ssb[:, :, 0:nk * 128], axis=AX.X,
                                                                              op=ALU.max), reads=["ssb"], writes=["mx"])
                                fw.op("dve", lambda e, g=g: e.tensor_tensor(out=mx[:], in0=mx[:], in1=sink_b[:, g * 4:(g + 1) * 4],
                                                                             op=ALU.max), reads=["mx", "sink_b"], writes=["mx"])
                                fw.op("dve", lambda e: e.tensor_scalar(out=negm[:], in0=mx[:], scalar1=-1.0, scalar2=None,
                                                                        op0=ALU.mult), reads=["mx"], writes=["negm"])
                                for c in range(4):
                                    fw.op("act", lambda e, c=c, nk=nk: e.activation(
                                        out=pbf[:, c, 0:nk * 128], in_=ssb[:, c, 0:nk * 128], func=AF.Exp,
                                        bias=negm[:, c:c + 1], accum_out=rs[:, c:c + 1]),
                                        reads=["ssb", "negm"], writes=["pbf", "rs"])
                                fw.op("dve", lambda e, g=g: e.tensor_tensor(out=tmp4[:], in0=sink_b[:, g * 4:(g + 1) * 4], in1=mx[:],
                                                                             op=ALU.subtract), reads=["mx", "sink_b"], writes=["tmp4"])
                                fw.op("act", lambda e: e.activation(out=tmp4[:], in_=tmp4[:], func=AF.Exp),
                                      reads=["tmp4"], writes=["tmp4"])
                                fw.op("dve", lambda e: e.tensor_tensor(out=tmp4[:], in0=tmp4[:], in1=rs[:], op=ALU.add),
                                      reads=["tmp4", "rs"], writes=["tmp4"])
                                fw.op("dve", lambda e: e.reciprocal(out=rinv[:], in_=tmp4[:]), reads=["tmp4"], writes=["rinv"])
                                for c in range(4):
                                    pt = pts[c % 2]; ptk = "pts%d" % (c % 2)

                                    def trp(e, c=c, nk=nk):
                                        r = None
                                        for kb in range(nk):
                                            r = e.transpose(out=NB["ptp"][:, kb, :], in_=pbf[:, c, kb * 128:(kb + 1) * 128],
                                                            identity=ident[:])
                                        return r
                                    fw.op("pe", trp, reads=["pbf", "ident"], writes=["ptp"])
                                    fw.op("act", lambda e, pt=pt, nk=nk: e.copy(out=pt[:, 0:nk, :], in_=NB["ptp"][:, 0:nk, :]),
                                          reads=["ptp"], writes=[ptk])

                                    def pv(e, c=c, nk=nk, lo=lo, g=g, pt=pt):
                                        r = None
                                        for kb in range(nk):
                                            r = e.matmul(pO[:, c * 64:(c + 1) * 64], lhsT=pt[:, kb, :],
                                                         rhs=vtm[:, lo + kb, g * 64:(g + 1) * 64],
                                                         start=(kb == 0), stop=(kb == nk - 1))
                                        return r
                                    fw.op("pe", pv, reads=[ptk, "vtm"], writes=["pa4"])
                                fw.op("dve", lambda e, g=g, ma=ma: e.tensor_tensor(
                                    out=ma[:, g * 256:(g + 1) * 256].rearrange("p (c d) -> p c d", c=4),
                                    in0=pO[:, 0:256].rearrange("p (c d) -> p c d", c=4),
                                    in1=rinv[:].unsqueeze(2).to_broadcast([128, 4, 64]), op=ALU.mult),
                                    reads=["pa4", "rinv"], writes=[mak])
                            fw.dma("sp", mix_dst[i * 128:(i + 1) * 128, 0:512], ma[:], reads=[mak], writes=["mixdst"])
                        fw.barrier()

                if full:
                    chk("s2a")
                finals, fin_A = finals_t, fin_A_t
                with ExitStack() as ml:
                    pre = [sb(ml, "pre%d" % i, [128, S + 4]) for i in range(2)]
                    acc = sb(ml, "acc", [128, S])
                    qkT = [sb(ml, "qkT%d" % i, [128, S], BF16) for i in range(2)]
                    ktok = sb(ml, "ktok", [128, NT, 128], BF16)
                    v1 = sb(ml, "v1", [128, NT, 2, VS], BF16)
                    v1s = [sb(ml, "v1s%d" % d, [128, NT, 2, VS], BF16) for d in range(2)]
                    v1e = [sb(ml, "v1e%d" % d, [128, NT, 2, VS], BF16) for d in range(2)]
                    og = sb(ml, "og", [128, NT, 128])
                    hsum = sb(ml, "hsum", [128, NT, 128])
                    stt = [sb(ml, "stt%d" % d, [128, 65]) for d in range(2)]
                    stb = [sb(ml, "stb%d" % d, [128, 2, VS], BF16) for d in range(2)]
                    qblk = sb(ml, "qblk", [128, NT, 2, 128], BF16)
                    PTs = [sb(ml, "PT%d" % d, [128, 2, 128], BF16) for d in range(2)]
                    dd = [sb(ml, "dd%d" % d, [128, 2]) for d in range(2)]
                    rr = [sb(ml, "rr%d" % d, [128, 2]) for d in range(2)]
                    msq = sb(ml, "msq", [128, NT, 2])
                    mixm = sb(ml, "mixm", [128, NT, 128], BF16)
                    fw.op("dve", lambda e: e.memset(v1[:], 1.0), writes=["v1"])
                    fw.op("dve", lambda e: e.memset(qblk[:], 0.0), writes=["qblk"])
                    for d in range(2):
                        fw.op("dve", lambda e, d=d: e.memset(stb[d][:], 0.0), writes=["stb%d" % d])
                    for j in range(4):
                        for bi, c0 in enumerate((768, 1280, 1792, 2304)):
                            fw.dma("sp", wsb[:, :, bi * 128:(bi + 1) * 128], winv[:, :, c0 + j * 128:c0 + (j + 1) * 128],
                                   reads=["win_s"], writes=["wsb"])
                        for qi in range(2):
                            if qi == 0 and not full:
                                continue
                            pr = pre[qi]; prk = "pre%d" % qi
                            if prompt:
                                for (t0, n, dcol) in ((off * 128 - 2, 2, 0), ((off + NT) * 128, 2, S + 2)):
                                    pp = pA[5]; pk = "pa5"
                                    fw.op("pe", proj_fm(qi * 128, 128, t0, n, pp), reads=["wsb", "unT"], writes=[pk])
                                    fw.op("act", lambda e, pr=pr, dcol=dcol, pp=pp: e.copy(out=pr[:, dcol:dcol + 2], in_=pp[:, 0:2]),
                                          reads=[pk], writes=[prk])
                                    fcol_ = (18 + pos) if dcol == 0 else (26 + pos)
                                    fw.op("dve", lambda e, pr=pr, dcol=dcol, fcol_=fcol_: e.tensor_scalar(
                                        out=pr[:, dcol:dcol + 2], in0=pr[:, dcol:dcol + 2], scalar1=flg[:, fcol_:fcol_ + 1],
                                        scalar2=None, op0=ALU.mult), reads=[prk, "flg"], writes=[prk])
                            else:
                                fw.op("dve", lambda e, pr=pr: e.memset(pr[:, 0:2], 0.0), writes=[prk])
                                fw.op("dve", lambda e, pr=pr: e.memset(pr[:, S + 2:S + 4], 0.0), writes=[prk])
                            for t0 in range(0, S, 512):
                                n = min(512, S - t0)
                                pp = pA[5 + (t0 // 512) % 2]; pk = "pa%d" % (5 + (t0 // 512) % 2)
                                fw.op("pe", proj_fm(qi * 128, 128, off * 128 + t0, n, pp), reads=["wsb", "unT"], writes=[pk])
                                fw.op("act", lambda e, pr=pr, t0=t0, n=n, pp=pp: e.copy(out=pr[:, 2 + t0:2 + t0 + n], in_=pp[:, 0:n]),
                                      reads=[pk], writes=[prk])
                            ch = qi * 4 + j
                            fw.op("dve", lambda e, pr=pr, ch=ch: e.tensor_scalar(out=acc[:], in0=pr[:, 0:S], scalar1=wcv[:, ch, 0:1],
                                                                                  scalar2=None, op0=ALU.mult),
                                  reads=[prk, "wcv"], writes=["acc"])
                            for tap in range(1, 5):
                                fw.op("dve", lambda e, pr=pr, ch=ch, tap=tap: e.scalar_tensor_tensor(
                                    out=acc[:], in0=pr[:, tap:tap + S], scalar=wcv[:, ch, tap:tap + 1], in1=acc[:],
                                    op0=ALU.mult, op1=ALU.add), reads=[prk, "wcv", "acc"], writes=["acc"])
                            fw.op("act", lambda e, qi=qi: e.activation(out=qkT[qi][:], in_=acc[:], func=AF.Silu),
                                  reads=["acc"], writes=["qkT%d" % qi])
                            if qi == 0 and full:
                                for hh in range(2):
                                    hs = slice(hh * 64, (hh + 1) * 64)
                                    fw.op("act", lambda e, hh=hh, hs=hs: e.activation(
                                        out=qblk[hs, :, hh, :], in_=acc[hs, :].rearrange("p (n t) -> p n t", t=128), func=AF.Silu),
                                        reads=["acc"], writes=["qblk"])
                        chk("p2d")
                        for i in range(NT):
                            fw.op("pe", lambda e, i=i: e.transpose(out=NB["ptp"][:, i % 8, :], in_=qkT[1][:, i * 128:(i + 1) * 128],
                                                                    identity=ident[:]), reads=["qkT1", "ident"], writes=["ptp"])
                            fw.op("act", lambda e, i=i: e.copy(out=ktok[:, i, :], in_=NB["ptp"][:, i % 8, :]),
                                  reads=["ptp"], writes=["ktok"])
                        for i in range(NT):
                            pp = pA[5 + i % 2]; pk = "pa%d" % (5 + i % 2)
                            fw.op("pe", proj_tm(256, 256, i + off, pp), reads=["wsb", "unT"], writes=[pk])
                            fw.op("dve", lambda e, i=i, pp=pp: e.tensor_copy(
                                out=v1[:, i, :, 0:64], in_=pp[:, 0:128].rearrange("p (a d) -> p a d", a=2)),
                                reads=[pk], writes=["v1"])
                            if full:
                                fw.op("act", lambda e, i=i, pp=pp: e.activation(out=og[:, i, :], in_=pp[:, 128:256], func=AF.Sigmoid),
                                      reads=[pk], writes=["og"])
                        for d in range(2):
                            for (dst, scal, dk, sk) in ((v1s[d], cs, "v1s%d" % d, "cs"), (v1e[d], es, "v1e%d" % d, "es")):
                                if dst is v1s[d] and not full:
                                    continue
                                fw.op("dve", lambda e, dst=dst, scal=scal, d=d: e.tensor_tensor(
                                    out=dst[:], in0=v1[:],
                                    in1=scal[:, d, :, 2 * j:2 * j + 2].unsqueeze(3).to_broadcast([128, NT, 2, VS]),
                                    op=ALU.mult), reads=["v1", sk], writes=[dk])
                        chk("p2e")
                        if not full:
                            for d in range(2):
                                pC = pA[d]; pCk = "pa%d" % d

                                def accmm(e, d=d, pC=pC):
                                    r = None
                                    for c in range(NT):
                                        r = e.matmul(pC[:, 0:2 * VS], lhsT=ktok[:, c, :],
                                                     rhs=v1e[d][:, c, :, :].rearrange("p a c -> p (a c)"),
                                                     start=(c == 0), stop=(c == NT - 1))
                                    return r
                                fw.op("pe", accmm, reads=["ktok", "v1e%d" % d], writes=[pCk])
                                for hh in range(2):
                                    hs = slice(hh * 64, (hh + 1) * 64)
                                    fw.op("dve", lambda e, d=d, hh=hh, hs=hs, pC=pC: e.tensor_copy(
                                        out=finals[hs, j, d, :], in_=pC[hs, hh * VS:hh * VS + 65]),
                                        reads=[pCk], writes=["finals"])
                                    fw.op("dve", lambda e, d=d, hh=hh, hs=hs: e.tensor_copy(
                                        out=fin_A[hs, j, d:d + 1], in_=Asum[hs, d, 2 * j + hh:2 * j + hh + 1]),
                                        reads=["Asum"], writes=["fin_A"])
                            continue
                        for d in range(2):
                            if init_state is not None:
                                fw.op("dve", lambda e, d=d: e.tensor_copy(out=stt[d][:], in_=init_state[:, j, d, :]),
                                      reads=["init_state"], writes=["stt%d" % d])
                            else:
                                fw.op("dve", lambda e, d=d: e.memset(stt[d][:], 0.0), writes=["stt%d" % d])
                            for hh in range(2):
                                hs = slice(hh * 64, (hh + 1) * 64)
                                fw.op("act", lambda e, d=d, hh=hh, hs=hs: e.copy(out=stb[d][hs, hh, 0:65], in_=stt[d][hs, :]),
                                      reads=["stt%d" % d], writes=["stb%d" % d])
                        if full:
                            chk("m0")
                        for step in range(NT):
                            if full and step == 1:
                                chk("m1")
                            for d in range(2):
                                c = step if d == 0 else NT - 1 - step
                                pS, pN = pA[0 + d], pA[2 + d]
                                pSk, pNk = "pa%d" % d, "pa%d" % (2 + d)
                                pC = pA[4]; pCk = "pa4"
                                mk_ = maskF if d == 0 else maskB
                                if full:
                                    def smm(e, c=c, pS=pS):
                                        return e.matmul(pS[:, 0:256], lhsT=qkT[1][:, c * 128:(c + 1) * 128],
                                                        rhs=qblk[:, c, :, :].rearrange("p a t -> p (a t)"), start=True, stop=True)
                                    fw.op("pe", smm, reads=["qblk", "qkT1"], writes=[pSk])
                                    chk("q1")
                                    fw.op("dve", lambda e, d=d, pS=pS, mk_=mk_: e.tensor_tensor(
                                        out=PTs[d][:], in0=pS[:, 0:256].rearrange("p (a t) -> p a t", a=2),
                                        in1=mk_[:].unsqueeze(1).to_broadcast([128, 2, 128]), op=ALU.mult),
                                        reads=[pSk, "maskF", "maskB"], writes=["PT%d" % d])
                                    chk("q2")

                                    def nmm(e, c=c, d=d, pN=pN):
                                        e.matmul(pN[:, 0:2 * VS], lhsT=qkT[0][:, c * 128:(c + 1) * 128],
                                                 rhs=stb[d][:].rearrange("p a c -> p (a c)"), start=True, stop=False)
                                        r = None
                                        for hh in range(2):
                                            r = e.matmul(pN[:, hh * VS:(hh + 1) * VS], lhsT=PTs[d][:, hh, :], rhs=v1s[d][:, c, hh, :],
                                                         start=False, stop=(hh == 1))
                                        return r
                                    fw.op("pe", nmm, reads=["PT%d" % d, "v1s%d" % d, "qkT0", "stb%d" % d], writes=[pNk])
                                    chk("q3")
                                    pNv = pN[:, 0:2 * VS].rearrange("p (a c) -> p a c", a=2)
                                    rtv = rt[:, d, c, 2 * j:2 * j + 2]
                                    fw.op("dve", lambda e, d=d, pNv=pNv, rtv=rtv: e.tensor_tensor(
                                        out=dd[d][:].unsqueeze(2), in0=pNv[:, :, 64:65], in1=rtv.unsqueeze(2), op=ALU.mult),
                                        reads=[pNk, "rt"], writes=["dd%d" % d])
                                    fw.op("dve", lambda e, d=d: e.scalar_tensor_tensor(out=rr[d][:], in0=dd[d][:], scalar=-1.0,
                                                                                        in1=dd[d][:], op0=ALU.mult, op1=ALU.max),
                                          reads=["dd%d" % d], writes=["rr%d" % d])
                                    fw.op("dve", lambda e, d=d: e.tensor_scalar(out=rr[d][:], in0=rr[d][:], scalar1=1.0, scalar2=None,
                                                                                 op0=ALU.max),
                                          reads=["rr%d" % d], writes=["rr%d" % d])
                                    fw.op("dve", lambda e, d=d: e.reciprocal(out=rr[d][:], in_=rr[d][:]),
                                          reads=["rr%d" % d], writes=["rr%d" % d])
                                    fw.op("dve", lambda e, d=d, rtv=rtv: e.tensor_tensor(out=rr[d][:], in0=rtv, in1=rr[d][:],
                                                                                        op=ALU.mult),
                                          reads=["rr%d" % d, "rt"], writes=["rr%d" % d])
                                    chk("q4")
                                    step_f, step_b = c, NT - 1 - c
                                    first = (step_f < step_b) if d == 0 else (step_b < step_f)
                                    if step_f == step_b:
                                        first = (d == 0)
                                    for hh in range(2):
                                        if first:
                                            fw.op("dve", lambda e, d=d, c=c, hh=hh, pNv=pNv: e.tensor_scalar(
                                                out=hsum[:, c, hh * 64:(hh + 1) * 64], in0=pNv[:, hh, 0:64],
                                                scalar1=rr[d][:, hh:hh + 1], scalar2=None, op0=ALU.mult),
                                                reads=[pNk, "rr%d" % d], writes=["hsum"])
                                        else:
                                            fw.op("dve", lambda e, d=d, c=c, hh=hh, pNv=pNv: e.scalar_tensor_tensor(
                                                out=hsum[:, c, hh * 64:(hh + 1) * 64], in0=pNv[:, hh, 0:64],
                                                scalar=rr[d][:, hh:hh + 1], in1=hsum[:, c, hh * 64:(hh + 1) * 64],
                                                op0=ALU.mult, op1=ALU.add), reads=[pNk, "rr%d" % d, "hsum"], writes=["hsum"])
                                fw.op("pe", lambda e, c=c, d=d: e.matmul(
                                    pC[:, 0:2 * VS], lhsT=ktok[:, c, :], rhs=v1e[d][:, c, :, :].rearrange("p a c -> p (a c)"),
                                    start=True, stop=True), reads=["ktok", "v1e%d" % d], writes=[pCk])
                                for hh in range(2):
                                    hs = slice(hh * 64, (hh + 1) * 64)
                                    fw.op("dve", lambda e, d=d, c=c, hh=hh, hs=hs: e.scalar_tensor_tensor(
                                        out=stt[d][hs, :], in0=stt[d][hs, :], scalar=eA[hs, d, c, 2 * j + hh:2 * j + hh + 1],
                                        in1=pC[hs, hh * VS:hh * VS + 65], op0=ALU.mult, op1=ALU.add),
                                        reads=["stt%d" % d, "eA", pCk], writes=["stt%d" % d])
                                for hh in range(2):
                                    hs = slice(hh * 64, (hh + 1) * 64)
                                    fw.op("act", lambda e, d=d, hh=hh, hs=hs: e.copy(out=stb[d][hs, hh, 0:65], in_=stt[d][hs, :]),
                                          reads=["stt%d" % d], writes=["stb%d" % d])
                        if full:
                            chk("m2")
                            fw.op("dve", lambda e: e.tensor_tensor(out=hsum[:], in0=hsum[:], in1=og[:], op=ALU.mult),
                                  reads=["hsum", "og"], writes=["hsum"])
                            fw.op("dve", lambda e: e.tensor_tensor(out=og[:], in0=hsum[:], in1=hsum[:], op=ALU.mult),
                                  reads=["hsum", "og"], writes=["og"])
                            fw.op("dve", lambda e: e.tensor_reduce(out=msq[:], in_=og[:].rearrange("p n (a d) -> p n a d", a=2),
                                                                   axis=AX.X, op=ALU.add), reads=["og"], writes=["msq"])
                            fw.op("act", lambda e: e.activation(out=msq[:], in_=msq[:], func=AF.Sqrt, scale=1.0 / 64, bias=EPS),
                                  reads=["msq"], writes=["msq"])
                            fw.op("dve", lambda e: e.reciprocal(out=msq[:], in_=msq[:]), reads=["msq"], writes=["msq"])
                            fw.op("dve", lambda e: e.tensor_tensor(
                                out=hsum[:].rearrange("p n (a d) -> p n a d", a=2), in0=hsum[:].rearrange("p n (a d) -> p n a d", a=2),
                                in1=msq[:].unsqueeze(3).to_broadcast([128, NT, 2, 64]), op=ALU.mult),
                                reads=["hsum", "msq"], writes=["hsum"])
                            fw.op("dve", lambda e: e.tensor_tensor(
                                out=mixm[:], in0=hsum[:],
                                in1=gml_b[:, j * 128:(j + 1) * 128].unsqueeze(1).to_broadcast([128, NT, 128]), op=ALU.mult),
                                reads=["hsum", "gml_b"], writes=["mixm"])
                            chk("m3")
                            fw.dma("sp", mix_dst[:, 512 + j * 128:512 + (j + 1) * 128].rearrange("(n p) c -> p n c", p=128),
                                   mixm[:], reads=["mixm"], writes=["mixdst"])
                            chk("m4")
                        else:
                            for d in range(2):
                                fw.op("dve", lambda e, d=d: e.tensor_copy(out=finals[:, j, d, :], in_=stt[d][:]),
                                      reads=["stt%d" % d], writes=["finals"])
                                for hh in range(2):
                                    hs = slice(hh * 64, (hh + 1) * 64)
                                    fw.op("dve", lambda e, d=d, hh=hh, hs=hs: e.tensor_copy(
                                        out=fin_A[hs, j, d:d + 1], in_=Asum[hs, d, 2 * j + hh:2 * j + hh + 1]),
                                        reads=["Asum"], writes=["fin_A"])
                    fw.barrier()
                fw.barrier()
            return (finals, fin_A) if not full else None

        def main_schedule():
            if with_prompt:
                phase1(x_pfull, hbuf_pf, n_ranks * NT + 2)
                chk("p1")
                for r in range(1, n_ranks):
                    finals, fin_A = phase2(hbuf_pf[r * S:r * S + NTP * 128, :], None, True, "summary", pos=r)
                    chk("p2s")
                    fw.dma("sp", g_dst[r * 128:(r + 1) * 128, 0:520], finals[:].rearrange("p a d c -> p (a d c)"),
                           reads=["finals"], writes=["g_dst"])
                    fw.dma("sp", g_dst[r * 128:(r + 1) * 128, 520:528], fin_A[:].rearrange("p a d -> p (a d)"),
                           reads=["fin_A"], writes=["g_dst"])
                    fw.barrier()
                chk("cc")
            for s in range(NSAMP):
                rows = slice(s * S, (s + 1) * S)
                phase1(x_samp[rows, :], hbuf_s[rows, :], NT)
                chk("s1")
                phase2(hbuf_s[rows, :], mix_s[rows, :], False, "full")
                chk("s2")
                phase3(hbuf_s[rows, :], mix_s[rows, :], y_samp[rows, :], NT)
                chk("s3")
            if with_prompt:
                cst = ExitStack()
                gat = sb(cst, "gat", [128, n_ranks, GWc])
                fw.dma("sp", gat[:, 1:n_ranks, :], g_dst[128:n_ranks * 128, :].rearrange("(r p) w -> p r w", p=128),
                       reads=["g_dst"], writes=["gat"])
                fw.op("dve", lambda e: e.memset(ist[:], 0.0), writes=["ist"])
                for d in range(2):
                    order = range(1, n_ranks) if d == 0 else range(n_ranks - 1, 0, -1)
                    for r in order:
                        fcol = flg[:, 2 + 8 * d + r:3 + 8 * d + r]
                        Av = gat[:, r, 520:528].rearrange("p (a d) -> p a d", a=4)[:, :, d:d + 1]
                        Sv = gat[:, r, 0:520].rearrange("p (a d c) -> p a d c", a=4, d=2)[:, :, d, :]
                        fw.op("dve", lambda e, d=d, Av=Av, fcol=fcol: e.tensor_scalar(out=dec[:, :, d:d + 1], in0=Av, scalar1=fcol,
                                                                                      scalar2=None, op0=ALU.mult),
                              reads=["gat", "flg"], writes=["dec"])
                        fw.op("act", lambda e, d=d: e.activation(out=dec[:, :, d:d + 1], in_=dec[:, :, d:d + 1], func=AF.Exp),
                              reads=["dec"], writes=["dec"])
                        fw.op("dve", lambda e, d=d: e.tensor_tensor(out=ist[:, :, d, :], in0=ist[:, :, d, :],
                                                                     in1=dec[:, :, d:d + 1].to_broadcast([128, 4, 65]), op=ALU.mult),
                              reads=["ist", "dec"], writes=["ist"])
                        fw.op("dve", lambda e, d=d, Sv=Sv, fcol=fcol: e.scalar_tensor_tensor(
                            out=ist[:, :, d, :], in0=Sv, scalar=fcol, in1=ist[:, :, d, :], op0=ALU.mult, op1=ALU.add),
                            reads=["ist", "gat", "flg"], writes=["ist"])
                fw.barrier()
                cst.close()
                chk("comb")
                fw.res["init_state"] = _Res()
                phase2(hbuf_pf[0:NTP * 128, :], mix_p, True, "full", init_state=ist, pos=0)
                phase3(hbuf_pf[128:128 + S, :], mix_p, y_prm, NT)


        try:
            main_schedule()
        except _Stop:
            pass
        fw.finish()
    return nc


_CACHE = {}


def kernel(x_prompt, x_sample, g_ffn1, w_ffn1_gu, w_ffn1_down, g_mix, w_in, w_conv, b_gates,
           attn_sink, g_mlstm_out, w_out, g_ffn2, w_ffn2_gu, w_ffn2_down, rel_bias_table, g_final):
    f32 = np.float32
    S = 2048
    DM = 1024
    n = N_CORES
    x_prompt = np.asarray(x_prompt, f32)
    x_sample = np.asarray(x_sample, f32)
    if "nc" not in _CACHE:
        _CACHE["nc"] = build_program()
    nc = _CACHE["nc"]
    xp = x_prompt.reshape(-1, DM)
    gains = np.stack([np.asarray(g_ffn1, f32)[0], np.asarray(g_mix, f32)[0], np.asarray(g_ffn2, f32)[0],
                      np.asarray(g_final, f32)])
    common = {
        "w1gu": np.ascontiguousarray(np.asarray(w_ffn1_gu, f32)[0]),
        "w1d": np.ascontiguousarray(np.asarray(w_ffn1_down, f32)[0]),
        "w2gu": np.ascontiguousarray(np.asarray(w_ffn2_gu, f32)[0]),
        "w2d": np.ascontiguousarray(np.asarray(w_ffn2_down, f32)[0]),
        "win": np.ascontiguousarray(np.asarray(w_in, f32)[0]),
        "wout": np.ascontiguousarray(np.asarray(w_out, f32)[0]),
        "gains": np.ascontiguousarray(gains),
        "wconv": np.ascontiguousarray(np.asarray(w_conv, f32)[0]),
        "bgates": np.ascontiguousarray(np.asarray(b_gates, f32)[0].reshape(1, 32)),
        "sink": np.ascontiguousarray(np.asarray(attn_sink, f32).reshape(1, 8)),
        "gml": np.ascontiguousarray(np.asarray(g_mlstm_out, f32).reshape(1, 512)),
        "reltab": np.ascontiguousarray(np.asarray(rel_bias_table, f32)),
        "onehot": _bucket_onehot(),
    }
    in_maps = []
    for c in range(n):
        fl = np.zeros((1, 34), f32)
        fl[0, 0] = -30000.0 if c == 0 else 0.0
        fl[0, 1] = -30000.0 if c == n - 1 else 0.0
        xr = np.empty((n * S + 256, DM), f32)
        xr[128:128 + n * S] = np.roll(xp, -c * S, axis=0)
        xr[0:128] = xr[n * S:n * S + 128]
        xr[128 + n * S:] = xr[128:256]
        if c == 0:
            xr[0:128] = 0.0
            xr[128 + n * S:] = 0.0
        for k in range(n):
            r = (c + k) % n
            fl[0, 2 + k] = 1.0 if r < c else 0.0
            fl[0, 10 + k] = 1.0 if r > c else 0.0
            fl[0, 18 + k] = 0.0 if r == 0 else 1.0
            fl[0, 26 + k] = 0.0 if r == n - 1 else 1.0
        m = dict(common)
        m["x_samp"] = np.ascontiguousarray(x_sample[4 * c:4 * c + 4].reshape(4 * S, DM))
        m["x_pfull"] = xr
        m["flags"] = fl
        in_maps.append(m)
    res = run_bass_kernel_spmd(nc, in_maps, core_ids=list(range(n)))
    y_p = np.concatenate([np.asarray(res.results[c]["y_prm"], f32) for c in range(n)], axis=0).reshape(x_prompt.shape)
    y_s = np.concatenate([np.asarray(res.results[c]["y_samp"], f32).reshape(4, S, DM) for c in range(n)], axis=0)
    return (y_p, y_s.reshape(x_sample.shape))
```
